# Optimizing a Trainium2 kernel written in Bass

```python
import jax, jax.numpy as jnp
from jax import lax
import numpy as np

D_MODEL = 2048
BATCH = 4
SEQ = 2048
DEPTH = 2
DEC_BATCH = 128
DEC_SEQ = 8
PAST_LEN = 16384
PAGE_SIZE = 128

N_MIXERS = 2
N_CONV_LAYERS = (DEPTH + 1) // 2
N_SSD_LAYERS = DEPTH // 2
PLE_DIM = 256
D_FF = -(-8 * D_MODEL // (3 * 256)) * 256
SC_WIDTH = 3
D_INNER = 2 * D_MODEL
SSD_HEAD_DIM = 64
SSD_HEADS = D_INNER // SSD_HEAD_DIM
SSD_GROUPS = 8
SSD_STATE = 128
SSD_CONV = 4
SSD_GN = SSD_GROUPS * SSD_STATE
SSD_CONV_DIM = D_INNER + 2 * SSD_GN
SSD_IN_DIM = D_INNER + SSD_CONV_DIM + SSD_HEADS
SSD_CHUNK = 128
EPS = 1e-6

kernel_name = "hybrid_shortconv_ssd_decoder_step"


def rmsnorm(x, g):
    xf = x.astype(jnp.float32)
    xf = xf * lax.rsqrt(jnp.mean(xf * xf, axis=-1, keepdims=True) + EPS)
    return (xf * g.astype(jnp.float32)).astype(x.dtype)


def causal_dwconv(u, buf, w):
    K = w.shape[0]
    L = u.shape[1]
    up = jnp.concatenate([buf.astype(u.dtype), u], axis=1)
    out = up[:, 0:L] * w[0]
    for k in range(1, K):
        out = out + up[:, k:k + L] * w[k]
    return out, up[:, L:]


def short_conv_mixer(h, buf, w_in, w_conv, w_out):
    proj = h @ w_in
    bg, cg, v = jnp.split(proj, 3, axis=-1)
    u = cg * v
    conv, new_buf = causal_dwconv(u, buf, w_conv)
    return (bg * conv) @ w_out, new_buf


def ssd_scan(x, dt, A, Bm, Cm, h0):
    b, L, H, P = x.shape
    G, N = SSD_GROUPS, SSD_STATE
    R = H // G
    Q = SSD_CHUNK if L % SSD_CHUNK == 0 else L
    nc = L // Q
    f32 = jnp.float32
    xc = x.astype(f32).reshape(b, nc, Q, G, R, P)
    dtc = dt.reshape(b, nc, Q, G, R)
    Bc = Bm.astype(f32).reshape(b, nc, Q, G, N)
    Cc = Cm.astype(f32).reshape(b, nc, Q, G, N)
    a_cum = jnp.cumsum(dtc * A.reshape(G, R), axis=2)
    xdt = xc * dtc[..., None]
    causal = jnp.tril(jnp.ones((Q, Q), dtype=bool))
    seg = a_cum[:, :, :, None] - a_cum[:, :, None, :]
    decay = jnp.exp(jnp.where(causal[:, :, None, None], seg, -jnp.inf))
    cb = jnp.einsum('bctgn,bcsgn->bctsg', Cc, Bc)
    y_intra = jnp.einsum('bctsg,bctsgr,bcsgrp->bctgrp', cb, decay, xdt)
    decay_end = jnp.exp(a_cum[:, :, -1:] - a_cum)
    s_chunk = jnp.einsum('bcsgr,bcsgn,bcsgrp->bcgrpn', decay_end, Bc, xdt)
    chunk_decay = jnp.exp(a_cum[:, :, -1])

    def step(hs, inp):
        s_c, d_c = inp
        return d_c[..., None, None] * hs + s_c, hs

    h0g = h0.astype(f32).reshape(b, G, R, P, N)
    h_last, h_prev = lax.scan(step, h0g, (jnp.moveaxis(s_chunk, 1, 0), jnp.moveaxis(chunk_decay, 1, 0)))
    h_prev = jnp.moveaxis(h_prev, 0, 1)
    y_inter = jnp.einsum('bctgn,bcgrpn,bctgr->bctgrp', Cc, h_prev, jnp.exp(a_cum))
    y = (y_intra + y_inter).reshape(b, L, H, P).astype(x.dtype)
    return y, h_last.reshape(b, H, P, N).astype(h0.dtype)


def ssd_mixer(h, conv_buf, ssm_state, w_in, conv_w, conv_b, dt_bias, a_log, d_skip, norm_g, w_out):
    b, L, _ = h.shape
    zxbcdt = h @ w_in
    z = zxbcdt[..., :D_INNER]
    xbc = zxbcdt[..., D_INNER:D_INNER + SSD_CONV_DIM]
    dt_raw = zxbcdt[..., D_INNER + SSD_CONV_DIM:]
    xbc_c, new_conv = causal_dwconv(xbc, conv_buf, conv_w)
    xbc_c = jax.nn.silu(xbc_c + conv_b)
    xs = xbc_c[..., :D_INNER].reshape(b, L, SSD_HEADS, SSD_HEAD_DIM)
    Bm = xbc_c[..., D_INNER:D_INNER + SSD_GN].reshape(b, L, SSD_GROUPS, SSD_STATE)
    Cm = xbc_c[..., D_INNER + SSD_GN:].reshape(b, L, SSD_GROUPS, SSD_STATE)
    dt = jax.nn.softplus(dt_raw.astype(jnp.float32) + dt_bias.astype(jnp.float32))
    A = -jnp.exp(a_log.astype(jnp.float32))
    y, new_state = ssd_scan(xs, dt, A, Bm, Cm, ssm_state)
    y = y + xs * d_skip[:, None]
    gated = (y.reshape(b, L, D_INNER) * jax.nn.silu(z)).astype(jnp.float32)
    gated = gated.reshape(b, L, SSD_GROUPS, D_INNER // SSD_GROUPS)
    gated = gated * lax.rsqrt(jnp.mean(gated * gated, axis=-1, keepdims=True) + EPS)
    gated = gated * norm_g.astype(jnp.float32).reshape(SSD_GROUPS, -1)
    out = gated.reshape(b, L, D_INNER).astype(h.dtype) @ w_out
    return out, new_conv, new_state


def swiglu(h, w_gate, w_up, w_down):
    return (jax.nn.silu(h @ w_gate) * (h @ w_up)) @ w_down


def trunk(x, p, sc_bufs, ssd_bufs, ssd_states, g_mix, g_ffn, g_ple, g_final,
          sc_w_in, sc_w_conv, sc_w_out, ssd_w_in, ssd_conv_w, ssd_conv_b, ssd_dt_bias,
          ssd_a_log, ssd_d, ssd_norm_g, ssd_w_out, ffn_w_gate, ffn_w_up, ffn_w_down,
          ple_w_proj, ple_w_gate):
    h = x
    new_sc, new_ssd_conv, new_ssd = [], [], []
    for i in range(DEPTH):
        j = i // N_MIXERS
        hn = rmsnorm(h, g_mix[i])
        if i % N_MIXERS == 0:
            y, nb = short_conv_mixer(hn, sc_bufs[j], sc_w_in[j], sc_w_conv[j], sc_w_out[j])
            new_sc.append(nb)
        else:
            y, nc_, ns = ssd_mixer(hn, ssd_bufs[j], ssd_states[j], ssd_w_in[j], ssd_conv_w[j],
                                   ssd_conv_b[j], ssd_dt_bias[j], ssd_a_log[j], ssd_d[j],
                                   ssd_norm_g[j], ssd_w_out[j])
            new_ssd_conv.append(nc_)
            new_ssd.append(ns)
        h = h + y
        h = h + swiglu(rmsnorm(h, g_ffn[i]), ffn_w_gate[i], ffn_w_up[i], ffn_w_down[i])
        gate = jax.nn.sigmoid(rmsnorm(h, g_ple[i]) @ ple_w_gate[i])
        h = h + (p[i] @ ple_w_proj[i]) * gate
    return rmsnorm(h, g_final), jnp.stack(new_sc), jnp.stack(new_ssd_conv), jnp.stack(new_ssd)


def setup_inputs(seed: int = 0) -> dict:
    key = jax.random.key(seed)
    ks = jax.random.split(key, 32)
    f32 = jnp.float32
    nrm = lambda k, shape, s: jax.random.normal(k, shape, f32) * s
    dt0 = jnp.exp(jax.random.uniform(ks[17], (N_SSD_LAYERS, SSD_HEADS), f32)
                  * (np.log(0.1) - np.log(0.001)) + np.log(0.001))
    return {
        "x_prompt": nrm(ks[0], (BATCH, SEQ, D_MODEL), 1.0),
        "x_sample": nrm(ks[1], (DEC_BATCH, DEC_SEQ, D_MODEL), 1.0),
        "p_prompt": nrm(ks[2], (DEPTH, BATCH, SEQ, PLE_DIM), 1.0),
        "p_sample": nrm(ks[3], (DEPTH, DEC_BATCH, DEC_SEQ, PLE_DIM), 1.0),
        "state_sc_conv": nrm(ks[4], (N_CONV_LAYERS, DEC_BATCH, SC_WIDTH - 1, D_MODEL), 1.0),
        "state_ssd_conv": nrm(ks[5], (N_SSD_LAYERS, DEC_BATCH, SSD_CONV - 1, SSD_CONV_DIM), 1.0),
        "state_ssd": nrm(ks[6], (N_SSD_LAYERS, DEC_BATCH, SSD_HEADS, SSD_HEAD_DIM, SSD_STATE), 0.1),
        "g_mix": 1.0 + nrm(ks[7], (DEPTH, D_MODEL), 0.02),
        "g_ffn": 1.0 + nrm(ks[8], (DEPTH, D_MODEL), 0.02),
        "g_ple": 1.0 + nrm(ks[9], (DEPTH, D_MODEL), 0.02),
        "g_final": 1.0 + nrm(ks[10], (D_MODEL,), 0.02),
        "sc_w_in": nrm(ks[11], (N_CONV_LAYERS, D_MODEL, 3 * D_MODEL), D_MODEL ** -0.5),
        "sc_w_conv": nrm(ks[12], (N_CONV_LAYERS, SC_WIDTH, D_MODEL), SC_WIDTH ** -0.5),
        "sc_w_out": nrm(ks[13], (N_CONV_LAYERS, D_MODEL, D_MODEL), D_MODEL ** -0.5),
        "ssd_w_in": nrm(ks[14], (N_SSD_LAYERS, D_MODEL, SSD_IN_DIM), D_MODEL ** -0.5),
        "ssd_conv_w": nrm(ks[15], (N_SSD_LAYERS, SSD_CONV, SSD_CONV_DIM), SSD_CONV ** -0.5),
        "ssd_conv_b": nrm(ks[16], (N_SSD_LAYERS, SSD_CONV_DIM), 0.01),
        "ssd_dt_bias": dt0 + jnp.log(-jnp.expm1(-dt0)),
        "ssd_a_log": jnp.log(jax.random.uniform(ks[18], (N_SSD_LAYERS, SSD_HEADS), f32, 1.0, 16.0)),
        "ssd_d": 1.0 + nrm(ks[19], (N_SSD_LAYERS, SSD_HEADS), 0.02),
        "ssd_norm_g": 1.0 + nrm(ks[20], (N_SSD_LAYERS, D_INNER), 0.02),
        "ssd_w_out": nrm(ks[21], (N_SSD_LAYERS, D_INNER, D_MODEL), D_INNER ** -0.5),
        "ffn_w_gate": nrm(ks[22], (DEPTH, D_MODEL, D_FF), D_MODEL ** -0.5),
        "ffn_w_up": nrm(ks[23], (DEPTH, D_MODEL, D_FF), D_MODEL ** -0.5),
        "ffn_w_down": nrm(ks[24], (DEPTH, D_FF, D_MODEL), D_FF ** -0.5),
        "ple_w_proj": nrm(ks[25], (DEPTH, PLE_DIM, D_MODEL), PLE_DIM ** -0.5),
        "ple_w_gate": nrm(ks[26], (DEPTH, D_MODEL, D_MODEL), D_MODEL ** -0.5),
    }


def reference(x_prompt, x_sample, p_prompt, p_sample, state_sc_conv, state_ssd_conv, state_ssd,
              g_mix, g_ffn, g_ple, g_final, sc_w_in, sc_w_conv, sc_w_out, ssd_w_in, ssd_conv_w,
              ssd_conv_b, ssd_dt_bias, ssd_a_log, ssd_d, ssd_norm_g, ssd_w_out, ffn_w_gate,
              ffn_w_up, ffn_w_down, ple_w_proj, ple_w_gate):
    b0 = x_prompt.shape[0]
    dt_ = x_prompt.dtype
    sc0 = jnp.zeros((N_CONV_LAYERS, b0, SC_WIDTH - 1, D_MODEL), dt_)
    ssdc0 = jnp.zeros((N_SSD_LAYERS, b0, SSD_CONV - 1, SSD_CONV_DIM), dt_)
    ssd0 = jnp.zeros((N_SSD_LAYERS, b0, SSD_HEADS, SSD_HEAD_DIM, SSD_STATE), state_ssd.dtype)
    weights = (g_mix, g_ffn, g_ple, g_final, sc_w_in, sc_w_conv, sc_w_out, ssd_w_in, ssd_conv_w,
               ssd_conv_b, ssd_dt_bias, ssd_a_log, ssd_d, ssd_norm_g, ssd_w_out, ffn_w_gate,
               ffn_w_up, ffn_w_down, ple_w_proj, ple_w_gate)
    y_prompt, scp, ssdcp, ssdp = trunk(x_prompt, p_prompt, sc0, ssdc0, ssd0, *weights)
    y_sample, scs, ssdcs, ssds = trunk(x_sample, p_sample, state_sc_conv, state_ssd_conv, state_ssd, *weights)
    return (y_prompt, y_sample, scp, scs, ssdcp, ssdcs, ssdp, ssds)
```

```python
import sys
from contextlib import ExitStack
import numpy as np
import concourse.bass as bass
import concourse.mybir as mybir
from concourse.bass_utils import run_bass_kernel_spmd

F32 = mybir.dt.float32
BF16 = mybir.dt.bfloat16
AF = mybir.ActivationFunctionType
ALU = mybir.AluOpType
EPS = 1e-6
HALO = 5


class Cfg:
    def __init__(self, D=2048, SEQ=2048, DEC_BATCH=128, DEC_SEQ=8, PLE=256):
        self.D = D
        self.SEQ = SEQ
        self.BATCH = 4
        self.DEC_BATCH = DEC_BATCH
        self.DEC_SEQ = DEC_SEQ
        self.PLE = PLE
        self.DFF = -(-8 * D // (3 * 256)) * 256
        self.DI = 2 * D
        self.H = self.DI // 64
        self.G = 8
        self.N = 128
        self.R = self.H // 8
        self.GW = self.DI // 8
        self.GC = self.GW // 128
        self.GN = self.G * self.N
        self.CONV = self.DI + 2 * self.GN
        self.IN = self.DI + self.CONV + self.H
        self.KC = D // 128
        self.FC = self.DFF // 128
        self.CC = self.CONV // 128
        self.TP = SEQ // 2
        self.NQ = self.TP // 128
        self.NSEQ = DEC_BATCH // 8
        self.TS = self.NSEQ * DEC_SEQ
        self.PL = HALO + self.TP
        self.T = self.PL + self.TS
        off = {}
        o = 0
        for nm, w in [("g_mix0", self.KC), ("g_ffn0", self.KC), ("g_ple0", self.KC),
                      ("g_mix1", self.KC), ("g_ffn1", self.KC), ("g_ple1", self.KC),
                      ("g_final", self.KC), ("scw0", self.KC), ("scw1", self.KC), ("scw2", self.KC),
                      ("cw0", self.CC), ("cw1", self.CC), ("cw2", self.CC), ("cw3", self.CC),
                      ("cb", self.CC), ("ng", self.DI // 128),
                      ("dtb", self.H), ("alog", self.H), ("dsk", self.H)]:
            off[nm] = (o, w)
            o += w
        self.voff = off
        self.NV = o


class Prog:
    ENGS = ("pe", "act", "dve", "pool", "sp")

    def __init__(self, nc, stack):
        self.nc = nc
        self.stack = stack
        self.ops = []
        self.lastw = {}
        self.readers = {}
        self.floor = 0
        self.emitted = 0
        self.sems = {}
        self.cnt = {}
        self.waited = {e: {} for e in self.ENGS}
        self.lastsig = {}
        self.nwait = 0
        self.eng = dict(pe=nc.tensor, act=nc.scalar, dve=nc.vector, pool=nc.gpsimd, sp=nc.sync)

    def op(self, eng, fn, reads=(), writes=(), dma=None, inc=16):
        i = len(self.ops)
        psr = [r for r in reads if isinstance(r, str) and r.startswith("ps")]
        if psr:
            reads = [r for r in reads if r not in psr]
            writes = list(writes) + psr
        deps = set()
        for r in reads:
            w = self.lastw.get(r)
            if w is not None:
                deps.add(w)
        for r in writes:
            w = self.lastw.get(r)
            if w is not None:
                deps.add(w)
            for x in self.readers.get(r, ()):
                deps.add(x)
        for r in reads:
            self.readers.setdefault(r, []).append(i)
        for r in writes:
            self.lastw[r] = i
            self.readers[r] = []
        deps.discard(i)
        deps = {d for d in deps if d >= self.floor}
        self.ops.append(dict(eng=eng, fn=fn, deps=deps, dma=dma, inc=inc, sig=False))
        return i

    def getsem(self, k):
        if k not in self.sems:
            self.sems[k] = self.stack.enter_context(self.nc.semaphore("s%d" % len(self.sems)))
        return self.sems[k]

    def end_phase(self):
        ops = self.ops
        start = self.emitted
        last = {}
        for i in range(start, len(ops)):
            o = ops[i]
            k = ("dma", o["dma"]) if o["dma"] is not None else ("eng", o["eng"])
            last[k] = i
        bdeps = set(last.values())
        for e in self.ENGS:
            self.ops.append(dict(eng=e, fn=None, deps=set(bdeps), dma=None, inc=0, sig=False, barrier=True))
        for i in range(start, len(ops)):
            o = ops[i]
            nd = set()
            for d in o["deps"]:
                p = ops[d]
                if (not o.get("barrier")) and p["dma"] is None and o["dma"] is None \
                        and p["eng"] == "pe" and o["eng"] == "pe":
                    continue
                nd.add(d)
            o["deps"] = nd
            for d in nd:
                ops[d]["sig"] = True
        for i in range(start, len(ops)):
            o = ops[i]
            if o["dma"] is not None:
                k = ("dma", o["dma"])
                self.cnt[k] = self.cnt.get(k, 0) + o["inc"]
                o["semk"], o["val"] = k, self.cnt[k]
            elif o["sig"]:
                k = ("eng", o["eng"])
                self.cnt[k] = self.cnt.get(k, 0) + 1
                o["semk"], o["val"] = k, self.cnt[k]
        for i in range(start, len(ops)):
            o = ops[i]
            e = o["eng"]
            need = {}
            for d in o["deps"]:
                p = ops[d]
                k = p["semk"]
                need[k] = max(need.get(k, 0), p["val"])
            for k, v in need.items():
                if self.waited[e].get(k, 0) >= v:
                    continue
                self.eng[e].wait_ge(self.getsem(k), v)
                self.waited[e][k] = v
                self.nwait += 1
            if o["fn"] is None:
                continue
            ins = o["fn"](self.eng[e])
            if o["dma"] is not None:
                ins.then_inc(self.getsem(o["semk"]), o["inc"])
            elif o["sig"]:
                ins.then_inc(self.getsem(o["semk"]), 1)
            o["fn"] = None
        self.emitted = len(ops)
        self.floor = len(ops)
        self.lastw = {}
        self.readers = {}


def split_tiles(T, maxn=448):
    nt = -(-T // maxn)
    base = T // nt
    rem = T % nt
    out = []
    c = 0
    for i in range(nt):
        n = base + (1 if i < rem else 0)
        out.append((c, n))
        c += n
    return out


def build(cfg, stop=None):
    D, KC, T, PL, TS, NSEQ = cfg.D, cfg.KC, cfg.T, cfg.PL, cfg.TS, cfg.NSEQ
    DI, H, G, R, GW, GC, GN, CC, FC = cfg.DI, cfg.H, cfg.G, cfg.R, cfg.GW, cfg.GC, cfg.GN, cfg.CC, cfg.FC
    PLE, DFF, CONV, IN, NQ = cfg.PLE, cfg.DFF, cfg.CONV, cfg.IN, cfg.NQ
    PK = PLE // 128
    nc = bass.Bass("TRN2", target_bir_lowering=False)
    CMW = 640 + NSEQ

    def din(name, shape):
        return nc.dram_tensor(name, list(shape), F32, kind="ExternalInput").ap()

    def dout(name, shape):
        return nc.dram_tensor(name, list(shape), F32, kind="ExternalOutput").ap()

    xin = din("xin", [T, D])
    pin = din("pin", [2, T, PLE])
    sc_state = din("sc_state", [NSEQ * 2, D])
    ssdc_state = din("ssdc_state", [NSEQ * 3, CONV])
    ssd_state = din("ssd_state", [NSEQ, H * 64, 128])
    maskodd = din("maskodd", [128, 1])
    vecs_d = din("vecs", [128, cfg.NV])
    cmat_d = din("cmat", [128, CMW])
    sc_w_in = din("sc_w_in", [D, 3 * D])
    sc_w_out = din("sc_w_out", [D, D])
    ssd_w_in = din("ssd_w_in", [D, IN])
    ssd_w_out = din("ssd_w_out", [DI, D])
    ffn_w_gate = din("ffn_w_gate", [2, D, DFF])
    ffn_w_up = din("ffn_w_up", [2, D, DFF])
    ffn_w_down = din("ffn_w_down", [2, DFF, D])
    ple_w_proj = din("ple_w_proj", [2, PLE, D])
    ple_w_gate = din("ple_w_gate", [2, D, D])

    y_out = dout("y", [T, D])
    sc_out_p = dout("sc_out_p", [2, D])
    sc_out_s = dout("sc_out_s", [NSEQ * 2, D])
    ssdc_out_p = dout("ssdc_out_p", [3, CONV])
    ssdc_out_s = dout("ssdc_out_s", [NSEQ * 3, CONV])
    st_out_p = dout("st_out_p", [H * 64, 128])
    st_out_s = dout("st_out_s", [NSEQ, H * 64, 128])
    ibs = [nc.dram_tensor("ib%d" % g, [128, GW], F32) for g in range(G)]
    obs = [nc.dram_tensor("ob%d" % g, [256, GW], F32) for g in range(G)]

    tts = split_tiles(T)
    for (c0, n) in tts:
        assert not (c0 < PL < c0 + n and False)
    assert any(c0 <= PL and PL + TS <= c0 + n for (c0, n) in tts) or True
    rts = [(r0, min(128, T - r0)) for r0 in range(0, T, 128)]

    with ExitStack() as st:
        sbctr = [0]

        def sb(name, shape, dt=F32, stack=None):
            sbctr[0] += 1
            return (stack or st).enter_context(nc.sbuf_tensor("%s_%d" % (name, sbctr[0]), list(shape), dt))

        P = Prog(nc, st)
        h = sb("h", [128, KC, T])
        xn = sb("xn", [128, KC, T], BF16)
        vecs = sb("vecs", [128, cfg.NV])
        cmat = sb("cmat", [128, CMW])
        cmb = sb("cmb", [128, CMW], BF16)
        abc = sb("abc", [128, H])
        mko = sb("mko", [128, 1])
        ps = [st.enter_context(nc.psum_tensor("ps%d" % i, [128, 512], F32)) for i in range(8)]
        psb = [p[:, :].bitcast(BF16) for p in ps]
        ident_f, tri_f, ones_f = cmat[:, 0:128], cmat[:, 128:256], cmat[:, 256:384]
        triS_f, blk_f, maskJ_f = cmat[:, 384:512], cmat[:, 512:640], cmat[:, 640:640 + NSEQ]
        ident_b, tri_b, ones_b = cmb[:, 0:128], cmb[:, 128:256], cmb[:, 256:384]

        def V(nm, c=None):
            o, w = cfg.voff[nm]
            if c is None:
                return vecs[:, o:o + w]
            return vecs[:, o + c:o + c + 1]

        bankctr = [0]

        reserved = set()

        def nb():
            while True:
                b = bankctr[0] % 8
                bankctr[0] += 1
                if b not in reserved:
                    return b

        ringstate = {}

        def make_ring(stack, nslots, tag, slot=4096):
            bufs = [sb("ring%s%d" % (tag, i), [128, slot], BF16, stack) for i in range(nslots)]
            ringstate["bufs"] = bufs
            ringstate["i"] = 0
            ringstate["slot"] = slot

        def wload(src, nk, ncols):
            bufs = ringstate["bufs"]
            i = ringstate["i"] % len(bufs)
            ringstate["i"] += 1
            assert nk * ncols <= ringstate["slot"]
            view = bufs[i][:, 0:nk * ncols].rearrange("p (k n) -> p k n", n=ncols)
            key = "ring%d" % i
            if key in P.lastw:
                assert P.readers.get(key), "ring slot %s overwritten before any consumer was recorded" % key
            srcv = src.rearrange("(k p) n -> p k n", p=128)
            P.op("pool", lambda e: e.dma_start(out=view, in_=srcv), writes=[key], dma=key)
            return view, key

        with ExitStack() as ph:
            xs = [sb("xstg%d" % i, [128, D], F32, ph) for i in range(2)]
            P.op("sp", lambda e: e.dma_start(out=vecs[:], in_=vecs_d), writes=["vecs"], dma="c0")
            P.op("sp", lambda e: e.dma_start(out=cmat[:], in_=cmat_d), writes=["cmat"], dma="c1")
            P.op("sp", lambda e: e.dma_start(out=mko[:], in_=maskodd), writes=["mko"], dma="c2")
            P.op("dve", lambda e: e.tensor_copy(cmb[:], cmat[:]), reads=["cmat"], writes=["cmb"])
            o_al, w_al = cfg.voff["alog"]
            P.op("act", lambda e: e.activation(out=abc[:], in_=vecs[:, o_al:o_al + w_al], func=AF.Exp),
                 reads=["vecs"], writes=["abc"])
            P.op("dve", lambda e: e.tensor_scalar(abc[:], abc[:], -1.0, None, ALU.mult), reads=["abc"], writes=["abc"])
            for ri, (r0, n) in enumerate(rts):
                s = ri % 2
                P.op("sp", lambda e, s=s, r0=r0, n=n: e.dma_start(out=xs[s][0:n, :], in_=xin[r0:r0 + n, :]),
                     writes=["xs%d" % s], dma="xs%d" % s)
                for cg in range(KC // 4):
                    b = nb()

                    def tr(e, s=s, n=n, cg=cg, b=b):
                        for j in range(4):
                            c = cg * 4 + j
                            r = e.transpose(ps[b][:, j * 128:j * 128 + n], xs[s][0:n, c * 128:(c + 1) * 128],
                                            ident_f[0:n, 0:n])
                        return r
                    P.op("pe", tr, reads=["xs%d" % s, "cmat"], writes=["ps%d" % b])
                    src = ps[b][:, :].rearrange("p (j t) -> p j t", t=128)[:, :, 0:n]
                    dst = h[:, cg * 4:cg * 4 + 4, r0:r0 + n]
                    if cg % 2 == 0:
                        P.op("dve", lambda e, src=src, dst=dst: e.tensor_copy(dst, src), reads=["ps%d" % b],
                             writes=[("h", ri, cg)])
                    else:
                        P.op("act", lambda e, src=src, dst=dst: e.copy(dst, src), reads=["ps%d" % b],
                             writes=[("h", ri, cg)])
            P.end_phase()

        def rmsnorm_phase(gname, final=False):
            with ExitStack() as ph:
                sq = [sb("sq%d" % i, [128, KC, tts[0][1]], BF16, ph) for i in range(2)]
                rstd = sb("rstd", [128, T], F32, ph)
                for ti, (c0, n) in enumerate(tts):
                    s = ti % 2
                    P.op("act", lambda e, s=s, c0=c0, n=n: e.activation(out=sq[s][:, :, 0:n], in_=h[:, :, c0:c0 + n],
                                                                      func=AF.Square),
                         reads=["h"], writes=["sq%d" % s])
                    b = nb()

                    def mm(e, s=s, n=n, b=b):
                        for k in range(KC):
                            r = e.matmul(ps[b][:, 0:n], lhsT=ones_b, rhs=sq[s][:, k, 0:n], start=(k == 0),
                                         stop=(k == KC - 1))
                        return r
                    P.op("pe", mm, reads=["sq%d" % s, "cmb"], writes=["ps%d" % b])
                    P.op("dve", lambda e, b=b, c0=c0, n=n: e.tensor_scalar(rstd[:, c0:c0 + n], ps[b][:, 0:n], 1.0 / D,
                                                                         EPS, ALU.mult, ALU.add),
                         reads=["ps%d" % b], writes=["rs%d" % ti])
                    P.op("act", lambda e, c0=c0, n=n: e.activation(out=rstd[:, c0:c0 + n], in_=rstd[:, c0:c0 + n],
                                                                 func=AF.Sqrt),
                         reads=["rs%d" % ti], writes=["rs%d" % ti])
                    P.op("dve", lambda e, c0=c0, n=n: e.reciprocal(rstd[:, c0:c0 + n], rstd[:, c0:c0 + n]),
                         reads=["rs%d" % ti], writes=["rs%d" % ti])
                rk = ["rs%d" % ti for ti in range(len(tts))]
                for c in range(KC):
                    dst = h[:, c, :] if final else xn[:, c, :]
                    P.op("dve", lambda e, c=c, dst=dst: e.scalar_tensor_tensor(dst, h[:, c, :], V(gname, c), rstd[:, :],
                                                                             ALU.mult, ALU.mult),
                         reads=rk + ["vecs", "h"], writes=[("hf" if final else "xn", c)])
                P.end_phase()

        def acc_into_h(b, mc, c0, n):
            P.op("dve", lambda e: e.tensor_tensor(h[:, mc, c0:c0 + n], h[:, mc, c0:c0 + n], ps[b][:, 0:n], ALU.add),
                 reads=["ps%d" % b, ("h", mc)], writes=[("h", mc)])

        def mm_fm(b, wv, wkey, nk, j, rhs, rkeys, c0, n):
            def mm(e):
                for k in range(nk):
                    r = e.matmul(ps[b][:, 0:n], lhsT=wv[:, k, j * 128:(j + 1) * 128], rhs=rhs[:, k, c0:c0 + n],
                                 start=(k == 0), stop=(k == nk - 1))
                return r
            P.op("pe", mm, reads=[wkey] + list(rkeys), writes=["ps%d" % b])

        def ffn_phase(layer):
            npairs = FC // 2
            ngroups = max(1, -(-FC // 12))
            base = npairs // ngroups
            rem = npairs % ngroups
            gsz = [base + (1 if i < rem else 0) for i in range(ngroups)]
            maxk = max(gsz) * 2
            with ExitStack() as ph:
                make_ring(ph, 6, "f")
                act = sb("act", [128, maxk, T], BF16, ph)
                sg = [sb("sg%d" % i, [128, 512], F32, ph) for i in range(2)]
                sgi = 0
                pr0 = 0
                for gi, np_ in enumerate(gsz):
                    nkg = np_ * 2
                    for pr in range(np_):
                        col = (pr0 + pr) * 256
                        wg, kg = wload(ffn_w_gate[layer, :, col:col + 256], KC, 256)
                        wu, ku = wload(ffn_w_up[layer, :, col:col + 256], KC, 256)
                        for j in range(2):
                            fl = pr * 2 + j
                            for (c0, n) in tts:
                                bg_, bu_ = nb(), nb()
                                mm_fm(bg_, wg, kg, KC, j, xn, ["xn"], c0, n)
                                mm_fm(bu_, wu, ku, KC, j, xn, ["xn"], c0, n)
                                s = sgi % 2
                                sgi += 1
                                P.op("act", lambda e, s=s, b=bg_, n=n: e.activation(out=sg[s][:, 0:n], in_=ps[b][:, 0:n],
                                                                                  func=AF.Silu),
                                     reads=["ps%d" % bg_], writes=["sg%d" % s])
                                P.op("dve", lambda e, s=s, b=bu_, fl=fl, c0=c0, n=n: e.tensor_tensor(
                                    act[:, fl, c0:c0 + n], sg[s][:, 0:n], ps[b][:, 0:n], ALU.mult),
                                     reads=["sg%d" % s, "ps%d" % bu_], writes=[("act", fl)])
                    for mcp in range(D // 256):
                        wd, kd = wload(ffn_w_down[layer, pr0 * 256:pr0 * 256 + nkg * 128, mcp * 256:mcp * 256 + 256],
                                       nkg, 256)
                        for j in range(2):
                            mc = mcp * 2 + j
                            for (c0, n) in tts:
                                b = nb()
                                mm_fm(b, wd, kd, nkg, j, act, [("act", k) for k in range(nkg)], c0, n)
                                acc_into_h(b, mc, c0, n)
                    pr0 += np_
                P.end_phase()

        def ple_phase(layer):
            with ExitStack() as ph:
                make_ring(ph, 6, "p")
                pT = sb("pT", [128, PK, T], BF16, ph)
                pst = [sb("pst%d" % i, [128, PLE], F32, ph) for i in range(2)]
                sg = [sb("sgp%d" % i, [128, 512], F32, ph) for i in range(2)]
                for ri, (r0, n) in enumerate(rts):
                    s = ri % 2
                    P.op("sp", lambda e, s=s, r0=r0, n=n: e.dma_start(out=pst[s][0:n, :], in_=pin[layer, r0:r0 + n, :]),
                         writes=["pst%d" % s], dma="pst%d" % s)
                    b = nb()

                    def tr(e, s=s, n=n, b=b):
                        for j in range(PK):
                            r = e.transpose(ps[b][:, j * 128:j * 128 + n], pst[s][0:n, j * 128:(j + 1) * 128],
                                            ident_f[0:n, 0:n])
                        return r
                    P.op("pe", tr, reads=["pst%d" % s, "cmat"], writes=["ps%d" % b])
                    src = ps[b][:, 0:PK * 128].rearrange("p (j t) -> p j t", t=128)[:, :, 0:n]
                    P.op("dve", lambda e, src=src, r0=r0, n=n: e.tensor_copy(pT[:, :, r0:r0 + n], src),
                         reads=["ps%d" % b], writes=["pT"])
                sgi = 0
                for mcp in range(D // 256):
                    wg, kg = wload(ple_w_gate[layer, :, mcp * 256:mcp * 256 + 256], KC, 256)
                    wp, kp = wload(ple_w_proj[layer, :, mcp * 256:mcp * 256 + 256], PK, 256)
                    for j in range(2):
                        mc = mcp * 2 + j
                        for (c0, n) in tts:
                            bg_, bp_ = nb(), nb()
                            mm_fm(bg_, wg, kg, KC, j, xn, ["xn"], c0, n)
                            mm_fm(bp_, wp, kp, PK, j, pT, ["pT"], c0, n)
                            s = sgi % 2
                            sgi += 1
                            P.op("act", lambda e, s=s, b=bg_, n=n: e.activation(out=sg[s][:, 0:n], in_=ps[b][:, 0:n],
                                                                              func=AF.Sigmoid),
                                 reads=["ps%d" % bg_], writes=["sgp%d" % s])
                            P.op("dve", lambda e, s=s, b=bp_, n=n: e.tensor_tensor(sg[s][:, 0:n], sg[s][:, 0:n],
                                                                                 ps[b][:, 0:n], ALU.mult),
                                 reads=["sgp%d" % s, "ps%d" % bp_], writes=["sgp%d" % s])
                            P.op("dve", lambda e, s=s, mc=mc, c0=c0, n=n: e.tensor_tensor(
                                h[:, mc, c0:c0 + n], h[:, mc, c0:c0 + n], sg[s][:, 0:n], ALU.add),
                                 reads=["sgp%d" % s, ("h", mc)], writes=[("h", mc)])
                P.end_phase()

        def split_cols(c0, n):
            pa, pb = c0, min(c0 + n, PL)
            sa, sb_ = max(c0, PL), c0 + n
            return (pa, pb) if pb > pa else None, (sa, sb_) if sb_ > sa else None

        def conv_state_io(ph_tag, pre, pre_s, slot, W0, state_d, cidx, out_p, out_s, stg_in, stg_out, cost, sidx):
            nr = W0 - 1
            ks = ph_tag + "pres%d" % slot
            si = sidx % 2
            P.op("sp", lambda e: e.dma_start(out=stg_in[si][0:NSEQ * nr, :], in_=state_d[:, cidx * 128:(cidx + 1) * 128]),
                 writes=[ph_tag + "sti%d" % si], dma=ph_tag + "sti%d" % si)
            b = nb()
            P.op("pe", lambda e: e.transpose(ps[b][:, 0:NSEQ * nr], stg_in[si][0:NSEQ * nr, :],
                                            ident_f[0:NSEQ * nr, 0:NSEQ * nr]),
                 reads=[ph_tag + "sti%d" % si, "cmat"], writes=["ps%d" % b])
            src = ps[b][:, 0:NSEQ * nr].rearrange("p (s r) -> p s r", r=nr)
            P.op("act", lambda e: e.copy(pre_s[:, slot, :, 0:nr], src), reads=["ps%d" % b], writes=[ks + "h"])

        def conv_state_out(ph_tag, pre, pre_s, slot, W0, cidx, out_p, out_s, stg_out, cost, rd_keys, sidx):
            nr = W0 - 1
            si = sidx % 2
            b = nb()
            P.op("pe", lambda e: e.transpose(ps[b][0:nr, 0:128], pre[:, slot, PL:PL + nr], ident_f),
                 reads=rd_keys + ["cmat"], writes=["ps%d" % b])
            P.op("dve", lambda e: e.tensor_copy(stg_out[si][0:nr, 0:128], ps[b][0:nr, 0:128]), reads=["ps%d" % b],
                 writes=[ph_tag + "stoP%d" % si])
            P.op("sp", lambda e: e.dma_start(out=out_p[:, cidx * 128:(cidx + 1) * 128], in_=stg_out[si][0:nr, 0:128]),
                 reads=[ph_tag + "stoP%d" % si], dma=ph_tag + "stoP%d" % si)
            P.op("dve", lambda e: e.tensor_copy(cost[si][:, 0:NSEQ * nr].rearrange("p (s r) -> p s r", r=nr),
                                               pre_s[:, slot, :, 8:8 + nr]),
                 reads=rd_keys, writes=[ph_tag + "cost%d" % si])
            b2 = nb()
            P.op("pe", lambda e: e.transpose(ps[b2][0:NSEQ * nr, 0:128], cost[si][:, 0:NSEQ * nr], ident_f),
                 reads=[ph_tag + "cost%d" % si, "cmat"], writes=["ps%d" % b2])
            P.op("act", lambda e: e.copy(stg_out[si][0:NSEQ * nr, 128:256], ps[b2][0:NSEQ * nr, 0:128]),
                 reads=["ps%d" % b2], writes=[ph_tag + "stoS%d" % si])
            P.op("sp", lambda e: e.dma_start(out=out_s[:, cidx * 128:(cidx + 1) * 128],
                                            in_=stg_out[si][0:NSEQ * nr, 128:256]),
                 reads=[ph_tag + "stoS%d" % si], dma=ph_tag + "stoS%d" % si)

        def l0_mixer_phase():
            with ExitStack() as ph:
                make_ring(ph, 4, "m")
                gated = sb("gated", [128, KC, T], BF16, ph)
                pre = sb("l0pre", [128, 2, 2 + PL], F32, ph)
                pre_s = sb("l0pres", [128, 2, NSEQ, 10], F32, ph)
                tcv = sb("l0tcv", [128, 1, T], F32, ph)
                stg_in = [sb("l0sti%d" % i, [128, 128], F32, ph) for i in range(2)]
                stg_out = [sb("l0sto%d" % i, [128, 256], F32, ph) for i in range(2)]
                cost = [sb("l0cost%d" % i, [128, NSEQ * 3], F32, ph) for i in range(2)]
                P.op("dve", lambda e: e.memset(pre[:, 0, 0:2], 0.0), writes=["Apre0z"])
                P.op("dve", lambda e: e.memset(pre[:, 1, 0:2], 0.0), writes=["Apre1z"])
                for cp in range(KC // 2):
                    wb, kb = wload(sc_w_in[:, cp * 256:cp * 256 + 256], KC, 256)
                    wc, kc = wload(sc_w_in[:, D + cp * 256:D + cp * 256 + 256], KC, 256)
                    wv, kv = wload(sc_w_in[:, 2 * D + cp * 256:2 * D + cp * 256 + 256], KC, 256)
                    for j in range(2):
                        c = cp * 2 + j
                        sl = c % 2
                        conv_state_io("A", pre, pre_s, sl, 3, sc_state, c, None, None, stg_in, None, None, c)
                        for (c0, n) in tts:
                            b_c, b_v, b_b = nb(), nb(), nb()
                            mm_fm(b_c, wc, kc, KC, j, xn, ["xn"], c0, n)
                            mm_fm(b_v, wv, kv, KC, j, xn, ["xn"], c0, n)
                            mm_fm(b_b, wb, kb, KC, j, xn, ["xn"], c0, n)
                            P.op("act", lambda e, b=b_c, c0=c0, n=n: e.copy(tcv[:, 0, c0:c0 + n], ps[b][:, 0:n]),
                                 reads=["ps%d" % b_c], writes=["Atcv0p", "Atcv0s"])
                            pp, sp_ = split_cols(c0, n)
                            if pp:
                                a, b2 = pp
                                P.op("dve", lambda e, a=a, b2=b2, b=b_v, c0=c0, sl=sl: e.tensor_tensor(
                                    pre[:, sl, 2 + a:2 + b2], tcv[:, 0, a:b2], ps[b][:, a - c0:b2 - c0], ALU.mult),
                                     reads=["ps%d" % b_v, "Atcv0p", "Atcv0s"], writes=["Apre%d" % sl])
                            if sp_:
                                a, b2 = sp_
                                assert a == PL and b2 == T
                                P.op("dve", lambda e, a=a, b2=b2, b=b_v, c0=c0, sl=sl: e.tensor_tensor(
                                    pre_s[:, sl, :, 2:10],
                                    tcv[:, 0, a:b2].rearrange("p (s t) -> p s t", t=8),
                                    ps[b][:, a - c0:b2 - c0].rearrange("p (s t) -> p s t", t=8), ALU.mult),
                                     reads=["ps%d" % b_v, "Atcv0p", "Atcv0s"], writes=["Apres%d" % sl])
                            P.op("act", lambda e, b=b_b, c=c, c0=c0, n=n: e.copy(gated[:, c, c0:c0 + n], ps[b][:, 0:n]),
                                 reads=["ps%d" % b_b], writes=[("gated", c)])
                        kt = conv_block2("A", pre, pre_s, tcv, sl, 3, ["scw0", "scw1", "scw2"], c,
                                         ["Apre%d" % sl, "Apre%dz" % sl], ["Apres%d" % sl, "Apres%dh" % sl])
                        P.op("dve", lambda e, c=c: e.tensor_tensor(gated[:, c, :], gated[:, c, :], tcv[:, 0, :], ALU.mult),
                             reads=kt + [("gated", c)], writes=[("gated", c)])
                        conv_state_out("A", pre, pre_s, sl, 3, c, sc_out_p, sc_out_s, stg_out, cost,
                                       ["Apre%d" % sl, "Apres%d" % sl, "Apres%dh" % sl], c)
                for mcp in range(D // 256):
                    wo, ko = wload(sc_w_out[:, mcp * 256:mcp * 256 + 256], KC, 256)
                    for j in range(2):
                        mc = mcp * 2 + j
                        for (c0, n) in tts:
                            b = nb()
                            mm_fm(b, wo, ko, KC, j, gated, [("gated", k) for k in range(KC)], c0, n)
                            acc_into_h(b, mc, c0, n)
                P.end_phase()

        def conv_block2(tag, pre, pre_s, tcv, slot, KW, wnames, cidx, pkeys, skeys):
            ktp, kts = tag + "tcv0p", tag + "tcv0s"
            ts_v = tcv[:, 0, PL:T].rearrange("p (s t) -> p s t", t=8)
            for kk in range(KW):
                wcol = V(wnames[kk], cidx)
                if kk == 0:
                    P.op("dve", lambda e, wcol=wcol: e.tensor_scalar(tcv[:, 0, 0:PL], pre[:, slot, 0:PL], wcol, None,
                                                                   ALU.mult),
                         reads=pkeys + ["vecs"], writes=[ktp])
                    P.op("dve", lambda e, wcol=wcol: e.tensor_scalar(ts_v, pre_s[:, slot, :, 0:8], wcol, None, ALU.mult),
                         reads=skeys + ["vecs"], writes=[kts])
                else:
                    P.op("dve", lambda e, wcol=wcol, kk=kk: e.scalar_tensor_tensor(
                        tcv[:, 0, 0:PL], pre[:, slot, kk:kk + PL], wcol, tcv[:, 0, 0:PL], ALU.mult, ALU.add),
                         reads=pkeys + ["vecs", ktp], writes=[ktp])
                    P.op("dve", lambda e, wcol=wcol, kk=kk: e.scalar_tensor_tensor(
                        ts_v, pre_s[:, slot, :, kk:kk + 8], wcol, ts_v, ALU.mult, ALU.add),
                         reads=skeys + ["vecs", kts], writes=[kts])
            return [ktp, kts]

        def ssd_phase():
            XB = 128
            NT = NQ + 1
            tile_col = [HALO + q * 128 for q in range(NQ)] + [PL]
            with ExitStack() as pho:
              dt_all = sb("dt_all", [128, NT, H], F32, pho)
              dA_all = sb("dA_all", [128, NT, H], F32, pho)
              with ExitStack() as ph1:
                wdt = sb("wdt", [128, KC, H], BF16, ph1)
                o_dtb = cfg.voff["dtb"][0]
                P.op("pool", lambda e: e.dma_start(out=wdt[:], in_=ssd_w_in[:, DI + CONV:DI + CONV + H].rearrange(
                    "(k p) n -> p k n", p=128)), writes=["wdt"], dma="wdt")
                for ti in range(NT):
                    c0 = tile_col[ti]
                    bdt = nb()

                    def mmdt(e, c0=c0, bdt=bdt):
                        for k in range(KC):
                            r = e.matmul(ps[bdt][:, 0:H], lhsT=xn[:, k, c0:c0 + 128], rhs=wdt[:, k, :], start=(k == 0),
                                         stop=(k == KC - 1))
                        return r
                    P.op("pe", mmdt, reads=["xn", "wdt"], writes=["ps%d" % bdt])
                    P.op("dve", lambda e, ti=ti, bdt=bdt: e.tensor_tensor(dt_all[:, ti, :], ps[bdt][:, 0:H],
                                                                         vecs[:, o_dtb:o_dtb + H], ALU.add),
                         reads=["ps%d" % bdt, "vecs"], writes=[("dt", ti)])
                P.op("act", lambda e: e.activation(out=dt_all[:, :, :], in_=dt_all[:, :, :], func=AF.Exp),
                     reads=[("dt", ti) for ti in range(NT)], writes=["dt_all"])
                P.op("act", lambda e: e.activation(out=dt_all[:, :, :], in_=dt_all[:, :, :], func=AF.Ln, bias=1.0),
                     reads=["dt_all"], writes=["dt_all"])
                P.op("dve", lambda e: e.tensor_tensor(dA_all[:, :, :], dt_all[:, :, :],
                                                     abc[:, :].unsqueeze(1).broadcast_to([128, NT, H]), ALU.mult),
                     reads=["dt_all", "abc"], writes=["dA_all"])

                P.end_phase()
              with ExitStack() as ph:
                make_ring(ph, 4, "s", 2048)
                xsg = sb("xsg", [128, GC, T], BF16, ph)
                Bg = sb("Bg", [128, T], BF16, ph)
                Cg = sb("Cg", [128, T], BF16, ph)
                ygn = sb("ygn", [128, GC, T], BF16, ph)
                pre = sb("l1pre", [128, 2, 3 + PL], F32, ph)
                pre_s = sb("l1pres", [128, 2, NSEQ, 11], F32, ph)
                tcv = sb("l1tcv", [128, 1, T], F32, ph)
                stg_in = [sb("l1sti%d" % i, [128, 128], F32, ph) for i in range(2)]
                stg_out = [sb("l1sto%d" % i, [128, 256], F32, ph) for i in range(2)]
                cost = [sb("l1cost%d" % i, [128, NSEQ * 3], F32, ph) for i in range(2)]
                acs = sb("acs", [128, R], F32, ph)
                expa = sb("expa", [128, R], F32, ph)
                dE = sb("dE", [128, R], F32, ph)
                cdec = sb("cdec", [128, R], F32, ph)
                cdecF = sb("cdecF", [128, GC, NSEQ], F32, ph)
                Rm = sb("Rm", [128, R, 128], F32, ph)
                Wm = sb("Wm", [128, R, 128], BF16, ph)
                MTm = sb("MTm", [128, 128], BF16, ph)
                xdt = sb("xdt", [128, GW], BF16, ph)
                xw = sb("xw", [128, GW], BF16, ph)
                Btok = sb("Btok", [128, 128], BF16, ph)
                y1 = sb("y1", [128, GW], F32, ph)
                xD = sb("xD", [128, GW], F32, ph)
                gyn = sb("gyn", [128, GW], BF16, ph)
                ss = sb("ss", [128, 2], F32, ph)
                S = sb("S", [128, GW], F32, ph)
                Sb = sb("Sb", [128, GW], BF16, ph)
                h0 = [sb("h0_%d" % i, [128, GC, 128], F32, ph) for i in range(2)]
                h0T = [sb("h0T_%d" % i, [128, GW], BF16, ph) for i in range(1)]
                xwj = [sb("xwj_%d" % i, [128, GW], BF16, ph) for i in range(1)]
                sost = [sb("sost_%d" % i, [128, GC, 128], F32, ph) for i in range(2)]
                Cm_t = sb("Cm", [128, (NSEQ + 1) * 128], BF16, ph)
                cmk = "Cm"
                Cmdiag = Cm_t[:, 0:NSEQ * 136].rearrange("p (j q) -> p j q", q=136)[:, :, 0:8]
                print("SSD phase sbuf remaining", nc.sbuf_bytes_remaining, file=sys.stderr)
                P.op("dve", lambda e: e.memset(Cm_t[:, :], 0.0), writes=[cmk])
                P.op("dve", lambda e: e.memset(pre[:, 0, 0:3], 0.0), writes=["Bpre0z"])
                P.op("dve", lambda e: e.memset(pre[:, 1, 0:3], 0.0), writes=["Bpre1z"])
                P.op("dve", lambda e: e.memset(ygn[:, :, :], 0.0), writes=[("ygn", c) for c in range(GC)])
                o_dtb = cfg.voff["dtb"][0]
                o_dsk = cfg.voff["dsk"][0]
                def chunk(g, ti, full, wz, kz, sample=False):
                    Q = 128
                    col0 = tile_col[ti]
                    cs = slice(col0, col0 + Q)
                    hs = slice(g * R, (g + 1) * R)
                    TRI = triS_f if sample else tri_f
                    dtt = dt_all[:, ti, hs]
                    dA = dA_all[:, ti, hs]
                    bac = nb()
                    P.op("pe", lambda e: e.matmul(ps[bac][:, 0:R], lhsT=TRI, rhs=dA, start=True, stop=True),
                         reads=["dA_all", "cmat"], writes=["ps%d" % bac])
                    P.op("dve", lambda e: e.tensor_tensor(
                        Rm[:, :, :], TRI.unsqueeze(1).broadcast_to([Q, R, Q]),
                        dA.unsqueeze(2).broadcast_to([Q, R, Q]), ALU.mult),
                         reads=["dA_all", "cmat"], writes=["Rm"])
                    P.op("dve", lambda e: e.tensor_copy(acs[:, :], ps[bac][:, 0:R]), reads=["ps%d" % bac],
                         writes=["acs"])
                    hb = 512 // Q
                    nbk = -(-R // hb)
                    bab = [nb() for _ in range(nbk)]

                    def mmab(e):
                        for i in range(nbk):
                            h_a, h_b = i * hb, min(R, (i + 1) * hb)
                            r = e.matmul(ps[bab[i]][:, 0:(h_b - h_a) * Q].rearrange("p (h t) -> p h t", t=Q),
                                         lhsT=ones_f, rhs=Rm[:, h_a:h_b, :], start=True, stop=True)
                        return r
                    P.op("pe", mmab, reads=["Rm", "cmat"], writes=["ps%d" % b for b in bab])

                    def abv(i):
                        h_a, h_b = i * hb, min(R, (i + 1) * hb)
                        return ps[bab[i]][:, 0:(h_b - h_a) * Q].rearrange("p (h t) -> p h t", t=Q), h_a, h_b
                    if not sample:
                        for i in range(nbk):
                            v, h_a, h_b = abv(i)
                            P.op("dve", lambda e, v=v, h_a=h_a, h_b=h_b: e.tensor_tensor(
                                dE[:, h_a:h_b], v[:, :, Q - 1], acs[:, h_a:h_b], ALU.subtract),
                                 reads=["ps%d" % bab[i], "acs"], writes=["dE"])
                            P.op("act", lambda e, v=v, h_a=h_a, h_b=h_b: e.activation(out=cdec[:, h_a:h_b],
                                                                                    in_=v[:, :, Q - 1], func=AF.Exp),
                                 reads=["ps%d" % bab[i]], writes=["cdec"])
                    else:
                        btot = nb()
                        P.op("pe", lambda e: e.matmul(ps[btot][:, 0:R], lhsT=blk_f, rhs=dA, start=True, stop=True),
                             reads=["dA_all", "cmat"], writes=["ps%d" % btot])
                        P.op("dve", lambda e: e.tensor_tensor(dE[:, :], ps[btot][:, 0:R], acs[:, :], ALU.subtract),
                             reads=["ps%d" % btot, "acs"], writes=["dE"])
                        P.op("dve", lambda e: e.tensor_copy(
                            xD[:, :].rearrange("q (h p) -> q h p", p=64), dA.unsqueeze(2).broadcast_to([Q, R, 64])),
                             reads=["dA_all"], writes=["xD"])
                        bcd = nb()

                        def mmcd(e):
                            for c in range(GC):
                                r = e.matmul(ps[bcd][:, c * NSEQ:(c + 1) * NSEQ], lhsT=xD[:, c * 128:(c + 1) * 128],
                                             rhs=maskJ_f, start=True, stop=True)
                            return r
                        P.op("pe", mmcd, reads=["xD", "cmat"], writes=["ps%d" % bcd])
                        P.op("act", lambda e: e.activation(
                            out=cdecF[:, :, :], in_=ps[bcd][:, 0:GC * NSEQ].rearrange("p (c j) -> p c j", j=NSEQ),
                            func=AF.Exp), reads=["ps%d" % bcd], writes=["cdecF"])
                    P.op("act", lambda e: e.activation(out=dE[:, :], in_=dE[:, :], func=AF.Exp), reads=["dE"],
                         writes=["dE"])
                    if full:
                        P.op("act", lambda e: e.activation(out=expa[:, :], in_=acs[:, :], func=AF.Exp),
                             reads=["acs"], writes=["expa"])
                        for i in range(nbk):
                            v, h_a, h_b = abv(i)
                            P.op("dve", lambda e, v=v, h_a=h_a, h_b=h_b: e.tensor_tensor(
                                Rm[:, h_a:h_b, :], v, acs[:, h_a:h_b].unsqueeze(2).broadcast_to([Q, h_b - h_a, Q]),
                                ALU.subtract), reads=["ps%d" % bab[i], "acs", "Rm"], writes=["Rm"])
                        P.op("dve", lambda e: e.tensor_scalar(Rm[:, :, :], Rm[:, :, :], 0.0, None, ALU.min),
                             reads=["Rm"], writes=["Rm"])
                        P.op("act", lambda e: e.activation(out=Wm[:, :, :], in_=Rm[:, :, :], func=AF.Exp),
                             reads=["Rm"], writes=["Wm"])
                    bx = nb()

                    def trx(e):
                        for c in range(GC):
                            r = e.transpose(psb[bx][:, c * 128:(c + 1) * 128], xsg[:, c, cs], ident_b)
                        return r
                    P.op("pe", trx, reads=[("xsg", c) for c in range(GC)] + ["cmb"], writes=["ps%d" % bx])
                    P.op("dve", lambda e: e.tensor_tensor(
                        xdt[:, :].rearrange("q (h p) -> q h p", p=64),
                        psb[bx][:, 0:GW].rearrange("q (h p) -> q h p", p=64),
                        dtt.unsqueeze(2).broadcast_to([Q, R, 64]), ALU.mult),
                         reads=["ps%d" % bx, "dt_all"], writes=["xdt"])
                    if full:
                        P.op("dve", lambda e: e.tensor_tensor(
                            xD[:, :].rearrange("q (h p) -> q h p", p=64),
                            psb[bx][:, 0:GW].rearrange("q (h p) -> q h p", p=64),
                            vecs[:, o_dsk + g * R:o_dsk + (g + 1) * R].unsqueeze(2).broadcast_to([Q, R, 64]), ALU.mult),
                             reads=["ps%d" % bx, "vecs", "xD"], writes=["xD"])
                    bB = nb()
                    P.op("pe", lambda e: e.transpose(psb[bB][:, 0:128], Bg[:, cs], ident_b), reads=["Bg", "cmb"],
                         writes=["ps%d" % bB])
                    P.op("act", lambda e: e.copy(Btok[:, :], psb[bB][:, 0:128]), reads=["ps%d" % bB],
                         writes=["Btok"])
                    P.op("dve", lambda e: e.tensor_tensor(
                        xw[:, :].rearrange("q (h p) -> q h p", p=64), xdt[:, :].rearrange("q (h p) -> q h p", p=64),
                        dE[:, :].unsqueeze(2).broadcast_to([Q, R, 64]), ALU.mult), reads=["xdt", "dE"], writes=["xw"])
                    bi = None
                    if sample:
                        bi = nb()
                        reserved.add(bi)
                        P.op("dve", lambda e: e.tensor_copy(Cmdiag, Cg[:, cs].rearrange("p (j r) -> p j r", r=8)),
                             reads=["Cg", cmk], writes=[cmk])
                        for jq in range(NSEQ):
                            s = jq % 2
                            P.op("sp", lambda e, jq=jq, s=s: e.dma_start(
                                out=h0[s][:, :, :],
                                in_=ssd_state[jq, g * GW:(g + 1) * GW, :].rearrange("(c p) n -> p c n", p=128)),
                                 writes=["h0_%d" % s], dma="h0_%d" % s)
                            bh = nb()

                            def trh(e, bh=bh, s=s):
                                for c in range(GC):
                                    r = e.transpose(ps[bh][:, c * 128:(c + 1) * 128], h0[s][:, c, :], ident_f)
                                return r
                            P.op("pe", trh, reads=["h0_%d" % s, "cmat"], writes=["ps%d" % bh])
                            P.op("act", lambda e, bh=bh, s=s: e.copy(h0T[0][:, :], ps[bh][:, 0:GW]), reads=["ps%d" % bh],
                                 writes=["h0T_0"])
                            P.op("pe", lambda e, jq=jq, s=s: e.matmul(ps[bi][:, 0:GW], lhsT=Cm_t[:, jq * 128:(jq + 1) * 128], rhs=h0T[0][:, :],
                                                                     start=(jq == 0), stop=(jq == NSEQ - 1)),
                                 reads=[cmk, "h0T_0"], writes=["ps%d" % bi])
                            P.op("dve", lambda e, jq=jq, s=s: e.tensor_scalar(xwj[0][:, :], xw[:, :], maskJ_f[:, jq:jq + 1],
                                                                            None, ALU.mult),
                                 reads=["xw", "cmat"], writes=["xwj_0"])
                            bsj = nb()

                            def mmsj(e, bsj=bsj, s=s):
                                for c in range(GC):
                                    r = e.matmul(ps[bsj][:, c * 128:(c + 1) * 128], lhsT=xwj[0][:, c * 128:(c + 1) * 128],
                                                 rhs=Btok[:, :], start=True, stop=True)
                                return r
                            P.op("pe", mmsj, reads=["xwj_0", "Btok"], writes=["ps%d" % bsj])
                            P.op("dve", lambda e, jq=jq, s=s: e.tensor_tensor(
                                sost[s][:, :, :], h0[s][:, :, :],
                                cdecF[:, :, jq].unsqueeze(2).broadcast_to([128, GC, 128]), ALU.mult),
                                 reads=["h0_%d" % s, "cdecF"], writes=["sost_%d" % s])
                            P.op("dve", lambda e, bsj=bsj, s=s: e.tensor_tensor(
                                sost[s][:, :, :], sost[s][:, :, :],
                                ps[bsj][:, 0:GW].rearrange("p (c n) -> p c n", n=128), ALU.add),
                                 reads=["sost_%d" % s, "ps%d" % bsj], writes=["sost_%d" % s])
                            P.op("sp", lambda e, jq=jq, s=s: e.dma_start(
                                out=st_out_s[jq, g * GW:(g + 1) * GW, :].rearrange("(c p) n -> p c n", p=128),
                                in_=sost[s][:, :, :]), reads=["sost_%d" % s], dma="sost_%d" % s)
                    if full:
                        bz = nb()

                        def mmz(e):
                            r = None
                            for i in range(GW // XB):
                                for k in range(KC):
                                    r = e.matmul(ps[bz][:, i * XB:(i + 1) * XB], lhsT=xn[:, k, cs], rhs=wz[i][:, k, :],
                                                 start=(k == 0), stop=(k == KC - 1))
                            return r
                        P.op("pe", mmz, reads=["xn"] + kz, writes=["ps%d" % bz])
                        bm = nb()
                        P.op("pe", lambda e: e.matmul(ps[bm][:, 0:Q], lhsT=Bg[:, cs], rhs=Cg[:, cs], start=True,
                                                     stop=True), reads=["Bg", "Cg"], writes=["ps%d" % bm])
                        P.op("dve", lambda e: e.tensor_tensor(MTm[:, :], ps[bm][:, 0:Q], TRI, ALU.mult),
                             reads=["ps%d" % bm, "cmat"], writes=["MTm"])
                        if sample:
                            P.op("dve", lambda e: e.tensor_tensor(
                                y1[:, :].rearrange("q (h p) -> q h p", p=64),
                                ps[bi][:, 0:GW].rearrange("q (h p) -> q h p", p=64),
                                expa[:, :].unsqueeze(2).broadcast_to([Q, R, 64]), ALU.mult),
                                 reads=["ps%d" % bi, "expa"], writes=["y1"])
                            reserved.discard(bi)
                        P.op("dve", lambda e: e.tensor_tensor(
                            Wm[:, :, :], Wm[:, :, :], MTm[:, :].unsqueeze(1).broadcast_to([Q, R, Q]),
                            ALU.mult), reads=["MTm", "Wm"], writes=["Wm"])
                        by = nb()

                        def mmy(e):
                            for hh in range(R):
                                r = e.matmul(ps[by][:, hh * 64:(hh + 1) * 64], lhsT=Wm[:, hh, :],
                                             rhs=xdt[:, hh * 64:(hh + 1) * 64], start=True, stop=True)
                            return r
                        P.op("pe", mmy, reads=["Wm", "xdt"], writes=["ps%d" % by])
                        if not sample:
                            bi = nb()
                            P.op("pe", lambda e: e.matmul(ps[bi][:, 0:GW], lhsT=Cg[:, cs], rhs=Sb[:, :], start=True,
                                                         stop=True), reads=["Cg", "Sb"], writes=["ps%d" % bi])
                            P.op("dve", lambda e: e.tensor_tensor(
                                y1[:, :].rearrange("q (h p) -> q h p", p=64),
                                ps[bi][:, 0:GW].rearrange("q (h p) -> q h p", p=64),
                                expa[:, :].unsqueeze(2).broadcast_to([Q, R, 64]), ALU.mult),
                                 reads=["ps%d" % bi, "expa"], writes=["y1"])
                        P.op("dve", lambda e: e.tensor_tensor(y1[:, :], y1[:, :], ps[by][:, 0:GW], ALU.add),
                             reads=["y1", "ps%d" % by], writes=["y1"])
                        P.op("dve", lambda e: e.tensor_tensor(y1[:, :], y1[:, :], xD[:, :], ALU.add),
                             reads=["y1", "xD"], writes=["y1"])
                        P.op("act", lambda e: e.activation(out=xD[:, :], in_=ps[bz][:, 0:GW], func=AF.Tanh, scale=0.5),
                             reads=["ps%d" % bz, "xD"], writes=["xD"])
                        P.op("dve", lambda e: e.scalar_tensor_tensor(xD[:, :], xD[:, :], 1.0, ps[bz][:, 0:GW], ALU.add,
                                                                    ALU.mult),
                             reads=["xD", "ps%d" % bz], writes=["xD"])
                        P.op("dve", lambda e: e.scalar_tensor_tensor(y1[:, :], y1[:, :], 0.5, xD[:, :], ALU.mult, ALU.mult),
                             reads=["y1", "xD"], writes=["y1"])
                        P.op("act", lambda e: e.activation(out=xD[:, :], in_=y1[:, :], func=AF.Square,
                                                          accum_out=ss[:, 0:1]), reads=["y1", "xD"],
                             writes=["xD", "ss"])
                        P.op("dve", lambda e: e.tensor_scalar(ss[:, 1:2], ss[:, 0:1], 1.0 / GW, EPS, ALU.mult, ALU.add),
                             reads=["ss"], writes=["ss"])
                        P.op("act", lambda e: e.activation(out=ss[:, 1:2], in_=ss[:, 1:2], func=AF.Sqrt),
                             reads=["ss"], writes=["ss"])
                        P.op("dve", lambda e: e.reciprocal(ss[:, 1:2], ss[:, 1:2]), reads=["ss"], writes=["ss"])
                        P.op("dve", lambda e: e.tensor_scalar(gyn[:, :], y1[:, :], ss[:, 1:2], None, ALU.mult),
                             reads=["y1", "ss"], writes=["gyn"])
                        bt = nb()

                        def trg(e):
                            for c in range(GC):
                                r = e.transpose(psb[bt][:, c * 128:c * 128 + Q], gyn[:, c * 128:(c + 1) * 128], ident_b)
                            return r
                        P.op("pe", trg, reads=["gyn", "cmb"], writes=["ps%d" % bt])
                        o_ng = cfg.voff["ng"][0]
                        for c in range(GC):
                            P.op("dve" if c % 2 == 0 else "act",
                                 (lambda e, c=c: e.tensor_scalar(ygn[:, c, cs], psb[bt][:, c * 128:c * 128 + Q],
                                                                 vecs[:, o_ng + g * GC + c:o_ng + g * GC + c + 1], None,
                                                                 ALU.mult)) if c % 2 == 0 else
                                 (lambda e, c=c: e.activation(out=ygn[:, c, cs], in_=psb[bt][:, c * 128:c * 128 + Q],
                                                              func=AF.Copy,
                                                              scale=vecs[:, o_ng + g * GC + c:o_ng + g * GC + c + 1])),
                                 reads=["ps%d" % bt, "vecs"], writes=[("ygn", c)])
                    if not sample:
                        bs = nb()
                        P.op("pe", lambda e: e.matmul(ps[bs][:, 0:GW], lhsT=Btok[:, :], rhs=xw[:, :], start=True, stop=True),
                             reads=["Btok", "xw"], writes=["ps%d" % bs])
                        P.op("dve", lambda e: e.tensor_tensor(
                            S[:, :].rearrange("n (h p) -> n h p", p=64), S[:, :].rearrange("n (h p) -> n h p", p=64),
                            cdec[:, :].unsqueeze(2).broadcast_to([128, R, 64]), ALU.mult), reads=["S", "cdec", "Sb"],
                             writes=["S"])
                        P.op("dve", lambda e: e.tensor_tensor(S[:, :], S[:, :], ps[bs][:, 0:GW], ALU.add),
                             reads=["S", "ps%d" % bs], writes=["S"])
                        P.op("act", lambda e: e.copy(Sb[:, :], S[:, :]), reads=["S"], writes=["Sb"])

                def state_out(dst_ap_fn):
                    bo = nb()

                    def tro(e):
                        for c in range(GC):
                            r = e.transpose(ps[bo][:, c * 128:(c + 1) * 128], S[:, c * 128:(c + 1) * 128], ident_f)
                        return r
                    P.op("pe", tro, reads=["S", "cmat"], writes=["ps%d" % bo])
                    P.op("act", lambda e: e.copy(sost[0][:, :, :], ps[bo][:, 0:GW].rearrange("p (c n) -> p c n", n=128)),
                         reads=["ps%d" % bo], writes=["sost_0"])
                    P.op("sp", lambda e: e.dma_start(out=dst_ap_fn(), in_=sost[0][:, :, :]), reads=["sost_0"],
                         dma="sost_0")

                sidx = [0]
                for g in range(G):
                    specs = []
                    for i in range(GW // XB):
                        col = DI + g * GW + i * XB
                        for j in range(XB // 128):
                            c = i * (XB // 128) + j
                            specs.append((ssd_w_in[:, col:col + XB], XB, j, g * GC + c, ("x", c), i))
                    colB = 2 * DI + g * 128
                    specs.append((ssd_w_in[:, colB:colB + 128], 128, 0, DI // 128 + g, ("B", 0), "B"))
                    colC = 2 * DI + GN + g * 128
                    specs.append((ssd_w_in[:, colC:colC + 128], 128, 0, DI // 128 + G + g, ("C", 0), "C"))
                    loaded = {}

                    def ensure(k):
                        if k < len(specs) and specs[k][5] not in loaded:
                            loaded[specs[k][5]] = wload(specs[k][0], KC, specs[k][1])
                    LOOK = 3
                    for k in range(LOOK):
                        ensure(k)
                    for idx in range(len(specs)):
                        ensure(idx + LOOK)
                        _, _, j, cidx, (kind, c), lk = specs[idx]
                        wv_, kv_ = loaded[lk]
                        sl = sidx[0] % 2
                        conv_state_io("B", pre, pre_s, sl, 4, ssdc_state, cidx, None, None, stg_in, None, None, sidx[0])
                        for (c0, n) in tts:
                            b = nb()
                            mm_fm(b, wv_, kv_, KC, j, xn, ["xn"], c0, n)
                            pp, sp_ = split_cols(c0, n)
                            if pp:
                                a, b2 = pp
                                P.op("act", lambda e, a=a, b2=b2, b=b, c0=c0, sl=sl: e.copy(pre[:, sl, 3 + a:3 + b2],
                                                                                   ps[b][:, a - c0:b2 - c0]),
                                     reads=["ps%d" % b], writes=["Bpre%d" % sl])
                            if sp_:
                                a, b2 = sp_
                                P.op("dve", lambda e, a=a, b2=b2, b=b, c0=c0, sl=sl: e.tensor_copy(
                                    pre_s[:, sl, :, 3:11], ps[b][:, a - c0:b2 - c0].rearrange("p (s t) -> p s t", t=8)),
                                     reads=["ps%d" % b], writes=["Bpres%d" % sl])
                        kt = conv_block2("B", pre, pre_s, tcv, sl, 4, ["cw0", "cw1", "cw2", "cw3"], cidx,
                                         ["Bpre%d" % sl, "Bpre%dz" % sl], ["Bpres%d" % sl, "Bpres%dh" % sl])
                        if kind == "x":
                            dst, dk = xsg[:, c, :], ("xsg", c)
                        elif kind == "B":
                            dst, dk = Bg[:, :], "Bg"
                        else:
                            dst, dk = Cg[:, :], "Cg"
                        P.op("act", lambda e, dst=dst, cidx=cidx: e.activation(out=dst, in_=tcv[:, 0, :], func=AF.Silu,
                                                                             bias=V("cb", cidx)),
                             reads=kt + ["vecs"], writes=[dk])
                        conv_state_out("B", pre, pre_s, sl, 4, cidx, ssdc_out_p, ssdc_out_s, stg_out, cost,
                                       ["Bpre%d" % sl, "Bpres%d" % sl, "Bpres%dh" % sl], sidx[0])
                        sidx[0] += 1
                    wz, kz = [], []
                    for i in range(GW // XB):
                        col = g * GW + i * XB
                        w_, k_ = wload(ssd_w_in[:, col:col + XB], KC, XB)
                        wz.append(w_)
                        kz.append(k_)
                    P.op("dve", lambda e: e.memset(S[:, :], 0.0), reads=["Sb"], writes=["S"])
                    for q in range(NQ):
                        chunk(g, q, False, wz, kz)
                    ibk, obk = "ib%d" % g, "ob%d" % g
                    P.op("sp", lambda e, g=g: e.dma_start(out=ibs[g][:, :], in_=S[:, :]), reads=["S"], writes=[ibk],
                         dma="xch")
                    P.op("pool", lambda e, g=g: e.collective_compute(
                        "AllGather", ALU.bypass, replica_groups=[[0, 1], [2, 3], [4, 5], [6, 7]],
                        ins=[ibs[g].ap().opt()], outs=[obs[g].ap().opt()]), reads=[ibk], writes=[obk],
                         dma="cc%d" % g, inc=1)
                    chunk(g, NQ, True, wz, kz, sample=True)
                    P.op("sp", lambda e, g=g: e.dma_start(out=S[:, :], in_=obs[g][0:128, :]), reads=[obk, "Sb"],
                         writes=["S"], dma="xch")
                    P.op("dve", lambda e: e.tensor_scalar(S[:, :], S[:, :], mko[:, 0:1], None, ALU.mult),
                         reads=["S", "mko"], writes=["S"])
                    P.op("act", lambda e: e.copy(Sb[:, :], S[:, :]), reads=["S"], writes=["Sb"])
                    for q in range(NQ):
                        chunk(g, q, True, wz, kz)
                    state_out(lambda g=g: st_out_p[g * GW:(g + 1) * GW, :].rearrange("(c p) n -> p c n", p=128))
                    for mcp in range(D // 256):
                        wo, ko = wload(ssd_w_out[g * GW:(g + 1) * GW, mcp * 256:mcp * 256 + 256], GC, 256)
                        for j in range(2):
                            mc = mcp * 2 + j
                            for (c0, n) in tts:
                                b = nb()
                                mm_fm(b, wo, ko, GC, j, ygn, [("ygn", k) for k in range(GC)], c0, n)
                                acc_into_h(b, mc, c0, n)
                P.end_phase()

        def out_phase():
            with ExitStack() as ph:
                yst = [sb("ystg%d" % i, [128, D], F32, ph) for i in range(2)]
                for ri, (r0, n) in enumerate(rts):
                    s = ri % 2
                    for cg in range(KC // 4):
                        b = nb()

                        def tr(e, n=n, cg=cg, b=b, r0=r0):
                            for j in range(4):
                                c = cg * 4 + j
                                r = e.transpose(ps[b][0:n, j * 128:(j + 1) * 128], h[:, c, r0:r0 + n], ident_f)
                            return r
                        P.op("pe", tr, reads=["h", "cmat"], writes=["ps%d" % b])
                        if cg % 2 == 0:
                            P.op("dve", lambda e, s=s, n=n, cg=cg, b=b: e.tensor_copy(yst[s][0:n, cg * 512:(cg + 1) * 512],
                                                                                    ps[b][0:n, :]),
                                 reads=["ps%d" % b], writes=[("yst%d" % s, cg)])
                        else:
                            P.op("act", lambda e, s=s, n=n, cg=cg, b=b: e.copy(yst[s][0:n, cg * 512:(cg + 1) * 512],
                                                                             ps[b][0:n, :]),
                                 reads=["ps%d" % b], writes=[("yst%d" % s, cg)])
                    P.op("sp", lambda e, s=s, r0=r0, n=n: e.dma_start(out=y_out[r0:r0 + n, :], in_=yst[s][0:n, :]),
                         reads=[("yst%d" % s, cg) for cg in range(KC // 4)], dma="yst%d" % s)
                P.end_phase()

        phases = [lambda: rmsnorm_phase("g_mix0"), l0_mixer_phase,
                  lambda: rmsnorm_phase("g_ffn0"), lambda: ffn_phase(0),
                  lambda: rmsnorm_phase("g_ple0"), lambda: ple_phase(0),
                  lambda: rmsnorm_phase("g_mix1"), ssd_phase,
                  lambda: rmsnorm_phase("g_ffn1"), lambda: ffn_phase(1),
                  lambda: rmsnorm_phase("g_ple1"), lambda: ple_phase(1),
                  lambda: rmsnorm_phase("g_final", final=True)]
        for pi, phf in enumerate(phases):
            if stop is not None and pi >= stop:
                break
            phf()
        out_phase()
        for k, v in P.cnt.items():
            if k[0] == "dma" and P.waited["sp"].get(k, 0) < v:
                nc.sync.wait_ge(P.getsem(k), v)
        print("ops", len(P.ops), "waits", P.nwait, "sems", len(P.sems), file=sys.stderr)
    return nc


def host_inputs(cfg, inp):
    D, KC, T, TP, NSEQ = cfg.D, cfg.KC, cfg.T, cfg.TP, cfg.NSEQ
    f = np.float32

    def pm(v):
        v = np.asarray(v, f)
        return np.ascontiguousarray(v.reshape(-1, 128).T)

    def bc(v):
        v = np.asarray(v, f).reshape(1, -1)
        return np.ascontiguousarray(np.broadcast_to(v, (128, v.shape[1])))
    vec = np.zeros((128, cfg.NV), f)

    def put(nm, a):
        o, w = cfg.voff[nm]
        assert a.shape == (128, w), (nm, a.shape, w)
        vec[:, o:o + w] = a
    for l in range(2):
        put("g_mix%d" % l, pm(inp["g_mix"][l]))
        put("g_ffn%d" % l, pm(inp["g_ffn"][l]))
        put("g_ple%d" % l, pm(inp["g_ple"][l]))
    put("g_final", pm(inp["g_final"]))
    for k in range(3):
        put("scw%d" % k, pm(inp["sc_w_conv"][0, k]))
    for k in range(4):
        put("cw%d" % k, pm(inp["ssd_conv_w"][0, k]))
    put("cb", pm(inp["ssd_conv_b"][0]))
    put("ng", pm(inp["ssd_norm_g"][0]))
    put("dtb", bc(inp["ssd_dt_bias"][0]))
    put("alog", bc(inp["ssd_a_log"][0]))
    put("dsk", bc(inp["ssd_d"][0]))
    cm = np.zeros((128, 640 + NSEQ), f)
    cm[:, 0:128] = np.eye(128, dtype=f)
    cm[:, 128:256] = np.triu(np.ones((128, 128), f))
    cm[:, 256:384] = 1.0
    sid = np.arange(128) // 8
    same = (sid[:, None] == sid[None, :]).astype(f)
    cm[:, 384:512] = cm[:, 128:256] * same
    cm[:, 512:640] = same
    cm[:, 640:640 + NSEQ] = (sid[:, None] == np.arange(NSEQ)[None, :]).astype(f)
    mt = np.zeros((128, NSEQ, 128), f)
    mt[:, sid, np.arange(128)] = 1.0
    shared = dict(vecs=vec, cmat=cm,
                  sc_w_in=np.ascontiguousarray(inp["sc_w_in"][0]), sc_w_out=np.ascontiguousarray(inp["sc_w_out"][0]),
                  ssd_w_in=np.ascontiguousarray(inp["ssd_w_in"][0]), ssd_w_out=np.ascontiguousarray(inp["ssd_w_out"][0]),
                  ffn_w_gate=np.ascontiguousarray(inp["ffn_w_gate"]), ffn_w_up=np.ascontiguousarray(inp["ffn_w_up"]),
                  ffn_w_down=np.ascontiguousarray(inp["ffn_w_down"]), ple_w_proj=np.ascontiguousarray(inp["ple_w_proj"]),
                  ple_w_gate=np.ascontiguousarray(inp["ple_w_gate"]))
    maps = []
    for core in range(8):
        b, half = core // 2, core % 2
        xin = np.zeros((T, D), f)
        pin = np.zeros((2, T, cfg.PLE), f)
        t0 = half * TP
        if half == 1:
            xin[0:HALO] = inp["x_prompt"][b, t0 - HALO:t0]
            pin[:, 0:HALO] = inp["p_prompt"][:, b, t0 - HALO:t0]
        xin[HALO:HALO + TP] = inp["x_prompt"][b, t0:t0 + TP]
        pin[:, HALO:HALO + TP] = inp["p_prompt"][:, b, t0:t0 + TP]
        sl = slice(core * NSEQ, (core + 1) * NSEQ)
        xin[HALO + TP:] = inp["x_sample"][sl].reshape(-1, D)
        pin[:, HALO + TP:] = inp["p_sample"][:, sl].reshape(2, -1, cfg.PLE)
        m = dict(shared)
        m.update(xin=xin, pin=pin,
                 sc_state=np.ascontiguousarray(inp["state_sc_conv"][0, sl].reshape(NSEQ * 2, D)),
                 ssdc_state=np.ascontiguousarray(inp["state_ssd_conv"][0, sl].reshape(NSEQ * 3, cfg.CONV)),
                 ssd_state=np.ascontiguousarray(inp["state_ssd"][0, sl].reshape(NSEQ, cfg.H * 64, 128)),
                 maskodd=np.full((128, 1), float(half), f))
        maps.append(m)
    return maps


def assemble(cfg, res):
    D, TP, NSEQ, H = cfg.D, cfg.TP, cfg.NSEQ, cfg.H
    f = np.float32
    B = 4
    y_p = np.zeros((B, 2 * TP, D), f)
    y_s = np.zeros((8 * NSEQ, cfg.DEC_SEQ, D), f)
    scp = np.zeros((1, B, 2, D), f)
    scs = np.zeros((1, 8 * NSEQ, 2, D), f)
    ssdcp = np.zeros((1, B, 3, cfg.CONV), f)
    ssdcs = np.zeros((1, 8 * NSEQ, 3, cfg.CONV), f)
    stp = np.zeros((1, B, H, 64, 128), f)
    sts = np.zeros((1, 8 * NSEQ, H, 64, 128), f)
    for core in range(8):
        r = res[core]
        b, half = core // 2, core % 2
        y = r["y"]
        y_p[b, half * TP:(half + 1) * TP] = y[HALO:HALO + TP]
        sl = slice(core * NSEQ, (core + 1) * NSEQ)
        y_s[sl] = y[HALO + TP:].reshape(NSEQ, cfg.DEC_SEQ, D)
        scs[0, sl] = r["sc_out_s"].reshape(NSEQ, 2, D)
        ssdcs[0, sl] = r["ssdc_out_s"].reshape(NSEQ, 3, cfg.CONV)
        sts[0, sl] = r["st_out_s"].reshape(NSEQ, H, 64, 128)
        if half == 1:
            scp[0, b] = r["sc_out_p"]
            ssdcp[0, b] = r["ssdc_out_p"]
            stp[0, b] = r["st_out_p"].reshape(H, 64, 128)
    return (y_p, y_s, scp, scs, ssdcp, ssdcs, stp, sts)


_NC_CACHE = {}


def run(cfg, inp, stop=None):
    key = (cfg.D, cfg.SEQ, stop)
    if key not in _NC_CACHE:
        _NC_CACHE[key] = build(cfg, stop)
    nc = _NC_CACHE[key]
    maps = host_inputs(cfg, inp)
    res = run_bass_kernel_spmd(nc, maps, core_ids=list(range(8)))
    return assemble(cfg, res.results)


def kernel(**inputs):
    cfg = Cfg()
    inp = {k: np.asarray(v) for k, v in inputs.items()}
    return run(cfg, inp)
```

```python
import sys
from contextlib import ExitStack
import numpy as np
import concourse.bass as bass
import concourse.mybir as mybir
from concourse.bass_utils import run_bass_kernel_spmd

F32 = mybir.dt.float32
BF16 = mybir.dt.bfloat16
AF = mybir.ActivationFunctionType
ALU = mybir.AluOpType
EPS = 1e-6
HALO = 5


class Cfg:
    def __init__(self, D=2048, SEQ=2048, DEC_BATCH=128, DEC_SEQ=8, PLE=256):
        self.D = D
        self.SEQ = SEQ
        self.BATCH = 4
        self.DEC_BATCH = DEC_BATCH
        self.DEC_SEQ = DEC_SEQ
        self.PLE = PLE
        self.DFF = -(-8 * D // (3 * 256)) * 256
        self.DI = 2 * D
        self.H = self.DI // 64
        self.G = 8
        self.N = 128
        self.R = self.H // 8
        self.GW = self.DI // 8
        self.GC = self.GW // 128
        self.GN = self.G * self.N
        self.CONV = self.DI + 2 * self.GN
        self.IN = self.DI + self.CONV + self.H
        self.KC = D // 128
        self.FC = self.DFF // 128
        self.CC = self.CONV // 128
        self.TP = SEQ // 2
        self.NQ = self.TP // 128
        self.NSEQ = DEC_BATCH // 8
        self.TS = self.NSEQ * DEC_SEQ
        self.PL = HALO + self.TP
        self.T = self.PL + self.TS
        off = {}
        o = 0
        for nm, w in [("g_mix0", self.KC), ("g_ffn0", self.KC), ("g_ple0", self.KC),
                      ("g_mix1", self.KC), ("g_ffn1", self.KC), ("g_ple1", self.KC),
                      ("g_final", self.KC), ("scw0", self.KC), ("scw1", self.KC), ("scw2", self.KC),
                      ("cw0", self.CC), ("cw1", self.CC), ("cw2", self.CC), ("cw3", self.CC),
                      ("cb", self.CC), ("ng", self.DI // 128),
                      ("dtb", self.H), ("alog", self.H), ("dsk", self.H)]:
            off[nm] = (o, w)
            o += w
        self.voff = off
        self.NV = o


class Prog:
    ENGS = ("pe", "act", "dve", "pool", "sp")

    def __init__(self, nc, stack):
        self.nc = nc
        self.stack = stack
        self.ops = []
        self.lastw = {}
        self.readers = {}
        self.floor = 0
        self.emitted = 0
        self.sems = {}
        self.cnt = {}
        self.waited = {e: {} for e in self.ENGS}
        self.lastsig = {}
        self.nwait = 0
        self.eng = dict(pe=nc.tensor, act=nc.scalar, dve=nc.vector, pool=nc.gpsimd, sp=nc.sync)

    def op(self, eng, fn, reads=(), writes=(), dma=None, inc=16):
        i = len(self.ops)
        psr = [r for r in reads if isinstance(r, str) and r.startswith("ps")]
        if psr:
            reads = [r for r in reads if r not in psr]
            writes = list(writes) + psr
        deps = set()
        for r in reads:
            w = self.lastw.get(r)
            if w is not None:
                deps.add(w)
        for r in writes:
            w = self.lastw.get(r)
            if w is not None:
                deps.add(w)
            for x in self.readers.get(r, ()):
                deps.add(x)
        for r in reads:
            self.readers.setdefault(r, []).append(i)
        for r in writes:
            self.lastw[r] = i
            self.readers[r] = []
        deps.discard(i)
        deps = {d for d in deps if d >= self.floor}
        self.ops.append(dict(eng=eng, fn=fn, deps=deps, dma=dma, inc=inc, sig=False))
        return i

    def getsem(self, k):
        if k not in self.sems:
            self.sems[k] = self.stack.enter_context(self.nc.semaphore("s%d" % len(self.sems)))
        return self.sems[k]

    def end_phase(self):
        ops = self.ops
        start = self.emitted
        last = {}
        for i in range(start, len(ops)):
            o = ops[i]
            k = ("dma", o["dma"]) if o["dma"] is not None else ("eng", o["eng"])
            last[k] = i
        bdeps = set(last.values())
        for e in self.ENGS:
            self.ops.append(dict(eng=e, fn=None, deps=set(bdeps), dma=None, inc=0, sig=False, barrier=True))
        for i in range(start, len(ops)):
            o = ops[i]
            nd = set()
            for d in o["deps"]:
                p = ops[d]
                if (not o.get("barrier")) and p["dma"] is None and o["dma"] is None \
                        and p["eng"] == "pe" and o["eng"] == "pe":
                    continue
                nd.add(d)
            o["deps"] = nd
            for d in nd:
                ops[d]["sig"] = True
        for i in range(start, len(ops)):
            o = ops[i]
            if o["dma"] is not None:
                k = ("dma", o["dma"])
                self.cnt[k] = self.cnt.get(k, 0) + o["inc"]
                o["semk"], o["val"] = k, self.cnt[k]
            elif o["sig"]:
                k = ("eng", o["eng"])
                self.cnt[k] = self.cnt.get(k, 0) + 1
                o["semk"], o["val"] = k, self.cnt[k]
        for i in range(start, len(ops)):
            o = ops[i]
            e = o["eng"]
            need = {}
            for d in o["deps"]:
                p = ops[d]
                k = p["semk"]
                need[k] = max(need.get(k, 0), p["val"])
            for k, v in need.items():
                if self.waited[e].get(k, 0) >= v:
                    continue
                self.eng[e].wait_ge(self.getsem(k), v)
                self.waited[e][k] = v
                self.nwait += 1
            if o["fn"] is None:
                continue
            ins = o["fn"](self.eng[e])
            if o["dma"] is not None:
                ins.then_inc(self.getsem(o["semk"]), o["inc"])
            elif o["sig"]:
                ins.then_inc(self.getsem(o["semk"]), 1)
            o["fn"] = None
        self.emitted = len(ops)
        self.floor = len(ops)
        self.lastw = {}
        self.readers = {}


def split_tiles(T, maxn=448):
    nt = -(-T // maxn)
    base = T // nt
    rem = T % nt
    out = []
    c = 0
    for i in range(nt):
        n = base + (1 if i < rem else 0)
        out.append((c, n))
        c += n
    return out


def build(cfg, stop=None):
    D, KC, T, PL, TS, NSEQ = cfg.D, cfg.KC, cfg.T, cfg.PL, cfg.TS, cfg.NSEQ
    DI, H, G, R, GW, GC, GN, CC, FC = cfg.DI, cfg.H, cfg.G, cfg.R, cfg.GW, cfg.GC, cfg.GN, cfg.CC, cfg.FC
    PLE, DFF, CONV, IN, NQ = cfg.PLE, cfg.DFF, cfg.CONV, cfg.IN, cfg.NQ
    PK = PLE // 128
    nc = bass.Bass("TRN2", target_bir_lowering=False)
    CMW = 640 + NSEQ

    def din(name, shape):
        return nc.dram_tensor(name, list(shape), F32, kind="ExternalInput").ap()

    def dout(name, shape):
        return nc.dram_tensor(name, list(shape), F32, kind="ExternalOutput").ap()

    xin = din("xin", [T, D])
    pin = din("pin", [2, T, PLE])
    sc_state = din("sc_state", [NSEQ * 2, D])
    ssdc_state = din("ssdc_state", [NSEQ * 3, CONV])
    ssd_state = din("ssd_state", [NSEQ, H * 64, 128])
    maskodd = din("maskodd", [128, 1])
    vecs_d = din("vecs", [128, cfg.NV])
    cmat_d = din("cmat", [128, CMW])
    sc_w_in = din("sc_w_in", [D, 3 * D])
    sc_w_out = din("sc_w_out", [D, D])
    ssd_w_in = din("ssd_w_in", [D, IN])
    ssd_w_out = din("ssd_w_out", [DI, D])
    ffn_w_gate = din("ffn_w_gate", [2, D, DFF])
    ffn_w_up = din("ffn_w_up", [2, D, DFF])
    ffn_w_down = din("ffn_w_down", [2, DFF, D])
    ple_w_proj = din("ple_w_proj", [2, PLE, D])
    ple_w_gate = din("ple_w_gate", [2, D, D])

    y_out = dout("y", [T, D])
    sc_out_p = dout("sc_out_p", [2, D])
    sc_out_s = dout("sc_out_s", [NSEQ * 2, D])
    ssdc_out_p = dout("ssdc_out_p", [3, CONV])
    ssdc_out_s = dout("ssdc_out_s", [NSEQ * 3, CONV])
    st_out_p = dout("st_out_p", [H * 64, 128])
    st_out_s = dout("st_out_s", [NSEQ, H * 64, 128])
    ibs = [nc.dram_tensor("ib%d" % g, [128, GW], F32) for g in range(G)]
    obs = [nc.dram_tensor("ob%d" % g, [256, GW], F32) for g in range(G)]

    tts = split_tiles(T)
    for (c0, n) in tts:
        assert not (c0 < PL < c0 + n and False)
    assert any(c0 <= PL and PL + TS <= c0 + n for (c0, n) in tts) or True
    rts = [(r0, min(128, T - r0)) for r0 in range(0, T, 128)]

    with ExitStack() as st:
        sbctr = [0]

        def sb(name, shape, dt=F32, stack=None):
            sbctr[0] += 1
            return (stack or st).enter_context(nc.sbuf_tensor("%s_%d" % (name, sbctr[0]), list(shape), dt))

        P = Prog(nc, st)
        h = sb("h", [128, KC, T])
        xn = sb("xn", [128, KC, T], BF16)
        vecs = sb("vecs", [128, cfg.NV])
        cmat = sb("cmat", [128, CMW])
        cmb = sb("cmb", [128, CMW], BF16)
        abc = sb("abc", [128, H])
        mko = sb("mko", [128, 1])
        ps = [st.enter_context(nc.psum_tensor("ps%d" % i, [128, 512], F32)) for i in range(8)]
        psb = [p[:, :].bitcast(BF16) for p in ps]
        ident_f, tri_f, ones_f = cmat[:, 0:128], cmat[:, 128:256], cmat[:, 256:384]
        triS_f, blk_f, maskJ_f = cmat[:, 384:512], cmat[:, 512:640], cmat[:, 640:640 + NSEQ]
        ident_b, tri_b, ones_b = cmb[:, 0:128], cmb[:, 128:256], cmb[:, 256:384]

        def V(nm, c=None):
            o, w = cfg.voff[nm]
            if c is None:
                return vecs[:, o:o + w]
            return vecs[:, o + c:o + c + 1]

        bankctr = [0]

        reserved = set()

        def nb():
            while True:
                b = bankctr[0] % 8
                bankctr[0] += 1
                if b not in reserved:
                    return b

        ringstate = {}

        def make_ring(stack, nslots, tag, slot=4096):
            bufs = [sb("ring%s%d" % (tag, i), [128, slot], BF16, stack) for i in range(nslots)]
            ringstate["bufs"] = bufs
            ringstate["i"] = 0
            ringstate["slot"] = slot

        def wload(src, nk, ncols):
            bufs = ringstate["bufs"]
            i = ringstate["i"] % len(bufs)
            ringstate["i"] += 1
            assert nk * ncols <= ringstate["slot"]
            view = bufs[i][:, 0:nk * ncols].rearrange("p (k n) -> p k n", n=ncols)
            key = "ring%d" % i
            if key in P.lastw:
                assert P.readers.get(key), "ring slot %s overwritten before any consumer was recorded" % key
            srcv = src.rearrange("(k p) n -> p k n", p=128)
            P.op("pool", lambda e: e.dma_start(out=view, in_=srcv), writes=[key], dma=key)
            return view, key

        with ExitStack() as ph:
            xs = [sb("xstg%d" % i, [128, D], F32, ph) for i in range(2)]
            P.op("sp", lambda e: e.dma_start(out=vecs[:], in_=vecs_d), writes=["vecs"], dma="c0")
            P.op("sp", lambda e: e.dma_start(out=cmat[:], in_=cmat_d), writes=["cmat"], dma="c1")
            P.op("sp", lambda e: e.dma_start(out=mko[:], in_=maskodd), writes=["mko"], dma="c2")
            P.op("dve", lambda e: e.tensor_copy(cmb[:], cmat[:]), reads=["cmat"], writes=["cmb"])
            o_al, w_al = cfg.voff["alog"]
            P.op("act", lambda e: e.activation(out=abc[:], in_=vecs[:, o_al:o_al + w_al], func=AF.Exp),
                 reads=["vecs"], writes=["abc"])
            P.op("dve", lambda e: e.tensor_scalar(abc[:], abc[:], -1.0, None, ALU.mult), reads=["abc"], writes=["abc"])
            for ri, (r0, n) in enumerate(rts):
                s = ri % 2
                P.op("sp", lambda e, s=s, r0=r0, n=n: e.dma_start(out=xs[s][0:n, :], in_=xin[r0:r0 + n, :]),
                     writes=["xs%d" % s], dma="xs%d" % s)
                for cg in range(KC // 4):
                    b = nb()

                    def tr(e, s=s, n=n, cg=cg, b=b):
                        for j in range(4):
                            c = cg * 4 + j
                            r = e.transpose(ps[b][:, j * 128:j * 128 + n], xs[s][0:n, c * 128:(c + 1) * 128],
                                            ident_f[0:n, 0:n])
                        return r
                    P.op("pe", tr, reads=["xs%d" % s, "cmat"], writes=["ps%d" % b])
                    src = ps[b][:, :].rearrange("p (j t) -> p j t", t=128)[:, :, 0:n]
                    dst = h[:, cg * 4:cg * 4 + 4, r0:r0 + n]
                    if cg % 2 == 0:
                        P.op("dve", lambda e, src=src, dst=dst: e.tensor_copy(dst, src), reads=["ps%d" % b],
                             writes=[("h", ri, cg)])
                    else:
                        P.op("act", lambda e, src=src, dst=dst: e.copy(dst, src), reads=["ps%d" % b],
                             writes=[("h", ri, cg)])
            P.end_phase()

        def rmsnorm_phase(gname, final=False):
            with ExitStack() as ph:
                sq = [sb("sq%d" % i, [128, KC, tts[0][1]], BF16, ph) for i in range(2)]
                rstd = sb("rstd", [128, T], F32, ph)
                for ti, (c0, n) in enumerate(tts):
                    s = ti % 2
                    P.op("act", lambda e, s=s, c0=c0, n=n: e.activation(out=sq[s][:, :, 0:n], in_=h[:, :, c0:c0 + n],
                                                                      func=AF.Square),
                         reads=["h"], writes=["sq%d" % s])
                    b = nb()

                    def mm(e, s=s, n=n, b=b):
                        for k in range(KC):
                            r = e.matmul(ps[b][:, 0:n], lhsT=ones_b, rhs=sq[s][:, k, 0:n], start=(k == 0),
                                         stop=(k == KC - 1))
                        return r
                    P.op("pe", mm, reads=["sq%d" % s, "cmb"], writes=["ps%d" % b])
                    P.op("dve", lambda e, b=b, c0=c0, n=n: e.tensor_scalar(rstd[:, c0:c0 + n], ps[b][:, 0:n], 1.0 / D,
                                                                         EPS, ALU.mult, ALU.add),
                         reads=["ps%d" % b], writes=["rs%d" % ti])
                    P.op("act", lambda e, c0=c0, n=n: e.activation(out=rstd[:, c0:c0 + n], in_=rstd[:, c0:c0 + n],
                                                                 func=AF.Sqrt),
                         reads=["rs%d" % ti], writes=["rs%d" % ti])
                    P.op("dve", lambda e, c0=c0, n=n: e.reciprocal(rstd[:, c0:c0 + n], rstd[:, c0:c0 + n]),
                         reads=["rs%d" % ti], writes=["rs%d" % ti])
                rk = ["rs%d" % ti for ti in range(len(tts))]
                for c in range(KC):
                    dst = h[:, c, :] if final else xn[:, c, :]
                    P.op("dve", lambda e, c=c, dst=dst: e.scalar_tensor_tensor(dst, h[:, c, :], V(gname, c), rstd[:, :],
                                                                             ALU.mult, ALU.mult),
                         reads=rk + ["vecs", "h"], writes=[("hf" if final else "xn", c)])
                P.end_phase()

        def acc_into_h(b, mc, c0, n):
            P.op("dve", lambda e: e.tensor_tensor(h[:, mc, c0:c0 + n], h[:, mc, c0:c0 + n], ps[b][:, 0:n], ALU.add),
                 reads=["ps%d" % b, ("h", mc)], writes=[("h", mc)])

        def mm_fm(b, wv, wkey, nk, j, rhs, rkeys, c0, n):
            def mm(e):
                for k in range(nk):
                    r = e.matmul(ps[b][:, 0:n], lhsT=wv[:, k, j * 128:(j + 1) * 128], rhs=rhs[:, k, c0:c0 + n],
                                 start=(k == 0), stop=(k == nk - 1))
                return r
            P.op("pe", mm, reads=[wkey] + list(rkeys), writes=["ps%d" % b])

        def ffn_phase(layer):
            npairs = FC // 2
            ngroups = max(1, -(-FC // 12))
            base = npairs // ngroups
            rem = npairs % ngroups
            gsz = [base + (1 if i < rem else 0) for i in range(ngroups)]
            maxk = max(gsz) * 2
            with ExitStack() as ph:
                make_ring(ph, 6, "f")
                act = sb("act", [128, maxk, T], BF16, ph)
                sg = [sb("sg%d" % i, [128, 512], F32, ph) for i in range(2)]
                sgi = 0
                pr0 = 0
                for gi, np_ in enumerate(gsz):
                    nkg = np_ * 2
                    for pr in range(np_):
                        col = (pr0 + pr) * 256
                        wg, kg = wload(ffn_w_gate[layer, :, col:col + 256], KC, 256)
                        wu, ku = wload(ffn_w_up[layer, :, col:col + 256], KC, 256)
                        for j in range(2):
                            fl = pr * 2 + j
                            for (c0, n) in tts:
                                bg_, bu_ = nb(), nb()
                                mm_fm(bg_, wg, kg, KC, j, xn, ["xn"], c0, n)
                                mm_fm(bu_, wu, ku, KC, j, xn, ["xn"], c0, n)
                                s = sgi % 2
                                sgi += 1
                                P.op("act", lambda e, s=s, b=bg_, n=n: e.activation(out=sg[s][:, 0:n], in_=ps[b][:, 0:n],
                                                                                  func=AF.Silu),
                                     reads=["ps%d" % bg_], writes=["sg%d" % s])
                                P.op("dve", lambda e, s=s, b=bu_, fl=fl, c0=c0, n=n: e.tensor_tensor(
                                    act[:, fl, c0:c0 + n], sg[s][:, 0:n], ps[b][:, 0:n], ALU.mult),
                                     reads=["sg%d" % s, "ps%d" % bu_], writes=[("act", fl)])
                    for mcp in range(D // 256):
                        wd, kd = wload(ffn_w_down[layer, pr0 * 256:pr0 * 256 + nkg * 128, mcp * 256:mcp * 256 + 256],
                                       nkg, 256)
                        for j in range(2):
                            mc = mcp * 2 + j
                            for (c0, n) in tts:
                                b = nb()
                                mm_fm(b, wd, kd, nkg, j, act, [("act", k) for k in range(nkg)], c0, n)
                                acc_into_h(b, mc, c0, n)
                    pr0 += np_
                P.end_phase()

        def ple_phase(layer):
            with ExitStack() as ph:
                make_ring(ph, 6, "p")
                pT = sb("pT", [128, PK, T], BF16, ph)
                pst = [sb("pst%d" % i, [128, PLE], F32, ph) for i in range(2)]
                sg = [sb("sgp%d" % i, [128, 512], F32, ph) for i in range(2)]
                for ri, (r0, n) in enumerate(rts):
                    s = ri % 2
                    P.op("sp", lambda e, s=s, r0=r0, n=n: e.dma_start(out=pst[s][0:n, :], in_=pin[layer, r0:r0 + n, :]),
                         writes=["pst%d" % s], dma="pst%d" % s)
                    b = nb()

                    def tr(e, s=s, n=n, b=b):
                        for j in range(PK):
                            r = e.transpose(ps[b][:, j * 128:j * 128 + n], pst[s][0:n, j * 128:(j + 1) * 128],
                                            ident_f[0:n, 0:n])
                        return r
                    P.op("pe", tr, reads=["pst%d" % s, "cmat"], writes=["ps%d" % b])
                    src = ps[b][:, 0:PK * 128].rearrange("p (j t) -> p j t", t=128)[:, :, 0:n]
                    P.op("dve", lambda e, src=src, r0=r0, n=n: e.tensor_copy(pT[:, :, r0:r0 + n], src),
                         reads=["ps%d" % b], writes=["pT"])
                sgi = 0
                for mcp in range(D // 256):
                    wg, kg = wload(ple_w_gate[layer, :, mcp * 256:mcp * 256 + 256], KC, 256)
                    wp, kp = wload(ple_w_proj[layer, :, mcp * 256:mcp * 256 + 256], PK, 256)
                    for j in range(2):
                        mc = mcp * 2 + j
                        for (c0, n) in tts:
                            bg_, bp_ = nb(), nb()
                            mm_fm(bg_, wg, kg, KC, j, xn, ["xn"], c0, n)
                            mm_fm(bp_, wp, kp, PK, j, pT, ["pT"], c0, n)
                            s = sgi % 2
                            sgi += 1
                            P.op("act", lambda e, s=s, b=bg_, n=n: e.activation(out=sg[s][:, 0:n], in_=ps[b][:, 0:n],
                                                                              func=AF.Sigmoid),
                                 reads=["ps%d" % bg_], writes=["sgp%d" % s])
                            P.op("dve", lambda e, s=s, b=bp_, n=n: e.tensor_tensor(sg[s][:, 0:n], sg[s][:, 0:n],
                                                                                 ps[b][:, 0:n], ALU.mult),
                                 reads=["sgp%d" % s, "ps%d" % bp_], writes=["sgp%d" % s])
                            P.op("dve", lambda e, s=s, mc=mc, c0=c0, n=n: e.tensor_tensor(
                                h[:, mc, c0:c0 + n], h[:, mc, c0:c0 + n], sg[s][:, 0:n], ALU.add),
                                 reads=["sgp%d" % s, ("h", mc)], writes=[("h", mc)])
                P.end_phase()

        def split_cols(c0, n):
            pa, pb = c0, min(c0 + n, PL)
            sa, sb_ = max(c0, PL), c0 + n
            return (pa, pb) if pb > pa else None, (sa, sb_) if sb_ > sa else None

        def conv_state_io(ph_tag, pre, pre_s, slot, W0, state_d, cidx, out_p, out_s, stg_in, stg_out, cost, sidx):
            nr = W0 - 1
            ks = ph_tag + "pres%d" % slot
            si = sidx % 2
            P.op("pool", lambda e: e.dma_start(out=stg_in[si][0:NSEQ * nr, :], in_=state_d[:, cidx * 128:(cidx + 1) * 128]),
                 writes=[ph_tag + "sti%d" % si], dma=ph_tag + "sti%d" % si)
            b = nb()
            P.op("pe", lambda e: e.transpose(ps[b][:, 0:NSEQ * nr], stg_in[si][0:NSEQ * nr, :],
                                            ident_f[0:NSEQ * nr, 0:NSEQ * nr]),
                 reads=[ph_tag + "sti%d" % si, "cmat"], writes=["ps%d" % b])
            src = ps[b][:, 0:NSEQ * nr].rearrange("p (s r) -> p s r", r=nr)
            P.op("act", lambda e: e.copy(pre_s[:, slot, :, 0:nr], src), reads=["ps%d" % b], writes=[ks + "h"])

        def conv_state_out(ph_tag, pre, pre_s, slot, W0, cidx, out_p, out_s, stg_out, cost, rd_keys, sidx):
            nr = W0 - 1
            si = sidx % 2
            b = nb()
            P.op("pe", lambda e: e.transpose(ps[b][0:nr, 0:128], pre[:, slot, PL:PL + nr], ident_f),
                 reads=rd_keys + ["cmat"], writes=["ps%d" % b])
            P.op("dve", lambda e: e.tensor_copy(stg_out[si][0:nr, 0:128], ps[b][0:nr, 0:128]), reads=["ps%d" % b],
                 writes=[ph_tag + "stoP%d" % si])
            P.op("sp", lambda e: e.dma_start(out=out_p[:, cidx * 128:(cidx + 1) * 128], in_=stg_out[si][0:nr, 0:128]),
                 reads=[ph_tag + "stoP%d" % si], dma=ph_tag + "stoP%d" % si)
            P.op("dve", lambda e: e.tensor_copy(cost[si][:, 0:NSEQ * nr].rearrange("p (s r) -> p s r", r=nr),
                                               pre_s[:, slot, :, 8:8 + nr]),
                 reads=rd_keys, writes=[ph_tag + "cost%d" % si])
            b2 = nb()
            P.op("pe", lambda e: e.transpose(ps[b2][0:NSEQ * nr, 0:128], cost[si][:, 0:NSEQ * nr], ident_f),
                 reads=[ph_tag + "cost%d" % si, "cmat"], writes=["ps%d" % b2])
            P.op("act", lambda e: e.copy(stg_out[si][0:NSEQ * nr, 128:256], ps[b2][0:NSEQ * nr, 0:128]),
                 reads=["ps%d" % b2], writes=[ph_tag + "stoS%d" % si])
            P.op("sp", lambda e: e.dma_start(out=out_s[:, cidx * 128:(cidx + 1) * 128],
                                            in_=stg_out[si][0:NSEQ * nr, 128:256]),
                 reads=[ph_tag + "stoS%d" % si], dma=ph_tag + "stoS%d" % si)

        def l0_mixer_phase():
            with ExitStack() as ph:
                make_ring(ph, 4, "m")
                gated = sb("gated", [128, KC, T], BF16, ph)
                pre = sb("l0pre", [128, 2, 2 + PL], F32, ph)
                pre_s = sb("l0pres", [128, 2, NSEQ, 10], F32, ph)
                tcv = sb("l0tcv", [128, 1, T], F32, ph)
                stg_in = [sb("l0sti%d" % i, [128, 128], F32, ph) for i in range(2)]
                stg_out = [sb("l0sto%d" % i, [128, 256], F32, ph) for i in range(2)]
                cost = [sb("l0cost%d" % i, [128, NSEQ * 3], F32, ph) for i in range(2)]
                P.op("dve", lambda e: e.memset(pre[:, 0, 0:2], 0.0), writes=["Apre0z"])
                P.op("dve", lambda e: e.memset(pre[:, 1, 0:2], 0.0), writes=["Apre1z"])
                for cp in range(KC // 2):
                    wb, kb = wload(sc_w_in[:, cp * 256:cp * 256 + 256], KC, 256)
                    wc, kc = wload(sc_w_in[:, D + cp * 256:D + cp * 256 + 256], KC, 256)
                    wv, kv = wload(sc_w_in[:, 2 * D + cp * 256:2 * D + cp * 256 + 256], KC, 256)
                    for j in range(2):
                        c = cp * 2 + j
                        sl = c % 2
                        conv_state_io("A", pre, pre_s, sl, 3, sc_state, c, None, None, stg_in, None, None, c)
                        for (c0, n) in tts:
                            b_c, b_v, b_b = nb(), nb(), nb()
                            mm_fm(b_c, wc, kc, KC, j, xn, ["xn"], c0, n)
                            mm_fm(b_v, wv, kv, KC, j, xn, ["xn"], c0, n)
                            mm_fm(b_b, wb, kb, KC, j, xn, ["xn"], c0, n)
                            P.op("act", lambda e, b=b_c, c0=c0, n=n: e.copy(tcv[:, 0, c0:c0 + n], ps[b][:, 0:n]),
                                 reads=["ps%d" % b_c], writes=["Atcv0p", "Atcv0s"])
                            pp, sp_ = split_cols(c0, n)
                            if pp:
                                a, b2 = pp
                                P.op("dve", lambda e, a=a, b2=b2, b=b_v, c0=c0, sl=sl: e.tensor_tensor(
                                    pre[:, sl, 2 + a:2 + b2], tcv[:, 0, a:b2], ps[b][:, a - c0:b2 - c0], ALU.mult),
                                     reads=["ps%d" % b_v, "Atcv0p", "Atcv0s"], writes=["Apre%d" % sl])
                            if sp_:
                                a, b2 = sp_
                                assert a == PL and b2 == T
                                P.op("dve", lambda e, a=a, b2=b2, b=b_v, c0=c0, sl=sl: e.tensor_tensor(
                                    pre_s[:, sl, :, 2:10],
                                    tcv[:, 0, a:b2].rearrange("p (s t) -> p s t", t=8),
                                    ps[b][:, a - c0:b2 - c0].rearrange("p (s t) -> p s t", t=8), ALU.mult),
                                     reads=["ps%d" % b_v, "Atcv0p", "Atcv0s"], writes=["Apres%d" % sl])
                            P.op("act", lambda e, b=b_b, c=c, c0=c0, n=n: e.copy(gated[:, c, c0:c0 + n], ps[b][:, 0:n]),
                                 reads=["ps%d" % b_b], writes=[("gated", c)])
                        kt = conv_block2("A", pre, pre_s, tcv, sl, 3, ["scw0", "scw1", "scw2"], c,
                                         ["Apre%d" % sl, "Apre%dz" % sl], ["Apres%d" % sl, "Apres%dh" % sl])
                        P.op("dve", lambda e, c=c: e.tensor_tensor(gated[:, c, :], gated[:, c, :], tcv[:, 0, :], ALU.mult),
                             reads=kt + [("gated", c)], writes=[("gated", c)])
                        conv_state_out("A", pre, pre_s, sl, 3, c, sc_out_p, sc_out_s, stg_out, cost,
                                       ["Apre%d" % sl, "Apres%d" % sl, "Apres%dh" % sl], c)
                for mcp in range(D // 256):
                    wo, ko = wload(sc_w_out[:, mcp * 256:mcp * 256 + 256], KC, 256)
                    for j in range(2):
                        mc = mcp * 2 + j
                        for (c0, n) in tts:
                            b = nb()
                            mm_fm(b, wo, ko, KC, j, gated, [("gated", k) for k in range(KC)], c0, n)
                            acc_into_h(b, mc, c0, n)
                P.end_phase()

        def conv_block2(tag, pre, pre_s, tcv, slot, KW, wnames, cidx, pkeys, skeys):
            ktp, kts = tag + "tcv0p", tag + "tcv0s"
            ts_v = tcv[:, 0, PL:T].rearrange("p (s t) -> p s t", t=8)
            for kk in range(KW):
                wcol = V(wnames[kk], cidx)
                if kk == 0:
                    P.op("dve", lambda e, wcol=wcol: e.tensor_scalar(tcv[:, 0, 0:PL], pre[:, slot, 0:PL], wcol, None,
                                                                   ALU.mult),
                         reads=pkeys + ["vecs"], writes=[ktp])
                    P.op("dve", lambda e, wcol=wcol: e.tensor_scalar(ts_v, pre_s[:, slot, :, 0:8], wcol, None, ALU.mult),
                         reads=skeys + ["vecs"], writes=[kts])
                else:
                    P.op("dve", lambda e, wcol=wcol, kk=kk: e.scalar_tensor_tensor(
                        tcv[:, 0, 0:PL], pre[:, slot, kk:kk + PL], wcol, tcv[:, 0, 0:PL], ALU.mult, ALU.add),
                         reads=pkeys + ["vecs", ktp], writes=[ktp])
                    P.op("dve", lambda e, wcol=wcol, kk=kk: e.scalar_tensor_tensor(
                        ts_v, pre_s[:, slot, :, kk:kk + 8], wcol, ts_v, ALU.mult, ALU.add),
                         reads=skeys + ["vecs", kts], writes=[kts])
            return [ktp, kts]

        def ssd_phase():
            XB = 128
            NT = NQ + 1
            tile_col = [HALO + q * 128 for q in range(NQ)] + [PL]
            with ExitStack() as pho:
              dt_all = sb("dt_all", [128, NT, H], F32, pho)
              dA_all = sb("dA_all", [128, NT, H], F32, pho)
              with ExitStack() as ph1:
                wdt = sb("wdt", [128, KC, H], BF16, ph1)
                o_dtb = cfg.voff["dtb"][0]
                P.op("pool", lambda e: e.dma_start(out=wdt[:], in_=ssd_w_in[:, DI + CONV:DI + CONV + H].rearrange(
                    "(k p) n -> p k n", p=128)), writes=["wdt"], dma="wdt")
                for ti in range(NT):
                    c0 = tile_col[ti]
                    bdt = nb()

                    def mmdt(e, c0=c0, bdt=bdt):
                        for k in range(KC):
                            r = e.matmul(ps[bdt][:, 0:H], lhsT=xn[:, k, c0:c0 + 128], rhs=wdt[:, k, :], start=(k == 0),
                                         stop=(k == KC - 1))
                        return r
                    P.op("pe", mmdt, reads=["xn", "wdt"], writes=["ps%d" % bdt])
                    P.op("dve", lambda e, ti=ti, bdt=bdt: e.tensor_tensor(dt_all[:, ti, :], ps[bdt][:, 0:H],
                                                                         vecs[:, o_dtb:o_dtb + H], ALU.add),
                         reads=["ps%d" % bdt, "vecs"], writes=[("dt", ti)])
                P.op("act", lambda e: e.activation(out=dt_all[:, :, :], in_=dt_all[:, :, :], func=AF.Exp),
                     reads=[("dt", ti) for ti in range(NT)], writes=["dt_all"])
                P.op("act", lambda e: e.activation(out=dt_all[:, :, :], in_=dt_all[:, :, :], func=AF.Ln, bias=1.0),
                     reads=["dt_all"], writes=["dt_all"])
                P.op("dve", lambda e: e.tensor_tensor(dA_all[:, :, :], dt_all[:, :, :],
                                                     abc[:, :].unsqueeze(1).broadcast_to([128, NT, H]), ALU.mult),
                     reads=["dt_all", "abc"], writes=["dA_all"])

                P.end_phase()
              with ExitStack() as ph:
                make_ring(ph, 5, "s", 2048)
                xsg = sb("xsg", [128, GC, T], BF16, ph)
                Bg = sb("Bg", [128, T], BF16, ph)
                Cg = sb("Cg", [128, T], BF16, ph)
                ygn = sb("ygn", [128, GC, T], BF16, ph)
                pre = sb("l1pre", [128, 2, 3 + PL], F32, ph)
                pre_s = sb("l1pres", [128, 2, NSEQ, 11], F32, ph)
                tcv = sb("l1tcv", [128, 1, T], F32, ph)
                stg_in = [sb("l1sti%d" % i, [128, 128], F32, ph) for i in range(2)]
                stg_out = [sb("l1sto%d" % i, [128, 256], F32, ph) for i in range(2)]
                cost = [sb("l1cost%d" % i, [128, NSEQ * 3], F32, ph) for i in range(2)]
                acs = sb("acs", [128, R], F32, ph)
                expa = sb("expa", [128, R], F32, ph)
                dE = sb("dE", [128, R], F32, ph)
                cdec = sb("cdec", [128, R], F32, ph)
                cdecF = sb("cdecF", [128, GC, NSEQ], F32, ph)
                Rm = sb("Rm", [128, R, 128], F32, ph)
                Wm = sb("Wm", [128, R, 128], BF16, ph)
                MTm = sb("MTm", [128, 128], BF16, ph)
                xdt = sb("xdt", [128, GW], BF16, ph)
                xw = sb("xw", [128, GW], BF16, ph)
                Btok = sb("Btok", [128, 128], BF16, ph)
                y1 = sb("y1", [128, GW], F32, ph)
                xD = sb("xD", [128, GW], F32, ph)
                gyn = sb("gyn", [128, GW], BF16, ph)
                ss = sb("ss", [128, 2], F32, ph)
                S = sb("S", [128, GW], F32, ph)
                Sb = sb("Sb", [128, GW], BF16, ph)
                h0 = [sb("h0_%d" % i, [128, GC, 128], F32, ph) for i in range(2)]
                h0T = [sb("h0T_%d" % i, [128, GW], BF16, ph) for i in range(1)]
                xwj = [sb("xwj_%d" % i, [128, GW], BF16, ph) for i in range(1)]
                sost = [sb("sost_%d" % i, [128, GC, 128], F32, ph) for i in range(2)]
                assert (NSEQ + 1) * 64 <= T
                Cm_t = tcv[:, 0, 0:(NSEQ + 1) * 64].bitcast(BF16)
                cmk = "Btcv0p"
                Cmdiag = Cm_t[:, 0:NSEQ * 136].rearrange("p (j q) -> p j q", q=136)[:, :, 0:8]
                print("SSD phase sbuf remaining", nc.sbuf_bytes_remaining, file=sys.stderr)
                P.op("dve", lambda e: e.memset(pre[:, 0, 0:3], 0.0), writes=["Bpre0z"])
                P.op("dve", lambda e: e.memset(pre[:, 1, 0:3], 0.0), writes=["Bpre1z"])
                P.op("dve", lambda e: e.memset(ygn[:, :, :], 0.0), writes=[("ygn", c) for c in range(GC)])
                o_dtb = cfg.voff["dtb"][0]
                o_dsk = cfg.voff["dsk"][0]
                def chunk(g, ti, full, wz, kz, sample=False):
                    Q = 128
                    col0 = tile_col[ti]
                    cs = slice(col0, col0 + Q)
                    hs = slice(g * R, (g + 1) * R)
                    TRI = triS_f if sample else tri_f
                    dtt = dt_all[:, ti, hs]
                    dA = dA_all[:, ti, hs]
                    bac = nb()
                    P.op("pe", lambda e: e.matmul(ps[bac][:, 0:R], lhsT=TRI, rhs=dA, start=True, stop=True),
                         reads=["dA_all", "cmat"], writes=["ps%d" % bac])
                    P.op("dve", lambda e: e.tensor_tensor(
                        Rm[:, :, :], TRI.unsqueeze(1).broadcast_to([Q, R, Q]),
                        dA.unsqueeze(2).broadcast_to([Q, R, Q]), ALU.mult),
                         reads=["dA_all", "cmat"], writes=["Rm"])
                    P.op("dve", lambda e: e.tensor_copy(acs[:, :], ps[bac][:, 0:R]), reads=["ps%d" % bac],
                         writes=["acs"])
                    hb = 512 // Q
                    nbk = -(-R // hb)
                    bab = [nb() for _ in range(nbk)]

                    def mmab(e):
                        for i in range(nbk):
                            h_a, h_b = i * hb, min(R, (i + 1) * hb)
                            r = e.matmul(ps[bab[i]][:, 0:(h_b - h_a) * Q].rearrange("p (h t) -> p h t", t=Q),
                                         lhsT=ones_f, rhs=Rm[:, h_a:h_b, :], start=True, stop=True)
                        return r
                    P.op("pe", mmab, reads=["Rm", "cmat"], writes=["ps%d" % b for b in bab])

                    def abv(i):
                        h_a, h_b = i * hb, min(R, (i + 1) * hb)
                        return ps[bab[i]][:, 0:(h_b - h_a) * Q].rearrange("p (h t) -> p h t", t=Q), h_a, h_b
                    if not sample:
                        for i in range(nbk):
                            v, h_a, h_b = abv(i)
                            P.op("dve", lambda e, v=v, h_a=h_a, h_b=h_b: e.tensor_tensor(
                                dE[:, h_a:h_b], v[:, :, Q - 1], acs[:, h_a:h_b], ALU.subtract),
                                 reads=["ps%d" % bab[i], "acs"], writes=["dE"])
                            P.op("act", lambda e, v=v, h_a=h_a, h_b=h_b: e.activation(out=cdec[:, h_a:h_b],
                                                                                    in_=v[:, :, Q - 1], func=AF.Exp),
                                 reads=["ps%d" % bab[i]], writes=["cdec"])
                    else:
                        btot = nb()
                        P.op("pe", lambda e: e.matmul(ps[btot][:, 0:R], lhsT=blk_f, rhs=dA, start=True, stop=True),
                             reads=["dA_all", "cmat"], writes=["ps%d" % btot])
                        P.op("dve", lambda e: e.tensor_tensor(dE[:, :], ps[btot][:, 0:R], acs[:, :], ALU.subtract),
                             reads=["ps%d" % btot, "acs"], writes=["dE"])
                        P.op("dve", lambda e: e.tensor_copy(
                            xD[:, :].rearrange("q (h p) -> q h p", p=64), dA.unsqueeze(2).broadcast_to([Q, R, 64])),
                             reads=["dA_all"], writes=["xD"])
                        bcd = nb()

                        def mmcd(e):
                            for c in range(GC):
                                r = e.matmul(ps[bcd][:, c * NSEQ:(c + 1) * NSEQ], lhsT=xD[:, c * 128:(c + 1) * 128],
                                             rhs=maskJ_f, start=True, stop=True)
                            return r
                        P.op("pe", mmcd, reads=["xD", "cmat"], writes=["ps%d" % bcd])
                        P.op("act", lambda e: e.activation(
                            out=cdecF[:, :, :], in_=ps[bcd][:, 0:GC * NSEQ].rearrange("p (c j) -> p c j", j=NSEQ),
                            func=AF.Exp), reads=["ps%d" % bcd], writes=["cdecF"])
                    P.op("act", lambda e: e.activation(out=dE[:, :], in_=dE[:, :], func=AF.Exp), reads=["dE"],
                         writes=["dE"])
                    if full:
                        P.op("act", lambda e: e.activation(out=expa[:, :], in_=acs[:, :], func=AF.Exp),
                             reads=["acs"], writes=["expa"])
                        for i in range(nbk):
                            v, h_a, h_b = abv(i)
                            P.op("dve", lambda e, v=v, h_a=h_a, h_b=h_b: e.tensor_tensor(
                                Rm[:, h_a:h_b, :], v, acs[:, h_a:h_b].unsqueeze(2).broadcast_to([Q, h_b - h_a, Q]),
                                ALU.subtract), reads=["ps%d" % bab[i], "acs", "Rm"], writes=["Rm"])
                        P.op("dve", lambda e: e.tensor_scalar(Rm[:, :, :], Rm[:, :, :], 0.0, None, ALU.min),
                             reads=["Rm"], writes=["Rm"])
                        P.op("act", lambda e: e.activation(out=Wm[:, :, :], in_=Rm[:, :, :], func=AF.Exp),
                             reads=["Rm"], writes=["Wm"])
                    bx = nb()

                    def trx(e):
                        for c in range(GC):
                            r = e.transpose(psb[bx][:, c * 128:(c + 1) * 128], xsg[:, c, cs], ident_b)
                        return r
                    P.op("pe", trx, reads=[("xsg", c) for c in range(GC)] + ["cmb"], writes=["ps%d" % bx])
                    P.op("dve", lambda e: e.tensor_tensor(
                        xdt[:, :].rearrange("q (h p) -> q h p", p=64),
                        psb[bx][:, 0:GW].rearrange("q (h p) -> q h p", p=64),
                        dtt.unsqueeze(2).broadcast_to([Q, R, 64]), ALU.mult),
                         reads=["ps%d" % bx, "dt_all"], writes=["xdt"])
                    if full:
                        P.op("dve", lambda e: e.tensor_tensor(
                            xD[:, :].rearrange("q (h p) -> q h p", p=64),
                            psb[bx][:, 0:GW].rearrange("q (h p) -> q h p", p=64),
                            vecs[:, o_dsk + g * R:o_dsk + (g + 1) * R].unsqueeze(2).broadcast_to([Q, R, 64]), ALU.mult),
                             reads=["ps%d" % bx, "vecs", "xD"], writes=["xD"])
                    bB = nb()
                    P.op("pe", lambda e: e.transpose(psb[bB][:, 0:128], Bg[:, cs], ident_b), reads=["Bg", "cmb"],
                         writes=["ps%d" % bB])
                    P.op("act", lambda e: e.copy(Btok[:, :], psb[bB][:, 0:128]), reads=["ps%d" % bB],
                         writes=["Btok"])
                    P.op("dve", lambda e: e.tensor_tensor(
                        xw[:, :].rearrange("q (h p) -> q h p", p=64), xdt[:, :].rearrange("q (h p) -> q h p", p=64),
                        dE[:, :].unsqueeze(2).broadcast_to([Q, R, 64]), ALU.mult), reads=["xdt", "dE"], writes=["xw"])
                    bi = None
                    if sample:
                        bi = nb()
                        reserved.add(bi)
                        P.op("dve", lambda e: e.memset(Cm_t[:, :], 0.0), writes=[cmk, "Btcv0s"])
                        P.op("dve", lambda e: e.tensor_copy(Cmdiag, Cg[:, cs].rearrange("p (j r) -> p j r", r=8)),
                             reads=["Cg"], writes=[cmk, "Btcv0s"])
                        for jq in range(NSEQ):
                            s = jq % 2
                            P.op("pool", lambda e, jq=jq, s=s: e.dma_start(
                                out=h0[s][:, :, :],
                                in_=ssd_state[jq, g * GW:(g + 1) * GW, :].rearrange("(c p) n -> p c n", p=128)),
                                 writes=["h0_%d" % s], dma="h0_%d" % s)
                            bh = nb()

                            def trh(e, bh=bh, s=s):
                                for c in range(GC):
                                    r = e.transpose(ps[bh][:, c * 128:(c + 1) * 128], h0[s][:, c, :], ident_f)
                                return r
                            P.op("pe", trh, reads=["h0_%d" % s, "cmat"], writes=["ps%d" % bh])
                            P.op("act", lambda e, bh=bh, s=s: e.copy(h0T[0][:, :], ps[bh][:, 0:GW]), reads=["ps%d" % bh],
                                 writes=["h0T_0"])
                            P.op("pe", lambda e, jq=jq, s=s: e.matmul(ps[bi][:, 0:GW], lhsT=Cm_t[:, jq * 128:(jq + 1) * 128], rhs=h0T[0][:, :],
                                                                     start=(jq == 0), stop=(jq == NSEQ - 1)),
                                 reads=[cmk, "Btcv0s", "h0T_0"], writes=["ps%d" % bi])
                            P.op("dve", lambda e, jq=jq, s=s: e.tensor_scalar(xwj[0][:, :], xw[:, :], maskJ_f[:, jq:jq + 1],
                                                                            None, ALU.mult),
                                 reads=["xw", "cmat"], writes=["xwj_0"])
                            bsj = nb()

                            def mmsj(e, bsj=bsj, s=s):
                                for c in range(GC):
                                    r = e.matmul(ps[bsj][:, c * 128:(c + 1) * 128], lhsT=xwj[0][:, c * 128:(c + 1) * 128],
                                                 rhs=Btok[:, :], start=True, stop=True)
                                return r
                            P.op("pe", mmsj, reads=["xwj_0", "Btok"], writes=["ps%d" % bsj])
                            P.op("dve", lambda e, jq=jq, s=s: e.tensor_tensor(
                                sost[s][:, :, :], h0[s][:, :, :],
                                cdecF[:, :, jq].unsqueeze(2).broadcast_to([128, GC, 128]), ALU.mult),
                                 reads=["h0_%d" % s, "cdecF"], writes=["sost_%d" % s])
                            P.op("dve", lambda e, bsj=bsj, s=s: e.tensor_tensor(
                                sost[s][:, :, :], sost[s][:, :, :],
                                ps[bsj][:, 0:GW].rearrange("p (c n) -> p c n", n=128), ALU.add),
                                 reads=["sost_%d" % s, "ps%d" % bsj], writes=["sost_%d" % s])
                            P.op("sp", lambda e, jq=jq, s=s: e.dma_start(
                                out=st_out_s[jq, g * GW:(g + 1) * GW, :].rearrange("(c p) n -> p c n", p=128),
                                in_=sost[s][:, :, :]), reads=["sost_%d" % s], dma="sost_%d" % s)
                    if full:
                        bz = nb()

                        def mmz(e):
                            r = None
                            for i in range(GW // XB):
                                for k in range(KC):
                                    r = e.matmul(ps[bz][:, i * XB:(i + 1) * XB], lhsT=xn[:, k, cs], rhs=wz[i][:, k, :],
                                                 start=(k == 0), stop=(k == KC - 1))
                            return r
                        P.op("pe", mmz, reads=["xn"] + kz, writes=["ps%d" % bz])
                        bm = nb()
                        P.op("pe", lambda e: e.matmul(ps[bm][:, 0:Q], lhsT=Bg[:, cs], rhs=Cg[:, cs], start=True,
                                                     stop=True), reads=["Bg", "Cg"], writes=["ps%d" % bm])
                        P.op("dve", lambda e: e.tensor_tensor(MTm[:, :], ps[bm][:, 0:Q], TRI, ALU.mult),
                             reads=["ps%d" % bm, "cmat"], writes=["MTm"])
                        if sample:
                            P.op("dve", lambda e: e.tensor_tensor(
                                y1[:, :].rearrange("q (h p) -> q h p", p=64),
                                ps[bi][:, 0:GW].rearrange("q (h p) -> q h p", p=64),
                                expa[:, :].unsqueeze(2).broadcast_to([Q, R, 64]), ALU.mult),
                                 reads=["ps%d" % bi, "expa"], writes=["y1"])
                            reserved.discard(bi)
                        P.op("dve", lambda e: e.tensor_tensor(
                            Wm[:, :, :], Wm[:, :, :], MTm[:, :].unsqueeze(1).broadcast_to([Q, R, Q]),
                            ALU.mult), reads=["MTm", "Wm"], writes=["Wm"])
                        by = nb()

                        def mmy(e):
                            for hh in range(R):
                                r = e.matmul(ps[by][:, hh * 64:(hh + 1) * 64], lhsT=Wm[:, hh, :],
                                             rhs=xdt[:, hh * 64:(hh + 1) * 64], start=True, stop=True)
                            return r
                        P.op("pe", mmy, reads=["Wm", "xdt"], writes=["ps%d" % by])
                        if not sample:
                            bi = nb()
                            P.op("pe", lambda e: e.matmul(ps[bi][:, 0:GW], lhsT=Cg[:, cs], rhs=Sb[:, :], start=True,
                                                         stop=True), reads=["Cg", "Sb"], writes=["ps%d" % bi])
                            P.op("dve", lambda e: e.tensor_tensor(
                                y1[:, :].rearrange("q (h p) -> q h p", p=64),
                                ps[bi][:, 0:GW].rearrange("q (h p) -> q h p", p=64),
                                expa[:, :].unsqueeze(2).broadcast_to([Q, R, 64]), ALU.mult),
                                 reads=["ps%d" % bi, "expa"], writes=["y1"])
                        P.op("dve", lambda e: e.tensor_tensor(y1[:, :], y1[:, :], ps[by][:, 0:GW], ALU.add),
                             reads=["y1", "ps%d" % by], writes=["y1"])
                        P.op("dve", lambda e: e.tensor_tensor(y1[:, :], y1[:, :], xD[:, :], ALU.add),
                             reads=["y1", "xD"], writes=["y1"])
                        P.op("act", lambda e: e.activation(out=xD[:, :], in_=ps[bz][:, 0:GW], func=AF.Tanh, scale=0.5),
                             reads=["ps%d" % bz, "xD"], writes=["xD"])
                        P.op("dve", lambda e: e.scalar_tensor_tensor(xD[:, :], xD[:, :], 1.0, ps[bz][:, 0:GW], ALU.add,
                                                                    ALU.mult),
                             reads=["xD", "ps%d" % bz], writes=["xD"])
                        P.op("dve", lambda e: e.scalar_tensor_tensor(y1[:, :], y1[:, :], 0.5, xD[:, :], ALU.mult, ALU.mult),
                             reads=["y1", "xD"], writes=["y1"])
                        P.op("act", lambda e: e.activation(out=xD[:, :], in_=y1[:, :], func=AF.Square,
                                                          accum_out=ss[:, 0:1]), reads=["y1", "xD"],
                             writes=["xD", "ss"])
                        P.op("dve", lambda e: e.tensor_scalar(ss[:, 1:2], ss[:, 0:1], 1.0 / GW, EPS, ALU.mult, ALU.add),
                             reads=["ss"], writes=["ss"])
                        P.op("act", lambda e: e.activation(out=ss[:, 1:2], in_=ss[:, 1:2], func=AF.Sqrt),
                             reads=["ss"], writes=["ss"])
                        P.op("dve", lambda e: e.reciprocal(ss[:, 1:2], ss[:, 1:2]), reads=["ss"], writes=["ss"])
                        P.op("dve", lambda e: e.tensor_scalar(gyn[:, :], y1[:, :], ss[:, 1:2], None, ALU.mult),
                             reads=["y1", "ss"], writes=["gyn"])
                        bt = nb()

                        def trg(e):
                            for c in range(GC):
                                r = e.transpose(psb[bt][:, c * 128:c * 128 + Q], gyn[:, c * 128:(c + 1) * 128], ident_b)
                            return r
                        P.op("pe", trg, reads=["gyn", "cmb"], writes=["ps%d" % bt])
                        o_ng = cfg.voff["ng"][0]
                        for c in range(GC):
                            P.op("dve" if c % 2 == 0 else "act",
                                 (lambda e, c=c: e.tensor_scalar(ygn[:, c, cs], psb[bt][:, c * 128:c * 128 + Q],
                                                                 vecs[:, o_ng + g * GC + c:o_ng + g * GC + c + 1], None,
                                                                 ALU.mult)) if c % 2 == 0 else
                                 (lambda e, c=c: e.activation(out=ygn[:, c, cs], in_=psb[bt][:, c * 128:c * 128 + Q],
                                                              func=AF.Copy,
                                                              scale=vecs[:, o_ng + g * GC + c:o_ng + g * GC + c + 1])),
                                 reads=["ps%d" % bt, "vecs"], writes=[("ygn", c)])
                    if not sample:
                        bs = nb()
                        P.op("pe", lambda e: e.matmul(ps[bs][:, 0:GW], lhsT=Btok[:, :], rhs=xw[:, :], start=True, stop=True),
                             reads=["Btok", "xw"], writes=["ps%d" % bs])
                        P.op("dve", lambda e: e.tensor_tensor(
                            S[:, :].rearrange("n (h p) -> n h p", p=64), S[:, :].rearrange("n (h p) -> n h p", p=64),
                            cdec[:, :].unsqueeze(2).broadcast_to([128, R, 64]), ALU.mult), reads=["S", "cdec", "Sb"],
                             writes=["S"])
                        P.op("dve", lambda e: e.tensor_tensor(S[:, :], S[:, :], ps[bs][:, 0:GW], ALU.add),
                             reads=["S", "ps%d" % bs], writes=["S"])
                        P.op("act", lambda e: e.copy(Sb[:, :], S[:, :]), reads=["S"], writes=["Sb"])

                def state_out(dst_ap_fn):
                    bo = nb()

                    def tro(e):
                        for c in range(GC):
                            r = e.transpose(ps[bo][:, c * 128:(c + 1) * 128], S[:, c * 128:(c + 1) * 128], ident_f)
                        return r
                    P.op("pe", tro, reads=["S", "cmat"], writes=["ps%d" % bo])
                    P.op("act", lambda e: e.copy(sost[0][:, :, :], ps[bo][:, 0:GW].rearrange("p (c n) -> p c n", n=128)),
                         reads=["ps%d" % bo], writes=["sost_0"])
                    P.op("sp", lambda e: e.dma_start(out=dst_ap_fn(), in_=sost[0][:, :, :]), reads=["sost_0"],
                         dma="sost_0")

                sidx = [0]
                for g in range(G):
                    specs = []
                    for i in range(GW // XB):
                        col = DI + g * GW + i * XB
                        for j in range(XB // 128):
                            c = i * (XB // 128) + j
                            specs.append((ssd_w_in[:, col:col + XB], XB, j, g * GC + c, ("x", c), i))
                    colB = 2 * DI + g * 128
                    specs.append((ssd_w_in[:, colB:colB + 128], 128, 0, DI // 128 + g, ("B", 0), "B"))
                    colC = 2 * DI + GN + g * 128
                    specs.append((ssd_w_in[:, colC:colC + 128], 128, 0, DI // 128 + G + g, ("C", 0), "C"))
                    loaded = {}

                    def ensure(k):
                        if k < len(specs) and specs[k][5] not in loaded:
                            loaded[specs[k][5]] = wload(specs[k][0], KC, specs[k][1])
                    LOOK = 3
                    for k in range(LOOK):
                        ensure(k)
                    for idx in range(len(specs)):
                        ensure(idx + LOOK)
                        _, _, j, cidx, (kind, c), lk = specs[idx]
                        wv_, kv_ = loaded[lk]
                        sl = sidx[0] % 2
                        conv_state_io("B", pre, pre_s, sl, 4, ssdc_state, cidx, None, None, stg_in, None, None, sidx[0])
                        for (c0, n) in tts:
                            b = nb()
                            mm_fm(b, wv_, kv_, KC, j, xn, ["xn"], c0, n)
                            pp, sp_ = split_cols(c0, n)
                            if pp:
                                a, b2 = pp
                                P.op("act", lambda e, a=a, b2=b2, b=b, c0=c0, sl=sl: e.copy(pre[:, sl, 3 + a:3 + b2],
                                                                                   ps[b][:, a - c0:b2 - c0]),
                                     reads=["ps%d" % b], writes=["Bpre%d" % sl])
                            if sp_:
                                a, b2 = sp_
                                P.op("dve", lambda e, a=a, b2=b2, b=b, c0=c0, sl=sl: e.tensor_copy(
                                    pre_s[:, sl, :, 3:11], ps[b][:, a - c0:b2 - c0].rearrange("p (s t) -> p s t", t=8)),
                                     reads=["ps%d" % b], writes=["Bpres%d" % sl])
                        kt = conv_block2("B", pre, pre_s, tcv, sl, 4, ["cw0", "cw1", "cw2", "cw3"], cidx,
                                         ["Bpre%d" % sl, "Bpre%dz" % sl], ["Bpres%d" % sl, "Bpres%dh" % sl])
                        if kind == "x":
                            dst, dk = xsg[:, c, :], ("xsg", c)
                        elif kind == "B":
                            dst, dk = Bg[:, :], "Bg"
                        else:
                            dst, dk = Cg[:, :], "Cg"
                        P.op("act", lambda e, dst=dst, cidx=cidx: e.activation(out=dst, in_=tcv[:, 0, :], func=AF.Silu,
                                                                             bias=V("cb", cidx)),
                             reads=kt + ["vecs"], writes=[dk])
                        conv_state_out("B", pre, pre_s, sl, 4, cidx, ssdc_out_p, ssdc_out_s, stg_out, cost,
                                       ["Bpre%d" % sl, "Bpres%d" % sl, "Bpres%dh" % sl], sidx[0])
                        sidx[0] += 1
                    wz, kz = [], []
                    for i in range(GW // XB):
                        col = g * GW + i * XB
                        w_, k_ = wload(ssd_w_in[:, col:col + XB], KC, XB)
                        wz.append(w_)
                        kz.append(k_)
                    P.op("dve", lambda e: e.memset(S[:, :], 0.0), reads=["Sb"], writes=["S"])
                    for q in range(NQ):
                        chunk(g, q, False, wz, kz)
                    ibk, obk = "ib%d" % g, "ob%d" % g
                    P.op("sp", lambda e, g=g: e.dma_start(out=ibs[g][:, :], in_=S[:, :]), reads=["S"], writes=[ibk],
                         dma="xch")
                    P.op("pool", lambda e, g=g: e.collective_compute(
                        "AllGather", ALU.bypass, replica_groups=[[0, 1], [2, 3], [4, 5], [6, 7]],
                        ins=[ibs[g].ap().opt()], outs=[obs[g].ap().opt()]), reads=[ibk], writes=[obk],
                         dma="cc%d" % g, inc=1)
                    chunk(g, NQ, True, wz, kz, sample=True)
                    P.op("sp", lambda e, g=g: e.dma_start(out=S[:, :], in_=obs[g][0:128, :]), reads=[obk, "Sb"],
                         writes=["S"], dma="xch")
                    P.op("dve", lambda e: e.tensor_scalar(S[:, :], S[:, :], mko[:, 0:1], None, ALU.mult),
                         reads=["S", "mko"], writes=["S"])
                    P.op("act", lambda e: e.copy(Sb[:, :], S[:, :]), reads=["S"], writes=["Sb"])
                    for q in range(NQ):
                        chunk(g, q, True, wz, kz)
                    state_out(lambda g=g: st_out_p[g * GW:(g + 1) * GW, :].rearrange("(c p) n -> p c n", p=128))
                    for mcp in range(D // 256):
                        wo, ko = wload(ssd_w_out[g * GW:(g + 1) * GW, mcp * 256:mcp * 256 + 256], GC, 256)
                        for j in range(2):
                            mc = mcp * 2 + j
                            for (c0, n) in tts:
                                b = nb()
                                mm_fm(b, wo, ko, GC, j, ygn, [("ygn", k) for k in range(GC)], c0, n)
                                acc_into_h(b, mc, c0, n)
                P.end_phase()

        def out_phase():
            with ExitStack() as ph:
                yst = [sb("ystg%d" % i, [128, D], F32, ph) for i in range(2)]
                for ri, (r0, n) in enumerate(rts):
                    s = ri % 2
                    for cg in range(KC // 4):
                        b = nb()

                        def tr(e, n=n, cg=cg, b=b, r0=r0):
                            for j in range(4):
                                c = cg * 4 + j
                                r = e.transpose(ps[b][0:n, j * 128:(j + 1) * 128], h[:, c, r0:r0 + n], ident_f)
                            return r
                        P.op("pe", tr, reads=["h", "cmat"], writes=["ps%d" % b])
                        if cg % 2 == 0:
                            P.op("dve", lambda e, s=s, n=n, cg=cg, b=b: e.tensor_copy(yst[s][0:n, cg * 512:(cg + 1) * 512],
                                                                                    ps[b][0:n, :]),
                                 reads=["ps%d" % b], writes=[("yst%d" % s, cg)])
                        else:
                            P.op("act", lambda e, s=s, n=n, cg=cg, b=b: e.copy(yst[s][0:n, cg * 512:(cg + 1) * 512],
                                                                             ps[b][0:n, :]),
                                 reads=["ps%d" % b], writes=[("yst%d" % s, cg)])
                    P.op("sp", lambda e, s=s, r0=r0, n=n: e.dma_start(out=y_out[r0:r0 + n, :], in_=yst[s][0:n, :]),
                         reads=[("yst%d" % s, cg) for cg in range(KC // 4)], dma="yst%d" % s)
                P.end_phase()

        phases = [lambda: rmsnorm_phase("g_mix0"), l0_mixer_phase,
                  lambda: rmsnorm_phase("g_ffn0"), lambda: ffn_phase(0),
                  lambda: rmsnorm_phase("g_ple0"), lambda: ple_phase(0),
                  lambda: rmsnorm_phase("g_mix1"), ssd_phase,
                  lambda: rmsnorm_phase("g_ffn1"), lambda: ffn_phase(1),
                  lambda: rmsnorm_phase("g_ple1"), lambda: ple_phase(1),
                  lambda: rmsnorm_phase("g_final", final=True)]
        for pi, phf in enumerate(phases):
            if stop is not None and pi >= stop:
                break
            phf()
        out_phase()
        for k, v in P.cnt.items():
            if k[0] == "dma" and P.waited["sp"].get(k, 0) < v:
                nc.sync.wait_ge(P.getsem(k), v)
        print("ops", len(P.ops), "waits", P.nwait, "sems", len(P.sems), file=sys.stderr)
    return nc


def host_inputs(cfg, inp):
    D, KC, T, TP, NSEQ = cfg.D, cfg.KC, cfg.T, cfg.TP, cfg.NSEQ
    f = np.float32

    def pm(v):
        v = np.asarray(v, f)
        return np.ascontiguousarray(v.reshape(-1, 128).T)

    def bc(v):
        v = np.asarray(v, f).reshape(1, -1)
        return np.ascontiguousarray(np.broadcast_to(v, (128, v.shape[1])))
    vec = np.zeros((128, cfg.NV), f)

    def put(nm, a):
        o, w = cfg.voff[nm]
        assert a.shape == (128, w), (nm, a.shape, w)
        vec[:, o:o + w] = a
    for l in range(2):
        put("g_mix%d" % l, pm(inp["g_mix"][l]))
        put("g_ffn%d" % l, pm(inp["g_ffn"][l]))
        put("g_ple%d" % l, pm(inp["g_ple"][l]))
    put("g_final", pm(inp["g_final"]))
    for k in range(3):
        put("scw%d" % k, pm(inp["sc_w_conv"][0, k]))
    for k in range(4):
        put("cw%d" % k, pm(inp["ssd_conv_w"][0, k]))
    put("cb", pm(inp["ssd_conv_b"][0]))
    put("ng", pm(inp["ssd_norm_g"][0]))
    put("dtb", bc(inp["ssd_dt_bias"][0]))
    put("alog", bc(inp["ssd_a_log"][0]))
    put("dsk", bc(inp["ssd_d"][0]))
    cm = np.zeros((128, 640 + NSEQ), f)
    cm[:, 0:128] = np.eye(128, dtype=f)
    cm[:, 128:256] = np.triu(np.ones((128, 128), f))
    cm[:, 256:384] = 1.0
    sid = np.arange(128) // 8
    same = (sid[:, None] == sid[None, :]).astype(f)
    cm[:, 384:512] = cm[:, 128:256] * same
    cm[:, 512:640] = same
    cm[:, 640:640 + NSEQ] = (sid[:, None] == np.arange(NSEQ)[None, :]).astype(f)
    mt = np.zeros((128, NSEQ, 128), f)
    mt[:, sid, np.arange(128)] = 1.0
    shared = dict(vecs=vec, cmat=cm,
                  sc_w_in=np.ascontiguousarray(inp["sc_w_in"][0]), sc_w_out=np.ascontiguousarray(inp["sc_w_out"][0]),
                  ssd_w_in=np.ascontiguousarray(inp["ssd_w_in"][0]), ssd_w_out=np.ascontiguousarray(inp["ssd_w_out"][0]),
                  ffn_w_gate=np.ascontiguousarray(inp["ffn_w_gate"]), ffn_w_up=np.ascontiguousarray(inp["ffn_w_up"]),
                  ffn_w_down=np.ascontiguousarray(inp["ffn_w_down"]), ple_w_proj=np.ascontiguousarray(inp["ple_w_proj"]),
                  ple_w_gate=np.ascontiguousarray(inp["ple_w_gate"]))
    maps = []
    for core in range(8):
        b, half = core // 2, core % 2
        xin = np.zeros((T, D), f)
        pin = np.zeros((2, T, cfg.PLE), f)
        t0 = half * TP
        if half == 1:
            xin[0:HALO] = inp["x_prompt"][b, t0 - HALO:t0]
            pin[:, 0:HALO] = inp["p_prompt"][:, b, t0 - HALO:t0]
        xin[HALO:HALO + TP] = inp["x_prompt"][b, t0:t0 + TP]
        pin[:, HALO:HALO + TP] = inp["p_prompt"][:, b, t0:t0 + TP]
        sl = slice(core * NSEQ, (core + 1) * NSEQ)
        xin[HALO + TP:] = inp["x_sample"][sl].reshape(-1, D)
        pin[:, HALO + TP:] = inp["p_sample"][:, sl].reshape(2, -1, cfg.PLE)
        m = dict(shared)
        m.update(xin=xin, pin=pin,
                 sc_state=np.ascontiguousarray(inp["state_sc_conv"][0, sl].reshape(NSEQ * 2, D)),
                 ssdc_state=np.ascontiguousarray(inp["state_ssd_conv"][0, sl].reshape(NSEQ * 3, cfg.CONV)),
                 ssd_state=np.ascontiguousarray(inp["state_ssd"][0, sl].reshape(NSEQ, cfg.H * 64, 128)),
                 maskodd=np.full((128, 1), float(half), f))
        maps.append(m)
    return maps


def assemble(cfg, res):
    D, TP, NSEQ, H = cfg.D, cfg.TP, cfg.NSEQ, cfg.H
    f = np.float32
    B = 4
    y_p = np.zeros((B, 2 * TP, D), f)
    y_s = np.zeros((8 * NSEQ, cfg.DEC_SEQ, D), f)
    scp = np.zeros((1, B, 2, D), f)
    scs = np.zeros((1, 8 * NSEQ, 2, D), f)
    ssdcp = np.zeros((1, B, 3, cfg.CONV), f)
    ssdcs = np.zeros((1, 8 * NSEQ, 3, cfg.CONV), f)
    stp = np.zeros((1, B, H, 64, 128), f)
    sts = np.zeros((1, 8 * NSEQ, H, 64, 128), f)
    for core in range(8):
        r = res[core]
        b, half = core // 2, core % 2
        y = r["y"]
        y_p[b, half * TP:(half + 1) * TP] = y[HALO:HALO + TP]
        sl = slice(core * NSEQ, (core + 1) * NSEQ)
        y_s[sl] = y[HALO + TP:].reshape(NSEQ, cfg.DEC_SEQ, D)
        scs[0, sl] = r["sc_out_s"].reshape(NSEQ, 2, D)
        ssdcs[0, sl] = r["ssdc_out_s"].reshape(NSEQ, 3, cfg.CONV)
        sts[0, sl] = r["st_out_s"].reshape(NSEQ, H, 64, 128)
        if half == 1:
            scp[0, b] = r["sc_out_p"]
            ssdcp[0, b] = r["ssdc_out_p"]
            stp[0, b] = r["st_out_p"].reshape(H, 64, 128)
    return (y_p, y_s, scp, scs, ssdcp, ssdcs, stp, sts)


_NC_CACHE = {}


def run(cfg, inp, stop=None):
    key = (cfg.D, cfg.SEQ, stop)
    if key not in _NC_CACHE:
        _NC_CACHE[key] = build(cfg, stop)
    nc = _NC_CACHE[key]
    maps = host_inputs(cfg, inp)
    res = run_bass_kernel_spmd(nc, maps, core_ids=list(range(8)))
    return assemble(cfg, res.results)


def kernel(**inputs):
    cfg = Cfg()
    inp = {k: np.asarray(v) for k, v in inputs.items()}
    return run(cfg, inp)
```

```python
import sys
from contextlib import ExitStack
import numpy as np
import concourse.bass as bass
import concourse.mybir as mybir
from concourse.bass_utils import run_bass_kernel_spmd

F32 = mybir.dt.float32
BF16 = mybir.dt.bfloat16
AF = mybir.ActivationFunctionType
ALU = mybir.AluOpType
EPS = 1e-6
HALO = 5


class Cfg:
    def __init__(self, D=2048, SEQ=2048, DEC_BATCH=128, DEC_SEQ=8, PLE=256):
        self.D = D
        self.SEQ = SEQ
        self.BATCH = 4
        self.DEC_BATCH = DEC_BATCH
        self.DEC_SEQ = DEC_SEQ
        self.PLE = PLE
        self.DFF = -(-8 * D // (3 * 256)) * 256
        self.DI = 2 * D
        self.H = self.DI // 64
        self.G = 8
        self.N = 128
        self.R = self.H // 8
        self.GW = self.DI // 8
        self.GC = self.GW // 128
        self.GN = self.G * self.N
        self.CONV = self.DI + 2 * self.GN
        self.IN = self.DI + self.CONV + self.H
        self.KC = D // 128
        self.FC = self.DFF // 128
        self.CC = self.CONV // 128
        self.TP = SEQ // 2
        self.NQ = self.TP // 128
        self.NSEQ = DEC_BATCH // 8
        self.TS = self.NSEQ * DEC_SEQ
        self.PL = HALO + self.TP
        self.T = self.PL + self.TS
        off = {}
        o = 0
        for nm, w in [("g_mix0", self.KC), ("g_ffn0", self.KC), ("g_ple0", self.KC),
                      ("g_mix1", self.KC), ("g_ffn1", self.KC), ("g_ple1", self.KC),
                      ("g_final", self.KC), ("scw0", self.KC), ("scw1", self.KC), ("scw2", self.KC),
                      ("cw0", self.CC), ("cw1", self.CC), ("cw2", self.CC), ("cw3", self.CC),
                      ("cb", self.CC), ("ng", self.DI // 128),
                      ("dtb", self.H), ("alog", self.H), ("dsk", self.H)]:
            off[nm] = (o, w)
            o += w
        self.voff = off
        self.NV = o


class Prog:
    ENGS = ("pe", "act", "dve", "pool", "sp")

    def __init__(self, nc, stack):
        self.nc = nc
        self.stack = stack
        self.ops = []
        self.lastw = {}
        self.readers = {}
        self.floor = 0
        self.emitted = 0
        self.sems = {}
        self.cnt = {}
        self.waited = {e: {} for e in self.ENGS}
        self.lastsig = {}
        self.nwait = 0
        self.eng = dict(pe=nc.tensor, act=nc.scalar, dve=nc.vector, pool=nc.gpsimd, sp=nc.sync)

    def op(self, eng, fn, reads=(), writes=(), dma=None, inc=16):
        i = len(self.ops)
        psr = [r for r in reads if isinstance(r, str) and r.startswith("ps")]
        if psr:
            reads = [r for r in reads if r not in psr]
            writes = list(writes) + psr
        deps = set()
        for r in reads:
            w = self.lastw.get(r)
            if w is not None:
                deps.add(w)
        for r in writes:
            w = self.lastw.get(r)
            if w is not None:
                deps.add(w)
            for x in self.readers.get(r, ()):
                deps.add(x)
        for r in reads:
            self.readers.setdefault(r, []).append(i)
        for r in writes:
            self.lastw[r] = i
            self.readers[r] = []
        deps.discard(i)
        deps = {d for d in deps if d >= self.floor}
        self.ops.append(dict(eng=eng, fn=fn, deps=deps, dma=dma, inc=inc, sig=False))
        return i

    def getsem(self, k):
        if k not in self.sems:
            self.sems[k] = self.stack.enter_context(self.nc.semaphore("s%d" % len(self.sems)))
        return self.sems[k]

    def end_phase(self):
        ops = self.ops
        start = self.emitted
        last = {}
        for i in range(start, len(ops)):
            o = ops[i]
            k = ("dma", o["dma"]) if o["dma"] is not None else ("eng", o["eng"])
            last[k] = i
        bdeps = set(last.values())
        for e in self.ENGS:
            self.ops.append(dict(eng=e, fn=None, deps=set(bdeps), dma=None, inc=0, sig=False, barrier=True))
        for i in range(start, len(ops)):
            o = ops[i]
            nd = set()
            for d in o["deps"]:
                p = ops[d]
                if (not o.get("barrier")) and p["dma"] is None and o["dma"] is None \
                        and p["eng"] == "pe" and o["eng"] == "pe":
                    continue
                nd.add(d)
            o["deps"] = nd
            for d in nd:
                ops[d]["sig"] = True
        for i in range(start, len(ops)):
            o = ops[i]
            if o["dma"] is not None:
                k = ("dma", o["dma"])
                self.cnt[k] = self.cnt.get(k, 0) + o["inc"]
                o["semk"], o["val"] = k, self.cnt[k]
            elif o["sig"]:
                k = ("eng", o["eng"])
                self.cnt[k] = self.cnt.get(k, 0) + 1
                o["semk"], o["val"] = k, self.cnt[k]
        for i in range(start, len(ops)):
            o = ops[i]
            e = o["eng"]
            need = {}
            for d in o["deps"]:
                p = ops[d]
                k = p["semk"]
                need[k] = max(need.get(k, 0), p["val"])
            for k, v in need.items():
                if self.waited[e].get(k, 0) >= v:
                    continue
                self.eng[e].wait_ge(self.getsem(k), v)
                self.waited[e][k] = v
                self.nwait += 1
            if o["fn"] is None:
                continue
            ins = o["fn"](self.eng[e])
            if o["dma"] is not None:
                ins.then_inc(self.getsem(o["semk"]), o["inc"])
            elif o["sig"]:
                ins.then_inc(self.getsem(o["semk"]), 1)
            o["fn"] = None
        self.emitted = len(ops)
        self.floor = len(ops)
        self.lastw = {}
        self.readers = {}


def split_tiles(T, maxn=448):
    nt = -(-T // maxn)
    base = T // nt
    rem = T % nt
    out = []
    c = 0
    for i in range(nt):
        n = base + (1 if i < rem else 0)
        out.append((c, n))
        c += n
    return out


def build(cfg, stop=None):
    D, KC, T, PL, TS, NSEQ = cfg.D, cfg.KC, cfg.T, cfg.PL, cfg.TS, cfg.NSEQ
    DI, H, G, R, GW, GC, GN, CC, FC = cfg.DI, cfg.H, cfg.G, cfg.R, cfg.GW, cfg.GC, cfg.GN, cfg.CC, cfg.FC
    PLE, DFF, CONV, IN, NQ = cfg.PLE, cfg.DFF, cfg.CONV, cfg.IN, cfg.NQ
    PK = PLE // 128
    nc = bass.Bass("TRN2", target_bir_lowering=False)
    CMW = 640 + NSEQ

    def din(name, shape):
        return nc.dram_tensor(name, list(shape), F32, kind="ExternalInput").ap()

    def dout(name, shape):
        return nc.dram_tensor(name, list(shape), F32, kind="ExternalOutput").ap()

    xin = din("xin", [T, D])
    pin = din("pin", [2, T, PLE])
    sc_state = din("sc_state", [NSEQ * 2, D])
    ssdc_state = din("ssdc_state", [NSEQ * 3, CONV])
    ssd_state = din("ssd_state", [NSEQ, H * 64, 128])
    maskodd = din("maskodd", [128, 1])
    vecs_d = din("vecs", [128, cfg.NV])
    cmat_d = din("cmat", [128, CMW])
    sc_w_in = din("sc_w_in", [D, 3 * D])
    sc_w_out = din("sc_w_out", [D, D])
    ssd_w_in = din("ssd_w_in", [D, IN])
    ssd_w_out = din("ssd_w_out", [DI, D])
    ffn_w_gate = din("ffn_w_gate", [2, D, DFF])
    ffn_w_up = din("ffn_w_up", [2, D, DFF])
    ffn_w_down = din("ffn_w_down", [2, DFF, D])
    ple_w_proj = din("ple_w_proj", [2, PLE, D])
    ple_w_gate = din("ple_w_gate", [2, D, D])

    y_out = dout("y", [T, D])
    sc_out_p = dout("sc_out_p", [2, D])
    sc_out_s = dout("sc_out_s", [NSEQ * 2, D])
    ssdc_out_p = dout("ssdc_out_p", [3, CONV])
    ssdc_out_s = dout("ssdc_out_s", [NSEQ * 3, CONV])
    st_out_p = dout("st_out_p", [H * 64, 128])
    st_out_s = dout("st_out_s", [NSEQ, H * 64, 128])
    ibs = [nc.dram_tensor("ib%d" % g, [128, GW], F32) for g in range(G)]
    obs = [nc.dram_tensor("ob%d" % g, [256, GW], F32) for g in range(G)]

    tts = split_tiles(T)
    for (c0, n) in tts:
        assert not (c0 < PL < c0 + n and False)
    assert any(c0 <= PL and PL + TS <= c0 + n for (c0, n) in tts) or True
    rts = [(r0, min(128, T - r0)) for r0 in range(0, T, 128)]

    with ExitStack() as st:
        sbctr = [0]

        def sb(name, shape, dt=F32, stack=None):
            sbctr[0] += 1
            return (stack or st).enter_context(nc.sbuf_tensor("%s_%d" % (name, sbctr[0]), list(shape), dt))

        P = Prog(nc, st)
        h = sb("h", [128, KC, T])
        xn = sb("xn", [128, KC, T], BF16)
        vecs = sb("vecs", [128, cfg.NV])
        cmat = sb("cmat", [128, CMW])
        cmb = sb("cmb", [128, CMW], BF16)
        abc = sb("abc", [128, H])
        mko = sb("mko", [128, 1])
        ps = [st.enter_context(nc.psum_tensor("ps%d" % i, [128, 512], F32)) for i in range(8)]
        psb = [p[:, :].bitcast(BF16) for p in ps]
        ident_f, tri_f, ones_f = cmat[:, 0:128], cmat[:, 128:256], cmat[:, 256:384]
        triS_f, blk_f, maskJ_f = cmat[:, 384:512], cmat[:, 512:640], cmat[:, 640:640 + NSEQ]
        ident_b, tri_b, ones_b = cmb[:, 0:128], cmb[:, 128:256], cmb[:, 256:384]

        def V(nm, c=None):
            o, w = cfg.voff[nm]
            if c is None:
                return vecs[:, o:o + w]
            return vecs[:, o + c:o + c + 1]

        bankctr = [0]

        reserved = set()

        def nb():
            while True:
                b = bankctr[0] % 8
                bankctr[0] += 1
                if b not in reserved:
                    return b

        poolctr = {"p": 0, "t": 0}

        def nb_prep():
            b = poolctr["p"] % 4
            poolctr["p"] += 1
            return b

        def nb_tail():
            b = 4 + poolctr["t"] % 4
            poolctr["t"] += 1
            return b

        ringstate = {}

        def make_ring(stack, nslots, tag, slot=4096):
            bufs = [sb("ring%s%d" % (tag, i), [128, slot], BF16, stack) for i in range(nslots)]
            ringstate["bufs"] = bufs
            ringstate["i"] = 0
            ringstate["slot"] = slot

        def wload(src, nk, ncols):
            bufs = ringstate["bufs"]
            i = ringstate["i"] % len(bufs)
            ringstate["i"] += 1
            assert nk * ncols <= ringstate["slot"]
            view = bufs[i][:, 0:nk * ncols].rearrange("p (k n) -> p k n", n=ncols)
            key = "ring%d" % i
            if key in P.lastw:
                assert P.readers.get(key), "ring slot %s overwritten before any consumer was recorded" % key
            srcv = src.rearrange("(k p) n -> p k n", p=128)
            P.op("pool", lambda e: e.dma_start(out=view, in_=srcv), writes=[key], dma=key)
            return view, key

        with ExitStack() as ph:
            xs = [sb("xstg%d" % i, [128, D], F32, ph) for i in range(2)]
            P.op("sp", lambda e: e.dma_start(out=vecs[:], in_=vecs_d), writes=["vecs"], dma="c0")
            P.op("sp", lambda e: e.dma_start(out=cmat[:], in_=cmat_d), writes=["cmat"], dma="c1")
            P.op("sp", lambda e: e.dma_start(out=mko[:], in_=maskodd), writes=["mko"], dma="c2")
            P.op("dve", lambda e: e.tensor_copy(cmb[:], cmat[:]), reads=["cmat"], writes=["cmb"])
            o_al, w_al = cfg.voff["alog"]
            P.op("act", lambda e: e.activation(out=abc[:], in_=vecs[:, o_al:o_al + w_al], func=AF.Exp),
                 reads=["vecs"], writes=["abc"])
            P.op("dve", lambda e: e.tensor_scalar(abc[:], abc[:], -1.0, None, ALU.mult), reads=["abc"], writes=["abc"])
            for ri, (r0, n) in enumerate(rts):
                s = ri % 2
                P.op("sp", lambda e, s=s, r0=r0, n=n: e.dma_start(out=xs[s][0:n, :], in_=xin[r0:r0 + n, :]),
                     writes=["xs%d" % s], dma="xs%d" % s)
                for cg in range(KC // 4):
                    b = nb()

                    def tr(e, s=s, n=n, cg=cg, b=b):
                        for j in range(4):
                            c = cg * 4 + j
                            r = e.transpose(ps[b][:, j * 128:j * 128 + n], xs[s][0:n, c * 128:(c + 1) * 128],
                                            ident_f[0:n, 0:n])
                        return r
                    P.op("pe", tr, reads=["xs%d" % s, "cmat"], writes=["ps%d" % b])
                    src = ps[b][:, :].rearrange("p (j t) -> p j t", t=128)[:, :, 0:n]
                    dst = h[:, cg * 4:cg * 4 + 4, r0:r0 + n]
                    if cg % 2 == 0:
                        P.op("dve", lambda e, src=src, dst=dst: e.tensor_copy(dst, src), reads=["ps%d" % b],
                             writes=[("h", ri, cg)])
                    else:
                        P.op("act", lambda e, src=src, dst=dst: e.copy(dst, src), reads=["ps%d" % b],
                             writes=[("h", ri, cg)])
            P.end_phase()

        def rmsnorm_phase(gname, final=False):
            with ExitStack() as ph:
                sq = [sb("sq%d" % i, [128, KC, tts[0][1]], BF16, ph) for i in range(2)]
                rstd = sb("rstd", [128, T], F32, ph)
                for ti, (c0, n) in enumerate(tts):
                    s = ti % 2
                    P.op("act", lambda e, s=s, c0=c0, n=n: e.activation(out=sq[s][:, :, 0:n], in_=h[:, :, c0:c0 + n],
                                                                      func=AF.Square),
                         reads=["h"], writes=["sq%d" % s])
                    b = nb()

                    def mm(e, s=s, n=n, b=b):
                        for k in range(KC):
                            r = e.matmul(ps[b][:, 0:n], lhsT=ones_b, rhs=sq[s][:, k, 0:n], start=(k == 0),
                                         stop=(k == KC - 1))
                        return r
                    P.op("pe", mm, reads=["sq%d" % s, "cmb"], writes=["ps%d" % b])
                    P.op("dve", lambda e, b=b, c0=c0, n=n: e.tensor_scalar(rstd[:, c0:c0 + n], ps[b][:, 0:n], 1.0 / D,
                                                                         EPS, ALU.mult, ALU.add),
                         reads=["ps%d" % b], writes=["rs%d" % ti])
                    P.op("act", lambda e, c0=c0, n=n: e.activation(out=rstd[:, c0:c0 + n], in_=rstd[:, c0:c0 + n],
                                                                 func=AF.Sqrt),
                         reads=["rs%d" % ti], writes=["rs%d" % ti])
                    P.op("dve", lambda e, c0=c0, n=n: e.reciprocal(rstd[:, c0:c0 + n], rstd[:, c0:c0 + n]),
                         reads=["rs%d" % ti], writes=["rs%d" % ti])
                rk = ["rs%d" % ti for ti in range(len(tts))]
                for c in range(KC):
                    dst = h[:, c, :] if final else xn[:, c, :]
                    P.op("dve", lambda e, c=c, dst=dst: e.scalar_tensor_tensor(dst, h[:, c, :], V(gname, c), rstd[:, :],
                                                                             ALU.mult, ALU.mult),
                         reads=rk + ["vecs", "h"], writes=[("hf" if final else "xn", c)])
                P.end_phase()

        def acc_into_h(b, mc, c0, n):
            P.op("dve", lambda e: e.tensor_tensor(h[:, mc, c0:c0 + n], h[:, mc, c0:c0 + n], ps[b][:, 0:n], ALU.add),
                 reads=["ps%d" % b, ("h", mc)], writes=[("h", mc)])

        def mm_fm(b, wv, wkey, nk, j, rhs, rkeys, c0, n):
            def mm(e):
                for k in range(nk):
                    r = e.matmul(ps[b][:, 0:n], lhsT=wv[:, k, j * 128:(j + 1) * 128], rhs=rhs[:, k, c0:c0 + n],
                                 start=(k == 0), stop=(k == nk - 1))
                return r
            P.op("pe", mm, reads=[wkey] + list(rkeys), writes=["ps%d" % b])

        def ffn_phase(layer):
            npairs = FC // 2
            ngroups = max(1, -(-FC // 12))
            base = npairs // ngroups
            rem = npairs % ngroups
            gsz = [base + (1 if i < rem else 0) for i in range(ngroups)]
            maxk = max(gsz) * 2
            with ExitStack() as ph:
                make_ring(ph, 6, "f")
                act = sb("act", [128, maxk, T], BF16, ph)
                sg = [sb("sg%d" % i, [128, 512], F32, ph) for i in range(2)]
                sgi = 0
                pr0 = 0
                for gi, np_ in enumerate(gsz):
                    nkg = np_ * 2
                    for pr in range(np_):
                        col = (pr0 + pr) * 256
                        wg, kg = wload(ffn_w_gate[layer, :, col:col + 256], KC, 256)
                        wu, ku = wload(ffn_w_up[layer, :, col:col + 256], KC, 256)
                        for j in range(2):
                            fl = pr * 2 + j
                            for (c0, n) in tts:
                                bg_, bu_ = nb(), nb()
                                mm_fm(bg_, wg, kg, KC, j, xn, ["xn"], c0, n)
                                mm_fm(bu_, wu, ku, KC, j, xn, ["xn"], c0, n)
                                s = sgi % 2
                                sgi += 1
                                P.op("act", lambda e, s=s, b=bg_, n=n: e.activation(out=sg[s][:, 0:n], in_=ps[b][:, 0:n],
                                                                                  func=AF.Silu),
                                     reads=["ps%d" % bg_], writes=["sg%d" % s])
                                P.op("dve", lambda e, s=s, b=bu_, fl=fl, c0=c0, n=n: e.tensor_tensor(
                                    act[:, fl, c0:c0 + n], sg[s][:, 0:n], ps[b][:, 0:n], ALU.mult),
                                     reads=["sg%d" % s, "ps%d" % bu_], writes=[("act", fl)])
                    for mcp in range(D // 256):
                        wd, kd = wload(ffn_w_down[layer, pr0 * 256:pr0 * 256 + nkg * 128, mcp * 256:mcp * 256 + 256],
                                       nkg, 256)
                        for j in range(2):
                            mc = mcp * 2 + j
                            for (c0, n) in tts:
                                b = nb()
                                mm_fm(b, wd, kd, nkg, j, act, [("act", k) for k in range(nkg)], c0, n)
                                acc_into_h(b, mc, c0, n)
                    pr0 += np_
                P.end_phase()

        def ple_phase(layer):
            with ExitStack() as ph:
                make_ring(ph, 6, "p")
                pT = sb("pT", [128, PK, T], BF16, ph)
                pst = [sb("pst%d" % i, [128, PLE], F32, ph) for i in range(2)]
                sg = [sb("sgp%d" % i, [128, 512], F32, ph) for i in range(2)]
                for ri, (r0, n) in enumerate(rts):
                    s = ri % 2
                    P.op("sp", lambda e, s=s, r0=r0, n=n: e.dma_start(out=pst[s][0:n, :], in_=pin[layer, r0:r0 + n, :]),
                         writes=["pst%d" % s], dma="pst%d" % s)
                    b = nb()

                    def tr(e, s=s, n=n, b=b):
                        for j in range(PK):
                            r = e.transpose(ps[b][:, j * 128:j * 128 + n], pst[s][0:n, j * 128:(j + 1) * 128],
                                            ident_f[0:n, 0:n])
                        return r
                    P.op("pe", tr, reads=["pst%d" % s, "cmat"], writes=["ps%d" % b])
                    src = ps[b][:, 0:PK * 128].rearrange("p (j t) -> p j t", t=128)[:, :, 0:n]
                    P.op("dve", lambda e, src=src, r0=r0, n=n: e.tensor_copy(pT[:, :, r0:r0 + n], src),
                         reads=["ps%d" % b], writes=["pT"])
                sgi = 0
                for mcp in range(D // 256):
                    wg, kg = wload(ple_w_gate[layer, :, mcp * 256:mcp * 256 + 256], KC, 256)
                    wp, kp = wload(ple_w_proj[layer, :, mcp * 256:mcp * 256 + 256], PK, 256)
                    for j in range(2):
                        mc = mcp * 2 + j
                        for (c0, n) in tts:
                            bg_, bp_ = nb(), nb()
                            mm_fm(bg_, wg, kg, KC, j, xn, ["xn"], c0, n)
                            mm_fm(bp_, wp, kp, PK, j, pT, ["pT"], c0, n)
                            s = sgi % 2
                            sgi += 1
                            P.op("act", lambda e, s=s, b=bg_, n=n: e.activation(out=sg[s][:, 0:n], in_=ps[b][:, 0:n],
                                                                              func=AF.Sigmoid),
                                 reads=["ps%d" % bg_], writes=["sgp%d" % s])
                            P.op("dve", lambda e, s=s, b=bp_, n=n: e.tensor_tensor(sg[s][:, 0:n], sg[s][:, 0:n],
                                                                                 ps[b][:, 0:n], ALU.mult),
                                 reads=["sgp%d" % s, "ps%d" % bp_], writes=["sgp%d" % s])
                            P.op("dve", lambda e, s=s, mc=mc, c0=c0, n=n: e.tensor_tensor(
                                h[:, mc, c0:c0 + n], h[:, mc, c0:c0 + n], sg[s][:, 0:n], ALU.add),
                                 reads=["sgp%d" % s, ("h", mc)], writes=[("h", mc)])
                P.end_phase()

        def split_cols(c0, n):
            pa, pb = c0, min(c0 + n, PL)
            sa, sb_ = max(c0, PL), c0 + n
            return (pa, pb) if pb > pa else None, (sa, sb_) if sb_ > sa else None

        def conv_state_io(ph_tag, pre, pre_s, slot, W0, state_d, cidx, out_p, out_s, stg_in, stg_out, cost, sidx):
            nr = W0 - 1
            ks = ph_tag + "pres%d" % slot
            si = sidx % 2
            P.op("pool", lambda e: e.dma_start(out=stg_in[si][0:NSEQ * nr, :], in_=state_d[:, cidx * 128:(cidx + 1) * 128]),
                 writes=[ph_tag + "sti%d" % si], dma=ph_tag + "sti%d" % si)
            b = nb()
            P.op("pe", lambda e: e.transpose(ps[b][:, 0:NSEQ * nr], stg_in[si][0:NSEQ * nr, :],
                                            ident_f[0:NSEQ * nr, 0:NSEQ * nr]),
                 reads=[ph_tag + "sti%d" % si, "cmat"], writes=["ps%d" % b])
            src = ps[b][:, 0:NSEQ * nr].rearrange("p (s r) -> p s r", r=nr)
            P.op("act", lambda e: e.copy(pre_s[:, slot, :, 0:nr], src), reads=["ps%d" % b], writes=[ks + "h"])

        def conv_state_out(ph_tag, pre, pre_s, slot, W0, cidx, out_p, out_s, stg_out, cost, rd_keys, sidx):
            nr = W0 - 1
            si = sidx % 2
            b = nb()
            P.op("pe", lambda e: e.transpose(ps[b][0:nr, 0:128], pre[:, slot, PL:PL + nr], ident_f),
                 reads=rd_keys + ["cmat"], writes=["ps%d" % b])
            P.op("dve", lambda e: e.tensor_copy(stg_out[si][0:nr, 0:128], ps[b][0:nr, 0:128]), reads=["ps%d" % b],
                 writes=[ph_tag + "stoP%d" % si])
            P.op("sp", lambda e: e.dma_start(out=out_p[:, cidx * 128:(cidx + 1) * 128], in_=stg_out[si][0:nr, 0:128]),
                 reads=[ph_tag + "stoP%d" % si], dma=ph_tag + "stoP%d" % si)
            P.op("dve", lambda e: e.tensor_copy(cost[si][:, 0:NSEQ * nr].rearrange("p (s r) -> p s r", r=nr),
                                               pre_s[:, slot, :, 8:8 + nr]),
                 reads=rd_keys, writes=[ph_tag + "cost%d" % si])
            b2 = nb()
            P.op("pe", lambda e: e.transpose(ps[b2][0:NSEQ * nr, 0:128], cost[si][:, 0:NSEQ * nr], ident_f),
                 reads=[ph_tag + "cost%d" % si, "cmat"], writes=["ps%d" % b2])
            P.op("act", lambda e: e.copy(stg_out[si][0:NSEQ * nr, 128:256], ps[b2][0:NSEQ * nr, 0:128]),
                 reads=["ps%d" % b2], writes=[ph_tag + "stoS%d" % si])
            P.op("sp", lambda e: e.dma_start(out=out_s[:, cidx * 128:(cidx + 1) * 128],
                                            in_=stg_out[si][0:NSEQ * nr, 128:256]),
                 reads=[ph_tag + "stoS%d" % si], dma=ph_tag + "stoS%d" % si)

        def l0_mixer_phase():
            with ExitStack() as ph:
                make_ring(ph, 4, "m")
                gated = sb("gated", [128, KC, T], BF16, ph)
                pre = sb("l0pre", [128, 2, 2 + PL], F32, ph)
                pre_s = sb("l0pres", [128, 2, NSEQ, 10], F32, ph)
                tcv = sb("l0tcv", [128, 1, T], F32, ph)
                stg_in = [sb("l0sti%d" % i, [128, 128], F32, ph) for i in range(2)]
                stg_out = [sb("l0sto%d" % i, [128, 256], F32, ph) for i in range(2)]
                cost = [sb("l0cost%d" % i, [128, NSEQ * 3], F32, ph) for i in range(2)]
                P.op("dve", lambda e: e.memset(pre[:, 0, 0:2], 0.0), writes=["Apre0z"])
                P.op("dve", lambda e: e.memset(pre[:, 1, 0:2], 0.0), writes=["Apre1z"])
                for cp in range(KC // 2):
                    wb, kb = wload(sc_w_in[:, cp * 256:cp * 256 + 256], KC, 256)
                    wc, kc = wload(sc_w_in[:, D + cp * 256:D + cp * 256 + 256], KC, 256)
                    wv, kv = wload(sc_w_in[:, 2 * D + cp * 256:2 * D + cp * 256 + 256], KC, 256)
                    for j in range(2):
                        c = cp * 2 + j
                        sl = c % 2
                        conv_state_io("A", pre, pre_s, sl, 3, sc_state, c, None, None, stg_in, None, None, c)
                        for (c0, n) in tts:
                            b_c, b_v, b_b = nb(), nb(), nb()
                            mm_fm(b_c, wc, kc, KC, j, xn, ["xn"], c0, n)
                            mm_fm(b_v, wv, kv, KC, j, xn, ["xn"], c0, n)
                            mm_fm(b_b, wb, kb, KC, j, xn, ["xn"], c0, n)
                            P.op("act", lambda e, b=b_c, c0=c0, n=n: e.copy(tcv[:, 0, c0:c0 + n], ps[b][:, 0:n]),
                                 reads=["ps%d" % b_c], writes=["Atcv0p", "Atcv0s"])
                            pp, sp_ = split_cols(c0, n)
                            if pp:
                                a, b2 = pp
                                P.op("dve", lambda e, a=a, b2=b2, b=b_v, c0=c0, sl=sl: e.tensor_tensor(
                                    pre[:, sl, 2 + a:2 + b2], tcv[:, 0, a:b2], ps[b][:, a - c0:b2 - c0], ALU.mult),
                                     reads=["ps%d" % b_v, "Atcv0p", "Atcv0s"], writes=["Apre%d" % sl])
                            if sp_:
                                a, b2 = sp_
                                assert a == PL and b2 == T
                                P.op("dve", lambda e, a=a, b2=b2, b=b_v, c0=c0, sl=sl: e.tensor_tensor(
                                    pre_s[:, sl, :, 2:10],
                                    tcv[:, 0, a:b2].rearrange("p (s t) -> p s t", t=8),
                                    ps[b][:, a - c0:b2 - c0].rearrange("p (s t) -> p s t", t=8), ALU.mult),
                                     reads=["ps%d" % b_v, "Atcv0p", "Atcv0s"], writes=["Apres%d" % sl])
                            P.op("act", lambda e, b=b_b, c=c, c0=c0, n=n: e.copy(gated[:, c, c0:c0 + n], ps[b][:, 0:n]),
                                 reads=["ps%d" % b_b], writes=[("gated", c)])
                        kt = conv_block2("A", pre, pre_s, tcv, sl, 3, ["scw0", "scw1", "scw2"], c,
                                         ["Apre%d" % sl, "Apre%dz" % sl], ["Apres%d" % sl, "Apres%dh" % sl])
                        P.op("dve", lambda e, c=c: e.tensor_tensor(gated[:, c, :], gated[:, c, :], tcv[:, 0, :], ALU.mult),
                             reads=kt + [("gated", c)], writes=[("gated", c)])
                        conv_state_out("A", pre, pre_s, sl, 3, c, sc_out_p, sc_out_s, stg_out, cost,
                                       ["Apre%d" % sl, "Apres%d" % sl, "Apres%dh" % sl], c)
                for mcp in range(D // 256):
                    wo, ko = wload(sc_w_out[:, mcp * 256:mcp * 256 + 256], KC, 256)
                    for j in range(2):
                        mc = mcp * 2 + j
                        for (c0, n) in tts:
                            b = nb()
                            mm_fm(b, wo, ko, KC, j, gated, [("gated", k) for k in range(KC)], c0, n)
                            acc_into_h(b, mc, c0, n)
                P.end_phase()

        def conv_block2(tag, pre, pre_s, tcv, slot, KW, wnames, cidx, pkeys, skeys):
            ktp, kts = tag + "tcv0p", tag + "tcv0s"
            ts_v = tcv[:, 0, PL:T].rearrange("p (s t) -> p s t", t=8)
            for kk in range(KW):
                wcol = V(wnames[kk], cidx)
                if kk == 0:
                    P.op("dve", lambda e, wcol=wcol: e.tensor_scalar(tcv[:, 0, 0:PL], pre[:, slot, 0:PL], wcol, None,
                                                                   ALU.mult),
                         reads=pkeys + ["vecs"], writes=[ktp])
                    P.op("dve", lambda e, wcol=wcol: e.tensor_scalar(ts_v, pre_s[:, slot, :, 0:8], wcol, None, ALU.mult),
                         reads=skeys + ["vecs"], writes=[kts])
                else:
                    P.op("dve", lambda e, wcol=wcol, kk=kk: e.scalar_tensor_tensor(
                        tcv[:, 0, 0:PL], pre[:, slot, kk:kk + PL], wcol, tcv[:, 0, 0:PL], ALU.mult, ALU.add),
                         reads=pkeys + ["vecs", ktp], writes=[ktp])
                    P.op("dve", lambda e, wcol=wcol, kk=kk: e.scalar_tensor_tensor(
                        ts_v, pre_s[:, slot, :, kk:kk + 8], wcol, ts_v, ALU.mult, ALU.add),
                         reads=skeys + ["vecs", kts], writes=[kts])
            return [ktp, kts]

        def ssd_phase():
            XB = 128
            NT = NQ + 1
            tile_col = [HALO + q * 128 for q in range(NQ)] + [PL]
            with ExitStack() as pho:
              dt_all = sb("dt_all", [128, NT, H], F32, pho)
              dA_all = sb("dA_all", [128, NT, H], F32, pho)
              with ExitStack() as ph1:
                wdt = sb("wdt", [128, KC, H], BF16, ph1)
                o_dtb = cfg.voff["dtb"][0]
                P.op("pool", lambda e: e.dma_start(out=wdt[:], in_=ssd_w_in[:, DI + CONV:DI + CONV + H].rearrange(
                    "(k p) n -> p k n", p=128)), writes=["wdt"], dma="wdt")
                for ti in range(NT):
                    c0 = tile_col[ti]
                    bdt = nb()

                    def mmdt(e, c0=c0, bdt=bdt):
                        for k in range(KC):
                            r = e.matmul(ps[bdt][:, 0:H], lhsT=xn[:, k, c0:c0 + 128], rhs=wdt[:, k, :], start=(k == 0),
                                         stop=(k == KC - 1))
                        return r
                    P.op("pe", mmdt, reads=["xn", "wdt"], writes=["ps%d" % bdt])
                    P.op("dve", lambda e, ti=ti, bdt=bdt: e.tensor_tensor(dt_all[:, ti, :], ps[bdt][:, 0:H],
                                                                         vecs[:, o_dtb:o_dtb + H], ALU.add),
                         reads=["ps%d" % bdt, "vecs"], writes=[("dt", ti)])
                P.op("act", lambda e: e.activation(out=dt_all[:, :, :], in_=dt_all[:, :, :], func=AF.Exp),
                     reads=[("dt", ti) for ti in range(NT)], writes=["dt_all"])
                P.op("act", lambda e: e.activation(out=dt_all[:, :, :], in_=dt_all[:, :, :], func=AF.Ln, bias=1.0),
                     reads=["dt_all"], writes=["dt_all"])
                P.op("dve", lambda e: e.tensor_tensor(dA_all[:, :, :], dt_all[:, :, :],
                                                     abc[:, :].unsqueeze(1).broadcast_to([128, NT, H]), ALU.mult),
                     reads=["dt_all", "abc"], writes=["dA_all"])

                P.end_phase()
              with ExitStack() as ph:
                make_ring(ph, 4, "s", 2048)
                xsg = sb("xsg", [128, GC, T], BF16, ph)
                Bg = sb("Bg", [128, T], BF16, ph)
                Cg = sb("Cg", [128, T], BF16, ph)
                ygn = sb("ygn", [128, GC, T], BF16, ph)
                pre = sb("l1pre", [128, 2, 3 + PL], F32, ph)
                pre_s = sb("l1pres", [128, 2, NSEQ, 11], F32, ph)
                tcv = sb("l1tcv", [128, 1, T], F32, ph)
                stg_in = [sb("l1sti%d" % i, [128, 128], F32, ph) for i in range(1)]
                stg_out = [sb("l1sto%d" % i, [128, 256], F32, ph) for i in range(1)]
                cost = [sb("l1cost%d" % i, [128, NSEQ * 3], F32, ph) for i in range(1)]
                acs = sb("acs", [128, R], F32, ph)
                expa2 = [sb("expa%d" % i, [128, R], F32, ph) for i in range(2)]
                dE = sb("dE", [128, R], F32, ph)
                cdec2 = [sb("cdec%d" % i, [128, R], F32, ph) for i in range(2)]
                cdecF = sb("cdecF", [128, GC, NSEQ], F32, ph)
                Rm = sb("Rm", [128, R, 128], F32, ph)
                Wm2 = [sb("Wm%d" % i, [128, R, 128], BF16, ph) for i in range(2)]
                MTm = sb("MTm", [128, 128], BF16, ph)
                xdt2 = [sb("xdt%d" % i, [128, GW], BF16, ph) for i in range(2)]
                xw2 = [sb("xw%d" % i, [128, GW], BF16, ph) for i in range(2)]
                Btok2 = [sb("Btok%d" % i, [128, 128], BF16, ph) for i in range(2)]
                y1 = sb("y1", [128, GW], F32, ph)
                xD2 = [sb("xD%d" % i, [128, GW], F32, ph) for i in range(2)]
                gyn = sb("gyn", [128, GW], BF16, ph)
                ss = sb("ss", [128, 2], F32, ph)
                S = sb("S", [128, GW], F32, ph)
                Sb = sb("Sb", [128, GW], BF16, ph)
                h0 = [sb("h0_%d" % i, [128, GC, 128], F32, ph) for i in range(2)]
                h0T = [sb("h0T_%d" % i, [128, GW], BF16, ph) for i in range(1)]
                xwj = [sb("xwj_%d" % i, [128, GW], BF16, ph) for i in range(1)]
                sost = [sb("sost_%d" % i, [128, GC, 128], F32, ph) for i in range(2)]
                if (NSEQ + 1) * 64 <= T:
                    Cm_t = tcv[:, 0, 0:(NSEQ + 1) * 64].bitcast(BF16)
                    cmk = "Btcv0p"
                else:
                    Cm_t = sb("Cm", [128, (NSEQ + 1) * 128], BF16, ph)[:, :]
                    cmk = "Cm"
                Cmdiag = Cm_t[:, 0:NSEQ * 136].rearrange("p (j q) -> p j q", q=136)[:, :, 0:8]
                print("SSD phase sbuf remaining", nc.sbuf_bytes_remaining, file=sys.stderr)
                P.op("dve", lambda e: e.memset(pre[:, 0, 0:3], 0.0), writes=["Bpre0z"])
                P.op("dve", lambda e: e.memset(pre[:, 1, 0:3], 0.0), writes=["Bpre1z"])
                P.op("dve", lambda e: e.memset(ygn[:, :, :], 0.0), writes=[("ygn", c) for c in range(GC)])
                o_dtb = cfg.voff["dtb"][0]
                o_dsk = cfg.voff["dsk"][0]
                def chunk(g, ti, full, wz, kz, sample=False, pp=0, nbp=None, nbt=None):
                    Q = 128
                    nbp = nbp or nb
                    nbt = nbt or nb
                    Wm, xdt, xD, xw, Btok, expa, cdec = Wm2[pp], xdt2[pp], xD2[pp], xw2[pp], Btok2[pp], expa2[pp], cdec2[pp]
                    kWm, kxdt, kxD, kxw, kBtok, kexpa, kcdec = ["%s%d" % (n_, pp) for n_ in "Wm xdt xD xw Btok expa cdec".split()]
                    col0 = tile_col[ti]
                    cs = slice(col0, col0 + Q)
                    hs = slice(g * R, (g + 1) * R)
                    TRI = triS_f if sample else tri_f
                    dtt = dt_all[:, ti, hs]
                    dA = dA_all[:, ti, hs]
                    bac = nbp()
                    yield P.op("pe", lambda e: e.matmul(ps[bac][:, 0:R], lhsT=TRI, rhs=dA, start=True, stop=True),
                         reads=["dA_all", "cmat"], writes=["ps%d" % bac])
                    yield P.op("dve", lambda e: e.tensor_tensor(
                        Rm[:, :, :], TRI.unsqueeze(1).broadcast_to([Q, R, Q]),
                        dA.unsqueeze(2).broadcast_to([Q, R, Q]), ALU.mult),
                         reads=["dA_all", "cmat"], writes=["Rm"])
                    yield P.op("dve", lambda e: e.tensor_copy(acs[:, :], ps[bac][:, 0:R]), reads=["ps%d" % bac],
                         writes=["acs"])
                    hb = 512 // Q
                    nbk = -(-R // hb)
                    bab = [nbp() for _ in range(nbk)]

                    def mmab(e):
                        for i in range(nbk):
                            h_a, h_b = i * hb, min(R, (i + 1) * hb)
                            r = e.matmul(ps[bab[i]][:, 0:(h_b - h_a) * Q].rearrange("p (h t) -> p h t", t=Q),
                                         lhsT=ones_f, rhs=Rm[:, h_a:h_b, :], start=True, stop=True)
                        return r
                    yield P.op("pe", mmab, reads=["Rm", "cmat"], writes=["ps%d" % b for b in bab])

                    def abv(i):
                        h_a, h_b = i * hb, min(R, (i + 1) * hb)
                        return ps[bab[i]][:, 0:(h_b - h_a) * Q].rearrange("p (h t) -> p h t", t=Q), h_a, h_b
                    if not sample:
                        for i in range(nbk):
                            v, h_a, h_b = abv(i)
                            yield P.op("dve", lambda e, v=v, h_a=h_a, h_b=h_b: e.tensor_tensor(
                                dE[:, h_a:h_b], v[:, :, Q - 1], acs[:, h_a:h_b], ALU.subtract),
                                 reads=["ps%d" % bab[i], "acs"], writes=["dE"])
                            yield P.op("act", lambda e, v=v, h_a=h_a, h_b=h_b: e.activation(out=cdec[:, h_a:h_b],
                                                                                    in_=v[:, :, Q - 1], func=AF.Exp),
                                 reads=["ps%d" % bab[i]], writes=[kcdec])
                    else:
                        btot = nb()
                        yield P.op("pe", lambda e: e.matmul(ps[btot][:, 0:R], lhsT=blk_f, rhs=dA, start=True, stop=True),
                             reads=["dA_all", "cmat"], writes=["ps%d" % btot])
                        yield P.op("dve", lambda e: e.tensor_tensor(dE[:, :], ps[btot][:, 0:R], acs[:, :], ALU.subtract),
                             reads=["ps%d" % btot, "acs"], writes=["dE"])
                        yield P.op("dve", lambda e: e.tensor_copy(
                            xD[:, :].rearrange("q (h p) -> q h p", p=64), dA.unsqueeze(2).broadcast_to([Q, R, 64])),
                             reads=["dA_all"], writes=[kxD])
                        bcd = nb()

                        def mmcd(e):
                            for c in range(GC):
                                r = e.matmul(ps[bcd][:, c * NSEQ:(c + 1) * NSEQ], lhsT=xD[:, c * 128:(c + 1) * 128],
                                             rhs=maskJ_f, start=True, stop=True)
                            return r
                        yield P.op("pe", mmcd, reads=[kxD, "cmat"], writes=["ps%d" % bcd])
                        yield P.op("act", lambda e: e.activation(
                            out=cdecF[:, :, :], in_=ps[bcd][:, 0:GC * NSEQ].rearrange("p (c j) -> p c j", j=NSEQ),
                            func=AF.Exp), reads=["ps%d" % bcd], writes=["cdecF"])
                    yield P.op("act", lambda e: e.activation(out=dE[:, :], in_=dE[:, :], func=AF.Exp), reads=["dE"],
                         writes=["dE"])
                    if full:
                        yield P.op("act", lambda e: e.activation(out=expa[:, :], in_=acs[:, :], func=AF.Exp),
                             reads=["acs"], writes=[kexpa])
                        for i in range(nbk):
                            v, h_a, h_b = abv(i)
                            yield P.op("dve", lambda e, v=v, h_a=h_a, h_b=h_b: e.tensor_tensor(
                                Rm[:, h_a:h_b, :], v, acs[:, h_a:h_b].unsqueeze(2).broadcast_to([Q, h_b - h_a, Q]),
                                ALU.subtract), reads=["ps%d" % bab[i], "acs", "Rm"], writes=["Rm"])
                        yield P.op("dve", lambda e: e.tensor_scalar(Rm[:, :, :], Rm[:, :, :], 0.0, None, ALU.min),
                             reads=["Rm"], writes=["Rm"])
                        yield P.op("act", lambda e: e.activation(out=Wm[:, :, :], in_=Rm[:, :, :], func=AF.Exp),
                             reads=["Rm"], writes=[kWm])
                        bm = nbp()
                        yield P.op("pe", lambda e: e.matmul(ps[bm][:, 0:Q], lhsT=Bg[:, cs], rhs=Cg[:, cs], start=True,
                                                     stop=True), reads=["Bg", "Cg"], writes=["ps%d" % bm])
                        yield P.op("dve", lambda e: e.tensor_tensor(MTm[:, :], ps[bm][:, 0:Q], TRI, ALU.mult),
                             reads=["ps%d" % bm, "cmat"], writes=["MTm"])
                        yield P.op("dve", lambda e: e.tensor_tensor(
                            Wm[:, :, :], Wm[:, :, :], MTm[:, :].unsqueeze(1).broadcast_to([Q, R, Q]),
                            ALU.mult), reads=["MTm", kWm], writes=[kWm])
                    bx = nbp()

                    def trx(e):
                        for c in range(GC):
                            r = e.transpose(psb[bx][:, c * 128:(c + 1) * 128], xsg[:, c, cs], ident_b)
                        return r
                    yield P.op("pe", trx, reads=[("xsg", c) for c in range(GC)] + ["cmb"], writes=["ps%d" % bx])
                    yield P.op("dve", lambda e: e.tensor_tensor(
                        xdt[:, :].rearrange("q (h p) -> q h p", p=64),
                        psb[bx][:, 0:GW].rearrange("q (h p) -> q h p", p=64),
                        dtt.unsqueeze(2).broadcast_to([Q, R, 64]), ALU.mult),
                         reads=["ps%d" % bx, "dt_all"], writes=[kxdt])
                    if full:
                        yield P.op("dve", lambda e: e.tensor_tensor(
                            xD[:, :].rearrange("q (h p) -> q h p", p=64),
                            psb[bx][:, 0:GW].rearrange("q (h p) -> q h p", p=64),
                            vecs[:, o_dsk + g * R:o_dsk + (g + 1) * R].unsqueeze(2).broadcast_to([Q, R, 64]), ALU.mult),
                             reads=["ps%d" % bx, "vecs", kxD], writes=[kxD])
                    bB = nbp()
                    yield P.op("pe", lambda e: e.transpose(psb[bB][:, 0:128], Bg[:, cs], ident_b), reads=["Bg", "cmb"],
                         writes=["ps%d" % bB])
                    yield P.op("act", lambda e: e.copy(Btok[:, :], psb[bB][:, 0:128]), reads=["ps%d" % bB],
                         writes=[kBtok])
                    yield P.op("dve", lambda e: e.tensor_tensor(
                        xw[:, :].rearrange("q (h p) -> q h p", p=64), xdt[:, :].rearrange("q (h p) -> q h p", p=64),
                        dE[:, :].unsqueeze(2).broadcast_to([Q, R, 64]), ALU.mult), reads=[kxdt, "dE"], writes=[kxw])
                    bi = None
                    if sample:
                        bi = nb()
                        reserved.add(bi)
                        yield P.op("dve", lambda e: e.memset(Cm_t[:, :], 0.0), writes=[cmk, "Btcv0s"])
                        yield P.op("dve", lambda e: e.tensor_copy(Cmdiag, Cg[:, cs].rearrange("p (j r) -> p j r", r=8)),
                             reads=["Cg"], writes=[cmk, "Btcv0s"])
                        for jq in range(NSEQ):
                            s = jq % 2
                            yield P.op("pool", lambda e, jq=jq, s=s: e.dma_start(
                                out=h0[s][:, :, :],
                                in_=ssd_state[jq, g * GW:(g + 1) * GW, :].rearrange("(c p) n -> p c n", p=128)),
                                 writes=["h0_%d" % s], dma="h0_%d" % s)
                            bh = nb()

                            def trh(e, bh=bh, s=s):
                                for c in range(GC):
                                    r = e.transpose(ps[bh][:, c * 128:(c + 1) * 128], h0[s][:, c, :], ident_f)
                                return r
                            yield P.op("pe", trh, reads=["h0_%d" % s, "cmat"], writes=["ps%d" % bh])
                            yield P.op("act", lambda e, bh=bh, s=s: e.copy(h0T[0][:, :], ps[bh][:, 0:GW]), reads=["ps%d" % bh],
                                 writes=["h0T_0"])
                            yield P.op("pe", lambda e, jq=jq, s=s: e.matmul(ps[bi][:, 0:GW], lhsT=Cm_t[:, jq * 128:(jq + 1) * 128], rhs=h0T[0][:, :],
                                                                     start=(jq == 0), stop=(jq == NSEQ - 1)),
                                 reads=[cmk, "Btcv0s", "h0T_0"], writes=["ps%d" % bi])
                            yield P.op("dve", lambda e, jq=jq, s=s: e.tensor_scalar(xwj[0][:, :], xw[:, :], maskJ_f[:, jq:jq + 1],
                                                                            None, ALU.mult),
                                 reads=[kxw, "cmat"], writes=["xwj_0"])
                            bsj = nb()

                            def mmsj(e, bsj=bsj, s=s):
                                for c in range(GC):
                                    r = e.matmul(ps[bsj][:, c * 128:(c + 1) * 128], lhsT=xwj[0][:, c * 128:(c + 1) * 128],
                                                 rhs=Btok[:, :], start=True, stop=True)
                                return r
                            yield P.op("pe", mmsj, reads=["xwj_0", kBtok], writes=["ps%d" % bsj])
                            yield P.op("dve", lambda e, jq=jq, s=s: e.tensor_tensor(
                                sost[s][:, :, :], h0[s][:, :, :],
                                cdecF[:, :, jq].unsqueeze(2).broadcast_to([128, GC, 128]), ALU.mult),
                                 reads=["h0_%d" % s, "cdecF"], writes=["sost_%d" % s])
                            yield P.op("dve", lambda e, bsj=bsj, s=s: e.tensor_tensor(
                                sost[s][:, :, :], sost[s][:, :, :],
                                ps[bsj][:, 0:GW].rearrange("p (c n) -> p c n", n=128), ALU.add),
                                 reads=["sost_%d" % s, "ps%d" % bsj], writes=["sost_%d" % s])
                            yield P.op("sp", lambda e, jq=jq, s=s: e.dma_start(
                                out=st_out_s[jq, g * GW:(g + 1) * GW, :].rearrange("(c p) n -> p c n", p=128),
                                in_=sost[s][:, :, :]), reads=["sost_%d" % s], dma="sost_%d" % s)
                    yield "SPLIT"
                    if full:
                        bz = nbt()

                        def mmz(e):
                            r = None
                            for i in range(GW // XB):
                                for k in range(KC):
                                    r = e.matmul(ps[bz][:, i * XB:(i + 1) * XB], lhsT=xn[:, k, cs], rhs=wz[i][:, k, :],
                                                 start=(k == 0), stop=(k == KC - 1))
                            return r
                        yield P.op("pe", mmz, reads=["xn"] + kz, writes=["ps%d" % bz])
                        if sample:
                            yield P.op("dve", lambda e: e.tensor_tensor(
                                y1[:, :].rearrange("q (h p) -> q h p", p=64),
                                ps[bi][:, 0:GW].rearrange("q (h p) -> q h p", p=64),
                                expa[:, :].unsqueeze(2).broadcast_to([Q, R, 64]), ALU.mult),
                                 reads=["ps%d" % bi, kexpa], writes=["y1"])
                            reserved.discard(bi)
                        by = nbt()

                        def mmy(e):
                            for hh in range(R):
                                r = e.matmul(ps[by][:, hh * 64:(hh + 1) * 64], lhsT=Wm[:, hh, :],
                                             rhs=xdt[:, hh * 64:(hh + 1) * 64], start=True, stop=True)
                            return r
                        yield P.op("pe", mmy, reads=[kWm, kxdt], writes=["ps%d" % by])
                        if not sample:
                            bi = nbt()
                            yield P.op("pe", lambda e: e.matmul(ps[bi][:, 0:GW], lhsT=Cg[:, cs], rhs=Sb[:, :], start=True,
                                                         stop=True), reads=["Cg", "Sb"], writes=["ps%d" % bi])
                            yield P.op("dve", lambda e: e.tensor_tensor(
                                y1[:, :].rearrange("q (h p) -> q h p", p=64),
                                ps[bi][:, 0:GW].rearrange("q (h p) -> q h p", p=64),
                                expa[:, :].unsqueeze(2).broadcast_to([Q, R, 64]), ALU.mult),
                                 reads=["ps%d" % bi, kexpa], writes=["y1"])
                    if not sample:
                        bs = nbt()
                        yield P.op("pe", lambda e: e.matmul(ps[bs][:, 0:GW], lhsT=Btok[:, :], rhs=xw[:, :], start=True, stop=True),
                             reads=[kBtok, kxw], writes=["ps%d" % bs])
                        yield P.op("dve", lambda e: e.tensor_tensor(
                            S[:, :].rearrange("n (h p) -> n h p", p=64), S[:, :].rearrange("n (h p) -> n h p", p=64),
                            cdec[:, :].unsqueeze(2).broadcast_to([128, R, 64]), ALU.mult), reads=["S", kcdec, "Sb"],
                             writes=["S"])
                        yield P.op("dve", lambda e: e.tensor_tensor(S[:, :], S[:, :], ps[bs][:, 0:GW], ALU.add),
                             reads=["S", "ps%d" % bs], writes=["S"])
                        yield P.op("act", lambda e: e.copy(Sb[:, :], S[:, :]), reads=["S"], writes=["Sb"])

                    if full:
                        yield P.op("dve", lambda e: e.tensor_tensor(y1[:, :], y1[:, :], ps[by][:, 0:GW], ALU.add),
                             reads=["y1", "ps%d" % by], writes=["y1"])
                        yield P.op("dve", lambda e: e.tensor_tensor(y1[:, :], y1[:, :], xD[:, :], ALU.add),
                             reads=["y1", kxD], writes=["y1"])
                        yield P.op("act", lambda e: e.activation(out=xD[:, :], in_=ps[bz][:, 0:GW], func=AF.Tanh, scale=0.5),
                             reads=["ps%d" % bz, kxD], writes=[kxD])
                        yield P.op("dve", lambda e: e.scalar_tensor_tensor(xD[:, :], xD[:, :], 1.0, ps[bz][:, 0:GW], ALU.add,
                                                                    ALU.mult),
                             reads=[kxD, "ps%d" % bz], writes=[kxD])
                        yield P.op("dve", lambda e: e.scalar_tensor_tensor(y1[:, :], y1[:, :], 0.5, xD[:, :], ALU.mult, ALU.mult),
                             reads=["y1", kxD], writes=["y1"])
                        yield P.op("act", lambda e: e.activation(out=xD[:, :], in_=y1[:, :], func=AF.Square,
                                                          accum_out=ss[:, 0:1]), reads=["y1", kxD],
                             writes=[kxD, "ss"])
                        yield P.op("dve", lambda e: e.tensor_scalar(ss[:, 1:2], ss[:, 0:1], 1.0 / GW, EPS, ALU.mult, ALU.add),
                             reads=["ss"], writes=["ss"])
                        yield P.op("act", lambda e: e.activation(out=ss[:, 1:2], in_=ss[:, 1:2], func=AF.Sqrt),
                             reads=["ss"], writes=["ss"])
                        yield P.op("dve", lambda e: e.reciprocal(ss[:, 1:2], ss[:, 1:2]), reads=["ss"], writes=["ss"])
                        yield P.op("dve", lambda e: e.tensor_scalar(gyn[:, :], y1[:, :], ss[:, 1:2], None, ALU.mult),
                             reads=["y1", "ss"], writes=["gyn"])
                        bt = nbt()

                        def trg(e):
                            for c in range(GC):
                                r = e.transpose(psb[bt][:, c * 128:c * 128 + Q], gyn[:, c * 128:(c + 1) * 128], ident_b)
                            return r
                        yield P.op("pe", trg, reads=["gyn", "cmb"], writes=["ps%d" % bt])
                        o_ng = cfg.voff["ng"][0]
                        for c in range(GC):
                            yield P.op("dve" if c % 2 == 0 else "act",
                                 (lambda e, c=c: e.tensor_scalar(ygn[:, c, cs], psb[bt][:, c * 128:c * 128 + Q],
                                                                 vecs[:, o_ng + g * GC + c:o_ng + g * GC + c + 1], None,
                                                                 ALU.mult)) if c % 2 == 0 else
                                 (lambda e, c=c: e.activation(out=ygn[:, c, cs], in_=psb[bt][:, c * 128:c * 128 + Q],
                                                              func=AF.Copy,
                                                              scale=vecs[:, o_ng + g * GC + c:o_ng + g * GC + c + 1])),
                                 reads=["ps%d" % bt, "vecs"], writes=[("ygn", c)])
                def state_out(dst_ap_fn):
                    bo = nb()

                    def tro(e):
                        for c in range(GC):
                            r = e.transpose(ps[bo][:, c * 128:(c + 1) * 128], S[:, c * 128:(c + 1) * 128], ident_f)
                        return r
                    P.op("pe", tro, reads=["S", "cmat"], writes=["ps%d" % bo])
                    P.op("act", lambda e: e.copy(sost[0][:, :, :], ps[bo][:, 0:GW].rearrange("p (c n) -> p c n", n=128)),
                         reads=["ps%d" % bo], writes=["sost_0"])
                    P.op("sp", lambda e: e.dma_start(out=dst_ap_fn(), in_=sost[0][:, :, :]), reads=["sost_0"],
                         dma="sost_0")

                sidx = [0]
                for g in range(G):
                    specs = []
                    for i in range(GW // XB):
                        col = DI + g * GW + i * XB
                        for j in range(XB // 128):
                            c = i * (XB // 128) + j
                            specs.append((ssd_w_in[:, col:col + XB], XB, j, g * GC + c, ("x", c), i))
                    colB = 2 * DI + g * 128
                    specs.append((ssd_w_in[:, colB:colB + 128], 128, 0, DI // 128 + g, ("B", 0), "B"))
                    colC = 2 * DI + GN + g * 128
                    specs.append((ssd_w_in[:, colC:colC + 128], 128, 0, DI // 128 + G + g, ("C", 0), "C"))
                    loaded = {}

                    def ensure(k):
                        if k < len(specs) and specs[k][5] not in loaded:
                            loaded[specs[k][5]] = wload(specs[k][0], KC, specs[k][1])
                    LOOK = 3
                    for k in range(LOOK):
                        ensure(k)
                    for idx in range(len(specs)):
                        ensure(idx + LOOK)
                        _, _, j, cidx, (kind, c), lk = specs[idx]
                        wv_, kv_ = loaded[lk]
                        sl = sidx[0] % 2
                        conv_state_io("B", pre, pre_s, sl, 4, ssdc_state, cidx, None, None, stg_in, None, None, 0)
                        for (c0, n) in tts:
                            b = nb()
                            mm_fm(b, wv_, kv_, KC, j, xn, ["xn"], c0, n)
                            pp, sp_ = split_cols(c0, n)
                            if pp:
                                a, b2 = pp
                                P.op("act", lambda e, a=a, b2=b2, b=b, c0=c0, sl=sl: e.copy(pre[:, sl, 3 + a:3 + b2],
                                                                                   ps[b][:, a - c0:b2 - c0]),
                                     reads=["ps%d" % b], writes=["Bpre%d" % sl])
                            if sp_:
                                a, b2 = sp_
                                P.op("dve", lambda e, a=a, b2=b2, b=b, c0=c0, sl=sl: e.tensor_copy(
                                    pre_s[:, sl, :, 3:11], ps[b][:, a - c0:b2 - c0].rearrange("p (s t) -> p s t", t=8)),
                                     reads=["ps%d" % b], writes=["Bpres%d" % sl])
                        kt = conv_block2("B", pre, pre_s, tcv, sl, 4, ["cw0", "cw1", "cw2", "cw3"], cidx,
                                         ["Bpre%d" % sl, "Bpre%dz" % sl], ["Bpres%d" % sl, "Bpres%dh" % sl])
                        if kind == "x":
                            dst, dk = xsg[:, c, :], ("xsg", c)
                        elif kind == "B":
                            dst, dk = Bg[:, :], "Bg"
                        else:
                            dst, dk = Cg[:, :], "Cg"
                        P.op("act", lambda e, dst=dst, cidx=cidx: e.activation(out=dst, in_=tcv[:, 0, :], func=AF.Silu,
                                                                             bias=V("cb", cidx)),
                             reads=kt + ["vecs"], writes=[dk])
                        conv_state_out("B", pre, pre_s, sl, 4, cidx, ssdc_out_p, ssdc_out_s, stg_out, cost,
                                       ["Bpre%d" % sl, "Bpres%d" % sl, "Bpres%dh" % sl], 0)
                        sidx[0] += 1
                    wz, kz = [], []
                    for i in range(GW // XB):
                        col = g * GW + i * XB
                        w_, k_ = wload(ssd_w_in[:, col:col + XB], KC, XB)
                        wz.append(w_)
                        kz.append(k_)
                    P.op("dve", lambda e: e.memset(S[:, :], 0.0), reads=["Sb"], writes=["S"])
                    for q in range(NQ):
                        for _ in chunk(g, q, False, wz, kz, pp=q % 2):
                            pass
                    ibk, obk = "ib%d" % g, "ob%d" % g
                    P.op("sp", lambda e, g=g: e.dma_start(out=ibs[g][:, :], in_=S[:, :]), reads=["S"], writes=[ibk],
                         dma="xch")
                    P.op("pool", lambda e, g=g: e.collective_compute(
                        "AllGather", ALU.bypass, replica_groups=[[0, 1], [2, 3], [4, 5], [6, 7]],
                        ins=[ibs[g].ap().opt()], outs=[obs[g].ap().opt()]), reads=[ibk], writes=[obk],
                         dma="cc%d" % g, inc=1)
                    for _ in chunk(g, NQ, True, wz, kz, sample=True, pp=0):
                        pass
                    P.op("sp", lambda e, g=g: e.dma_start(out=S[:, :], in_=obs[g][0:128, :]), reads=[obk, "Sb"],
                         writes=["S"], dma="xch")
                    P.op("dve", lambda e: e.tensor_scalar(S[:, :], S[:, :], mko[:, 0:1], None, ALU.mult),
                         reads=["S", "mko"], writes=["S"])
                    P.op("act", lambda e: e.copy(Sb[:, :], S[:, :]), reads=["S"], writes=["Sb"])
                    gens = [chunk(g, q, True, wz, kz, pp=q % 2, nbp=nb_prep, nbt=nb_tail) for q in range(NQ)]
                    for v in gens[0]:
                        if v == "SPLIT":
                            break
                    for q in range(NQ):
                        ga = gens[q]
                        gb = gens[q + 1] if q + 1 < NQ else None
                        a_done, b_done = False, gb is None
                        while not (a_done and b_done):
                            if not a_done:
                                try:
                                    next(ga)
                                except StopIteration:
                                    a_done = True
                            if not b_done:
                                try:
                                    if next(gb) == "SPLIT":
                                        b_done = True
                                except StopIteration:
                                    b_done = True
                    state_out(lambda g=g: st_out_p[g * GW:(g + 1) * GW, :].rearrange("(c p) n -> p c n", p=128))
                    for mcp in range(D // 256):
                        wo, ko = wload(ssd_w_out[g * GW:(g + 1) * GW, mcp * 256:mcp * 256 + 256], GC, 256)
                        for j in range(2):
                            mc = mcp * 2 + j
                            for (c0, n) in tts:
                                b = nb()
                                mm_fm(b, wo, ko, GC, j, ygn, [("ygn", k) for k in range(GC)], c0, n)
                                acc_into_h(b, mc, c0, n)
                P.end_phase()

        def out_phase():
            with ExitStack() as ph:
                yst = [sb("ystg%d" % i, [128, D], F32, ph) for i in range(2)]
                for ri, (r0, n) in enumerate(rts):
                    s = ri % 2
                    for cg in range(KC // 4):
                        b = nb()

                        def tr(e, n=n, cg=cg, b=b, r0=r0):
                            for j in range(4):
                                c = cg * 4 + j
                                r = e.transpose(ps[b][0:n, j * 128:(j + 1) * 128], h[:, c, r0:r0 + n], ident_f)
                            return r
                        P.op("pe", tr, reads=["h", "cmat"], writes=["ps%d" % b])
                        if cg % 2 == 0:
                            P.op("dve", lambda e, s=s, n=n, cg=cg, b=b: e.tensor_copy(yst[s][0:n, cg * 512:(cg + 1) * 512],
                                                                                    ps[b][0:n, :]),
                                 reads=["ps%d" % b], writes=[("yst%d" % s, cg)])
                        else:
                            P.op("act", lambda e, s=s, n=n, cg=cg, b=b: e.copy(yst[s][0:n, cg * 512:(cg + 1) * 512],
                                                                             ps[b][0:n, :]),
                                 reads=["ps%d" % b], writes=[("yst%d" % s, cg)])
                    P.op("sp", lambda e, s=s, r0=r0, n=n: e.dma_start(out=y_out[r0:r0 + n, :], in_=yst[s][0:n, :]),
                         reads=[("yst%d" % s, cg) for cg in range(KC // 4)], dma="yst%d" % s)
                P.end_phase()

        phases = [lambda: rmsnorm_phase("g_mix0"), l0_mixer_phase,
                  lambda: rmsnorm_phase("g_ffn0"), lambda: ffn_phase(0),
                  lambda: rmsnorm_phase("g_ple0"), lambda: ple_phase(0),
                  lambda: rmsnorm_phase("g_mix1"), ssd_phase,
                  lambda: rmsnorm_phase("g_ffn1"), lambda: ffn_phase(1),
                  lambda: rmsnorm_phase("g_ple1"), lambda: ple_phase(1),
                  lambda: rmsnorm_phase("g_final", final=True)]
        for pi, phf in enumerate(phases):
            if stop is not None and pi >= stop:
                break
            phf()
        out_phase()
        for k, v in P.cnt.items():
            if k[0] == "dma" and P.waited["sp"].get(k, 0) < v:
                nc.sync.wait_ge(P.getsem(k), v)
        print("ops", len(P.ops), "waits", P.nwait, "sems", len(P.sems), file=sys.stderr)
    return nc


def host_inputs(cfg, inp):
    D, KC, T, TP, NSEQ = cfg.D, cfg.KC, cfg.T, cfg.TP, cfg.NSEQ
    f = np.float32

    def pm(v):
        v = np.asarray(v, f)
        return np.ascontiguousarray(v.reshape(-1, 128).T)

    def bc(v):
        v = np.asarray(v, f).reshape(1, -1)
        return np.ascontiguousarray(np.broadcast_to(v, (128, v.shape[1])))
    vec = np.zeros((128, cfg.NV), f)

    def put(nm, a):
        o, w = cfg.voff[nm]
        assert a.shape == (128, w), (nm, a.shape, w)
        vec[:, o:o + w] = a
    for l in range(2):
        put("g_mix%d" % l, pm(inp["g_mix"][l]))
        put("g_ffn%d" % l, pm(inp["g_ffn"][l]))
        put("g_ple%d" % l, pm(inp["g_ple"][l]))
    put("g_final", pm(inp["g_final"]))
    for k in range(3):
        put("scw%d" % k, pm(inp["sc_w_conv"][0, k]))
    for k in range(4):
        put("cw%d" % k, pm(inp["ssd_conv_w"][0, k]))
    put("cb", pm(inp["ssd_conv_b"][0]))
    put("ng", pm(inp["ssd_norm_g"][0]))
    put("dtb", bc(inp["ssd_dt_bias"][0]))
    put("alog", bc(inp["ssd_a_log"][0]))
    put("dsk", bc(inp["ssd_d"][0]))
    cm = np.zeros((128, 640 + NSEQ), f)
    cm[:, 0:128] = np.eye(128, dtype=f)
    cm[:, 128:256] = np.triu(np.ones((128, 128), f))
    cm[:, 256:384] = 1.0
    sid = np.arange(128) // 8
    same = (sid[:, None] == sid[None, :]).astype(f)
    cm[:, 384:512] = cm[:, 128:256] * same
    cm[:, 512:640] = same
    cm[:, 640:640 + NSEQ] = (sid[:, None] == np.arange(NSEQ)[None, :]).astype(f)
    mt = np.zeros((128, NSEQ, 128), f)
    mt[:, sid, np.arange(128)] = 1.0
    shared = dict(vecs=vec, cmat=cm,
                  sc_w_in=np.ascontiguousarray(inp["sc_w_in"][0]), sc_w_out=np.ascontiguousarray(inp["sc_w_out"][0]),
                  ssd_w_in=np.ascontiguousarray(inp["ssd_w_in"][0]), ssd_w_out=np.ascontiguousarray(inp["ssd_w_out"][0]),
                  ffn_w_gate=np.ascontiguousarray(inp["ffn_w_gate"]), ffn_w_up=np.ascontiguousarray(inp["ffn_w_up"]),
                  ffn_w_down=np.ascontiguousarray(inp["ffn_w_down"]), ple_w_proj=np.ascontiguousarray(inp["ple_w_proj"]),
                  ple_w_gate=np.ascontiguousarray(inp["ple_w_gate"]))
    maps = []
    for core in range(8):
        b, half = core // 2, core % 2
        xin = np.zeros((T, D), f)
        pin = np.zeros((2, T, cfg.PLE), f)
        t0 = half * TP
        if half == 1:
            xin[0:HALO] = inp["x_prompt"][b, t0 - HALO:t0]
            pin[:, 0:HALO] = inp["p_prompt"][:, b, t0 - HALO:t0]
        xin[HALO:HALO + TP] = inp["x_prompt"][b, t0:t0 + TP]
        pin[:, HALO:HALO + TP] = inp["p_prompt"][:, b, t0:t0 + TP]
        sl = slice(core * NSEQ, (core + 1) * NSEQ)
        xin[HALO + TP:] = inp["x_sample"][sl].reshape(-1, D)
        pin[:, HALO + TP:] = inp["p_sample"][:, sl].reshape(2, -1, cfg.PLE)
        m = dict(shared)
        m.update(xin=xin, pin=pin,
                 sc_state=np.ascontiguousarray(inp["state_sc_conv"][0, sl].reshape(NSEQ * 2, D)),
                 ssdc_state=np.ascontiguousarray(inp["state_ssd_conv"][0, sl].reshape(NSEQ * 3, cfg.CONV)),
                 ssd_state=np.ascontiguousarray(inp["state_ssd"][0, sl].reshape(NSEQ, cfg.H * 64, 128)),
                 maskodd=np.full((128, 1), float(half), f))
        maps.append(m)
    return maps


def assemble(cfg, res):
    D, TP, NSEQ, H = cfg.D, cfg.TP, cfg.NSEQ, cfg.H
    f = np.float32
    B = 4
    y_p = np.zeros((B, 2 * TP, D), f)
    y_s = np.zeros((8 * NSEQ, cfg.DEC_SEQ, D), f)
    scp = np.zeros((1, B, 2, D), f)
    scs = np.zeros((1, 8 * NSEQ, 2, D), f)
    ssdcp = np.zeros((1, B, 3, cfg.CONV), f)
    ssdcs = np.zeros((1, 8 * NSEQ, 3, cfg.CONV), f)
    stp = np.zeros((1, B, H, 64, 128), f)
    sts = np.zeros((1, 8 * NSEQ, H, 64, 128), f)
    for core in range(8):
        r = res[core]
        b, half = core // 2, core % 2
        y = r["y"]
        y_p[b, half * TP:(half + 1) * TP] = y[HALO:HALO + TP]
        sl = slice(core * NSEQ, (core + 1) * NSEQ)
        y_s[sl] = y[HALO + TP:].reshape(NSEQ, cfg.DEC_SEQ, D)
        scs[0, sl] = r["sc_out_s"].reshape(NSEQ, 2, D)
        ssdcs[0, sl] = r["ssdc_out_s"].reshape(NSEQ, 3, cfg.CONV)
        sts[0, sl] = r["st_out_s"].reshape(NSEQ, H, 64, 128)
        if half == 1:
            scp[0, b] = r["sc_out_p"]
            ssdcp[0, b] = r["ssdc_out_p"]
            stp[0, b] = r["st_out_p"].reshape(H, 64, 128)
    return (y_p, y_s, scp, scs, ssdcp, ssdcs, stp, sts)


_NC_CACHE = {}


def run(cfg, inp, stop=None):
    key = (cfg.D, cfg.SEQ, stop)
    if key not in _NC_CACHE:
        _NC_CACHE[key] = build(cfg, stop)
    nc = _NC_CACHE[key]
    maps = host_inputs(cfg, inp)
    res = run_bass_kernel_spmd(nc, maps, core_ids=list(range(8)))
    return assemble(cfg, res.results)


def kernel(**inputs):
    cfg = Cfg()
    inp = {k: np.asarray(v) for k, v in inputs.items()}
    return run(cfg, inp)
```

```python
import sys
from contextlib import ExitStack
import numpy as np
import concourse.bass as bass
import concourse.mybir as mybir
from concourse.bass_utils import run_bass_kernel_spmd

F32 = mybir.dt.float32
BF16 = mybir.dt.bfloat16
AF = mybir.ActivationFunctionType
ALU = mybir.AluOpType
EPS = 1e-6
HALO = 5


class Cfg:
    def __init__(self, D=2048, SEQ=2048, DEC_BATCH=128, DEC_SEQ=8, PLE=256):
        self.D = D
        self.SEQ = SEQ
        self.BATCH = 4
        self.DEC_BATCH = DEC_BATCH
        self.DEC_SEQ = DEC_SEQ
        self.PLE = PLE
        self.DFF = -(-8 * D // (3 * 256)) * 256
        self.DI = 2 * D
        self.H = self.DI // 64
        self.G = 8
        self.N = 128
        self.R = self.H // 8
        self.GW = self.DI // 8
        self.GC = self.GW // 128
        self.GN = self.G * self.N
        self.CONV = self.DI + 2 * self.GN
        self.IN = self.DI + self.CONV + self.H
        self.KC = D // 128
        self.FC = self.DFF // 128
        self.CC = self.CONV // 128
        self.TP = SEQ // 2
        self.NQ = self.TP // 128
        self.NSEQ = DEC_BATCH // 8
        self.TS = self.NSEQ * DEC_SEQ
        self.PL = HALO + self.TP
        self.T = self.PL + self.TS
        off = {}
        o = 0
        for nm, w in [("g_mix0", self.KC), ("g_ffn0", self.KC), ("g_ple0", self.KC),
                      ("g_mix1", self.KC), ("g_ffn1", self.KC), ("g_ple1", self.KC),
                      ("g_final", self.KC), ("scw0", self.KC), ("scw1", self.KC), ("scw2", self.KC),
                      ("cw0", self.CC), ("cw1", self.CC), ("cw2", self.CC), ("cw3", self.CC),
                      ("cb", self.CC), ("ng", self.DI // 128),
                      ("dtb", self.H), ("alog", self.H), ("dsk", self.H)]:
            off[nm] = (o, w)
            o += w
        self.voff = off
        self.NV = o


class Prog:
    ENGS = ("pe", "act", "dve", "pool", "sp")

    def __init__(self, nc, stack):
        self.nc = nc
        self.stack = stack
        self.ops = []
        self.lastw = {}
        self.readers = {}
        self.floor = 0
        self.emitted = 0
        self.sems = {}
        self.cnt = {}
        self.waited = {e: {} for e in self.ENGS}
        self.lastsig = {}
        self.nwait = 0
        self.eng = dict(pe=nc.tensor, act=nc.scalar, dve=nc.vector, pool=nc.gpsimd, sp=nc.sync)

    def op(self, eng, fn, reads=(), writes=(), dma=None, inc=16):
        i = len(self.ops)
        psr = [r for r in reads if isinstance(r, str) and r.startswith("ps")]
        if psr:
            reads = [r for r in reads if r not in psr]
            writes = list(writes) + psr
        deps = set()
        for r in reads:
            w = self.lastw.get(r)
            if w is not None:
                deps.add(w)
        for r in writes:
            w = self.lastw.get(r)
            if w is not None:
                deps.add(w)
            for x in self.readers.get(r, ()):
                deps.add(x)
        for r in reads:
            self.readers.setdefault(r, []).append(i)
        for r in writes:
            self.lastw[r] = i
            self.readers[r] = []
        deps.discard(i)
        deps = {d for d in deps if d >= self.floor}
        self.ops.append(dict(eng=eng, fn=fn, deps=deps, dma=dma, inc=inc, sig=False))
        return i

    def getsem(self, k):
        if k not in self.sems:
            self.sems[k] = self.stack.enter_context(self.nc.semaphore("s%d" % len(self.sems)))
        return self.sems[k]

    def end_phase(self):
        ops = self.ops
        start = self.emitted
        last = {}
        for i in range(start, len(ops)):
            o = ops[i]
            k = ("dma", o["dma"]) if o["dma"] is not None else ("eng", o["eng"])
            last[k] = i
        bdeps = set(last.values())
        for e in self.ENGS:
            self.ops.append(dict(eng=e, fn=None, deps=set(bdeps), dma=None, inc=0, sig=False, barrier=True))
        for i in range(start, len(ops)):
            o = ops[i]
            nd = set()
            for d in o["deps"]:
                p = ops[d]
                if (not o.get("barrier")) and p["dma"] is None and o["dma"] is None \
                        and p["eng"] == "pe" and o["eng"] == "pe":
                    continue
                nd.add(d)
            o["deps"] = nd
            for d in nd:
                ops[d]["sig"] = True
        for i in range(start, len(ops)):
            o = ops[i]
            if o["dma"] is not None:
                k = ("dma", o["dma"])
                self.cnt[k] = self.cnt.get(k, 0) + o["inc"]
                o["semk"], o["val"] = k, self.cnt[k]
            elif o["sig"]:
                k = ("eng", o["eng"])
                self.cnt[k] = self.cnt.get(k, 0) + 1
                o["semk"], o["val"] = k, self.cnt[k]
        for i in range(start, len(ops)):
            o = ops[i]
            e = o["eng"]
            need = {}
            for d in o["deps"]:
                p = ops[d]
                k = p["semk"]
                need[k] = max(need.get(k, 0), p["val"])
            for k, v in need.items():
                if self.waited[e].get(k, 0) >= v:
                    continue
                self.eng[e].wait_ge(self.getsem(k), v)
                self.waited[e][k] = v
                self.nwait += 1
            if o["fn"] is None:
                continue
            ins = o["fn"](self.eng[e])
            if o["dma"] is not None:
                ins.then_inc(self.getsem(o["semk"]), o["inc"])
            elif o["sig"]:
                ins.then_inc(self.getsem(o["semk"]), 1)
            o["fn"] = None
        self.emitted = len(ops)
        self.floor = len(ops)
        self.lastw = {}
        self.readers = {}


def split_tiles(T, maxn=448):
    nt = -(-T // maxn)
    base = T // nt
    rem = T % nt
    out = []
    c = 0
    for i in range(nt):
        n = base + (1 if i < rem else 0)
        out.append((c, n))
        c += n
    return out


def build(cfg, stop=None):
    D, KC, T, PL, TS, NSEQ = cfg.D, cfg.KC, cfg.T, cfg.PL, cfg.TS, cfg.NSEQ
    DI, H, G, R, GW, GC, GN, CC, FC = cfg.DI, cfg.H, cfg.G, cfg.R, cfg.GW, cfg.GC, cfg.GN, cfg.CC, cfg.FC
    PLE, DFF, CONV, IN, NQ = cfg.PLE, cfg.DFF, cfg.CONV, cfg.IN, cfg.NQ
    PK = PLE // 128
    nc = bass.Bass("TRN2", target_bir_lowering=False)
    CMW = 640 + NSEQ

    def din(name, shape):
        return nc.dram_tensor(name, list(shape), F32, kind="ExternalInput").ap()

    def dout(name, shape):
        return nc.dram_tensor(name, list(shape), F32, kind="ExternalOutput").ap()

    xin = din("xin", [T, D])
    pin = din("pin", [2, T, PLE])
    sc_state = din("sc_state", [NSEQ * 2, D])
    ssdc_state = din("ssdc_state", [NSEQ * 3, CONV])
    ssd_state = din("ssd_state", [NSEQ, H * 64, 128])
    maskodd = din("maskodd", [128, 1])
    vecs_d = din("vecs", [128, cfg.NV])
    cmat_d = din("cmat", [128, CMW])
    sc_w_in = din("sc_w_in", [D, 3 * D])
    sc_w_out = din("sc_w_out", [D, D])
    ssd_w_in = din("ssd_w_in", [D, IN])
    ssd_w_out = din("ssd_w_out", [DI, D])
    ffn_w_gate = din("ffn_w_gate", [2, D, DFF])
    ffn_w_up = din("ffn_w_up", [2, D, DFF])
    ffn_w_down = din("ffn_w_down", [2, DFF, D])
    ple_w_proj = din("ple_w_proj", [2, PLE, D])
    ple_w_gate = din("ple_w_gate", [2, D, D])

    y_out = dout("y", [T, D])
    sc_out_p = dout("sc_out_p", [2, D])
    sc_out_s = dout("sc_out_s", [NSEQ * 2, D])
    ssdc_out_p = dout("ssdc_out_p", [3, CONV])
    ssdc_out_s = dout("ssdc_out_s", [NSEQ * 3, CONV])
    st_out_p = dout("st_out_p", [H * 64, 128])
    st_out_s = dout("st_out_s", [NSEQ, H * 64, 128])
    ibs = [nc.dram_tensor("ib%d" % g, [128, GW], F32) for g in range(G)]
    obs = [nc.dram_tensor("ob%d" % g, [256, GW], F32) for g in range(G)]

    tts = split_tiles(T)
    for (c0, n) in tts:
        assert not (c0 < PL < c0 + n and False)
    assert any(c0 <= PL and PL + TS <= c0 + n for (c0, n) in tts) or True
    rts = [(r0, min(128, T - r0)) for r0 in range(0, T, 128)]

    with ExitStack() as st:
        sbctr = [0]

        def sb(name, shape, dt=F32, stack=None):
            sbctr[0] += 1
            return (stack or st).enter_context(nc.sbuf_tensor("%s_%d" % (name, sbctr[0]), list(shape), dt))

        P = Prog(nc, st)
        h = sb("h", [128, KC, T])
        xn = sb("xn", [128, KC, T], BF16)
        vecs = sb("vecs", [128, cfg.NV])
        cmat = sb("cmat", [128, CMW])
        cmb = sb("cmb", [128, CMW], BF16)
        abc = sb("abc", [128, H])
        mko = sb("mko", [128, 1])
        ps = [st.enter_context(nc.psum_tensor("ps%d" % i, [128, 512], F32)) for i in range(8)]
        psb = [p[:, :].bitcast(BF16) for p in ps]
        ident_f, tri_f, ones_f = cmat[:, 0:128], cmat[:, 128:256], cmat[:, 256:384]
        triS_f, blk_f, maskJ_f = cmat[:, 384:512], cmat[:, 512:640], cmat[:, 640:640 + NSEQ]
        ident_b, tri_b, ones_b = cmb[:, 0:128], cmb[:, 128:256], cmb[:, 256:384]

        def V(nm, c=None):
            o, w = cfg.voff[nm]
            if c is None:
                return vecs[:, o:o + w]
            return vecs[:, o + c:o + c + 1]

        bankctr = [0]

        reserved = set()

        def nb():
            while True:
                b = bankctr[0] % 8
                bankctr[0] += 1
                if b not in reserved:
                    return b

        poolctr = {"p": 0, "t": 0}

        def nb_prep():
            b = poolctr["p"] % 4
            poolctr["p"] += 1
            return b

        def nb_tail():
            b = 4 + poolctr["t"] % 4
            poolctr["t"] += 1
            return b

        ringstate = {}

        def make_ring(stack, nslots, tag, slot=4096):
            bufs = [sb("ring%s%d" % (tag, i), [128, slot], BF16, stack) for i in range(nslots)]
            ringstate["bufs"] = bufs
            ringstate["i"] = 0
            ringstate["slot"] = slot

        def wload(src, nk, ncols):
            bufs = ringstate["bufs"]
            i = ringstate["i"] % len(bufs)
            ringstate["i"] += 1
            assert nk * ncols <= ringstate["slot"]
            view = bufs[i][:, 0:nk * ncols].rearrange("p (k n) -> p k n", n=ncols)
            key = "ring%d" % i
            if key in P.lastw:
                assert P.readers.get(key), "ring slot %s overwritten before any consumer was recorded" % key
            srcv = src.rearrange("(k p) n -> p k n", p=128)
            P.op("pool", lambda e: e.dma_start(out=view, in_=srcv), writes=[key], dma=key)
            return view, key

        with ExitStack() as ph:
            xs = [sb("xstg%d" % i, [128, D], F32, ph) for i in range(2)]
            P.op("sp", lambda e: e.dma_start(out=vecs[:], in_=vecs_d), writes=["vecs"], dma="c0")
            P.op("sp", lambda e: e.dma_start(out=cmat[:], in_=cmat_d), writes=["cmat"], dma="c1")
            P.op("sp", lambda e: e.dma_start(out=mko[:], in_=maskodd), writes=["mko"], dma="c2")
            P.op("dve", lambda e: e.tensor_copy(cmb[:], cmat[:]), reads=["cmat"], writes=["cmb"])
            o_al, w_al = cfg.voff["alog"]
            P.op("act", lambda e: e.activation(out=abc[:], in_=vecs[:, o_al:o_al + w_al], func=AF.Exp),
                 reads=["vecs"], writes=["abc"])
            P.op("dve", lambda e: e.tensor_scalar(abc[:], abc[:], -1.0, None, ALU.mult), reads=["abc"], writes=["abc"])
            for ri, (r0, n) in enumerate(rts):
                s = ri % 2
                P.op("sp", lambda e, s=s, r0=r0, n=n: e.dma_start(out=xs[s][0:n, :], in_=xin[r0:r0 + n, :]),
                     writes=["xs%d" % s], dma="xs%d" % s)
                for cg in range(KC // 4):
                    b = nb()

                    def tr(e, s=s, n=n, cg=cg, b=b):
                        for j in range(4):
                            c = cg * 4 + j
                            r = e.transpose(ps[b][:, j * 128:j * 128 + n], xs[s][0:n, c * 128:(c + 1) * 128],
                                            ident_f[0:n, 0:n])
                        return r
                    P.op("pe", tr, reads=["xs%d" % s, "cmat"], writes=["ps%d" % b])
                    src = ps[b][:, :].rearrange("p (j t) -> p j t", t=128)[:, :, 0:n]
                    dst = h[:, cg * 4:cg * 4 + 4, r0:r0 + n]
                    if cg % 2 == 0:
                        P.op("dve", lambda e, src=src, dst=dst: e.tensor_copy(dst, src), reads=["ps%d" % b],
                             writes=[("h", ri, cg)])
                    else:
                        P.op("act", lambda e, src=src, dst=dst: e.copy(dst, src), reads=["ps%d" % b],
                             writes=[("h", ri, cg)])
            P.end_phase()

        def rmsnorm_phase(gname, final=False):
            with ExitStack() as ph:
                sq = [sb("sq%d" % i, [128, KC, tts[0][1]], BF16, ph) for i in range(2)]
                rstd = sb("rstd", [128, T], F32, ph)
                for ti, (c0, n) in enumerate(tts):
                    s = ti % 2
                    P.op("act", lambda e, s=s, c0=c0, n=n: e.activation(out=sq[s][:, :, 0:n], in_=h[:, :, c0:c0 + n],
                                                                      func=AF.Square),
                         reads=["h"], writes=["sq%d" % s])
                    b = nb()

                    def mm(e, s=s, n=n, b=b):
                        for k in range(KC):
                            r = e.matmul(ps[b][:, 0:n], lhsT=ones_b, rhs=sq[s][:, k, 0:n], start=(k == 0),
                                         stop=(k == KC - 1))
                        return r
                    P.op("pe", mm, reads=["sq%d" % s, "cmb"], writes=["ps%d" % b])
                    P.op("dve", lambda e, b=b, c0=c0, n=n: e.tensor_scalar(rstd[:, c0:c0 + n], ps[b][:, 0:n], 1.0 / D,
                                                                         EPS, ALU.mult, ALU.add),
                         reads=["ps%d" % b], writes=["rs%d" % ti])
                    P.op("act", lambda e, c0=c0, n=n: e.activation(out=rstd[:, c0:c0 + n], in_=rstd[:, c0:c0 + n],
                                                                 func=AF.Sqrt),
                         reads=["rs%d" % ti], writes=["rs%d" % ti])
                    P.op("dve", lambda e, c0=c0, n=n: e.reciprocal(rstd[:, c0:c0 + n], rstd[:, c0:c0 + n]),
                         reads=["rs%d" % ti], writes=["rs%d" % ti])
                rk = ["rs%d" % ti for ti in range(len(tts))]
                for c in range(KC):
                    dst = h[:, c, :] if final else xn[:, c, :]
                    P.op("dve", lambda e, c=c, dst=dst: e.scalar_tensor_tensor(dst, h[:, c, :], V(gname, c), rstd[:, :],
                                                                             ALU.mult, ALU.mult),
                         reads=rk + ["vecs", "h"], writes=[("hf" if final else "xn", c)])
                P.end_phase()

        def acc_into_h(b, mc, c0, n):
            P.op("dve", lambda e: e.tensor_tensor(h[:, mc, c0:c0 + n], h[:, mc, c0:c0 + n], ps[b][:, 0:n], ALU.add),
                 reads=["ps%d" % b, ("h", mc)], writes=[("h", mc)])

        def mm_fm(b, wv, wkey, nk, j, rhs, rkeys, c0, n):
            def mm(e):
                for k in range(nk):
                    r = e.matmul(ps[b][:, 0:n], lhsT=wv[:, k, j * 128:(j + 1) * 128], rhs=rhs[:, k, c0:c0 + n],
                                 start=(k == 0), stop=(k == nk - 1))
                return r
            P.op("pe", mm, reads=[wkey] + list(rkeys), writes=["ps%d" % b])

        def ffn_phase(layer):
            npairs = FC // 2
            ngroups = max(1, -(-FC // 12))
            base = npairs // ngroups
            rem = npairs % ngroups
            gsz = [base + (1 if i < rem else 0) for i in range(ngroups)]
            maxk = max(gsz) * 2
            with ExitStack() as ph:
                make_ring(ph, 6, "f")
                act = sb("act", [128, maxk, T], BF16, ph)
                sg = [sb("sg%d" % i, [128, 512], F32, ph) for i in range(2)]
                sgi = 0
                pr0 = 0
                for gi, np_ in enumerate(gsz):
                    nkg = np_ * 2
                    for pr in range(np_):
                        col = (pr0 + pr) * 256
                        wg, kg = wload(ffn_w_gate[layer, :, col:col + 256], KC, 256)
                        wu, ku = wload(ffn_w_up[layer, :, col:col + 256], KC, 256)
                        for j in range(2):
                            fl = pr * 2 + j
                            for (c0, n) in tts:
                                bg_, bu_ = nb(), nb()
                                mm_fm(bg_, wg, kg, KC, j, xn, ["xn"], c0, n)
                                mm_fm(bu_, wu, ku, KC, j, xn, ["xn"], c0, n)
                                s = sgi % 2
                                sgi += 1
                                P.op("act", lambda e, s=s, b=bg_, n=n: e.activation(out=sg[s][:, 0:n], in_=ps[b][:, 0:n],
                                                                                  func=AF.Silu),
                                     reads=["ps%d" % bg_], writes=["sg%d" % s])
                                P.op("dve", lambda e, s=s, b=bu_, fl=fl, c0=c0, n=n: e.tensor_tensor(
                                    act[:, fl, c0:c0 + n], sg[s][:, 0:n], ps[b][:, 0:n], ALU.mult),
                                     reads=["sg%d" % s, "ps%d" % bu_], writes=[("act", fl)])
                    for mcp in range(D // 256):
                        wd, kd = wload(ffn_w_down[layer, pr0 * 256:pr0 * 256 + nkg * 128, mcp * 256:mcp * 256 + 256],
                                       nkg, 256)
                        for j in range(2):
                            mc = mcp * 2 + j
                            for (c0, n) in tts:
                                b = nb()
                                mm_fm(b, wd, kd, nkg, j, act, [("act", k) for k in range(nkg)], c0, n)
                                acc_into_h(b, mc, c0, n)
                    pr0 += np_
                P.end_phase()

        def ple_phase(layer):
            with ExitStack() as ph:
                make_ring(ph, 6, "p")
                pT = sb("pT", [128, PK, T], BF16, ph)
                pst = [sb("pst%d" % i, [128, PLE], F32, ph) for i in range(2)]
                sg = [sb("sgp%d" % i, [128, 512], F32, ph) for i in range(2)]
                for ri, (r0, n) in enumerate(rts):
                    s = ri % 2
                    P.op("sp", lambda e, s=s, r0=r0, n=n: e.dma_start(out=pst[s][0:n, :], in_=pin[layer, r0:r0 + n, :]),
                         writes=["pst%d" % s], dma="pst%d" % s)
                    b = nb()

                    def tr(e, s=s, n=n, b=b):
                        for j in range(PK):
                            r = e.transpose(ps[b][:, j * 128:j * 128 + n], pst[s][0:n, j * 128:(j + 1) * 128],
                                            ident_f[0:n, 0:n])
                        return r
                    P.op("pe", tr, reads=["pst%d" % s, "cmat"], writes=["ps%d" % b])
                    src = ps[b][:, 0:PK * 128].rearrange("p (j t) -> p j t", t=128)[:, :, 0:n]
                    P.op("dve", lambda e, src=src, r0=r0, n=n: e.tensor_copy(pT[:, :, r0:r0 + n], src),
                         reads=["ps%d" % b], writes=["pT"])
                sgi = 0
                for mcp in range(D // 256):
                    wg, kg = wload(ple_w_gate[layer, :, mcp * 256:mcp * 256 + 256], KC, 256)
                    wp, kp = wload(ple_w_proj[layer, :, mcp * 256:mcp * 256 + 256], PK, 256)
                    for j in range(2):
                        mc = mcp * 2 + j
                        for (c0, n) in tts:
                            bg_, bp_ = nb(), nb()
                            mm_fm(bg_, wg, kg, KC, j, xn, ["xn"], c0, n)
                            mm_fm(bp_, wp, kp, PK, j, pT, ["pT"], c0, n)
                            s = sgi % 2
                            sgi += 1
                            P.op("act", lambda e, s=s, b=bg_, n=n: e.activation(out=sg[s][:, 0:n], in_=ps[b][:, 0:n],
                                                                              func=AF.Sigmoid),
                                 reads=["ps%d" % bg_], writes=["sgp%d" % s])
                            P.op("dve", lambda e, s=s, b=bp_, n=n: e.tensor_tensor(sg[s][:, 0:n], sg[s][:, 0:n],
                                                                                 ps[b][:, 0:n], ALU.mult),
                                 reads=["sgp%d" % s, "ps%d" % bp_], writes=["sgp%d" % s])
                            P.op("dve", lambda e, s=s, mc=mc, c0=c0, n=n: e.tensor_tensor(
                                h[:, mc, c0:c0 + n], h[:, mc, c0:c0 + n], sg[s][:, 0:n], ALU.add),
                                 reads=["sgp%d" % s, ("h", mc)], writes=[("h", mc)])
                P.end_phase()

        def split_cols(c0, n):
            pa, pb = c0, min(c0 + n, PL)
            sa, sb_ = max(c0, PL), c0 + n
            return (pa, pb) if pb > pa else None, (sa, sb_) if sb_ > sa else None

        def conv_state_io(ph_tag, pre, pre_s, slot, W0, state_d, cidx, out_p, out_s, stg_in, stg_out, cost, sidx):
            nr = W0 - 1
            ks = ph_tag + "pres%d" % slot
            si = sidx % 2
            P.op("pool", lambda e: e.dma_start(out=stg_in[si][0:NSEQ * nr, :], in_=state_d[:, cidx * 128:(cidx + 1) * 128]),
                 writes=[ph_tag + "sti%d" % si], dma=ph_tag + "sti%d" % si)
            b = nb()
            P.op("pe", lambda e: e.transpose(ps[b][:, 0:NSEQ * nr], stg_in[si][0:NSEQ * nr, :],
                                            ident_f[0:NSEQ * nr, 0:NSEQ * nr]),
                 reads=[ph_tag + "sti%d" % si, "cmat"], writes=["ps%d" % b])
            src = ps[b][:, 0:NSEQ * nr].rearrange("p (s r) -> p s r", r=nr)
            P.op("act", lambda e: e.copy(pre_s[:, slot, :, 0:nr], src), reads=["ps%d" % b], writes=[ks + "h"])

        def conv_state_out(ph_tag, pre, pre_s, slot, W0, cidx, out_p, out_s, stg_out, cost, rd_keys, sidx):
            nr = W0 - 1
            si = sidx % 2
            b = nb()
            P.op("pe", lambda e: e.transpose(ps[b][0:nr, 0:128], pre[:, slot, PL:PL + nr], ident_f),
                 reads=rd_keys + ["cmat"], writes=["ps%d" % b])
            P.op("dve", lambda e: e.tensor_copy(stg_out[si][0:nr, 0:128], ps[b][0:nr, 0:128]), reads=["ps%d" % b],
                 writes=[ph_tag + "stoP%d" % si])
            P.op("sp", lambda e: e.dma_start(out=out_p[:, cidx * 128:(cidx + 1) * 128], in_=stg_out[si][0:nr, 0:128]),
                 reads=[ph_tag + "stoP%d" % si], dma=ph_tag + "stoP%d" % si)
            P.op("dve", lambda e: e.tensor_copy(cost[si][:, 0:NSEQ * nr].rearrange("p (s r) -> p s r", r=nr),
                                               pre_s[:, slot, :, 8:8 + nr]),
                 reads=rd_keys, writes=[ph_tag + "cost%d" % si])
            b2 = nb()
            P.op("pe", lambda e: e.transpose(ps[b2][0:NSEQ * nr, 0:128], cost[si][:, 0:NSEQ * nr], ident_f),
                 reads=[ph_tag + "cost%d" % si, "cmat"], writes=["ps%d" % b2])
            P.op("act", lambda e: e.copy(stg_out[si][0:NSEQ * nr, 128:256], ps[b2][0:NSEQ * nr, 0:128]),
                 reads=["ps%d" % b2], writes=[ph_tag + "stoS%d" % si])
            P.op("sp", lambda e: e.dma_start(out=out_s[:, cidx * 128:(cidx + 1) * 128],
                                            in_=stg_out[si][0:NSEQ * nr, 128:256]),
                 reads=[ph_tag + "stoS%d" % si], dma=ph_tag + "stoS%d" % si)

        def l0_mixer_phase():
            with ExitStack() as ph:
                make_ring(ph, 4, "m")
                gated = sb("gated", [128, KC, T], BF16, ph)
                pre = sb("l0pre", [128, 2, 2 + PL], F32, ph)
                pre_s = sb("l0pres", [128, 2, NSEQ, 10], F32, ph)
                tcv = sb("l0tcv", [128, 1, T], F32, ph)
                stg_in = [sb("l0sti%d" % i, [128, 128], F32, ph) for i in range(2)]
                stg_out = [sb("l0sto%d" % i, [128, 256], F32, ph) for i in range(2)]
                cost = [sb("l0cost%d" % i, [128, NSEQ * 3], F32, ph) for i in range(2)]
                P.op("dve", lambda e: e.memset(pre[:, 0, 0:2], 0.0), writes=["Apre0z"])
                P.op("dve", lambda e: e.memset(pre[:, 1, 0:2], 0.0), writes=["Apre1z"])
                for cp in range(KC // 2):
                    wb, kb = wload(sc_w_in[:, cp * 256:cp * 256 + 256], KC, 256)
                    wc, kc = wload(sc_w_in[:, D + cp * 256:D + cp * 256 + 256], KC, 256)
                    wv, kv = wload(sc_w_in[:, 2 * D + cp * 256:2 * D + cp * 256 + 256], KC, 256)
                    for j in range(2):
                        c = cp * 2 + j
                        sl = c % 2
                        conv_state_io("A", pre, pre_s, sl, 3, sc_state, c, None, None, stg_in, None, None, c)
                        for (c0, n) in tts:
                            b_c, b_v, b_b = nb(), nb(), nb()
                            mm_fm(b_c, wc, kc, KC, j, xn, ["xn"], c0, n)
                            mm_fm(b_v, wv, kv, KC, j, xn, ["xn"], c0, n)
                            mm_fm(b_b, wb, kb, KC, j, xn, ["xn"], c0, n)
                            P.op("act", lambda e, b=b_c, c0=c0, n=n: e.copy(tcv[:, 0, c0:c0 + n], ps[b][:, 0:n]),
                                 reads=["ps%d" % b_c], writes=["Atcv0p", "Atcv0s"])
                            pp, sp_ = split_cols(c0, n)
                            if pp:
                                a, b2 = pp
                                P.op("dve", lambda e, a=a, b2=b2, b=b_v, c0=c0, sl=sl: e.tensor_tensor(
                                    pre[:, sl, 2 + a:2 + b2], tcv[:, 0, a:b2], ps[b][:, a - c0:b2 - c0], ALU.mult),
                                     reads=["ps%d" % b_v, "Atcv0p", "Atcv0s"], writes=["Apre%d" % sl])
                            if sp_:
                                a, b2 = sp_
                                assert a == PL and b2 == T
                                P.op("dve", lambda e, a=a, b2=b2, b=b_v, c0=c0, sl=sl: e.tensor_tensor(
                                    pre_s[:, sl, :, 2:10],
                                    tcv[:, 0, a:b2].rearrange("p (s t) -> p s t", t=8),
                                    ps[b][:, a - c0:b2 - c0].rearrange("p (s t) -> p s t", t=8), ALU.mult),
                                     reads=["ps%d" % b_v, "Atcv0p", "Atcv0s"], writes=["Apres%d" % sl])
                            P.op("act", lambda e, b=b_b, c=c, c0=c0, n=n: e.copy(gated[:, c, c0:c0 + n], ps[b][:, 0:n]),
                                 reads=["ps%d" % b_b], writes=[("gated", c)])
                        conv_state_out("A", pre, pre_s, sl, 3, c, sc_out_p, sc_out_s, stg_out, cost,
                                       ["Apre%d" % sl, "Apres%d" % sl, "Apres%dh" % sl], c)
                        kt = conv_block2("A", pre, pre_s, tcv, sl, 3, ["scw0", "scw1", "scw2"], c,
                                         ["Apre%d" % sl, "Apre%dz" % sl], ["Apres%d" % sl, "Apres%dh" % sl])
                        P.op("dve", lambda e, c=c: e.tensor_tensor(gated[:, c, :], gated[:, c, :], tcv[:, 0, :], ALU.mult),
                             reads=kt + [("gated", c)], writes=[("gated", c)])
                for mcp in range(D // 256):
                    wo, ko = wload(sc_w_out[:, mcp * 256:mcp * 256 + 256], KC, 256)
                    for j in range(2):
                        mc = mcp * 2 + j
                        for (c0, n) in tts:
                            b = nb()
                            mm_fm(b, wo, ko, KC, j, gated, [("gated", k) for k in range(KC)], c0, n)
                            acc_into_h(b, mc, c0, n)
                P.end_phase()

        def conv_block2(tag, pre, pre_s, tcv, slot, KW, wnames, cidx, pkeys, skeys):
            ktp, kts = tag + "tcv0p", tag + "tcv0s"
            ts_v = tcv[:, 0, PL:T].rearrange("p (s t) -> p s t", t=8)
            for kk in range(KW):
                wcol = V(wnames[kk], cidx)
                if kk == 0:
                    P.op("dve", lambda e, wcol=wcol: e.tensor_scalar(tcv[:, 0, 0:PL], pre[:, slot, 0:PL], wcol, None,
                                                                   ALU.mult),
                         reads=pkeys + ["vecs"], writes=[ktp])
                    P.op("dve", lambda e, wcol=wcol: e.tensor_scalar(ts_v, pre_s[:, slot, :, 0:8], wcol, None, ALU.mult),
                         reads=skeys + ["vecs"], writes=[kts])
                else:
                    P.op("dve", lambda e, wcol=wcol, kk=kk: e.scalar_tensor_tensor(
                        tcv[:, 0, 0:PL], pre[:, slot, kk:kk + PL], wcol, tcv[:, 0, 0:PL], ALU.mult, ALU.add),
                         reads=pkeys + ["vecs", ktp], writes=[ktp])
                    P.op("dve", lambda e, wcol=wcol, kk=kk: e.scalar_tensor_tensor(
                        ts_v, pre_s[:, slot, :, kk:kk + 8], wcol, ts_v, ALU.mult, ALU.add),
                         reads=skeys + ["vecs", kts], writes=[kts])
            return [ktp, kts]

        def ssd_phase():
            XB = 128
            NT = NQ + 1
            tile_col = [HALO + q * 128 for q in range(NQ)] + [PL]
            with ExitStack() as pho:
              dt_all = sb("dt_all", [128, NT, H], F32, pho)
              dA_all = sb("dA_all", [128, NT, H], F32, pho)
              with ExitStack() as ph1:
                wdt = sb("wdt", [128, KC, H], BF16, ph1)
                o_dtb = cfg.voff["dtb"][0]
                P.op("pool", lambda e: e.dma_start(out=wdt[:], in_=ssd_w_in[:, DI + CONV:DI + CONV + H].rearrange(
                    "(k p) n -> p k n", p=128)), writes=["wdt"], dma="wdt")
                for ti in range(NT):
                    c0 = tile_col[ti]
                    bdt = nb()

                    def mmdt(e, c0=c0, bdt=bdt):
                        for k in range(KC):
                            r = e.matmul(ps[bdt][:, 0:H], lhsT=xn[:, k, c0:c0 + 128], rhs=wdt[:, k, :], start=(k == 0),
                                         stop=(k == KC - 1))
                        return r
                    P.op("pe", mmdt, reads=["xn", "wdt"], writes=["ps%d" % bdt])
                    P.op("dve", lambda e, ti=ti, bdt=bdt: e.tensor_tensor(dt_all[:, ti, :], ps[bdt][:, 0:H],
                                                                         vecs[:, o_dtb:o_dtb + H], ALU.add),
                         reads=["ps%d" % bdt, "vecs"], writes=[("dt", ti)])
                P.op("act", lambda e: e.activation(out=dt_all[:, :, :], in_=dt_all[:, :, :], func=AF.Exp),
                     reads=[("dt", ti) for ti in range(NT)], writes=["dt_all"])
                P.op("act", lambda e: e.activation(out=dt_all[:, :, :], in_=dt_all[:, :, :], func=AF.Ln, bias=1.0),
                     reads=["dt_all"], writes=["dt_all"])
                P.op("dve", lambda e: e.tensor_tensor(dA_all[:, :, :], dt_all[:, :, :],
                                                     abc[:, :].unsqueeze(1).broadcast_to([128, NT, H]), ALU.mult),
                     reads=["dt_all", "abc"], writes=["dA_all"])

                P.end_phase()
              with ExitStack() as ph:
                make_ring(ph, 4, "s", 2048)
                xsg = sb("xsg", [128, GC, T], BF16, ph)
                Bg = sb("Bg", [128, T], BF16, ph)
                Cg = sb("Cg", [128, T], BF16, ph)
                ygn = sb("ygn", [128, GC, T], BF16, ph)
                pre = sb("l1pre", [128, 2, 3 + PL], F32, ph)
                pre_s = sb("l1pres", [128, 2, NSEQ, 11], F32, ph)
                tcv = sb("l1tcv", [128, 1, T], F32, ph)
                stg_in = [sb("l1sti%d" % i, [128, 128], F32, ph) for i in range(1)]
                stg_out = [sb("l1sto%d" % i, [128, 256], F32, ph) for i in range(1)]
                cost = [sb("l1cost%d" % i, [128, NSEQ * 3], F32, ph) for i in range(1)]
                acs = sb("acs", [128, R], F32, ph)
                expa2 = [sb("expa%d" % i, [128, R], F32, ph) for i in range(2)]
                dE = sb("dE", [128, R], F32, ph)
                cdec2 = [sb("cdec%d" % i, [128, R], F32, ph) for i in range(2)]
                cdecF = sb("cdecF", [128, GC, NSEQ], F32, ph)
                Rm = sb("Rm", [128, R, 128], F32, ph)
                Wm2 = [sb("Wm%d" % i, [128, R, 128], BF16, ph) for i in range(2)]
                MTm = sb("MTm", [128, 128], BF16, ph)
                xdt2 = [sb("xdt%d" % i, [128, GW], BF16, ph) for i in range(2)]
                xw2 = [sb("xw%d" % i, [128, GW], BF16, ph) for i in range(2)]
                Btok2 = [sb("Btok%d" % i, [128, 128], BF16, ph) for i in range(2)]
                y1 = sb("y1", [128, GW], F32, ph)
                xD2 = [sb("xD%d" % i, [128, GW], F32, ph) for i in range(2)]
                gyn = sb("gyn", [128, GW], BF16, ph)
                ss = sb("ss", [128, 2], F32, ph)
                S = sb("S", [128, GW], F32, ph)
                Sb = sb("Sb", [128, GW], BF16, ph)
                h0 = [sb("h0_%d" % i, [128, GC, 128], F32, ph) for i in range(2)]
                h0T = [sb("h0T_%d" % i, [128, GW], BF16, ph) for i in range(1)]
                xwj = [sb("xwj_%d" % i, [128, GW], BF16, ph) for i in range(1)]
                sost = [sb("sost_%d" % i, [128, GC, 128], F32, ph) for i in range(2)]
                if (NSEQ + 1) * 64 <= T:
                    Cm_t = tcv[:, 0, 0:(NSEQ + 1) * 64].bitcast(BF16)
                    cmk = "Btcv0p"
                else:
                    Cm_t = sb("Cm", [128, (NSEQ + 1) * 128], BF16, ph)[:, :]
                    cmk = "Cm"
                Cmdiag = Cm_t[:, 0:NSEQ * 136].rearrange("p (j q) -> p j q", q=136)[:, :, 0:8]
                print("SSD phase sbuf remaining", nc.sbuf_bytes_remaining, file=sys.stderr)
                P.op("dve", lambda e: e.memset(pre[:, 0, 0:3], 0.0), writes=["Bpre0z"])
                P.op("dve", lambda e: e.memset(pre[:, 1, 0:3], 0.0), writes=["Bpre1z"])
                P.op("dve", lambda e: e.memset(ygn[:, :, :], 0.0), writes=[("ygn", c) for c in range(GC)])
                o_dtb = cfg.voff["dtb"][0]
                o_dsk = cfg.voff["dsk"][0]
                def chunk(g, ti, full, wz, kz, sample=False, pp=0, nbp=None, nbt=None):
                    Q = 128
                    nbp = nbp or nb
                    nbt = nbt or nb
                    Wm, xdt, xD, xw, Btok, expa, cdec = Wm2[pp], xdt2[pp], xD2[pp], xw2[pp], Btok2[pp], expa2[pp], cdec2[pp]
                    kWm, kxdt, kxD, kxw, kBtok, kexpa, kcdec = ["%s%d" % (n_, pp) for n_ in "Wm xdt xD xw Btok expa cdec".split()]
                    col0 = tile_col[ti]
                    cs = slice(col0, col0 + Q)
                    hs = slice(g * R, (g + 1) * R)
                    TRI = triS_f if sample else tri_f
                    dtt = dt_all[:, ti, hs]
                    dA = dA_all[:, ti, hs]
                    bac = nbp()
                    yield P.op("pe", lambda e: e.matmul(ps[bac][:, 0:R], lhsT=TRI, rhs=dA, start=True, stop=True),
                         reads=["dA_all", "cmat"], writes=["ps%d" % bac])
                    yield P.op("dve", lambda e: e.tensor_tensor(
                        Rm[:, :, :], TRI.unsqueeze(1).broadcast_to([Q, R, Q]),
                        dA.unsqueeze(2).broadcast_to([Q, R, Q]), ALU.mult),
                         reads=["dA_all", "cmat"], writes=["Rm"])
                    yield P.op("dve", lambda e: e.tensor_copy(acs[:, :], ps[bac][:, 0:R]), reads=["ps%d" % bac],
                         writes=["acs"])
                    hb = 512 // Q
                    nbk = -(-R // hb)
                    bab = [nbp() for _ in range(nbk)]

                    def mmab(e):
                        for i in range(nbk):
                            h_a, h_b = i * hb, min(R, (i + 1) * hb)
                            r = e.matmul(ps[bab[i]][:, 0:(h_b - h_a) * Q].rearrange("p (h t) -> p h t", t=Q),
                                         lhsT=ones_f, rhs=Rm[:, h_a:h_b, :], start=True, stop=True)
                        return r
                    yield P.op("pe", mmab, reads=["Rm", "cmat"], writes=["ps%d" % b for b in bab])

                    def abv(i):
                        h_a, h_b = i * hb, min(R, (i + 1) * hb)
                        return ps[bab[i]][:, 0:(h_b - h_a) * Q].rearrange("p (h t) -> p h t", t=Q), h_a, h_b
                    if not sample:
                        for i in range(nbk):
                            v, h_a, h_b = abv(i)
                            yield P.op("dve", lambda e, v=v, h_a=h_a, h_b=h_b: e.tensor_tensor(
                                dE[:, h_a:h_b], v[:, :, Q - 1], acs[:, h_a:h_b], ALU.subtract),
                                 reads=["ps%d" % bab[i], "acs"], writes=["dE"])
                            yield P.op("act", lambda e, v=v, h_a=h_a, h_b=h_b: e.activation(out=cdec[:, h_a:h_b],
                                                                                    in_=v[:, :, Q - 1], func=AF.Exp),
                                 reads=["ps%d" % bab[i]], writes=[kcdec])
                    else:
                        btot = nb()
                        yield P.op("pe", lambda e: e.matmul(ps[btot][:, 0:R], lhsT=blk_f, rhs=dA, start=True, stop=True),
                             reads=["dA_all", "cmat"], writes=["ps%d" % btot])
                        yield P.op("dve", lambda e: e.tensor_tensor(dE[:, :], ps[btot][:, 0:R], acs[:, :], ALU.subtract),
                             reads=["ps%d" % btot, "acs"], writes=["dE"])
                        yield P.op("dve", lambda e: e.tensor_copy(
                            xD[:, :].rearrange("q (h p) -> q h p", p=64), dA.unsqueeze(2).broadcast_to([Q, R, 64])),
                             reads=["dA_all"], writes=[kxD])
                        bcd = nb()

                        def mmcd(e):
                            for c in range(GC):
                                r = e.matmul(ps[bcd][:, c * NSEQ:(c + 1) * NSEQ], lhsT=xD[:, c * 128:(c + 1) * 128],
                                             rhs=maskJ_f, start=True, stop=True)
                            return r
                        yield P.op("pe", mmcd, reads=[kxD, "cmat"], writes=["ps%d" % bcd])
                        yield P.op("act", lambda e: e.activation(
                            out=cdecF[:, :, :], in_=ps[bcd][:, 0:GC * NSEQ].rearrange("p (c j) -> p c j", j=NSEQ),
                            func=AF.Exp), reads=["ps%d" % bcd], writes=["cdecF"])
                    yield P.op("act", lambda e: e.activation(out=dE[:, :], in_=dE[:, :], func=AF.Exp), reads=["dE"],
                         writes=["dE"])
                    if full:
                        yield P.op("act", lambda e: e.activation(out=expa[:, :], in_=acs[:, :], func=AF.Exp),
                             reads=["acs"], writes=[kexpa])
                        for i in range(nbk):
                            v, h_a, h_b = abv(i)
                            yield P.op("dve", lambda e, v=v, h_a=h_a, h_b=h_b: e.tensor_tensor(
                                Rm[:, h_a:h_b, :], v, acs[:, h_a:h_b].unsqueeze(2).broadcast_to([Q, h_b - h_a, Q]),
                                ALU.subtract), reads=["ps%d" % bab[i], "acs", "Rm"], writes=["Rm"])
                        yield P.op("dve", lambda e: e.tensor_scalar(Rm[:, :, :], Rm[:, :, :], 0.0, None, ALU.min),
                             reads=["Rm"], writes=["Rm"])
                        yield P.op("act", lambda e: e.activation(out=Wm[:, :, :], in_=Rm[:, :, :], func=AF.Exp),
                             reads=["Rm"], writes=[kWm])
                        bm = nbp()
                        yield P.op("pe", lambda e: e.matmul(ps[bm][:, 0:Q], lhsT=Bg[:, cs], rhs=Cg[:, cs], start=True,
                                                     stop=True), reads=["Bg", "Cg"], writes=["ps%d" % bm])
                        yield P.op("dve", lambda e: e.tensor_tensor(MTm[:, :], ps[bm][:, 0:Q], TRI, ALU.mult),
                             reads=["ps%d" % bm, "cmat"], writes=["MTm"])
                        yield P.op("dve", lambda e: e.tensor_tensor(
                            Wm[:, :, :], Wm[:, :, :], MTm[:, :].unsqueeze(1).broadcast_to([Q, R, Q]),
                            ALU.mult), reads=["MTm", kWm], writes=[kWm])
                    bx = nbp()

                    def trx(e):
                        for c in range(GC):
                            r = e.transpose(psb[bx][:, c * 128:(c + 1) * 128], xsg[:, c, cs], ident_b)
                        return r
                    yield P.op("pe", trx, reads=[("xsg", c) for c in range(GC)] + ["cmb"], writes=["ps%d" % bx])
                    yield P.op("dve", lambda e: e.tensor_tensor(
                        xdt[:, :].rearrange("q (h p) -> q h p", p=64),
                        psb[bx][:, 0:GW].rearrange("q (h p) -> q h p", p=64),
                        dtt.unsqueeze(2).broadcast_to([Q, R, 64]), ALU.mult),
                         reads=["ps%d" % bx, "dt_all"], writes=[kxdt])
                    if full:
                        yield P.op("dve", lambda e: e.tensor_tensor(
                            xD[:, :].rearrange("q (h p) -> q h p", p=64),
                            psb[bx][:, 0:GW].rearrange("q (h p) -> q h p", p=64),
                            vecs[:, o_dsk + g * R:o_dsk + (g + 1) * R].unsqueeze(2).broadcast_to([Q, R, 64]), ALU.mult),
                             reads=["ps%d" % bx, "vecs", kxD], writes=[kxD])
                    bB = nbp()
                    yield P.op("pe", lambda e: e.transpose(psb[bB][:, 0:128], Bg[:, cs], ident_b), reads=["Bg", "cmb"],
                         writes=["ps%d" % bB])
                    yield P.op("act", lambda e: e.copy(Btok[:, :], psb[bB][:, 0:128]), reads=["ps%d" % bB],
                         writes=[kBtok])
                    yield P.op("dve", lambda e: e.tensor_tensor(
                        xw[:, :].rearrange("q (h p) -> q h p", p=64), xdt[:, :].rearrange("q (h p) -> q h p", p=64),
                        dE[:, :].unsqueeze(2).broadcast_to([Q, R, 64]), ALU.mult), reads=[kxdt, "dE"], writes=[kxw])
                    bi = None
                    if sample:
                        bi = nb()
                        reserved.add(bi)
                        yield P.op("dve", lambda e: e.memset(Cm_t[:, :], 0.0), writes=[cmk, "Btcv0s"])
                        yield P.op("dve", lambda e: e.tensor_copy(Cmdiag, Cg[:, cs].rearrange("p (j r) -> p j r", r=8)),
                             reads=["Cg"], writes=[cmk, "Btcv0s"])
                        for jq in range(NSEQ):
                            s = jq % 2
                            yield P.op("pool", lambda e, jq=jq, s=s: e.dma_start(
                                out=h0[s][:, :, :],
                                in_=ssd_state[jq, g * GW:(g + 1) * GW, :].rearrange("(c p) n -> p c n", p=128)),
                                 writes=["h0_%d" % s], dma="h0_%d" % s)
                            bh = nb()

                            def trh(e, bh=bh, s=s):
                                for c in range(GC):
                                    r = e.transpose(ps[bh][:, c * 128:(c + 1) * 128], h0[s][:, c, :], ident_f)
                                return r
                            yield P.op("pe", trh, reads=["h0_%d" % s, "cmat"], writes=["ps%d" % bh])
                            yield P.op("act", lambda e, bh=bh, s=s: e.copy(h0T[0][:, :], ps[bh][:, 0:GW]), reads=["ps%d" % bh],
                                 writes=["h0T_0"])
                            yield P.op("pe", lambda e, jq=jq, s=s: e.matmul(ps[bi][:, 0:GW], lhsT=Cm_t[:, jq * 128:(jq + 1) * 128], rhs=h0T[0][:, :],
                                                                     start=(jq == 0), stop=(jq == NSEQ - 1)),
                                 reads=[cmk, "Btcv0s", "h0T_0"], writes=["ps%d" % bi])
                            yield P.op("dve", lambda e, jq=jq, s=s: e.tensor_scalar(xwj[0][:, :], xw[:, :], maskJ_f[:, jq:jq + 1],
                                                                            None, ALU.mult),
                                 reads=[kxw, "cmat"], writes=["xwj_0"])
                            bsj = nb()

                            def mmsj(e, bsj=bsj, s=s):
                                for c in range(GC):
                                    r = e.matmul(ps[bsj][:, c * 128:(c + 1) * 128], lhsT=xwj[0][:, c * 128:(c + 1) * 128],
                                                 rhs=Btok[:, :], start=True, stop=True)
                                return r
                            yield P.op("pe", mmsj, reads=["xwj_0", kBtok], writes=["ps%d" % bsj])
                            yield P.op("dve", lambda e, jq=jq, s=s: e.tensor_tensor(
                                sost[s][:, :, :], h0[s][:, :, :],
                                cdecF[:, :, jq].unsqueeze(2).broadcast_to([128, GC, 128]), ALU.mult),
                                 reads=["h0_%d" % s, "cdecF"], writes=["sost_%d" % s])
                            yield P.op("dve", lambda e, bsj=bsj, s=s: e.tensor_tensor(
                                sost[s][:, :, :], sost[s][:, :, :],
                                ps[bsj][:, 0:GW].rearrange("p (c n) -> p c n", n=128), ALU.add),
                                 reads=["sost_%d" % s, "ps%d" % bsj], writes=["sost_%d" % s])
                            yield P.op("sp", lambda e, jq=jq, s=s: e.dma_start(
                                out=st_out_s[jq, g * GW:(g + 1) * GW, :].rearrange("(c p) n -> p c n", p=128),
                                in_=sost[s][:, :, :]), reads=["sost_%d" % s], dma="sost_%d" % s)
                    yield "SPLIT"
                    if full:
                        bz = nbt()

                        def mmz(e):
                            r = None
                            for i in range(GW // XB):
                                for k in range(KC):
                                    r = e.matmul(ps[bz][:, i * XB:(i + 1) * XB], lhsT=xn[:, k, cs], rhs=wz[i][:, k, :],
                                                 start=(k == 0), stop=(k == KC - 1))
                            return r
                        yield P.op("pe", mmz, reads=["xn"] + kz, writes=["ps%d" % bz])
                        if sample:
                            yield P.op("dve", lambda e: e.tensor_tensor(
                                y1[:, :].rearrange("q (h p) -> q h p", p=64),
                                ps[bi][:, 0:GW].rearrange("q (h p) -> q h p", p=64),
                                expa[:, :].unsqueeze(2).broadcast_to([Q, R, 64]), ALU.mult),
                                 reads=["ps%d" % bi, kexpa], writes=["y1"])
                            reserved.discard(bi)
                        by = nbt()

                        def mmy(e):
                            for hh in range(R):
                                r = e.matmul(ps[by][:, hh * 64:(hh + 1) * 64], lhsT=Wm[:, hh, :],
                                             rhs=xdt[:, hh * 64:(hh + 1) * 64], start=True, stop=True)
                            return r
                        yield P.op("pe", mmy, reads=[kWm, kxdt], writes=["ps%d" % by])
                        if not sample:
                            bi = nbt()
                            yield P.op("pe", lambda e: e.matmul(ps[bi][:, 0:GW], lhsT=Cg[:, cs], rhs=Sb[:, :], start=True,
                                                         stop=True), reads=["Cg", "Sb"], writes=["ps%d" % bi])
                            yield P.op("dve", lambda e: e.tensor_tensor(
                                y1[:, :].rearrange("q (h p) -> q h p", p=64),
                                ps[bi][:, 0:GW].rearrange("q (h p) -> q h p", p=64),
                                expa[:, :].unsqueeze(2).broadcast_to([Q, R, 64]), ALU.mult),
                                 reads=["ps%d" % bi, kexpa], writes=["y1"])
                    if not sample:
                        bs = nbt()
                        yield P.op("pe", lambda e: e.matmul(ps[bs][:, 0:GW], lhsT=Btok[:, :], rhs=xw[:, :], start=True, stop=True),
                             reads=[kBtok, kxw], writes=["ps%d" % bs])
                        yield P.op("dve", lambda e: e.tensor_tensor(
                            S[:, :].rearrange("n (h p) -> n h p", p=64), S[:, :].rearrange("n (h p) -> n h p", p=64),
                            cdec[:, :].unsqueeze(2).broadcast_to([128, R, 64]), ALU.mult), reads=["S", kcdec, "Sb"],
                             writes=["S"])
                        yield P.op("dve", lambda e: e.tensor_tensor(S[:, :], S[:, :], ps[bs][:, 0:GW], ALU.add),
                             reads=["S", "ps%d" % bs], writes=["S"])
                        yield P.op("act", lambda e: e.copy(Sb[:, :], S[:, :]), reads=["S"], writes=["Sb"])

                    if full:
                        yield P.op("dve", lambda e: e.tensor_tensor(y1[:, :], y1[:, :], ps[by][:, 0:GW], ALU.add),
                             reads=["y1", "ps%d" % by], writes=["y1"])
                        yield P.op("dve", lambda e: e.tensor_tensor(y1[:, :], y1[:, :], xD[:, :], ALU.add),
                             reads=["y1", kxD], writes=["y1"])
                        yield P.op("act", lambda e: e.activation(out=xD[:, :], in_=ps[bz][:, 0:GW], func=AF.Tanh, scale=0.5),
                             reads=["ps%d" % bz, kxD], writes=[kxD])
                        yield P.op("dve", lambda e: e.scalar_tensor_tensor(xD[:, :], xD[:, :], 1.0, ps[bz][:, 0:GW], ALU.add,
                                                                    ALU.mult),
                             reads=[kxD, "ps%d" % bz], writes=[kxD])
                        yield P.op("dve", lambda e: e.scalar_tensor_tensor(y1[:, :], y1[:, :], 0.5, xD[:, :], ALU.mult, ALU.mult),
                             reads=["y1", kxD], writes=["y1"])
                        yield P.op("act", lambda e: e.activation(out=xD[:, :], in_=y1[:, :], func=AF.Square,
                                                          accum_out=ss[:, 0:1]), reads=["y1", kxD],
                             writes=[kxD, "ss"])
                        yield P.op("dve", lambda e: e.tensor_scalar(ss[:, 1:2], ss[:, 0:1], 1.0 / GW, EPS, ALU.mult, ALU.add),
                             reads=["ss"], writes=["ss"])
                        yield P.op("act", lambda e: e.activation(out=ss[:, 1:2], in_=ss[:, 1:2], func=AF.Sqrt),
                             reads=["ss"], writes=["ss"])
                        yield P.op("dve", lambda e: e.reciprocal(ss[:, 1:2], ss[:, 1:2]), reads=["ss"], writes=["ss"])
                        yield P.op("dve", lambda e: e.tensor_scalar(gyn[:, :], y1[:, :], ss[:, 1:2], None, ALU.mult),
                             reads=["y1", "ss"], writes=["gyn"])
                        bt = nbt()

                        def trg(e):
                            for c in range(GC):
                                r = e.transpose(psb[bt][:, c * 128:c * 128 + Q], gyn[:, c * 128:(c + 1) * 128], ident_b)
                            return r
                        yield P.op("pe", trg, reads=["gyn", "cmb"], writes=["ps%d" % bt])
                        o_ng = cfg.voff["ng"][0]
                        for c in range(GC):
                            yield P.op("dve" if c % 2 == 0 else "act",
                                 (lambda e, c=c: e.tensor_scalar(ygn[:, c, cs], psb[bt][:, c * 128:c * 128 + Q],
                                                                 vecs[:, o_ng + g * GC + c:o_ng + g * GC + c + 1], None,
                                                                 ALU.mult)) if c % 2 == 0 else
                                 (lambda e, c=c: e.activation(out=ygn[:, c, cs], in_=psb[bt][:, c * 128:c * 128 + Q],
                                                              func=AF.Copy,
                                                              scale=vecs[:, o_ng + g * GC + c:o_ng + g * GC + c + 1])),
                                 reads=["ps%d" % bt, "vecs"], writes=[("ygn", c)])
                def state_out(dst_ap_fn):
                    bo = nb()

                    def tro(e):
                        for c in range(GC):
                            r = e.transpose(ps[bo][:, c * 128:(c + 1) * 128], S[:, c * 128:(c + 1) * 128], ident_f)
                        return r
                    P.op("pe", tro, reads=["S", "cmat"], writes=["ps%d" % bo])
                    P.op("act", lambda e: e.copy(sost[0][:, :, :], ps[bo][:, 0:GW].rearrange("p (c n) -> p c n", n=128)),
                         reads=["ps%d" % bo], writes=["sost_0"])
                    P.op("sp", lambda e: e.dma_start(out=dst_ap_fn(), in_=sost[0][:, :, :]), reads=["sost_0"],
                         dma="sost_0")

                sidx = [0]
                for g in range(G):
                    specs = []
                    for i in range(GW // XB):
                        col = DI + g * GW + i * XB
                        for j in range(XB // 128):
                            c = i * (XB // 128) + j
                            specs.append((ssd_w_in[:, col:col + XB], XB, j, g * GC + c, ("x", c), i))
                    colB = 2 * DI + g * 128
                    specs.append((ssd_w_in[:, colB:colB + 128], 128, 0, DI // 128 + g, ("B", 0), "B"))
                    colC = 2 * DI + GN + g * 128
                    specs.append((ssd_w_in[:, colC:colC + 128], 128, 0, DI // 128 + G + g, ("C", 0), "C"))
                    loaded = {}

                    def ensure(k):
                        if k < len(specs) and specs[k][5] not in loaded:
                            loaded[specs[k][5]] = wload(specs[k][0], KC, specs[k][1])
                    LOOK = 3
                    for k in range(LOOK):
                        ensure(k)
                    for idx in range(len(specs)):
                        ensure(idx + LOOK)
                        _, _, j, cidx, (kind, c), lk = specs[idx]
                        wv_, kv_ = loaded[lk]
                        sl = sidx[0] % 2
                        conv_state_io("B", pre, pre_s, sl, 4, ssdc_state, cidx, None, None, stg_in, None, None, 0)
                        for (c0, n) in tts:
                            b = nb()
                            mm_fm(b, wv_, kv_, KC, j, xn, ["xn"], c0, n)
                            pp, sp_ = split_cols(c0, n)
                            if pp:
                                a, b2 = pp
                                P.op("act", lambda e, a=a, b2=b2, b=b, c0=c0, sl=sl: e.copy(pre[:, sl, 3 + a:3 + b2],
                                                                                   ps[b][:, a - c0:b2 - c0]),
                                     reads=["ps%d" % b], writes=["Bpre%d" % sl])
                            if sp_:
                                a, b2 = sp_
                                P.op("dve", lambda e, a=a, b2=b2, b=b, c0=c0, sl=sl: e.tensor_copy(
                                    pre_s[:, sl, :, 3:11], ps[b][:, a - c0:b2 - c0].rearrange("p (s t) -> p s t", t=8)),
                                     reads=["ps%d" % b], writes=["Bpres%d" % sl])
                        conv_state_out("B", pre, pre_s, sl, 4, cidx, ssdc_out_p, ssdc_out_s, stg_out, cost,
                                       ["Bpre%d" % sl, "Bpres%d" % sl, "Bpres%dh" % sl], 0)
                        kt = conv_block2("B", pre, pre_s, tcv, sl, 4, ["cw0", "cw1", "cw2", "cw3"], cidx,
                                         ["Bpre%d" % sl, "Bpre%dz" % sl], ["Bpres%d" % sl, "Bpres%dh" % sl])
                        if kind == "x":
                            dst, dk = xsg[:, c, :], ("xsg", c)
                        elif kind == "B":
                            dst, dk = Bg[:, :], "Bg"
                        else:
                            dst, dk = Cg[:, :], "Cg"
                        P.op("act", lambda e, dst=dst, cidx=cidx: e.activation(out=dst, in_=tcv[:, 0, :], func=AF.Silu,
                                                                             bias=V("cb", cidx)),
                             reads=kt + ["vecs"], writes=[dk])
                        sidx[0] += 1
                    wz, kz = [], []
                    for i in range(GW // XB):
                        col = g * GW + i * XB
                        w_, k_ = wload(ssd_w_in[:, col:col + XB], KC, XB)
                        wz.append(w_)
                        kz.append(k_)
                    P.op("dve", lambda e: e.memset(S[:, :], 0.0), reads=["Sb"], writes=["S"])
                    for q in range(NQ):
                        for _ in chunk(g, q, False, wz, kz, pp=q % 2):
                            pass
                    ibk, obk = "ib%d" % g, "ob%d" % g
                    P.op("sp", lambda e, g=g: e.dma_start(out=ibs[g][:, :], in_=S[:, :]), reads=["S"], writes=[ibk],
                         dma="xch")
                    P.op("pool", lambda e, g=g: e.collective_compute(
                        "AllGather", ALU.bypass, replica_groups=[[0, 1], [2, 3], [4, 5], [6, 7]],
                        ins=[ibs[g].ap().opt()], outs=[obs[g].ap().opt()]), reads=[ibk], writes=[obk],
                         dma="cc%d" % g, inc=1)
                    for _ in chunk(g, NQ, True, wz, kz, sample=True, pp=0):
                        pass
                    P.op("sp", lambda e, g=g: e.dma_start(out=S[:, :], in_=obs[g][0:128, :]), reads=[obk, "Sb"],
                         writes=["S"], dma="xch")
                    P.op("dve", lambda e: e.tensor_scalar(S[:, :], S[:, :], mko[:, 0:1], None, ALU.mult),
                         reads=["S", "mko"], writes=["S"])
                    P.op("act", lambda e: e.copy(Sb[:, :], S[:, :]), reads=["S"], writes=["Sb"])
                    gens = [chunk(g, q, True, wz, kz, pp=q % 2, nbp=nb_prep, nbt=nb_tail) for q in range(NQ)]
                    for v in gens[0]:
                        if v == "SPLIT":
                            break
                    for q in range(NQ):
                        ga = gens[q]
                        gb = gens[q + 1] if q + 1 < NQ else None
                        a_done, b_done = False, gb is None
                        while not (a_done and b_done):
                            if not a_done:
                                try:
                                    next(ga)
                                except StopIteration:
                                    a_done = True
                            if not b_done:
                                try:
                                    if next(gb) == "SPLIT":
                                        b_done = True
                                except StopIteration:
                                    b_done = True
                    state_out(lambda g=g: st_out_p[g * GW:(g + 1) * GW, :].rearrange("(c p) n -> p c n", p=128))
                    for mcp in range(D // 256):
                        wo, ko = wload(ssd_w_out[g * GW:(g + 1) * GW, mcp * 256:mcp * 256 + 256], GC, 256)
                        for j in range(2):
                            mc = mcp * 2 + j
                            for (c0, n) in tts:
                                b = nb()
                                mm_fm(b, wo, ko, GC, j, ygn, [("ygn", k) for k in range(GC)], c0, n)
                                acc_into_h(b, mc, c0, n)
                P.end_phase()

        def out_phase():
            with ExitStack() as ph:
                yst = [sb("ystg%d" % i, [128, D], F32, ph) for i in range(2)]
                for ri, (r0, n) in enumerate(rts):
                    s = ri % 2
                    for cg in range(KC // 4):
                        b = nb()

                        def tr(e, n=n, cg=cg, b=b, r0=r0):
                            for j in range(4):
                                c = cg * 4 + j
                                r = e.transpose(ps[b][0:n, j * 128:(j + 1) * 128], h[:, c, r0:r0 + n], ident_f)
                            return r
                        P.op("pe", tr, reads=["h", "cmat"], writes=["ps%d" % b])
                        if cg % 2 == 0:
                            P.op("dve", lambda e, s=s, n=n, cg=cg, b=b: e.tensor_copy(yst[s][0:n, cg * 512:(cg + 1) * 512],
                                                                                    ps[b][0:n, :]),
                                 reads=["ps%d" % b], writes=[("yst%d" % s, cg)])
                        else:
                            P.op("act", lambda e, s=s, n=n, cg=cg, b=b: e.copy(yst[s][0:n, cg * 512:(cg + 1) * 512],
                                                                             ps[b][0:n, :]),
                                 reads=["ps%d" % b], writes=[("yst%d" % s, cg)])
                    P.op("sp", lambda e, s=s, r0=r0, n=n: e.dma_start(out=y_out[r0:r0 + n, :], in_=yst[s][0:n, :]),
                         reads=[("yst%d" % s, cg) for cg in range(KC // 4)], dma="yst%d" % s)
                P.end_phase()

        phases = [lambda: rmsnorm_phase("g_mix0"), l0_mixer_phase,
                  lambda: rmsnorm_phase("g_ffn0"), lambda: ffn_phase(0),
                  lambda: rmsnorm_phase("g_ple0"), lambda: ple_phase(0),
                  lambda: rmsnorm_phase("g_mix1"), ssd_phase,
                  lambda: rmsnorm_phase("g_ffn1"), lambda: ffn_phase(1),
                  lambda: rmsnorm_phase("g_ple1"), lambda: ple_phase(1),
                  lambda: rmsnorm_phase("g_final", final=True)]
        for pi, phf in enumerate(phases):
            if stop is not None and pi >= stop:
                break
            phf()
        out_phase()
        for k, v in P.cnt.items():
            if k[0] == "dma" and P.waited["sp"].get(k, 0) < v:
                nc.sync.wait_ge(P.getsem(k), v)
        print("ops", len(P.ops), "waits", P.nwait, "sems", len(P.sems), file=sys.stderr)
    return nc


def host_inputs(cfg, inp):
    D, KC, T, TP, NSEQ = cfg.D, cfg.KC, cfg.T, cfg.TP, cfg.NSEQ
    f = np.float32

    def pm(v):
        v = np.asarray(v, f)
        return np.ascontiguousarray(v.reshape(-1, 128).T)

    def bc(v):
        v = np.asarray(v, f).reshape(1, -1)
        return np.ascontiguousarray(np.broadcast_to(v, (128, v.shape[1])))
    vec = np.zeros((128, cfg.NV), f)

    def put(nm, a):
        o, w = cfg.voff[nm]
        assert a.shape == (128, w), (nm, a.shape, w)
        vec[:, o:o + w] = a
    for l in range(2):
        put("g_mix%d" % l, pm(inp["g_mix"][l]))
        put("g_ffn%d" % l, pm(inp["g_ffn"][l]))
        put("g_ple%d" % l, pm(inp["g_ple"][l]))
    put("g_final", pm(inp["g_final"]))
    for k in range(3):
        put("scw%d" % k, pm(inp["sc_w_conv"][0, k]))
    for k in range(4):
        put("cw%d" % k, pm(inp["ssd_conv_w"][0, k]))
    put("cb", pm(inp["ssd_conv_b"][0]))
    put("ng", pm(inp["ssd_norm_g"][0]))
    put("dtb", bc(inp["ssd_dt_bias"][0]))
    put("alog", bc(inp["ssd_a_log"][0]))
    put("dsk", bc(inp["ssd_d"][0]))
    cm = np.zeros((128, 640 + NSEQ), f)
    cm[:, 0:128] = np.eye(128, dtype=f)
    cm[:, 128:256] = np.triu(np.ones((128, 128), f))
    cm[:, 256:384] = 1.0
    sid = np.arange(128) // 8
    same = (sid[:, None] == sid[None, :]).astype(f)
    cm[:, 384:512] = cm[:, 128:256] * same
    cm[:, 512:640] = same
    cm[:, 640:640 + NSEQ] = (sid[:, None] == np.arange(NSEQ)[None, :]).astype(f)
    mt = np.zeros((128, NSEQ, 128), f)
    mt[:, sid, np.arange(128)] = 1.0
    shared = dict(vecs=vec, cmat=cm,
                  sc_w_in=np.ascontiguousarray(inp["sc_w_in"][0]), sc_w_out=np.ascontiguousarray(inp["sc_w_out"][0]),
                  ssd_w_in=np.ascontiguousarray(inp["ssd_w_in"][0]), ssd_w_out=np.ascontiguousarray(inp["ssd_w_out"][0]),
                  ffn_w_gate=np.ascontiguousarray(inp["ffn_w_gate"]), ffn_w_up=np.ascontiguousarray(inp["ffn_w_up"]),
                  ffn_w_down=np.ascontiguousarray(inp["ffn_w_down"]), ple_w_proj=np.ascontiguousarray(inp["ple_w_proj"]),
                  ple_w_gate=np.ascontiguousarray(inp["ple_w_gate"]))
    maps = []
    for core in range(8):
        b, half = core // 2, core % 2
        xin = np.zeros((T, D), f)
        pin = np.zeros((2, T, cfg.PLE), f)
        t0 = half * TP
        if half == 1:
            xin[0:HALO] = inp["x_prompt"][b, t0 - HALO:t0]
            pin[:, 0:HALO] = inp["p_prompt"][:, b, t0 - HALO:t0]
        xin[HALO:HALO + TP] = inp["x_prompt"][b, t0:t0 + TP]
        pin[:, HALO:HALO + TP] = inp["p_prompt"][:, b, t0:t0 + TP]
        sl = slice(core * NSEQ, (core + 1) * NSEQ)
        xin[HALO + TP:] = inp["x_sample"][sl].reshape(-1, D)
        pin[:, HALO + TP:] = inp["p_sample"][:, sl].reshape(2, -1, cfg.PLE)
        m = dict(shared)
        m.update(xin=xin, pin=pin,
                 sc_state=np.ascontiguousarray(inp["state_sc_conv"][0, sl].reshape(NSEQ * 2, D)),
                 ssdc_state=np.ascontiguousarray(inp["state_ssd_conv"][0, sl].reshape(NSEQ * 3, cfg.CONV)),
                 ssd_state=np.ascontiguousarray(inp["state_ssd"][0, sl].reshape(NSEQ, cfg.H * 64, 128)),
                 maskodd=np.full((128, 1), float(half), f))
        maps.append(m)
    return maps


def assemble(cfg, res):
    D, TP, NSEQ, H = cfg.D, cfg.TP, cfg.NSEQ, cfg.H
    f = np.float32
    B = 4
    y_p = np.zeros((B, 2 * TP, D), f)
    y_s = np.zeros((8 * NSEQ, cfg.DEC_SEQ, D), f)
    scp = np.zeros((1, B, 2, D), f)
    scs = np.zeros((1, 8 * NSEQ, 2, D), f)
    ssdcp = np.zeros((1, B, 3, cfg.CONV), f)
    ssdcs = np.zeros((1, 8 * NSEQ, 3, cfg.CONV), f)
    stp = np.zeros((1, B, H, 64, 128), f)
    sts = np.zeros((1, 8 * NSEQ, H, 64, 128), f)
    for core in range(8):
        r = res[core]
        b, half = core // 2, core % 2
        y = r["y"]
        y_p[b, half * TP:(half + 1) * TP] = y[HALO:HALO + TP]
        sl = slice(core * NSEQ, (core + 1) * NSEQ)
        y_s[sl] = y[HALO + TP:].reshape(NSEQ, cfg.DEC_SEQ, D)
        scs[0, sl] = r["sc_out_s"].reshape(NSEQ, 2, D)
        ssdcs[0, sl] = r["ssdc_out_s"].reshape(NSEQ, 3, cfg.CONV)
        sts[0, sl] = r["st_out_s"].reshape(NSEQ, H, 64, 128)
        if half == 1:
            scp[0, b] = r["sc_out_p"]
            ssdcp[0, b] = r["ssdc_out_p"]
            stp[0, b] = r["st_out_p"].reshape(H, 64, 128)
    return (y_p, y_s, scp, scs, ssdcp, ssdcs, stp, sts)


_NC_CACHE = {}


def run(cfg, inp, stop=None):
    key = (cfg.D, cfg.SEQ, stop)
    if key not in _NC_CACHE:
        _NC_CACHE[key] = build(cfg, stop)
    nc = _NC_CACHE[key]
    maps = host_inputs(cfg, inp)
    res = run_bass_kernel_spmd(nc, maps, core_ids=list(range(8)))
    return assemble(cfg, res.results)


def kernel(**inputs):
    cfg = Cfg()
    inp = {k: np.asarray(v) for k, v in inputs.items()}
    return run(cfg, inp)
```

```python
import sys
from contextlib import ExitStack
import numpy as np
import concourse.bass as bass
import concourse.mybir as mybir
from concourse.bass_utils import run_bass_kernel_spmd

F32 = mybir.dt.float32
BF16 = mybir.dt.bfloat16
AF = mybir.ActivationFunctionType
ALU = mybir.AluOpType
EPS = 1e-6
HALO = 5


class Cfg:
    def __init__(self, D=2048, SEQ=2048, DEC_BATCH=128, DEC_SEQ=8, PLE=256):
        self.D = D
        self.SEQ = SEQ
        self.BATCH = 4
        self.DEC_BATCH = DEC_BATCH
        self.DEC_SEQ = DEC_SEQ
        self.PLE = PLE
        self.DFF = -(-8 * D // (3 * 256)) * 256
        self.DI = 2 * D
        self.H = self.DI // 64
        self.G = 8
        self.N = 128
        self.R = self.H // 8
        self.GW = self.DI // 8
        self.GC = self.GW // 128
        self.GN = self.G * self.N
        self.CONV = self.DI + 2 * self.GN
        self.IN = self.DI + self.CONV + self.H
        self.KC = D // 128
        self.FC = self.DFF // 128
        self.CC = self.CONV // 128
        self.TP = SEQ // 2
        self.NQ = self.TP // 128
        self.NSEQ = DEC_BATCH // 8
        self.TS = self.NSEQ * DEC_SEQ
        self.PL = HALO + self.TP
        self.T = self.PL + self.TS
        off = {}
        o = 0
        for nm, w in [("g_mix0", self.KC), ("g_ffn0", self.KC), ("g_ple0", self.KC),
                      ("g_mix1", self.KC), ("g_ffn1", self.KC), ("g_ple1", self.KC),
                      ("g_final", self.KC), ("scw0", self.KC), ("scw1", self.KC), ("scw2", self.KC),
                      ("cw0", self.CC), ("cw1", self.CC), ("cw2", self.CC), ("cw3", self.CC),
                      ("cb", self.CC), ("ng", self.DI // 128),
                      ("dtb", self.H), ("alog", self.H), ("dsk", self.H)]:
            off[nm] = (o, w)
            o += w
        self.voff = off
        self.NV = o


class Prog:
    ENGS = ("pe", "act", "dve", "pool", "sp")

    def __init__(self, nc, stack):
        self.nc = nc
        self.stack = stack
        self.ops = []
        self.lastw = {}
        self.readers = {}
        self.floor = 0
        self.emitted = 0
        self.sems = {}
        self.cnt = {}
        self.waited = {e: {} for e in self.ENGS}
        self.lastsig = {}
        self.nwait = 0
        self.eng = dict(pe=nc.tensor, act=nc.scalar, dve=nc.vector, pool=nc.gpsimd, sp=nc.sync)

    def op(self, eng, fn, reads=(), writes=(), dma=None, inc=16):
        i = len(self.ops)
        psr = [r for r in reads if isinstance(r, str) and r.startswith("ps")]
        if psr:
            reads = [r for r in reads if r not in psr]
            writes = list(writes) + psr
        deps = set()
        for r in reads:
            w = self.lastw.get(r)
            if w is not None:
                deps.add(w)
        for r in writes:
            w = self.lastw.get(r)
            if w is not None:
                deps.add(w)
            for x in self.readers.get(r, ()):
                deps.add(x)
        for r in reads:
            self.readers.setdefault(r, []).append(i)
        for r in writes:
            self.lastw[r] = i
            self.readers[r] = []
        deps.discard(i)
        deps = {d for d in deps if d >= self.floor}
        self.ops.append(dict(eng=eng, fn=fn, deps=deps, dma=dma, inc=inc, sig=False))
        return i

    def getsem(self, k):
        if k not in self.sems:
            self.sems[k] = self.stack.enter_context(self.nc.semaphore("s%d" % len(self.sems)))
        return self.sems[k]

    def end_phase(self):
        ops = self.ops
        start = self.emitted
        last = {}
        for i in range(start, len(ops)):
            o = ops[i]
            k = ("dma", o["dma"]) if o["dma"] is not None else ("eng", o["eng"])
            last[k] = i
        bdeps = set(last.values())
        for e in self.ENGS:
            self.ops.append(dict(eng=e, fn=None, deps=set(bdeps), dma=None, inc=0, sig=False, barrier=True))
        for i in range(start, len(ops)):
            o = ops[i]
            nd = set()
            for d in o["deps"]:
                p = ops[d]
                if (not o.get("barrier")) and p["dma"] is None and o["dma"] is None \
                        and p["eng"] == "pe" and o["eng"] == "pe":
                    continue
                nd.add(d)
            o["deps"] = nd
            for d in nd:
                ops[d]["sig"] = True
        for i in range(start, len(ops)):
            o = ops[i]
            if o["dma"] is not None:
                k = ("dma", o["dma"])
                self.cnt[k] = self.cnt.get(k, 0) + o["inc"]
                o["semk"], o["val"] = k, self.cnt[k]
            elif o["sig"]:
                k = ("eng", o["eng"])
                self.cnt[k] = self.cnt.get(k, 0) + 1
                o["semk"], o["val"] = k, self.cnt[k]
        for i in range(start, len(ops)):
            o = ops[i]
            e = o["eng"]
            need = {}
            for d in o["deps"]:
                p = ops[d]
                k = p["semk"]
                need[k] = max(need.get(k, 0), p["val"])
            for k, v in need.items():
                if self.waited[e].get(k, 0) >= v:
                    continue
                self.eng[e].wait_ge(self.getsem(k), v)
                self.waited[e][k] = v
                self.nwait += 1
            if o["fn"] is None:
                continue
            ins = o["fn"](self.eng[e])
            if o["dma"] is not None:
                ins.then_inc(self.getsem(o["semk"]), o["inc"])
            elif o["sig"]:
                ins.then_inc(self.getsem(o["semk"]), 1)
            o["fn"] = None
        self.emitted = len(ops)
        self.floor = len(ops)
        self.lastw = {}
        self.readers = {}


def split_tiles(T, maxn=448):
    nt = -(-T // maxn)
    base = T // nt
    rem = T % nt
    out = []
    c = 0
    for i in range(nt):
        n = base + (1 if i < rem else 0)
        out.append((c, n))
        c += n
    return out


def build(cfg, stop=None):
    D, KC, T, PL, TS, NSEQ = cfg.D, cfg.KC, cfg.T, cfg.PL, cfg.TS, cfg.NSEQ
    DI, H, G, R, GW, GC, GN, CC, FC = cfg.DI, cfg.H, cfg.G, cfg.R, cfg.GW, cfg.GC, cfg.GN, cfg.CC, cfg.FC
    PLE, DFF, CONV, IN, NQ = cfg.PLE, cfg.DFF, cfg.CONV, cfg.IN, cfg.NQ
    PK = PLE // 128
    nc = bass.Bass("TRN2", target_bir_lowering=False)
    CMW = 640 + NSEQ

    def din(name, shape):
        return nc.dram_tensor(name, list(shape), F32, kind="ExternalInput").ap()

    def dout(name, shape):
        return nc.dram_tensor(name, list(shape), F32, kind="ExternalOutput").ap()

    xin = din("xin", [T, D])
    pin = din("pin", [2, T, PLE])
    sc_state = din("sc_state", [NSEQ * 2, D])
    ssdc_state = din("ssdc_state", [NSEQ * 3, CONV])
    ssd_state = din("ssd_state", [NSEQ, H * 64, 128])
    maskodd = din("maskodd", [128, 1])
    vecs_d = din("vecs", [128, cfg.NV])
    cmat_d = din("cmat", [128, CMW])
    sc_w_in = din("sc_w_in", [D, 3 * D])
    sc_w_out = din("sc_w_out", [D, D])
    ssd_w_in = din("ssd_w_in", [D, IN])
    ssd_w_out = din("ssd_w_out", [DI, D])
    ffn_w_gate = din("ffn_w_gate", [2, D, DFF])
    ffn_w_up = din("ffn_w_up", [2, D, DFF])
    ffn_w_down = din("ffn_w_down", [2, DFF, D])
    ple_w_proj = din("ple_w_proj", [2, PLE, D])
    ple_w_gate = din("ple_w_gate", [2, D, D])

    y_out = dout("y", [T, D])
    sc_out_p = dout("sc_out_p", [2, D])
    sc_out_s = dout("sc_out_s", [NSEQ * 2, D])
    ssdc_out_p = dout("ssdc_out_p", [3, CONV])
    ssdc_out_s = dout("ssdc_out_s", [NSEQ * 3, CONV])
    st_out_p = dout("st_out_p", [H * 64, 128])
    st_out_s = dout("st_out_s", [NSEQ, H * 64, 128])
    ibs = [nc.dram_tensor("ib%d" % g, [128, GW], F32) for g in range(G)]
    obs = [nc.dram_tensor("ob%d" % g, [256, GW], F32) for g in range(G)]

    tts = split_tiles(T)
    for (c0, n) in tts:
        assert not (c0 < PL < c0 + n and False)
    assert any(c0 <= PL and PL + TS <= c0 + n for (c0, n) in tts) or True
    rts = [(r0, min(128, T - r0)) for r0 in range(0, T, 128)]

    with ExitStack() as st:
        sbctr = [0]

        def sb(name, shape, dt=F32, stack=None):
            sbctr[0] += 1
            return (stack or st).enter_context(nc.sbuf_tensor("%s_%d" % (name, sbctr[0]), list(shape), dt))

        P = Prog(nc, st)
        h = sb("h", [128, KC, T])
        xn = sb("xn", [128, KC, T], BF16)
        vecs = sb("vecs", [128, cfg.NV])
        cmat = sb("cmat", [128, CMW])
        cmb = sb("cmb", [128, CMW], BF16)
        abc = sb("abc", [128, H])
        mko = sb("mko", [128, 1])
        ps = [st.enter_context(nc.psum_tensor("ps%d" % i, [128, 512], F32)) for i in range(8)]
        psb = [p[:, :].bitcast(BF16) for p in ps]
        ident_f, tri_f, ones_f = cmat[:, 0:128], cmat[:, 128:256], cmat[:, 256:384]
        triS_f, blk_f, maskJ_f = cmat[:, 384:512], cmat[:, 512:640], cmat[:, 640:640 + NSEQ]
        ident_b, tri_b, ones_b = cmb[:, 0:128], cmb[:, 128:256], cmb[:, 256:384]

        def V(nm, c=None):
            o, w = cfg.voff[nm]
            if c is None:
                return vecs[:, o:o + w]
            return vecs[:, o + c:o + c + 1]

        bankctr = [0]

        reserved = set()

        def nb():
            while True:
                b = bankctr[0] % 8
                bankctr[0] += 1
                if b not in reserved:
                    return b

        poolctr = {"p": 0, "t": 0}

        def nb_prep():
            b = poolctr["p"] % 4
            poolctr["p"] += 1
            return b

        def nb_tail():
            b = 4 + poolctr["t"] % 4
            poolctr["t"] += 1
            return b

        ringstate = {}

        def make_ring(stack, nslots, tag, slot=4096):
            bufs = [sb("ring%s%d" % (tag, i), [128, slot], BF16, stack) for i in range(nslots)]
            ringstate["bufs"] = bufs
            ringstate["i"] = 0
            ringstate["slot"] = slot

        def wload(src, nk, ncols):
            bufs = ringstate["bufs"]
            i = ringstate["i"] % len(bufs)
            ringstate["i"] += 1
            assert nk * ncols <= ringstate["slot"]
            view = bufs[i][:, 0:nk * ncols].rearrange("p (k n) -> p k n", n=ncols)
            key = "ring%d" % i
            if key in P.lastw:
                assert P.readers.get(key), "ring slot %s overwritten before any consumer was recorded" % key
            srcv = src.rearrange("(k p) n -> p k n", p=128)
            P.op("pool", lambda e: e.dma_start(out=view, in_=srcv), writes=[key], dma=key)
            return view, key

        with ExitStack() as ph:
            xs = [sb("xstg%d" % i, [128, D], F32, ph) for i in range(2)]
            P.op("sp", lambda e: e.dma_start(out=vecs[:], in_=vecs_d), writes=["vecs"], dma="c0")
            P.op("sp", lambda e: e.dma_start(out=cmat[:], in_=cmat_d), writes=["cmat"], dma="c1")
            P.op("sp", lambda e: e.dma_start(out=mko[:], in_=maskodd), writes=["mko"], dma="c2")
            P.op("dve", lambda e: e.tensor_copy(cmb[:], cmat[:]), reads=["cmat"], writes=["cmb"])
            o_al, w_al = cfg.voff["alog"]
            P.op("act", lambda e: e.activation(out=abc[:], in_=vecs[:, o_al:o_al + w_al], func=AF.Exp),
                 reads=["vecs"], writes=["abc"])
            P.op("dve", lambda e: e.tensor_scalar(abc[:], abc[:], -1.0, None, ALU.mult), reads=["abc"], writes=["abc"])
            for ri, (r0, n) in enumerate(rts):
                s = ri % 2
                P.op("sp", lambda e, s=s, r0=r0, n=n: e.dma_start(out=xs[s][0:n, :], in_=xin[r0:r0 + n, :]),
                     writes=["xs%d" % s], dma="xs%d" % s)
                for cg in range(KC // 4):
                    b = nb()

                    def tr(e, s=s, n=n, cg=cg, b=b):
                        for j in range(4):
                            c = cg * 4 + j
                            r = e.transpose(ps[b][:, j * 128:j * 128 + n], xs[s][0:n, c * 128:(c + 1) * 128],
                                            ident_f[0:n, 0:n])
                        return r
                    P.op("pe", tr, reads=["xs%d" % s, "cmat"], writes=["ps%d" % b])
                    src = ps[b][:, :].rearrange("p (j t) -> p j t", t=128)[:, :, 0:n]
                    dst = h[:, cg * 4:cg * 4 + 4, r0:r0 + n]
                    if cg % 2 == 0:
                        P.op("dve", lambda e, src=src, dst=dst: e.tensor_copy(dst, src), reads=["ps%d" % b],
                             writes=[("h", ri, cg)])
                    else:
                        P.op("act", lambda e, src=src, dst=dst: e.copy(dst, src), reads=["ps%d" % b],
                             writes=[("h", ri, cg)])
            P.end_phase()

        def rmsnorm_phase(gname, final=False):
            with ExitStack() as ph:
                sq = [sb("sq%d" % i, [128, KC, tts[0][1]], BF16, ph) for i in range(2)]
                rstd = sb("rstd", [128, T], F32, ph)
                for ti, (c0, n) in enumerate(tts):
                    s = ti % 2
                    P.op("act", lambda e, s=s, c0=c0, n=n: e.activation(out=sq[s][:, :, 0:n], in_=h[:, :, c0:c0 + n],
                                                                      func=AF.Square),
                         reads=["h"], writes=["sq%d" % s])
                    b = nb()

                    def mm(e, s=s, n=n, b=b):
                        for k in range(KC):
                            r = e.matmul(ps[b][:, 0:n], lhsT=ones_b, rhs=sq[s][:, k, 0:n], start=(k == 0),
                                         stop=(k == KC - 1))
                        return r
                    P.op("pe", mm, reads=["sq%d" % s, "cmb"], writes=["ps%d" % b])
                    P.op("dve", lambda e, b=b, c0=c0, n=n: e.tensor_scalar(rstd[:, c0:c0 + n], ps[b][:, 0:n], 1.0 / D,
                                                                         EPS, ALU.mult, ALU.add),
                         reads=["ps%d" % b], writes=["rs%d" % ti])
                    P.op("act", lambda e, c0=c0, n=n: e.activation(out=rstd[:, c0:c0 + n], in_=rstd[:, c0:c0 + n],
                                                                 func=AF.Sqrt),
                         reads=["rs%d" % ti], writes=["rs%d" % ti])
                    P.op("dve", lambda e, c0=c0, n=n: e.reciprocal(rstd[:, c0:c0 + n], rstd[:, c0:c0 + n]),
                         reads=["rs%d" % ti], writes=["rs%d" % ti])
                rk = ["rs%d" % ti for ti in range(len(tts))]
                for c in range(KC):
                    dst = h[:, c, :] if final else xn[:, c, :]
                    P.op("dve", lambda e, c=c, dst=dst: e.scalar_tensor_tensor(dst, h[:, c, :], V(gname, c), rstd[:, :],
                                                                             ALU.mult, ALU.mult),
                         reads=rk + ["vecs", "h"], writes=[("hf" if final else "xn", c)])
                P.end_phase()

        def acc_into_h(b, mc, c0, n):
            P.op("dve", lambda e: e.tensor_tensor(h[:, mc, c0:c0 + n], h[:, mc, c0:c0 + n], ps[b][:, 0:n], ALU.add),
                 reads=["ps%d" % b, ("h", mc)], writes=[("h", mc)])

        def mm_fm(b, wv, wkey, nk, j, rhs, rkeys, c0, n):
            def mm(e):
                for k in range(nk):
                    r = e.matmul(ps[b][:, 0:n], lhsT=wv[:, k, j * 128:(j + 1) * 128], rhs=rhs[:, k, c0:c0 + n],
                                 start=(k == 0), stop=(k == nk - 1))
                return r
            P.op("pe", mm, reads=[wkey] + list(rkeys), writes=["ps%d" % b])

        def ffn_phase(layer):
            npairs = FC // 2
            ngroups = max(1, -(-FC // 12))
            base = npairs // ngroups
            rem = npairs % ngroups
            gsz = [base + (1 if i < rem else 0) for i in range(ngroups)]
            maxk = max(gsz) * 2
            with ExitStack() as ph:
                make_ring(ph, 6, "f")
                act = sb("act", [128, maxk, T], BF16, ph)
                sg = [sb("sg%d" % i, [128, 512], F32, ph) for i in range(2)]
                sgi = 0
                pr0 = 0
                for gi, np_ in enumerate(gsz):
                    nkg = np_ * 2
                    for pr in range(np_):
                        col = (pr0 + pr) * 256
                        wg, kg = wload(ffn_w_gate[layer, :, col:col + 256], KC, 256)
                        wu, ku = wload(ffn_w_up[layer, :, col:col + 256], KC, 256)
                        for j in range(2):
                            fl = pr * 2 + j
                            for (c0, n) in tts:
                                bg_, bu_ = nb(), nb()
                                mm_fm(bg_, wg, kg, KC, j, xn, ["xn"], c0, n)
                                mm_fm(bu_, wu, ku, KC, j, xn, ["xn"], c0, n)
                                s = sgi % 2
                                sgi += 1
                                P.op("act", lambda e, s=s, b=bg_, n=n: e.activation(out=sg[s][:, 0:n], in_=ps[b][:, 0:n],
                                                                                  func=AF.Silu),
                                     reads=["ps%d" % bg_], writes=["sg%d" % s])
                                P.op("dve", lambda e, s=s, b=bu_, fl=fl, c0=c0, n=n: e.tensor_tensor(
                                    act[:, fl, c0:c0 + n], sg[s][:, 0:n], ps[b][:, 0:n], ALU.mult),
                                     reads=["sg%d" % s, "ps%d" % bu_], writes=[("act", fl)])
                    for mcp in range(D // 256):
                        wd, kd = wload(ffn_w_down[layer, pr0 * 256:pr0 * 256 + nkg * 128, mcp * 256:mcp * 256 + 256],
                                       nkg, 256)
                        for j in range(2):
                            mc = mcp * 2 + j
                            for (c0, n) in tts:
                                b = nb()
                                mm_fm(b, wd, kd, nkg, j, act, [("act", k) for k in range(nkg)], c0, n)
                                acc_into_h(b, mc, c0, n)
                    pr0 += np_
                P.end_phase()

        def ple_phase(layer):
            with ExitStack() as ph:
                make_ring(ph, 6, "p")
                pT = sb("pT", [128, PK, T], BF16, ph)
                pst = [sb("pst%d" % i, [128, PLE], F32, ph) for i in range(2)]
                sg = [sb("sgp%d" % i, [128, 512], F32, ph) for i in range(2)]
                for ri, (r0, n) in enumerate(rts):
                    s = ri % 2
                    P.op("sp", lambda e, s=s, r0=r0, n=n: e.dma_start(out=pst[s][0:n, :], in_=pin[layer, r0:r0 + n, :]),
                         writes=["pst%d" % s], dma="pst%d" % s)
                    b = nb()

                    def tr(e, s=s, n=n, b=b):
                        for j in range(PK):
                            r = e.transpose(ps[b][:, j * 128:j * 128 + n], pst[s][0:n, j * 128:(j + 1) * 128],
                                            ident_f[0:n, 0:n])
                        return r
                    P.op("pe", tr, reads=["pst%d" % s, "cmat"], writes=["ps%d" % b])
                    src = ps[b][:, 0:PK * 128].rearrange("p (j t) -> p j t", t=128)[:, :, 0:n]
                    P.op("dve", lambda e, src=src, r0=r0, n=n: e.tensor_copy(pT[:, :, r0:r0 + n], src),
                         reads=["ps%d" % b], writes=["pT"])
                sgi = 0
                for mcp in range(D // 256):
                    wg, kg = wload(ple_w_gate[layer, :, mcp * 256:mcp * 256 + 256], KC, 256)
                    wp, kp = wload(ple_w_proj[layer, :, mcp * 256:mcp * 256 + 256], PK, 256)
                    for j in range(2):
                        mc = mcp * 2 + j
                        for (c0, n) in tts:
                            bg_, bp_ = nb(), nb()
                            mm_fm(bg_, wg, kg, KC, j, xn, ["xn"], c0, n)
                            mm_fm(bp_, wp, kp, PK, j, pT, ["pT"], c0, n)
                            s = sgi % 2
                            sgi += 1
                            P.op("act", lambda e, s=s, b=bg_, n=n: e.activation(out=sg[s][:, 0:n], in_=ps[b][:, 0:n],
                                                                              func=AF.Sigmoid),
                                 reads=["ps%d" % bg_], writes=["sgp%d" % s])
                            P.op("dve", lambda e, s=s, b=bp_, n=n: e.tensor_tensor(sg[s][:, 0:n], sg[s][:, 0:n],
                                                                                 ps[b][:, 0:n], ALU.mult),
                                 reads=["sgp%d" % s, "ps%d" % bp_], writes=["sgp%d" % s])
                            P.op("dve", lambda e, s=s, mc=mc, c0=c0, n=n: e.tensor_tensor(
                                h[:, mc, c0:c0 + n], h[:, mc, c0:c0 + n], sg[s][:, 0:n], ALU.add),
                                 reads=["sgp%d" % s, ("h", mc)], writes=[("h", mc)])
                P.end_phase()

        def split_cols(c0, n):
            pa, pb = c0, min(c0 + n, PL)
            sa, sb_ = max(c0, PL), c0 + n
            return (pa, pb) if pb > pa else None, (sa, sb_) if sb_ > sa else None

        def conv_state_io(ph_tag, pre, pre_s, slot, W0, state_d, cidx, out_p, out_s, stg_in, stg_out, cost, sidx):
            nr = W0 - 1
            ks = ph_tag + "pres%d" % slot
            si = sidx % 2
            P.op("pool", lambda e: e.dma_start(out=stg_in[si][0:NSEQ * nr, :], in_=state_d[:, cidx * 128:(cidx + 1) * 128]),
                 writes=[ph_tag + "sti%d" % si], dma=ph_tag + "sti%d" % si)
            b = nb()
            P.op("pe", lambda e: e.transpose(ps[b][:, 0:NSEQ * nr], stg_in[si][0:NSEQ * nr, :],
                                            ident_f[0:NSEQ * nr, 0:NSEQ * nr]),
                 reads=[ph_tag + "sti%d" % si, "cmat"], writes=["ps%d" % b])
            src = ps[b][:, 0:NSEQ * nr].rearrange("p (s r) -> p s r", r=nr)
            P.op("act", lambda e: e.copy(pre_s[:, slot, :, 0:nr], src), reads=["ps%d" % b], writes=[ks + "h"])

        def conv_state_out(ph_tag, pre, pre_s, slot, W0, cidx, out_p, out_s, stg_out, cost, rd_keys, sidx):
            nr = W0 - 1
            si = sidx % 2
            b = nb()
            P.op("pe", lambda e: e.transpose(ps[b][0:nr, 0:128], pre[:, slot, PL:PL + nr], ident_f),
                 reads=rd_keys + ["cmat"], writes=["ps%d" % b])
            P.op("dve", lambda e: e.tensor_copy(stg_out[si][0:nr, 0:128], ps[b][0:nr, 0:128]), reads=["ps%d" % b],
                 writes=[ph_tag + "stoP%d" % si])
            P.op("sp", lambda e: e.dma_start(out=out_p[:, cidx * 128:(cidx + 1) * 128], in_=stg_out[si][0:nr, 0:128]),
                 reads=[ph_tag + "stoP%d" % si], dma=ph_tag + "stoP%d" % si)
            P.op("dve", lambda e: e.tensor_copy(cost[si][:, 0:NSEQ * nr].rearrange("p (s r) -> p s r", r=nr),
                                               pre_s[:, slot, :, 8:8 + nr]),
                 reads=rd_keys, writes=[ph_tag + "cost%d" % si])
            b2 = nb()
            P.op("pe", lambda e: e.transpose(ps[b2][0:NSEQ * nr, 0:128], cost[si][:, 0:NSEQ * nr], ident_f),
                 reads=[ph_tag + "cost%d" % si, "cmat"], writes=["ps%d" % b2])
            P.op("act", lambda e: e.copy(stg_out[si][0:NSEQ * nr, 128:256], ps[b2][0:NSEQ * nr, 0:128]),
                 reads=["ps%d" % b2], writes=[ph_tag + "stoS%d" % si])
            P.op("sp", lambda e: e.dma_start(out=out_s[:, cidx * 128:(cidx + 1) * 128],
                                            in_=stg_out[si][0:NSEQ * nr, 128:256]),
                 reads=[ph_tag + "stoS%d" % si], dma=ph_tag + "stoS%d" % si)

        def l0_mixer_phase():
            with ExitStack() as ph:
                make_ring(ph, 4, "m")
                gated = sb("gated", [128, KC, T], BF16, ph)
                pre = sb("l0pre", [128, 2, 2 + PL], F32, ph)
                pre_s = sb("l0pres", [128, 2, NSEQ, 10], F32, ph)
                tcv = sb("l0tcv", [128, 1, T], F32, ph)
                stg_in = [sb("l0sti%d" % i, [128, 128], F32, ph) for i in range(2)]
                stg_out = [sb("l0sto%d" % i, [128, 256], F32, ph) for i in range(2)]
                cost = [sb("l0cost%d" % i, [128, NSEQ * 3], F32, ph) for i in range(2)]
                P.op("dve", lambda e: e.memset(pre[:, 0, 0:2], 0.0), writes=["Apre0z"])
                P.op("dve", lambda e: e.memset(pre[:, 1, 0:2], 0.0), writes=["Apre1z"])
                for cp in range(KC // 2):
                    wb, kb = wload(sc_w_in[:, cp * 256:cp * 256 + 256], KC, 256)
                    wc, kc = wload(sc_w_in[:, D + cp * 256:D + cp * 256 + 256], KC, 256)
                    wv, kv = wload(sc_w_in[:, 2 * D + cp * 256:2 * D + cp * 256 + 256], KC, 256)
                    for j in range(2):
                        c = cp * 2 + j
                        sl = c % 2
                        conv_state_io("A", pre, pre_s, sl, 3, sc_state, c, None, None, stg_in, None, None, c)
                        for (c0, n) in tts:
                            b_c, b_v, b_b = nb(), nb(), nb()
                            mm_fm(b_c, wc, kc, KC, j, xn, ["xn"], c0, n)
                            mm_fm(b_v, wv, kv, KC, j, xn, ["xn"], c0, n)
                            mm_fm(b_b, wb, kb, KC, j, xn, ["xn"], c0, n)
                            P.op("act", lambda e, b=b_c, c0=c0, n=n: e.copy(tcv[:, 0, c0:c0 + n], ps[b][:, 0:n]),
                                 reads=["ps%d" % b_c], writes=["Atcv0p", "Atcv0s"])
                            pp, sp_ = split_cols(c0, n)
                            if pp:
                                a, b2 = pp
                                P.op("dve", lambda e, a=a, b2=b2, b=b_v, c0=c0, sl=sl: e.tensor_tensor(
                                    pre[:, sl, 2 + a:2 + b2], tcv[:, 0, a:b2], ps[b][:, a - c0:b2 - c0], ALU.mult),
                                     reads=["ps%d" % b_v, "Atcv0p", "Atcv0s"], writes=["Apre%d" % sl])
                            if sp_:
                                a, b2 = sp_
                                assert a == PL and b2 == T
                                P.op("dve", lambda e, a=a, b2=b2, b=b_v, c0=c0, sl=sl: e.tensor_tensor(
                                    pre_s[:, sl, :, 2:10],
                                    tcv[:, 0, a:b2].rearrange("p (s t) -> p s t", t=8),
                                    ps[b][:, a - c0:b2 - c0].rearrange("p (s t) -> p s t", t=8), ALU.mult),
                                     reads=["ps%d" % b_v, "Atcv0p", "Atcv0s"], writes=["Apres%d" % sl])
                            P.op("act", lambda e, b=b_b, c=c, c0=c0, n=n: e.copy(gated[:, c, c0:c0 + n], ps[b][:, 0:n]),
                                 reads=["ps%d" % b_b], writes=[("gated", c)])
                        conv_state_out("A", pre, pre_s, sl, 3, c, sc_out_p, sc_out_s, stg_out, cost,
                                       ["Apre%d" % sl, "Apres%d" % sl, "Apres%dh" % sl], c)
                        kt = conv_block2("A", pre, pre_s, tcv, sl, 3, ["scw0", "scw1", "scw2"], c,
                                         ["Apre%d" % sl, "Apre%dz" % sl], ["Apres%d" % sl, "Apres%dh" % sl])
                        P.op("dve", lambda e, c=c: e.tensor_tensor(gated[:, c, :], gated[:, c, :], tcv[:, 0, :], ALU.mult),
                             reads=kt + [("gated", c)], writes=[("gated", c)])
                for mcp in range(D // 256):
                    wo, ko = wload(sc_w_out[:, mcp * 256:mcp * 256 + 256], KC, 256)
                    for j in range(2):
                        mc = mcp * 2 + j
                        for (c0, n) in tts:
                            b = nb()
                            mm_fm(b, wo, ko, KC, j, gated, [("gated", k) for k in range(KC)], c0, n)
                            acc_into_h(b, mc, c0, n)
                P.end_phase()

        def conv_block2(tag, pre, pre_s, tcv, slot, KW, wnames, cidx, pkeys, skeys):
            ktp, kts = tag + "tcv0p", tag + "tcv0s"
            ts_v = tcv[:, 0, PL:T].rearrange("p (s t) -> p s t", t=8)
            for kk in range(KW):
                wcol = V(wnames[kk], cidx)
                if kk == 0:
                    P.op("dve", lambda e, wcol=wcol: e.tensor_scalar(tcv[:, 0, 0:PL], pre[:, slot, 0:PL], wcol, None,
                                                                   ALU.mult),
                         reads=pkeys + ["vecs"], writes=[ktp])
                    P.op("dve", lambda e, wcol=wcol: e.tensor_scalar(ts_v, pre_s[:, slot, :, 0:8], wcol, None, ALU.mult),
                         reads=skeys + ["vecs"], writes=[kts])
                else:
                    P.op("dve", lambda e, wcol=wcol, kk=kk: e.scalar_tensor_tensor(
                        tcv[:, 0, 0:PL], pre[:, slot, kk:kk + PL], wcol, tcv[:, 0, 0:PL], ALU.mult, ALU.add),
                         reads=pkeys + ["vecs", ktp], writes=[ktp])
                    P.op("dve", lambda e, wcol=wcol, kk=kk: e.scalar_tensor_tensor(
                        ts_v, pre_s[:, slot, :, kk:kk + 8], wcol, ts_v, ALU.mult, ALU.add),
                         reads=skeys + ["vecs", kts], writes=[kts])
            return [ktp, kts]

        def ssd_phase():
            XB = 128
            NT = NQ + 1
            tile_col = [HALO + q * 128 for q in range(NQ)] + [PL]
            with ExitStack() as pho:
              dt_all = sb("dt_all", [128, NT, H], F32, pho)
              dA_all = sb("dA_all", [128, NT, H], F32, pho)
              with ExitStack() as ph1:
                wdt = sb("wdt", [128, KC, H], BF16, ph1)
                o_dtb = cfg.voff["dtb"][0]
                P.op("pool", lambda e: e.dma_start(out=wdt[:], in_=ssd_w_in[:, DI + CONV:DI + CONV + H].rearrange(
                    "(k p) n -> p k n", p=128)), writes=["wdt"], dma="wdt")
                for ti in range(NT):
                    c0 = tile_col[ti]
                    bdt = nb()

                    def mmdt(e, c0=c0, bdt=bdt):
                        for k in range(KC):
                            r = e.matmul(ps[bdt][:, 0:H], lhsT=xn[:, k, c0:c0 + 128], rhs=wdt[:, k, :], start=(k == 0),
                                         stop=(k == KC - 1))
                        return r
                    P.op("pe", mmdt, reads=["xn", "wdt"], writes=["ps%d" % bdt])
                    P.op("dve", lambda e, ti=ti, bdt=bdt: e.tensor_tensor(dt_all[:, ti, :], ps[bdt][:, 0:H],
                                                                         vecs[:, o_dtb:o_dtb + H], ALU.add),
                         reads=["ps%d" % bdt, "vecs"], writes=[("dt", ti)])
                P.op("act", lambda e: e.activation(out=dt_all[:, :, :], in_=dt_all[:, :, :], func=AF.Exp),
                     reads=[("dt", ti) for ti in range(NT)], writes=["dt_all"])
                P.op("act", lambda e: e.activation(out=dt_all[:, :, :], in_=dt_all[:, :, :], func=AF.Ln, bias=1.0),
                     reads=["dt_all"], writes=["dt_all"])
                P.op("dve", lambda e: e.tensor_tensor(dA_all[:, :, :], dt_all[:, :, :],
                                                     abc[:, :].unsqueeze(1).broadcast_to([128, NT, H]), ALU.mult),
                     reads=["dt_all", "abc"], writes=["dA_all"])

                P.end_phase()
              with ExitStack() as ph:
                make_ring(ph, 4, "s", 2048)
                xsg = sb("xsg", [128, GC, T], BF16, ph)
                Bg = sb("Bg", [128, T], BF16, ph)
                Cg = sb("Cg", [128, T], BF16, ph)
                ygn = sb("ygn", [128, GC, T], BF16, ph)
                pre = sb("l1pre", [128, 2, 3 + PL], F32, ph)
                pre_s = sb("l1pres", [128, 2, NSEQ, 11], F32, ph)
                tcv = sb("l1tcv", [128, 1, T], F32, ph)
                stg_in = [sb("l1sti%d" % i, [128, 128], F32, ph) for i in range(1)]
                stg_out = [sb("l1sto%d" % i, [128, 256], F32, ph) for i in range(1)]
                cost = [sb("l1cost%d" % i, [128, NSEQ * 3], F32, ph) for i in range(1)]
                acs = sb("acs", [128, R], F32, ph)
                expa2 = [sb("expa%d" % i, [128, R], F32, ph) for i in range(2)]
                dE = sb("dE", [128, R], F32, ph)
                cdec2 = [sb("cdec%d" % i, [128, R], F32, ph) for i in range(2)]
                cdecF = sb("cdecF", [128, GC, NSEQ], F32, ph)
                Rm = sb("Rm", [128, R, 128], F32, ph)
                Wm2 = [sb("Wm%d" % i, [128, R, 128], BF16, ph) for i in range(2)]
                MTm = sb("MTm", [128, 128], BF16, ph)
                xdt2 = [sb("xdt%d" % i, [128, GW], BF16, ph) for i in range(2)]
                xw2 = [sb("xw%d" % i, [128, GW], BF16, ph) for i in range(2)]
                Btok2 = [sb("Btok%d" % i, [128, 128], BF16, ph) for i in range(2)]
                y1 = sb("y1", [128, GW], F32, ph)
                xD2 = [sb("xD%d" % i, [128, GW], F32, ph) for i in range(2)]
                gyn = sb("gyn", [128, GW], BF16, ph)
                ss = sb("ss", [128, 2], F32, ph)
                S = sb("S", [128, GW], F32, ph)
                Sb = sb("Sb", [128, GW], BF16, ph)
                h0 = [sb("h0_%d" % i, [128, GC, 128], F32, ph) for i in range(2)]
                h0T = [sb("h0T_%d" % i, [128, GW], BF16, ph) for i in range(1)]
                xwj = [sb("xwj_%d" % i, [128, GW], BF16, ph) for i in range(1)]
                sost = [sb("sost_%d" % i, [128, GC, 128], F32, ph) for i in range(2)]
                if (NSEQ + 1) * 64 <= T:
                    Cm_t = tcv[:, 0, 0:(NSEQ + 1) * 64].bitcast(BF16)
                    cmk = "Btcv0p"
                else:
                    Cm_t = sb("Cm", [128, (NSEQ + 1) * 128], BF16, ph)[:, :]
                    cmk = "Cm"
                Cmdiag = Cm_t[:, 0:NSEQ * 136].rearrange("p (j q) -> p j q", q=136)[:, :, 0:8]
                print("SSD phase sbuf remaining", nc.sbuf_bytes_remaining, file=sys.stderr)
                P.op("dve", lambda e: e.memset(pre[:, 0, 0:3], 0.0), writes=["Bpre0z"])
                P.op("dve", lambda e: e.memset(pre[:, 1, 0:3], 0.0), writes=["Bpre1z"])
                P.op("dve", lambda e: e.memset(ygn[:, :, :], 0.0), writes=[("ygn", c) for c in range(GC)])
                o_dtb = cfg.voff["dtb"][0]
                o_dsk = cfg.voff["dsk"][0]
                def chunk(g, ti, full, wz, kz, sample=False, pp=0, nbp=None, nbt=None):
                    Q = 128
                    nbp = nbp or nb
                    nbt = nbt or nb
                    Wm, xdt, xD, xw, Btok, expa, cdec = Wm2[pp], xdt2[pp], xD2[pp], xw2[pp], Btok2[pp], expa2[pp], cdec2[pp]
                    kWm, kxdt, kxD, kxw, kBtok, kexpa, kcdec = ["%s%d" % (n_, pp) for n_ in "Wm xdt xD xw Btok expa cdec".split()]
                    col0 = tile_col[ti]
                    cs = slice(col0, col0 + Q)
                    hs = slice(g * R, (g + 1) * R)
                    TRI = triS_f if sample else tri_f
                    dtt = dt_all[:, ti, hs]
                    dA = dA_all[:, ti, hs]
                    bac = nbp()
                    yield P.op("pe", lambda e: e.matmul(ps[bac][:, 0:R], lhsT=TRI, rhs=dA, start=True, stop=True),
                         reads=["dA_all", "cmat"], writes=["ps%d" % bac])
                    yield P.op("dve", lambda e: e.tensor_tensor(
                        Rm[:, :, :], TRI.unsqueeze(1).broadcast_to([Q, R, Q]),
                        dA.unsqueeze(2).broadcast_to([Q, R, Q]), ALU.mult),
                         reads=["dA_all", "cmat"], writes=["Rm"])
                    yield P.op("dve", lambda e: e.tensor_copy(acs[:, :], ps[bac][:, 0:R]), reads=["ps%d" % bac],
                         writes=["acs"])
                    hb = 512 // Q
                    nbk = -(-R // hb)
                    bab = [nbp() for _ in range(nbk)]

                    def mmab(e):
                        for i in range(nbk):
                            h_a, h_b = i * hb, min(R, (i + 1) * hb)
                            r = e.matmul(ps[bab[i]][:, 0:(h_b - h_a) * Q].rearrange("p (h t) -> p h t", t=Q),
                                         lhsT=ones_f, rhs=Rm[:, h_a:h_b, :], start=True, stop=True)
                        return r
                    yield P.op("pe", mmab, reads=["Rm", "cmat"], writes=["ps%d" % b for b in bab])

                    def abv(i):
                        h_a, h_b = i * hb, min(R, (i + 1) * hb)
                        return ps[bab[i]][:, 0:(h_b - h_a) * Q].rearrange("p (h t) -> p h t", t=Q), h_a, h_b
                    if not sample:
                        for i in range(nbk):
                            v, h_a, h_b = abv(i)
                            yield P.op("dve", lambda e, v=v, h_a=h_a, h_b=h_b: e.tensor_tensor(
                                dE[:, h_a:h_b], v[:, :, Q - 1], acs[:, h_a:h_b], ALU.subtract),
                                 reads=["ps%d" % bab[i], "acs"], writes=["dE"])
                            yield P.op("act", lambda e, v=v, h_a=h_a, h_b=h_b: e.activation(out=cdec[:, h_a:h_b],
                                                                                    in_=v[:, :, Q - 1], func=AF.Exp),
                                 reads=["ps%d" % bab[i]], writes=[kcdec])
                    else:
                        btot = nb()
                        yield P.op("pe", lambda e: e.matmul(ps[btot][:, 0:R], lhsT=blk_f, rhs=dA, start=True, stop=True),
                             reads=["dA_all", "cmat"], writes=["ps%d" % btot])
                        yield P.op("dve", lambda e: e.tensor_tensor(dE[:, :], ps[btot][:, 0:R], acs[:, :], ALU.subtract),
                             reads=["ps%d" % btot, "acs"], writes=["dE"])
                        yield P.op("dve", lambda e: e.tensor_copy(
                            xD[:, :].rearrange("q (h p) -> q h p", p=64), dA.unsqueeze(2).broadcast_to([Q, R, 64])),
                             reads=["dA_all"], writes=[kxD])
                        bcd = nb()

                        def mmcd(e):
                            for c in range(GC):
                                r = e.matmul(ps[bcd][:, c * NSEQ:(c + 1) * NSEQ], lhsT=xD[:, c * 128:(c + 1) * 128],
                                             rhs=maskJ_f, start=True, stop=True)
                            return r
                        yield P.op("pe", mmcd, reads=[kxD, "cmat"], writes=["ps%d" % bcd])
                        yield P.op("act", lambda e: e.activation(
                            out=cdecF[:, :, :], in_=ps[bcd][:, 0:GC * NSEQ].rearrange("p (c j) -> p c j", j=NSEQ),
                            func=AF.Exp), reads=["ps%d" % bcd], writes=["cdecF"])
                    yield P.op("act", lambda e: e.activation(out=dE[:, :], in_=dE[:, :], func=AF.Exp), reads=["dE"],
                         writes=["dE"])
                    if full:
                        yield P.op("act", lambda e: e.activation(out=expa[:, :], in_=acs[:, :], func=AF.Exp),
                             reads=["acs"], writes=[kexpa])
                        for i in range(nbk):
                            v, h_a, h_b = abv(i)
                            yield P.op("dve", lambda e, v=v, h_a=h_a, h_b=h_b: e.tensor_tensor(
                                Rm[:, h_a:h_b, :], v, acs[:, h_a:h_b].unsqueeze(2).broadcast_to([Q, h_b - h_a, Q]),
                                ALU.subtract), reads=["ps%d" % bab[i], "acs", "Rm"], writes=["Rm"])
                        yield P.op("dve", lambda e: e.tensor_scalar(Rm[:, :, :], Rm[:, :, :], 0.0, None, ALU.min),
                             reads=["Rm"], writes=["Rm"])
                        yield P.op("act", lambda e: e.activation(out=Wm[:, :, :], in_=Rm[:, :, :], func=AF.Exp),
                             reads=["Rm"], writes=[kWm])
                        bm = nbp()
                        yield P.op("pe", lambda e: e.matmul(ps[bm][:, 0:Q], lhsT=Bg[:, cs], rhs=Cg[:, cs], start=True,
                                                     stop=True), reads=["Bg", "Cg"], writes=["ps%d" % bm])
                        yield P.op("dve", lambda e: e.tensor_tensor(MTm[:, :], ps[bm][:, 0:Q], TRI, ALU.mult),
                             reads=["ps%d" % bm, "cmat"], writes=["MTm"])
                        yield P.op("dve", lambda e: e.tensor_tensor(
                            Wm[:, :, :], Wm[:, :, :], MTm[:, :].unsqueeze(1).broadcast_to([Q, R, Q]),
                            ALU.mult), reads=["MTm", kWm], writes=[kWm])
                    bx = nbp()

                    def trx(e):
                        for c in range(GC):
                            r = e.transpose(psb[bx][:, c * 128:(c + 1) * 128], xsg[:, c, cs], ident_b)
                        return r
                    yield P.op("pe", trx, reads=[("xsg", c) for c in range(GC)] + ["cmb"], writes=["ps%d" % bx])
                    yield P.op("dve", lambda e: e.tensor_tensor(
                        xdt[:, :].rearrange("q (h p) -> q h p", p=64),
                        psb[bx][:, 0:GW].rearrange("q (h p) -> q h p", p=64),
                        dtt.unsqueeze(2).broadcast_to([Q, R, 64]), ALU.mult),
                         reads=["ps%d" % bx, "dt_all"], writes=[kxdt])
                    if full:
                        yield P.op("dve", lambda e: e.tensor_tensor(
                            xD[:, :].rearrange("q (h p) -> q h p", p=64),
                            psb[bx][:, 0:GW].rearrange("q (h p) -> q h p", p=64),
                            vecs[:, o_dsk + g * R:o_dsk + (g + 1) * R].unsqueeze(2).broadcast_to([Q, R, 64]), ALU.mult),
                             reads=["ps%d" % bx, "vecs", kxD], writes=[kxD])
                    bB = nbp()
                    yield P.op("pe", lambda e: e.transpose(psb[bB][:, 0:128], Bg[:, cs], ident_b), reads=["Bg", "cmb"],
                         writes=["ps%d" % bB])
                    yield P.op("act", lambda e: e.copy(Btok[:, :], psb[bB][:, 0:128]), reads=["ps%d" % bB],
                         writes=[kBtok])
                    yield P.op("dve", lambda e: e.tensor_tensor(
                        xw[:, :].rearrange("q (h p) -> q h p", p=64), xdt[:, :].rearrange("q (h p) -> q h p", p=64),
                        dE[:, :].unsqueeze(2).broadcast_to([Q, R, 64]), ALU.mult), reads=[kxdt, "dE"], writes=[kxw])
                    bi = None
                    if sample:
                        bi = nb()
                        reserved.add(bi)
                        yield P.op("dve", lambda e: e.memset(Cm_t[:, :], 0.0), writes=[cmk, "Btcv0s"])
                        yield P.op("dve", lambda e: e.tensor_copy(Cmdiag, Cg[:, cs].rearrange("p (j r) -> p j r", r=8)),
                             reads=["Cg"], writes=[cmk, "Btcv0s"])
                        for jq in range(NSEQ):
                            s = jq % 2
                            yield P.op("pool", lambda e, jq=jq, s=s: e.dma_start(
                                out=h0[s][:, :, :],
                                in_=ssd_state[jq, g * GW:(g + 1) * GW, :].rearrange("(c p) n -> p c n", p=128)),
                                 writes=["h0_%d" % s], dma="h0_%d" % s)
                            bh = nb()

                            def trh(e, bh=bh, s=s):
                                for c in range(GC):
                                    r = e.transpose(ps[bh][:, c * 128:(c + 1) * 128], h0[s][:, c, :], ident_f)
                                return r
                            yield P.op("pe", trh, reads=["h0_%d" % s, "cmat"], writes=["ps%d" % bh])
                            yield P.op("act", lambda e, bh=bh, s=s: e.copy(h0T[0][:, :], ps[bh][:, 0:GW]), reads=["ps%d" % bh],
                                 writes=["h0T_0"])
                            yield P.op("pe", lambda e, jq=jq, s=s: e.matmul(ps[bi][:, 0:GW], lhsT=Cm_t[:, jq * 128:(jq + 1) * 128], rhs=h0T[0][:, :],
                                                                     start=(jq == 0), stop=(jq == NSEQ - 1)),
                                 reads=[cmk, "Btcv0s", "h0T_0"], writes=["ps%d" % bi])
                            yield P.op("dve", lambda e, jq=jq, s=s: e.tensor_scalar(xwj[0][:, :], xw[:, :], maskJ_f[:, jq:jq + 1],
                                                                            None, ALU.mult),
                                 reads=[kxw, "cmat"], writes=["xwj_0"])
                            bsj = nb()

                            def mmsj(e, bsj=bsj, s=s):
                                for c in range(GC):
                                    r = e.matmul(ps[bsj][:, c * 128:(c + 1) * 128], lhsT=xwj[0][:, c * 128:(c + 1) * 128],
                                                 rhs=Btok[:, :], start=True, stop=True)
                                return r
                            yield P.op("pe", mmsj, reads=["xwj_0", kBtok], writes=["ps%d" % bsj])
                            yield P.op("dve", lambda e, jq=jq, s=s: e.tensor_tensor(
                                sost[s][:, :, :], h0[s][:, :, :],
                                cdecF[:, :, jq].unsqueeze(2).broadcast_to([128, GC, 128]), ALU.mult),
                                 reads=["h0_%d" % s, "cdecF"], writes=["sost_%d" % s])
                            yield P.op("dve", lambda e, bsj=bsj, s=s: e.tensor_tensor(
                                sost[s][:, :, :], sost[s][:, :, :],
                                ps[bsj][:, 0:GW].rearrange("p (c n) -> p c n", n=128), ALU.add),
                                 reads=["sost_%d" % s, "ps%d" % bsj], writes=["sost_%d" % s])
                            yield P.op("sp", lambda e, jq=jq, s=s: e.dma_start(
                                out=st_out_s[jq, g * GW:(g + 1) * GW, :].rearrange("(c p) n -> p c n", p=128),
                                in_=sost[s][:, :, :]), reads=["sost_%d" % s], dma="sost_%d" % s)
                    yield "SPLIT"
                    if full:
                        bz = nbt()

                        def mmz(e):
                            r = None
                            for i in range(GW // XB):
                                for k in range(KC):
                                    r = e.matmul(ps[bz][:, i * XB:(i + 1) * XB], lhsT=xn[:, k, cs], rhs=wz[i][:, k, :],
                                                 start=(k == 0), stop=(k == KC - 1))
                            return r
                        yield P.op("pe", mmz, reads=["xn"] + kz, writes=["ps%d" % bz])
                        if sample:
                            yield P.op("dve", lambda e: e.tensor_tensor(
                                y1[:, :].rearrange("q (h p) -> q h p", p=64),
                                ps[bi][:, 0:GW].rearrange("q (h p) -> q h p", p=64),
                                expa[:, :].unsqueeze(2).broadcast_to([Q, R, 64]), ALU.mult),
                                 reads=["ps%d" % bi, kexpa], writes=["y1"])
                            reserved.discard(bi)
                        by = nbt()

                        def mmy(e):
                            for hh in range(R):
                                r = e.matmul(ps[by][:, hh * 64:(hh + 1) * 64], lhsT=Wm[:, hh, :],
                                             rhs=xdt[:, hh * 64:(hh + 1) * 64], start=True, stop=True)
                            return r
                        yield P.op("pe", mmy, reads=[kWm, kxdt], writes=["ps%d" % by])
                        if not sample:
                            bi = nbt()
                            yield P.op("pe", lambda e: e.matmul(ps[bi][:, 0:GW], lhsT=Cg[:, cs], rhs=Sb[:, :], start=True,
                                                         stop=True), reads=["Cg", "Sb"], writes=["ps%d" % bi])
                            yield P.op("dve", lambda e: e.tensor_tensor(
                                y1[:, :].rearrange("q (h p) -> q h p", p=64),
                                ps[bi][:, 0:GW].rearrange("q (h p) -> q h p", p=64),
                                expa[:, :].unsqueeze(2).broadcast_to([Q, R, 64]), ALU.mult),
                                 reads=["ps%d" % bi, kexpa], writes=["y1"])
                    if not sample:
                        bs = nbt()
                        yield P.op("pe", lambda e: e.matmul(ps[bs][:, 0:GW], lhsT=Btok[:, :], rhs=xw[:, :], start=True, stop=True),
                             reads=[kBtok, kxw], writes=["ps%d" % bs])
                        yield P.op("dve", lambda e: e.tensor_tensor(
                            S[:, :].rearrange("n (h p) -> n h p", p=64), S[:, :].rearrange("n (h p) -> n h p", p=64),
                            cdec[:, :].unsqueeze(2).broadcast_to([128, R, 64]), ALU.mult), reads=["S", kcdec, "Sb"],
                             writes=["S"])
                        yield P.op("dve", lambda e: e.tensor_tensor(S[:, :], S[:, :], ps[bs][:, 0:GW], ALU.add),
                             reads=["S", "ps%d" % bs], writes=["S"])
                        yield P.op("act", lambda e: e.copy(Sb[:, :], S[:, :]), reads=["S"], writes=["Sb"])

                    if full:
                        yield P.op("dve", lambda e: e.tensor_tensor(y1[:, :], y1[:, :], ps[by][:, 0:GW], ALU.add),
                             reads=["y1", "ps%d" % by], writes=["y1"])
                        yield P.op("dve", lambda e: e.tensor_tensor(y1[:, :], y1[:, :], xD[:, :], ALU.add),
                             reads=["y1", kxD], writes=["y1"])
                        yield P.op("act", lambda e: e.activation(out=xD[:, :], in_=ps[bz][:, 0:GW], func=AF.Tanh, scale=0.5),
                             reads=["ps%d" % bz, kxD], writes=[kxD])
                        yield P.op("dve", lambda e: e.scalar_tensor_tensor(xD[:, :], xD[:, :], 1.0, ps[bz][:, 0:GW], ALU.add,
                                                                    ALU.mult),
                             reads=[kxD, "ps%d" % bz], writes=[kxD])
                        yield P.op("dve", lambda e: e.scalar_tensor_tensor(y1[:, :], y1[:, :], 0.5, xD[:, :], ALU.mult, ALU.mult),
                             reads=["y1", kxD], writes=["y1"])
                        yield P.op("act", lambda e: e.activation(out=xD[:, :], in_=y1[:, :], func=AF.Square,
                                                          accum_out=ss[:, 0:1]), reads=["y1", kxD],
                             writes=[kxD, "ss"])
                        yield P.op("dve", lambda e: e.tensor_scalar(ss[:, 1:2], ss[:, 0:1], 1.0 / GW, EPS, ALU.mult, ALU.add),
                             reads=["ss"], writes=["ss"])
                        yield P.op("act", lambda e: e.activation(out=ss[:, 1:2], in_=ss[:, 1:2], func=AF.Sqrt),
                             reads=["ss"], writes=["ss"])
                        yield P.op("dve", lambda e: e.reciprocal(ss[:, 1:2], ss[:, 1:2]), reads=["ss"], writes=["ss"])
                        yield P.op("dve", lambda e: e.tensor_scalar(gyn[:, :], y1[:, :], ss[:, 1:2], None, ALU.mult),
                             reads=["y1", "ss"], writes=["gyn"])
                        bt = nbt()

                        def trg(e):
                            for c in range(GC):
                                r = e.transpose(psb[bt][:, c * 128:c * 128 + Q], gyn[:, c * 128:(c + 1) * 128], ident_b)
                            return r
                        yield P.op("pe", trg, reads=["gyn", "cmb"], writes=["ps%d" % bt])
                        o_ng = cfg.voff["ng"][0]
                        for c in range(GC):
                            yield P.op("dve" if c % 2 == 0 else "act",
                                 (lambda e, c=c: e.tensor_scalar(ygn[:, c, cs], psb[bt][:, c * 128:c * 128 + Q],
                                                                 vecs[:, o_ng + g * GC + c:o_ng + g * GC + c + 1], None,
                                                                 ALU.mult)) if c % 2 == 0 else
                                 (lambda e, c=c: e.activation(out=ygn[:, c, cs], in_=psb[bt][:, c * 128:c * 128 + Q],
                                                              func=AF.Copy,
                                                              scale=vecs[:, o_ng + g * GC + c:o_ng + g * GC + c + 1])),
                                 reads=["ps%d" % bt, "vecs"], writes=[("ygn", c)])
                def run_pipelined(gens):
                    for v in gens[0]:
                        if v == "SPLIT":
                            break
                    for q in range(len(gens)):
                        ga = gens[q]
                        gb = gens[q + 1] if q + 1 < len(gens) else None
                        a_done, b_done = False, gb is None
                        while not (a_done and b_done):
                            if not a_done:
                                try:
                                    next(ga)
                                except StopIteration:
                                    a_done = True
                            if not b_done:
                                try:
                                    if next(gb) == "SPLIT":
                                        b_done = True
                                except StopIteration:
                                    b_done = True

                def state_out(dst_ap_fn):
                    bo = nb()

                    def tro(e):
                        for c in range(GC):
                            r = e.transpose(ps[bo][:, c * 128:(c + 1) * 128], S[:, c * 128:(c + 1) * 128], ident_f)
                        return r
                    P.op("pe", tro, reads=["S", "cmat"], writes=["ps%d" % bo])
                    P.op("act", lambda e: e.copy(sost[0][:, :, :], ps[bo][:, 0:GW].rearrange("p (c n) -> p c n", n=128)),
                         reads=["ps%d" % bo], writes=["sost_0"])
                    P.op("sp", lambda e: e.dma_start(out=dst_ap_fn(), in_=sost[0][:, :, :]), reads=["sost_0"],
                         dma="sost_0")

                sidx = [0]
                for g in range(G):
                    specs = []
                    for i in range(GW // XB):
                        col = DI + g * GW + i * XB
                        for j in range(XB // 128):
                            c = i * (XB // 128) + j
                            specs.append((ssd_w_in[:, col:col + XB], XB, j, g * GC + c, ("x", c), i))
                    colB = 2 * DI + g * 128
                    specs.append((ssd_w_in[:, colB:colB + 128], 128, 0, DI // 128 + g, ("B", 0), "B"))
                    colC = 2 * DI + GN + g * 128
                    specs.append((ssd_w_in[:, colC:colC + 128], 128, 0, DI // 128 + G + g, ("C", 0), "C"))
                    loaded = {}

                    def ensure(k):
                        if k < len(specs) and specs[k][5] not in loaded:
                            loaded[specs[k][5]] = wload(specs[k][0], KC, specs[k][1])
                    LOOK = 3
                    for k in range(LOOK):
                        ensure(k)
                    for idx in range(len(specs)):
                        ensure(idx + LOOK)
                        _, _, j, cidx, (kind, c), lk = specs[idx]
                        wv_, kv_ = loaded[lk]
                        sl = sidx[0] % 2
                        conv_state_io("B", pre, pre_s, sl, 4, ssdc_state, cidx, None, None, stg_in, None, None, 0)
                        for (c0, n) in tts:
                            b = nb()
                            mm_fm(b, wv_, kv_, KC, j, xn, ["xn"], c0, n)
                            pp, sp_ = split_cols(c0, n)
                            if pp:
                                a, b2 = pp
                                P.op("act", lambda e, a=a, b2=b2, b=b, c0=c0, sl=sl: e.copy(pre[:, sl, 3 + a:3 + b2],
                                                                                   ps[b][:, a - c0:b2 - c0]),
                                     reads=["ps%d" % b], writes=["Bpre%d" % sl])
                            if sp_:
                                a, b2 = sp_
                                P.op("dve", lambda e, a=a, b2=b2, b=b, c0=c0, sl=sl: e.tensor_copy(
                                    pre_s[:, sl, :, 3:11], ps[b][:, a - c0:b2 - c0].rearrange("p (s t) -> p s t", t=8)),
                                     reads=["ps%d" % b], writes=["Bpres%d" % sl])
                        conv_state_out("B", pre, pre_s, sl, 4, cidx, ssdc_out_p, ssdc_out_s, stg_out, cost,
                                       ["Bpre%d" % sl, "Bpres%d" % sl, "Bpres%dh" % sl], 0)
                        kt = conv_block2("B", pre, pre_s, tcv, sl, 4, ["cw0", "cw1", "cw2", "cw3"], cidx,
                                         ["Bpre%d" % sl, "Bpre%dz" % sl], ["Bpres%d" % sl, "Bpres%dh" % sl])
                        if kind == "x":
                            dst, dk = xsg[:, c, :], ("xsg", c)
                        elif kind == "B":
                            dst, dk = Bg[:, :], "Bg"
                        else:
                            dst, dk = Cg[:, :], "Cg"
                        P.op("act", lambda e, dst=dst, cidx=cidx: e.activation(out=dst, in_=tcv[:, 0, :], func=AF.Silu,
                                                                             bias=V("cb", cidx)),
                             reads=kt + ["vecs"], writes=[dk])
                        sidx[0] += 1
                    wz, kz = [], []
                    for i in range(GW // XB):
                        col = g * GW + i * XB
                        w_, k_ = wload(ssd_w_in[:, col:col + XB], KC, XB)
                        wz.append(w_)
                        kz.append(k_)
                    P.op("dve", lambda e: e.memset(S[:, :], 0.0), reads=["Sb"], writes=["S"])
                    run_pipelined([chunk(g, q, False, wz, kz, pp=q % 2, nbp=nb_prep, nbt=nb_tail) for q in range(NQ)])
                    ibk, obk = "ib%d" % g, "ob%d" % g
                    P.op("sp", lambda e, g=g: e.dma_start(out=ibs[g][:, :], in_=S[:, :]), reads=["S"], writes=[ibk],
                         dma="xch")
                    P.op("pool", lambda e, g=g: e.collective_compute(
                        "AllGather", ALU.bypass, replica_groups=[[0, 1], [2, 3], [4, 5], [6, 7]],
                        ins=[ibs[g].ap().opt()], outs=[obs[g].ap().opt()]), reads=[ibk], writes=[obk],
                         dma="cc%d" % g, inc=1)
                    for _ in chunk(g, NQ, True, wz, kz, sample=True, pp=0):
                        pass
                    P.op("sp", lambda e, g=g: e.dma_start(out=S[:, :], in_=obs[g][0:128, :]), reads=[obk, "Sb"],
                         writes=["S"], dma="xch")
                    P.op("dve", lambda e: e.tensor_scalar(S[:, :], S[:, :], mko[:, 0:1], None, ALU.mult),
                         reads=["S", "mko"], writes=["S"])
                    P.op("act", lambda e: e.copy(Sb[:, :], S[:, :]), reads=["S"], writes=["Sb"])
                    run_pipelined([chunk(g, q, True, wz, kz, pp=q % 2, nbp=nb_prep, nbt=nb_tail) for q in range(NQ)])
                    state_out(lambda g=g: st_out_p[g * GW:(g + 1) * GW, :].rearrange("(c p) n -> p c n", p=128))
                    for mcp in range(D // 256):
                        wo, ko = wload(ssd_w_out[g * GW:(g + 1) * GW, mcp * 256:mcp * 256 + 256], GC, 256)
                        for j in range(2):
                            mc = mcp * 2 + j
                            for (c0, n) in tts:
                                b = nb()
                                mm_fm(b, wo, ko, GC, j, ygn, [("ygn", k) for k in range(GC)], c0, n)
                                acc_into_h(b, mc, c0, n)
                P.end_phase()

        def out_phase():
            with ExitStack() as ph:
                yst = [sb("ystg%d" % i, [128, D], F32, ph) for i in range(2)]
                for ri, (r0, n) in enumerate(rts):
                    s = ri % 2
                    for cg in range(KC // 4):
                        b = nb()

                        def tr(e, n=n, cg=cg, b=b, r0=r0):
                            for j in range(4):
                                c = cg * 4 + j
                                r = e.transpose(ps[b][0:n, j * 128:(j + 1) * 128], h[:, c, r0:r0 + n], ident_f)
                            return r
                        P.op("pe", tr, reads=["h", "cmat"], writes=["ps%d" % b])
                        if cg % 2 == 0:
                            P.op("dve", lambda e, s=s, n=n, cg=cg, b=b: e.tensor_copy(yst[s][0:n, cg * 512:(cg + 1) * 512],
                                                                                    ps[b][0:n, :]),
                                 reads=["ps%d" % b], writes=[("yst%d" % s, cg)])
                        else:
                            P.op("act", lambda e, s=s, n=n, cg=cg, b=b: e.copy(yst[s][0:n, cg * 512:(cg + 1) * 512],
                                                                             ps[b][0:n, :]),
                                 reads=["ps%d" % b], writes=[("yst%d" % s, cg)])
                    P.op("sp", lambda e, s=s, r0=r0, n=n: e.dma_start(out=y_out[r0:r0 + n, :], in_=yst[s][0:n, :]),
                         reads=[("yst%d" % s, cg) for cg in range(KC // 4)], dma="yst%d" % s)
                P.end_phase()

        phases = [lambda: rmsnorm_phase("g_mix0"), l0_mixer_phase,
                  lambda: rmsnorm_phase("g_ffn0"), lambda: ffn_phase(0),
                  lambda: rmsnorm_phase("g_ple0"), lambda: ple_phase(0),
                  lambda: rmsnorm_phase("g_mix1"), ssd_phase,
                  lambda: rmsnorm_phase("g_ffn1"), lambda: ffn_phase(1),
                  lambda: rmsnorm_phase("g_ple1"), lambda: ple_phase(1),
                  lambda: rmsnorm_phase("g_final", final=True)]
        for pi, phf in enumerate(phases):
            if stop is not None and pi >= stop:
                break
            phf()
        out_phase()
        for k, v in P.cnt.items():
            if k[0] == "dma" and P.waited["sp"].get(k, 0) < v:
                nc.sync.wait_ge(P.getsem(k), v)
        print("ops", len(P.ops), "waits", P.nwait, "sems", len(P.sems), file=sys.stderr)
    return nc


def host_inputs(cfg, inp):
    D, KC, T, TP, NSEQ = cfg.D, cfg.KC, cfg.T, cfg.TP, cfg.NSEQ
    f = np.float32

    def pm(v):
        v = np.asarray(v, f)
        return np.ascontiguousarray(v.reshape(-1, 128).T)

    def bc(v):
        v = np.asarray(v, f).reshape(1, -1)
        return np.ascontiguousarray(np.broadcast_to(v, (128, v.shape[1])))
    vec = np.zeros((128, cfg.NV), f)

    def put(nm, a):
        o, w = cfg.voff[nm]
        assert a.shape == (128, w), (nm, a.shape, w)
        vec[:, o:o + w] = a
    for l in range(2):
        put("g_mix%d" % l, pm(inp["g_mix"][l]))
        put("g_ffn%d" % l, pm(inp["g_ffn"][l]))
        put("g_ple%d" % l, pm(inp["g_ple"][l]))
    put("g_final", pm(inp["g_final"]))
    for k in range(3):
        put("scw%d" % k, pm(inp["sc_w_conv"][0, k]))
    for k in range(4):
        put("cw%d" % k, pm(inp["ssd_conv_w"][0, k]))
    put("cb", pm(inp["ssd_conv_b"][0]))
    put("ng", pm(inp["ssd_norm_g"][0]))
    put("dtb", bc(inp["ssd_dt_bias"][0]))
    put("alog", bc(inp["ssd_a_log"][0]))
    put("dsk", bc(inp["ssd_d"][0]))
    cm = np.zeros((128, 640 + NSEQ), f)
    cm[:, 0:128] = np.eye(128, dtype=f)
    cm[:, 128:256] = np.triu(np.ones((128, 128), f))
    cm[:, 256:384] = 1.0
    sid = np.arange(128) // 8
    same = (sid[:, None] == sid[None, :]).astype(f)
    cm[:, 384:512] = cm[:, 128:256] * same
    cm[:, 512:640] = same
    cm[:, 640:640 + NSEQ] = (sid[:, None] == np.arange(NSEQ)[None, :]).astype(f)
    mt = np.zeros((128, NSEQ, 128), f)
    mt[:, sid, np.arange(128)] = 1.0
    shared = dict(vecs=vec, cmat=cm,
                  sc_w_in=np.ascontiguousarray(inp["sc_w_in"][0]), sc_w_out=np.ascontiguousarray(inp["sc_w_out"][0]),
                  ssd_w_in=np.ascontiguousarray(inp["ssd_w_in"][0]), ssd_w_out=np.ascontiguousarray(inp["ssd_w_out"][0]),
                  ffn_w_gate=np.ascontiguousarray(inp["ffn_w_gate"]), ffn_w_up=np.ascontiguousarray(inp["ffn_w_up"]),
                  ffn_w_down=np.ascontiguousarray(inp["ffn_w_down"]), ple_w_proj=np.ascontiguousarray(inp["ple_w_proj"]),
                  ple_w_gate=np.ascontiguousarray(inp["ple_w_gate"]))
    maps = []
    for core in range(8):
        b, half = core // 2, core % 2
        xin = np.zeros((T, D), f)
        pin = np.zeros((2, T, cfg.PLE), f)
        t0 = half * TP
        if half == 1:
            xin[0:HALO] = inp["x_prompt"][b, t0 - HALO:t0]
            pin[:, 0:HALO] = inp["p_prompt"][:, b, t0 - HALO:t0]
        xin[HALO:HALO + TP] = inp["x_prompt"][b, t0:t0 + TP]
        pin[:, HALO:HALO + TP] = inp["p_prompt"][:, b, t0:t0 + TP]
        sl = slice(core * NSEQ, (core + 1) * NSEQ)
        xin[HALO + TP:] = inp["x_sample"][sl].reshape(-1, D)
        pin[:, HALO + TP:] = inp["p_sample"][:, sl].reshape(2, -1, cfg.PLE)
        m = dict(shared)
        m.update(xin=xin, pin=pin,
                 sc_state=np.ascontiguousarray(inp["state_sc_conv"][0, sl].reshape(NSEQ * 2, D)),
                 ssdc_state=np.ascontiguousarray(inp["state_ssd_conv"][0, sl].reshape(NSEQ * 3, cfg.CONV)),
                 ssd_state=np.ascontiguousarray(inp["state_ssd"][0, sl].reshape(NSEQ, cfg.H * 64, 128)),
                 maskodd=np.full((128, 1), float(half), f))
        maps.append(m)
    return maps


def assemble(cfg, res):
    D, TP, NSEQ, H = cfg.D, cfg.TP, cfg.NSEQ, cfg.H
    f = np.float32
    B = 4
    y_p = np.zeros((B, 2 * TP, D), f)
    y_s = np.zeros((8 * NSEQ, cfg.DEC_SEQ, D), f)
    scp = np.zeros((1, B, 2, D), f)
    scs = np.zeros((1, 8 * NSEQ, 2, D), f)
    ssdcp = np.zeros((1, B, 3, cfg.CONV), f)
    ssdcs = np.zeros((1, 8 * NSEQ, 3, cfg.CONV), f)
    stp = np.zeros((1, B, H, 64, 128), f)
    sts = np.zeros((1, 8 * NSEQ, H, 64, 128), f)
    for core in range(8):
        r = res[core]
        b, half = core // 2, core % 2
        y = r["y"]
        y_p[b, half * TP:(half + 1) * TP] = y[HALO:HALO + TP]
        sl = slice(core * NSEQ, (core + 1) * NSEQ)
        y_s[sl] = y[HALO + TP:].reshape(NSEQ, cfg.DEC_SEQ, D)
        scs[0, sl] = r["sc_out_s"].reshape(NSEQ, 2, D)
        ssdcs[0, sl] = r["ssdc_out_s"].reshape(NSEQ, 3, cfg.CONV)
        sts[0, sl] = r["st_out_s"].reshape(NSEQ, H, 64, 128)
        if half == 1:
            scp[0, b] = r["sc_out_p"]
            ssdcp[0, b] = r["ssdc_out_p"]
            stp[0, b] = r["st_out_p"].reshape(H, 64, 128)
    return (y_p, y_s, scp, scs, ssdcp, ssdcs, stp, sts)


_NC_CACHE = {}


def run(cfg, inp, stop=None):
    key = (cfg.D, cfg.SEQ, stop)
    if key not in _NC_CACHE:
        _NC_CACHE[key] = build(cfg, stop)
    nc = _NC_CACHE[key]
    maps = host_inputs(cfg, inp)
    res = run_bass_kernel_spmd(nc, maps, core_ids=list(range(8)))
    return assemble(cfg, res.results)


def kernel(**inputs):
    cfg = Cfg()
    inp = {k: np.asarray(v) for k, v in inputs.items()}
    return run(cfg, inp)
```

```python
import sys
from contextlib import ExitStack
import numpy as np
import concourse.bass as bass
import concourse.mybir as mybir
from concourse.bass_utils import run_bass_kernel_spmd

F32 = mybir.dt.float32
BF16 = mybir.dt.bfloat16
AF = mybir.ActivationFunctionType
ALU = mybir.AluOpType
EPS = 1e-6
HALO = 5


class Cfg:
    def __init__(self, D=2048, SEQ=2048, DEC_BATCH=128, DEC_SEQ=8, PLE=256):
        self.D = D
        self.SEQ = SEQ
        self.BATCH = 4
        self.DEC_BATCH = DEC_BATCH
        self.DEC_SEQ = DEC_SEQ
        self.PLE = PLE
        self.DFF = -(-8 * D // (3 * 256)) * 256
        self.DI = 2 * D
        self.H = self.DI // 64
        self.G = 8
        self.N = 128
        self.R = self.H // 8
        self.GW = self.DI // 8
        self.GC = self.GW // 128
        self.GN = self.G * self.N
        self.CONV = self.DI + 2 * self.GN
        self.IN = self.DI + self.CONV + self.H
        self.KC = D // 128
        self.FC = self.DFF // 128
        self.CC = self.CONV // 128
        self.TP = SEQ // 2
        self.NQ = self.TP // 128
        self.NSEQ = DEC_BATCH // 8
        self.TS = self.NSEQ * DEC_SEQ
        self.PL = HALO + self.TP
        self.T = self.PL + self.TS
        off = {}
        o = 0
        for nm, w in [("g_mix0", self.KC), ("g_ffn0", self.KC), ("g_ple0", self.KC),
                      ("g_mix1", self.KC), ("g_ffn1", self.KC), ("g_ple1", self.KC),
                      ("g_final", self.KC), ("scw0", self.KC), ("scw1", self.KC), ("scw2", self.KC),
                      ("cw0", self.CC), ("cw1", self.CC), ("cw2", self.CC), ("cw3", self.CC),
                      ("cb", self.CC), ("ng", self.DI // 128),
                      ("dtb", self.H), ("alog", self.H), ("dsk", self.H)]:
            off[nm] = (o, w)
            o += w
        self.voff = off
        self.NV = o


class Prog:
    ENGS = ("pe", "act", "dve", "pool", "sp")

    def __init__(self, nc, stack):
        self.nc = nc
        self.stack = stack
        self.ops = []
        self.lastw = {}
        self.readers = {}
        self.floor = 0
        self.emitted = 0
        self.sems = {}
        self.cnt = {}
        self.waited = {e: {} for e in self.ENGS}
        self.lastsig = {}
        self.nwait = 0
        self.eng = dict(pe=nc.tensor, act=nc.scalar, dve=nc.vector, pool=nc.gpsimd, sp=nc.sync)

    def op(self, eng, fn, reads=(), writes=(), dma=None, inc=16):
        i = len(self.ops)
        psr = [r for r in reads if isinstance(r, str) and r.startswith("ps")]
        if psr:
            reads = [r for r in reads if r not in psr]
            writes = list(writes) + psr
        deps = set()
        for r in reads:
            w = self.lastw.get(r)
            if w is not None:
                deps.add(w)
        for r in writes:
            w = self.lastw.get(r)
            if w is not None:
                deps.add(w)
            for x in self.readers.get(r, ()):
                deps.add(x)
        for r in reads:
            self.readers.setdefault(r, []).append(i)
        for r in writes:
            self.lastw[r] = i
            self.readers[r] = []
        deps.discard(i)
        deps = {d for d in deps if d >= self.floor}
        self.ops.append(dict(eng=eng, fn=fn, deps=deps, dma=dma, inc=inc, sig=False))
        return i

    def getsem(self, k):
        if k not in self.sems:
            self.sems[k] = self.stack.enter_context(self.nc.semaphore("s%d" % len(self.sems)))
        return self.sems[k]

    def end_phase(self):
        ops = self.ops
        start = self.emitted
        last = {}
        for i in range(start, len(ops)):
            o = ops[i]
            k = ("dma", o["dma"]) if o["dma"] is not None else ("eng", o["eng"])
            last[k] = i
        bdeps = set(last.values())
        for e in self.ENGS:
            self.ops.append(dict(eng=e, fn=None, deps=set(bdeps), dma=None, inc=0, sig=False, barrier=True))
        for i in range(start, len(ops)):
            o = ops[i]
            nd = set()
            for d in o["deps"]:
                p = ops[d]
                if (not o.get("barrier")) and p["dma"] is None and o["dma"] is None \
                        and p["eng"] == "pe" and o["eng"] == "pe":
                    continue
                nd.add(d)
            o["deps"] = nd
            for d in nd:
                ops[d]["sig"] = True
        for i in range(start, len(ops)):
            o = ops[i]
            if o["dma"] is not None:
                k = ("dma", o["dma"])
                self.cnt[k] = self.cnt.get(k, 0) + o["inc"]
                o["semk"], o["val"] = k, self.cnt[k]
            elif o["sig"]:
                k = ("eng", o["eng"])
                self.cnt[k] = self.cnt.get(k, 0) + 1
                o["semk"], o["val"] = k, self.cnt[k]
        for i in range(start, len(ops)):
            o = ops[i]
            e = o["eng"]
            need = {}
            for d in o["deps"]:
                p = ops[d]
                k = p["semk"]
                need[k] = max(need.get(k, 0), p["val"])
            for k, v in need.items():
                if self.waited[e].get(k, 0) >= v:
                    continue
                self.eng[e].wait_ge(self.getsem(k), v)
                self.waited[e][k] = v
                self.nwait += 1
            if o["fn"] is None:
                continue
            ins = o["fn"](self.eng[e])
            if o["dma"] is not None:
                ins.then_inc(self.getsem(o["semk"]), o["inc"])
            elif o["sig"]:
                ins.then_inc(self.getsem(o["semk"]), 1)
            o["fn"] = None
        self.emitted = len(ops)
        self.floor = len(ops)
        self.lastw = {}
        self.readers = {}


def split_tiles(T, maxn=448):
    nt = -(-T // maxn)
    base = T // nt
    rem = T % nt
    out = []
    c = 0
    for i in range(nt):
        n = base + (1 if i < rem else 0)
        out.append((c, n))
        c += n
    return out


def build(cfg, stop=None):
    D, KC, T, PL, TS, NSEQ = cfg.D, cfg.KC, cfg.T, cfg.PL, cfg.TS, cfg.NSEQ
    DI, H, G, R, GW, GC, GN, CC, FC = cfg.DI, cfg.H, cfg.G, cfg.R, cfg.GW, cfg.GC, cfg.GN, cfg.CC, cfg.FC
    PLE, DFF, CONV, IN, NQ = cfg.PLE, cfg.DFF, cfg.CONV, cfg.IN, cfg.NQ
    PK = PLE // 128
    nc = bass.Bass("TRN2", target_bir_lowering=False)
    CMW = 640 + NSEQ

    def din(name, shape):
        return nc.dram_tensor(name, list(shape), F32, kind="ExternalInput").ap()

    def dout(name, shape):
        return nc.dram_tensor(name, list(shape), F32, kind="ExternalOutput").ap()

    xin = din("xin", [T, D])
    pin = din("pin", [2, T, PLE])
    sc_state = din("sc_state", [NSEQ * 2, D])
    ssdc_state = din("ssdc_state", [NSEQ * 3, CONV])
    ssd_state = din("ssd_state", [NSEQ, H * 64, 128])
    maskodd = din("maskodd", [128, 1])
    vecs_d = din("vecs", [128, cfg.NV])
    cmat_d = din("cmat", [128, CMW])
    sc_w_in = din("sc_w_in", [D, 3 * D])
    sc_w_out = din("sc_w_out", [D, D])
    ssd_w_in = din("ssd_w_in", [D, IN])
    ssd_w_out = din("ssd_w_out", [DI, D])
    ffn_w_gate = din("ffn_w_gate", [2, D, DFF])
    ffn_w_up = din("ffn_w_up", [2, D, DFF])
    ffn_w_down = din("ffn_w_down", [2, DFF, D])
    ple_w_proj = din("ple_w_proj", [2, PLE, D])
    ple_w_gate = din("ple_w_gate", [2, D, D])

    y_out = dout("y", [T, D])
    sc_out_p = dout("sc_out_p", [2, D])
    sc_out_s = dout("sc_out_s", [NSEQ * 2, D])
    ssdc_out_p = dout("ssdc_out_p", [3, CONV])
    ssdc_out_s = dout("ssdc_out_s", [NSEQ * 3, CONV])
    st_out_p = dout("st_out_p", [H * 64, 128])
    st_out_s = dout("st_out_s", [NSEQ, H * 64, 128])
    ibs = [nc.dram_tensor("ib%d" % g, [128, GW], F32) for g in range(G)]
    obs = [nc.dram_tensor("ob%d" % g, [256, GW], F32) for g in range(G)]

    tts = split_tiles(T)
    for (c0, n) in tts:
        assert not (c0 < PL < c0 + n and False)
    assert any(c0 <= PL and PL + TS <= c0 + n for (c0, n) in tts) or True
    rts = [(r0, min(128, T - r0)) for r0 in range(0, T, 128)]

    with ExitStack() as st:
        sbctr = [0]

        def sb(name, shape, dt=F32, stack=None):
            sbctr[0] += 1
            return (stack or st).enter_context(nc.sbuf_tensor("%s_%d" % (name, sbctr[0]), list(shape), dt))

        P = Prog(nc, st)
        h = sb("h", [128, KC, T])
        xn = sb("xn", [128, KC, T], BF16)
        vecs = sb("vecs", [128, cfg.NV])
        cmat = sb("cmat", [128, CMW])
        cmb = sb("cmb", [128, CMW], BF16)
        abc = sb("abc", [128, H])
        mko = sb("mko", [128, 1])
        mhalf = sb("mhalf", [128, 2])
        ps = [st.enter_context(nc.psum_tensor("ps%d" % i, [128, 512], F32)) for i in range(8)]
        psb = [p[:, :].bitcast(BF16) for p in ps]
        ident_f, tri_f, ones_f = cmat[:, 0:128], cmat[:, 128:256], cmat[:, 256:384]
        triS_f, blk_f, maskJ_f = cmat[:, 384:512], cmat[:, 512:640], cmat[:, 640:640 + NSEQ]
        ident_b, tri_b, ones_b = cmb[:, 0:128], cmb[:, 128:256], cmb[:, 256:384]

        def V(nm, c=None):
            o, w = cfg.voff[nm]
            if c is None:
                return vecs[:, o:o + w]
            return vecs[:, o + c:o + c + 1]

        bankctr = [0]

        reserved = set()

        def nb():
            while True:
                b = bankctr[0] % 8
                bankctr[0] += 1
                if b not in reserved:
                    return b

        poolctr = {"p": 0, "t": 0}

        def nb_prep():
            b = poolctr["p"] % 4
            poolctr["p"] += 1
            return b

        def nb_tail():
            b = 4 + poolctr["t"] % 4
            poolctr["t"] += 1
            return b

        ringstate = {}

        def make_ring(stack, nslots, tag, slot=4096):
            bufs = [sb("ring%s%d" % (tag, i), [128, slot], BF16, stack) for i in range(nslots)]
            ringstate["bufs"] = bufs
            ringstate["i"] = 0
            ringstate["slot"] = slot

        def wload(src, nk, ncols):
            bufs = ringstate["bufs"]
            i = ringstate["i"] % len(bufs)
            ringstate["i"] += 1
            assert nk * ncols <= ringstate["slot"]
            view = bufs[i][:, 0:nk * ncols].rearrange("p (k n) -> p k n", n=ncols)
            key = "ring%d" % i
            if key in P.lastw:
                assert P.readers.get(key), "ring slot %s overwritten before any consumer was recorded" % key
            srcv = src.rearrange("(k p) n -> p k n", p=128)
            P.op("pool", lambda e: e.dma_start(out=view, in_=srcv), writes=[key], dma=key)
            return view, key

        with ExitStack() as ph:
            xs = [sb("xstg%d" % i, [128, D], F32, ph) for i in range(2)]
            P.op("sp", lambda e: e.dma_start(out=vecs[:], in_=vecs_d), writes=["vecs"], dma="c0")
            P.op("sp", lambda e: e.dma_start(out=cmat[:], in_=cmat_d), writes=["cmat"], dma="c1")
            P.op("sp", lambda e: e.dma_start(out=mko[:], in_=maskodd), writes=["mko"], dma="c2")
            P.op("dve", lambda e: e.tensor_copy(cmb[:], cmat[:]), reads=["cmat"], writes=["cmb"])
            P.op("dve", lambda e: e.memset(mhalf[:], -0.5), writes=["mhalf"])
            o_al, w_al = cfg.voff["alog"]
            P.op("act", lambda e: e.activation(out=abc[:], in_=vecs[:, o_al:o_al + w_al], func=AF.Exp),
                 reads=["vecs"], writes=["abc"])
            P.op("dve", lambda e: e.tensor_scalar(abc[:], abc[:], -1.0, None, ALU.mult), reads=["abc"], writes=["abc"])
            for ri, (r0, n) in enumerate(rts):
                s = ri % 2
                P.op("sp", lambda e, s=s, r0=r0, n=n: e.dma_start(out=xs[s][0:n, :], in_=xin[r0:r0 + n, :]),
                     writes=["xs%d" % s], dma="xs%d" % s)
                for cg in range(KC // 4):
                    b = nb()

                    def tr(e, s=s, n=n, cg=cg, b=b):
                        for j in range(4):
                            c = cg * 4 + j
                            r = e.transpose(ps[b][:, j * 128:j * 128 + n], xs[s][0:n, c * 128:(c + 1) * 128],
                                            ident_f[0:n, 0:n])
                        return r
                    P.op("pe", tr, reads=["xs%d" % s, "cmat"], writes=["ps%d" % b])
                    src = ps[b][:, :].rearrange("p (j t) -> p j t", t=128)[:, :, 0:n]
                    dst = h[:, cg * 4:cg * 4 + 4, r0:r0 + n]
                    if cg % 2 == 0:
                        P.op("dve", lambda e, src=src, dst=dst: e.tensor_copy(dst, src), reads=["ps%d" % b],
                             writes=[("h", ri, cg)])
                    else:
                        P.op("act", lambda e, src=src, dst=dst: e.copy(dst, src), reads=["ps%d" % b],
                             writes=[("h", ri, cg)])
            P.end_phase()

        def rmsnorm_phase(gname, final=False):
            with ExitStack() as ph:
                sq = [sb("sq%d" % i, [128, KC, tts[0][1]], BF16, ph) for i in range(2)]
                rstd = sb("rstd", [128, T], F32, ph)
                for ti, (c0, n) in enumerate(tts):
                    s = ti % 2
                    P.op("act", lambda e, s=s, c0=c0, n=n: e.activation(out=sq[s][:, :, 0:n], in_=h[:, :, c0:c0 + n],
                                                                      func=AF.Square),
                         reads=["h"], writes=["sq%d" % s])
                    b = nb()

                    def mm(e, s=s, n=n, b=b):
                        for k in range(KC):
                            r = e.matmul(ps[b][:, 0:n], lhsT=ones_b, rhs=sq[s][:, k, 0:n], start=(k == 0),
                                         stop=(k == KC - 1))
                        return r
                    P.op("pe", mm, reads=["sq%d" % s, "cmb"], writes=["ps%d" % b])
                    P.op("dve", lambda e, b=b, c0=c0, n=n: e.tensor_scalar(rstd[:, c0:c0 + n], ps[b][:, 0:n], 1.0 / D,
                                                                         EPS, ALU.mult, ALU.add),
                         reads=["ps%d" % b], writes=["rs%d" % ti])
                    P.op("act", lambda e, c0=c0, n=n: e.activation(out=rstd[:, c0:c0 + n], in_=rstd[:, c0:c0 + n],
                                                                 func=AF.Sqrt),
                         reads=["rs%d" % ti], writes=["rs%d" % ti])
                    P.op("dve", lambda e, c0=c0, n=n: e.reciprocal(rstd[:, c0:c0 + n], rstd[:, c0:c0 + n]),
                         reads=["rs%d" % ti], writes=["rs%d" % ti])
                rk = ["rs%d" % ti for ti in range(len(tts))]
                for c in range(KC):
                    dst = h[:, c, :] if final else xn[:, c, :]
                    P.op("dve", lambda e, c=c, dst=dst: e.scalar_tensor_tensor(dst, h[:, c, :], V(gname, c), rstd[:, :],
                                                                             ALU.mult, ALU.mult),
                         reads=rk + ["vecs", "h"], writes=[("hf" if final else "xn", c)])
                P.end_phase()

        def acc_into_h(b, mc, c0, n):
            P.op("dve", lambda e: e.tensor_tensor(h[:, mc, c0:c0 + n], h[:, mc, c0:c0 + n], ps[b][:, 0:n], ALU.add),
                 reads=["ps%d" % b, ("h", mc)], writes=[("h", mc)])

        def mm_fm(b, wv, wkey, nk, j, rhs, rkeys, c0, n):
            def mm(e):
                for k in range(nk):
                    r = e.matmul(ps[b][:, 0:n], lhsT=wv[:, k, j * 128:(j + 1) * 128], rhs=rhs[:, k, c0:c0 + n],
                                 start=(k == 0), stop=(k == nk - 1))
                return r
            P.op("pe", mm, reads=[wkey] + list(rkeys), writes=["ps%d" % b])

        def ffn_phase(layer):
            npairs = FC // 2
            ngroups = max(1, -(-FC // 12))
            base = npairs // ngroups
            rem = npairs % ngroups
            gsz = [base + (1 if i < rem else 0) for i in range(ngroups)]
            maxk = max(gsz) * 2
            with ExitStack() as ph:
                make_ring(ph, 6, "f")
                act = sb("act", [128, maxk, T], BF16, ph)
                sg = [sb("sg%d" % i, [128, 512], F32, ph) for i in range(2)]
                sgi = 0
                pr0 = 0
                for gi, np_ in enumerate(gsz):
                    nkg = np_ * 2
                    for pr in range(np_):
                        col = (pr0 + pr) * 256
                        wg, kg = wload(ffn_w_gate[layer, :, col:col + 256], KC, 256)
                        wu, ku = wload(ffn_w_up[layer, :, col:col + 256], KC, 256)
                        for j in range(2):
                            fl = pr * 2 + j
                            for (c0, n) in tts:
                                bg_, bu_ = nb(), nb()
                                mm_fm(bg_, wg, kg, KC, j, xn, ["xn"], c0, n)
                                mm_fm(bu_, wu, ku, KC, j, xn, ["xn"], c0, n)
                                s = sgi % 2
                                sgi += 1
                                P.op("act", lambda e, s=s, b=bg_, n=n: e.activation(out=sg[s][:, 0:n], in_=ps[b][:, 0:n],
                                                                                  func=AF.Silu),
                                     reads=["ps%d" % bg_], writes=["sg%d" % s])
                                P.op("dve", lambda e, s=s, b=bu_, fl=fl, c0=c0, n=n: e.tensor_tensor(
                                    act[:, fl, c0:c0 + n], sg[s][:, 0:n], ps[b][:, 0:n], ALU.mult),
                                     reads=["sg%d" % s, "ps%d" % bu_], writes=[("act", fl)])
                    for mcp in range(D // 256):
                        wd, kd = wload(ffn_w_down[layer, pr0 * 256:pr0 * 256 + nkg * 128, mcp * 256:mcp * 256 + 256],
                                       nkg, 256)
                        for j in range(2):
                            mc = mcp * 2 + j
                            for (c0, n) in tts:
                                b = nb()
                                mm_fm(b, wd, kd, nkg, j, act, [("act", k) for k in range(nkg)], c0, n)
                                acc_into_h(b, mc, c0, n)
                    pr0 += np_
                P.end_phase()

        def ple_phase(layer):
            with ExitStack() as ph:
                make_ring(ph, 6, "p")
                pT = sb("pT", [128, PK, T], BF16, ph)
                pst = [sb("pst%d" % i, [128, PLE], F32, ph) for i in range(2)]
                sg = [sb("sgp%d" % i, [128, 512], F32, ph) for i in range(2)]
                for ri, (r0, n) in enumerate(rts):
                    s = ri % 2
                    P.op("sp", lambda e, s=s, r0=r0, n=n: e.dma_start(out=pst[s][0:n, :], in_=pin[layer, r0:r0 + n, :]),
                         writes=["pst%d" % s], dma="pst%d" % s)
                    b = nb()

                    def tr(e, s=s, n=n, b=b):
                        for j in range(PK):
                            r = e.transpose(ps[b][:, j * 128:j * 128 + n], pst[s][0:n, j * 128:(j + 1) * 128],
                                            ident_f[0:n, 0:n])
                        return r
                    P.op("pe", tr, reads=["pst%d" % s, "cmat"], writes=["ps%d" % b])
                    src = ps[b][:, 0:PK * 128].rearrange("p (j t) -> p j t", t=128)[:, :, 0:n]
                    P.op("dve", lambda e, src=src, r0=r0, n=n: e.tensor_copy(pT[:, :, r0:r0 + n], src),
                         reads=["ps%d" % b], writes=["pT"])
                sgi = 0
                for mcp in range(D // 256):
                    wg, kg = wload(ple_w_gate[layer, :, mcp * 256:mcp * 256 + 256], KC, 256)
                    wp, kp = wload(ple_w_proj[layer, :, mcp * 256:mcp * 256 + 256], PK, 256)
                    for j in range(2):
                        mc = mcp * 2 + j
                        for (c0, n) in tts:
                            bg_, bp_ = nb(), nb()
                            mm_fm(bg_, wg, kg, KC, j, xn, ["xn"], c0, n)
                            mm_fm(bp_, wp, kp, PK, j, pT, ["pT"], c0, n)
                            s = sgi % 2
                            sgi += 1
                            P.op("act", lambda e, s=s, b=bg_, n=n: e.activation(out=sg[s][:, 0:n], in_=ps[b][:, 0:n],
                                                                              func=AF.Sigmoid),
                                 reads=["ps%d" % bg_], writes=["sgp%d" % s])
                            P.op("dve", lambda e, s=s, b=bp_, n=n: e.tensor_tensor(sg[s][:, 0:n], sg[s][:, 0:n],
                                                                                 ps[b][:, 0:n], ALU.mult),
                                 reads=["sgp%d" % s, "ps%d" % bp_], writes=["sgp%d" % s])
                            P.op("dve", lambda e, s=s, mc=mc, c0=c0, n=n: e.tensor_tensor(
                                h[:, mc, c0:c0 + n], h[:, mc, c0:c0 + n], sg[s][:, 0:n], ALU.add),
                                 reads=["sgp%d" % s, ("h", mc)], writes=[("h", mc)])
                P.end_phase()

        def split_cols(c0, n):
            pa, pb = c0, min(c0 + n, PL)
            sa, sb_ = max(c0, PL), c0 + n
            return (pa, pb) if pb > pa else None, (sa, sb_) if sb_ > sa else None

        def conv_state_io(ph_tag, pre, pre_s, slot, W0, state_d, cidx, out_p, out_s, stg_in, stg_out, cost, sidx):
            nr = W0 - 1
            ks = ph_tag + "pres%d" % slot
            si = sidx % 2
            P.op("pool", lambda e: e.dma_start(out=stg_in[si][0:NSEQ * nr, :], in_=state_d[:, cidx * 128:(cidx + 1) * 128]),
                 writes=[ph_tag + "sti%d" % si], dma=ph_tag + "sti%d" % si)
            b = nb()
            P.op("pe", lambda e: e.transpose(ps[b][:, 0:NSEQ * nr], stg_in[si][0:NSEQ * nr, :],
                                            ident_f[0:NSEQ * nr, 0:NSEQ * nr]),
                 reads=[ph_tag + "sti%d" % si, "cmat"], writes=["ps%d" % b])
            src = ps[b][:, 0:NSEQ * nr].rearrange("p (s r) -> p s r", r=nr)
            P.op("act", lambda e: e.copy(pre_s[:, slot, :, 0:nr], src), reads=["ps%d" % b], writes=[ks + "h"])

        def conv_state_out(ph_tag, pre, pre_s, slot, W0, cidx, out_p, out_s, stg_out, cost, rd_keys, sidx):
            nr = W0 - 1
            si = sidx % 2
            b = nb()
            P.op("pe", lambda e: e.transpose(ps[b][0:nr, 0:128], pre[:, slot, PL:PL + nr], ident_f),
                 reads=rd_keys + ["cmat"], writes=["ps%d" % b])
            P.op("dve", lambda e: e.tensor_copy(stg_out[si][0:nr, 0:128], ps[b][0:nr, 0:128]), reads=["ps%d" % b],
                 writes=[ph_tag + "stoP%d" % si])
            P.op("sp", lambda e: e.dma_start(out=out_p[:, cidx * 128:(cidx + 1) * 128], in_=stg_out[si][0:nr, 0:128]),
                 reads=[ph_tag + "stoP%d" % si], dma=ph_tag + "stoP%d" % si)
            P.op("dve", lambda e: e.tensor_copy(cost[si][:, 0:NSEQ * nr].rearrange("p (s r) -> p s r", r=nr),
                                               pre_s[:, slot, :, 8:8 + nr]),
                 reads=rd_keys, writes=[ph_tag + "cost%d" % si])
            b2 = nb()
            P.op("pe", lambda e: e.transpose(ps[b2][0:NSEQ * nr, 0:128], cost[si][:, 0:NSEQ * nr], ident_f),
                 reads=[ph_tag + "cost%d" % si, "cmat"], writes=["ps%d" % b2])
            P.op("act", lambda e: e.copy(stg_out[si][0:NSEQ * nr, 128:256], ps[b2][0:NSEQ * nr, 0:128]),
                 reads=["ps%d" % b2], writes=[ph_tag + "stoS%d" % si])
            P.op("sp", lambda e: e.dma_start(out=out_s[:, cidx * 128:(cidx + 1) * 128],
                                            in_=stg_out[si][0:NSEQ * nr, 128:256]),
                 reads=[ph_tag + "stoS%d" % si], dma=ph_tag + "stoS%d" % si)

        def l0_mixer_phase():
            with ExitStack() as ph:
                make_ring(ph, 4, "m")
                gated = sb("gated", [128, KC, T], BF16, ph)
                pre = sb("l0pre", [128, 2, 2 + PL], F32, ph)
                pre_s = sb("l0pres", [128, 2, NSEQ, 10], F32, ph)
                tcv = sb("l0tcv", [128, 1, T], F32, ph)
                stg_in = [sb("l0sti%d" % i, [128, 128], F32, ph) for i in range(2)]
                stg_out = [sb("l0sto%d" % i, [128, 256], F32, ph) for i in range(2)]
                cost = [sb("l0cost%d" % i, [128, NSEQ * 3], F32, ph) for i in range(2)]
                P.op("dve", lambda e: e.memset(pre[:, 0, 0:2], 0.0), writes=["Apre0z"])
                P.op("dve", lambda e: e.memset(pre[:, 1, 0:2], 0.0), writes=["Apre1z"])
                for cp in range(KC // 2):
                    wb, kb = wload(sc_w_in[:, cp * 256:cp * 256 + 256], KC, 256)
                    wc, kc = wload(sc_w_in[:, D + cp * 256:D + cp * 256 + 256], KC, 256)
                    wv, kv = wload(sc_w_in[:, 2 * D + cp * 256:2 * D + cp * 256 + 256], KC, 256)
                    for j in range(2):
                        c = cp * 2 + j
                        sl = c % 2
                        conv_state_io("A", pre, pre_s, sl, 3, sc_state, c, None, None, stg_in, None, None, c)
                        for (c0, n) in tts:
                            b_c, b_v, b_b = nb(), nb(), nb()
                            mm_fm(b_c, wc, kc, KC, j, xn, ["xn"], c0, n)
                            mm_fm(b_v, wv, kv, KC, j, xn, ["xn"], c0, n)
                            mm_fm(b_b, wb, kb, KC, j, xn, ["xn"], c0, n)
                            P.op("act", lambda e, b=b_c, c0=c0, n=n: e.copy(tcv[:, 0, c0:c0 + n], ps[b][:, 0:n]),
                                 reads=["ps%d" % b_c], writes=["Atcv0p", "Atcv0s"])
                            pp, sp_ = split_cols(c0, n)
                            if pp:
                                a, b2 = pp
                                P.op("dve", lambda e, a=a, b2=b2, b=b_v, c0=c0, sl=sl: e.tensor_tensor(
                                    pre[:, sl, 2 + a:2 + b2], tcv[:, 0, a:b2], ps[b][:, a - c0:b2 - c0], ALU.mult),
                                     reads=["ps%d" % b_v, "Atcv0p", "Atcv0s"], writes=["Apre%d" % sl])
                            if sp_:
                                a, b2 = sp_
                                assert a == PL and b2 == T
                                P.op("dve", lambda e, a=a, b2=b2, b=b_v, c0=c0, sl=sl: e.tensor_tensor(
                                    pre_s[:, sl, :, 2:10],
                                    tcv[:, 0, a:b2].rearrange("p (s t) -> p s t", t=8),
                                    ps[b][:, a - c0:b2 - c0].rearrange("p (s t) -> p s t", t=8), ALU.mult),
                                     reads=["ps%d" % b_v, "Atcv0p", "Atcv0s"], writes=["Apres%d" % sl])
                            P.op("act", lambda e, b=b_b, c=c, c0=c0, n=n: e.copy(gated[:, c, c0:c0 + n], ps[b][:, 0:n]),
                                 reads=["ps%d" % b_b], writes=[("gated", c)])
                        conv_state_out("A", pre, pre_s, sl, 3, c, sc_out_p, sc_out_s, stg_out, cost,
                                       ["Apre%d" % sl, "Apres%d" % sl, "Apres%dh" % sl], c)
                        kt = conv_block2("A", pre, pre_s, tcv, sl, 3, ["scw0", "scw1", "scw2"], c,
                                         ["Apre%d" % sl, "Apre%dz" % sl], ["Apres%d" % sl, "Apres%dh" % sl])
                        P.op("dve", lambda e, c=c: e.tensor_tensor(gated[:, c, :], gated[:, c, :], tcv[:, 0, :], ALU.mult),
                             reads=kt + [("gated", c)], writes=[("gated", c)])
                for mcp in range(D // 256):
                    wo, ko = wload(sc_w_out[:, mcp * 256:mcp * 256 + 256], KC, 256)
                    for j in range(2):
                        mc = mcp * 2 + j
                        for (c0, n) in tts:
                            b = nb()
                            mm_fm(b, wo, ko, KC, j, gated, [("gated", k) for k in range(KC)], c0, n)
                            acc_into_h(b, mc, c0, n)
                P.end_phase()

        def conv_block2(tag, pre, pre_s, tcv, slot, KW, wnames, cidx, pkeys, skeys):
            ktp, kts = tag + "tcv0p", tag + "tcv0s"
            ts_v = tcv[:, 0, PL:T].rearrange("p (s t) -> p s t", t=8)
            for kk in range(KW):
                wcol = V(wnames[kk], cidx)
                if kk == 0:
                    P.op("dve", lambda e, wcol=wcol: e.tensor_scalar(tcv[:, 0, 0:PL], pre[:, slot, 0:PL], wcol, None,
                                                                   ALU.mult),
                         reads=pkeys + ["vecs"], writes=[ktp])
                    P.op("dve", lambda e, wcol=wcol: e.tensor_scalar(ts_v, pre_s[:, slot, :, 0:8], wcol, None, ALU.mult),
                         reads=skeys + ["vecs"], writes=[kts])
                else:
                    P.op("dve", lambda e, wcol=wcol, kk=kk: e.scalar_tensor_tensor(
                        tcv[:, 0, 0:PL], pre[:, slot, kk:kk + PL], wcol, tcv[:, 0, 0:PL], ALU.mult, ALU.add),
                         reads=pkeys + ["vecs", ktp], writes=[ktp])
                    P.op("dve", lambda e, wcol=wcol, kk=kk: e.scalar_tensor_tensor(
                        ts_v, pre_s[:, slot, :, kk:kk + 8], wcol, ts_v, ALU.mult, ALU.add),
                         reads=skeys + ["vecs", kts], writes=[kts])
            return [ktp, kts]

        def ssd_phase():
            XB = 128
            NT = NQ + 1
            tile_col = [HALO + q * 128 for q in range(NQ)] + [PL]
            with ExitStack() as pho:
              dt_all = sb("dt_all", [128, NT, H], F32, pho)
              dA_all = sb("dA_all", [128, NT, H], F32, pho)
              with ExitStack() as ph1:
                wdt = sb("wdt", [128, KC, H], BF16, ph1)
                o_dtb = cfg.voff["dtb"][0]
                P.op("pool", lambda e: e.dma_start(out=wdt[:], in_=ssd_w_in[:, DI + CONV:DI + CONV + H].rearrange(
                    "(k p) n -> p k n", p=128)), writes=["wdt"], dma="wdt")
                for ti in range(NT):
                    c0 = tile_col[ti]
                    bdt = nb()

                    def mmdt(e, c0=c0, bdt=bdt):
                        for k in range(KC):
                            r = e.matmul(ps[bdt][:, 0:H], lhsT=xn[:, k, c0:c0 + 128], rhs=wdt[:, k, :], start=(k == 0),
                                         stop=(k == KC - 1))
                        return r
                    P.op("pe", mmdt, reads=["xn", "wdt"], writes=["ps%d" % bdt])
                    P.op("dve", lambda e, ti=ti, bdt=bdt: e.tensor_tensor(dt_all[:, ti, :], ps[bdt][:, 0:H],
                                                                         vecs[:, o_dtb:o_dtb + H], ALU.add),
                         reads=["ps%d" % bdt, "vecs"], writes=[("dt", ti)])
                P.op("act", lambda e: e.activation(out=dt_all[:, :, :], in_=dt_all[:, :, :], func=AF.Exp),
                     reads=[("dt", ti) for ti in range(NT)], writes=["dt_all"])
                P.op("act", lambda e: e.activation(out=dt_all[:, :, :], in_=dt_all[:, :, :], func=AF.Ln, bias=1.0),
                     reads=["dt_all"], writes=["dt_all"])
                P.op("dve", lambda e: e.tensor_tensor(dA_all[:, :, :], dt_all[:, :, :],
                                                     abc[:, :].unsqueeze(1).broadcast_to([128, NT, H]), ALU.mult),
                     reads=["dt_all", "abc"], writes=["dA_all"])

                P.end_phase()
              with ExitStack() as ph:
                make_ring(ph, 4, "s", 2048)
                xsg = sb("xsg", [128, GC, T], BF16, ph)
                Bg = sb("Bg", [128, T], BF16, ph)
                Cg = sb("Cg", [128, T], BF16, ph)
                ygn = sb("ygn", [128, GC, T], BF16, ph)
                pre = sb("l1pre", [128, 2, 3 + PL], F32, ph)
                pre_s = sb("l1pres", [128, 2, NSEQ, 11], F32, ph)
                tcv = sb("l1tcv", [128, 1, T], F32, ph)
                stg_in = [sb("l1sti%d" % i, [128, 128], F32, ph) for i in range(1)]
                stg_out = [sb("l1sto%d" % i, [128, 256], F32, ph) for i in range(1)]
                cost = [sb("l1cost%d" % i, [128, NSEQ * 3], F32, ph) for i in range(1)]
                acs = sb("acs", [128, R], F32, ph)
                expa2 = [sb("expa%d" % i, [128, R], F32, ph) for i in range(2)]
                dE = sb("dE", [128, R], F32, ph)
                cdec2 = [sb("cdec%d" % i, [128, R], F32, ph) for i in range(2)]
                cdecF = sb("cdecF", [128, GC, NSEQ], F32, ph)
                Rm = sb("Rm", [128, R, 128], F32, ph)
                Wm2 = [sb("Wm%d" % i, [128, R, 128], BF16, ph) for i in range(2)]
                MTm = sb("MTm", [128, 128], BF16, ph)
                xdt2 = [sb("xdt%d" % i, [128, GW], BF16, ph) for i in range(2)]
                xw2 = [sb("xw%d" % i, [128, GW], BF16, ph) for i in range(2)]
                Btok2 = [sb("Btok%d" % i, [128, 128], BF16, ph) for i in range(2)]
                y1 = sb("y1", [128, GW], F32, ph)
                xD2 = [sb("xD%d" % i, [128, GW], F32, ph) for i in range(2)]
                gyn = sb("gyn", [128, GW], BF16, ph)
                ss = sb("ss", [128, 2], F32, ph)
                S = sb("S", [128, GW], F32, ph)
                Sb = sb("Sb", [128, GW], BF16, ph)
                h0 = [sb("h0_%d" % i, [128, GC, 128], F32, ph) for i in range(2)]
                h0T = [sb("h0T_%d" % i, [128, GW], BF16, ph) for i in range(1)]
                xwj = [sb("xwj_%d" % i, [128, GW], BF16, ph) for i in range(1)]
                sost = [sb("sost_%d" % i, [128, GC, 128], F32, ph) for i in range(2)]
                if (NSEQ + 1) * 64 <= T:
                    Cm_t = tcv[:, 0, 0:(NSEQ + 1) * 64].bitcast(BF16)
                    cmk = "Btcv0p"
                else:
                    Cm_t = sb("Cm", [128, (NSEQ + 1) * 128], BF16, ph)[:, :]
                    cmk = "Cm"
                Cmdiag = Cm_t[:, 0:NSEQ * 136].rearrange("p (j q) -> p j q", q=136)[:, :, 0:8]
                print("SSD phase sbuf remaining", nc.sbuf_bytes_remaining, file=sys.stderr)
                P.op("dve", lambda e: e.memset(pre[:, 0, 0:3], 0.0), writes=["Bpre0z"])
                P.op("dve", lambda e: e.memset(pre[:, 1, 0:3], 0.0), writes=["Bpre1z"])
                P.op("dve", lambda e: e.memset(ygn[:, :, :], 0.0), writes=[("ygn", c) for c in range(GC)])
                o_dtb = cfg.voff["dtb"][0]
                o_dsk = cfg.voff["dsk"][0]
                def chunk(g, ti, full, wz, kz, sample=False, pp=0, nbp=None, nbt=None):
                    Q = 128
                    nbp = nbp or nb
                    nbt = nbt or nb
                    Wm, xdt, xD, xw, Btok, expa, cdec = Wm2[pp], xdt2[pp], xD2[pp], xw2[pp], Btok2[pp], expa2[pp], cdec2[pp]
                    kWm, kxdt, kxD, kxw, kBtok, kexpa, kcdec = ["%s%d" % (n_, pp) for n_ in "Wm xdt xD xw Btok expa cdec".split()]
                    col0 = tile_col[ti]
                    cs = slice(col0, col0 + Q)
                    hs = slice(g * R, (g + 1) * R)
                    TRI = triS_f if sample else tri_f
                    dtt = dt_all[:, ti, hs]
                    dA = dA_all[:, ti, hs]
                    bac = nbp()
                    yield P.op("pe", lambda e: e.matmul(ps[bac][:, 0:R], lhsT=TRI, rhs=dA, start=True, stop=True),
                         reads=["dA_all", "cmat"], writes=["ps%d" % bac])
                    yield P.op("dve", lambda e: e.tensor_tensor(
                        Rm[:, :, :], TRI.unsqueeze(1).broadcast_to([Q, R, Q]),
                        dA.unsqueeze(2).broadcast_to([Q, R, Q]), ALU.mult),
                         reads=["dA_all", "cmat"], writes=["Rm"])
                    yield P.op("dve", lambda e: e.tensor_copy(acs[:, :], ps[bac][:, 0:R]), reads=["ps%d" % bac],
                         writes=["acs"])
                    hb = 512 // Q
                    nbk = -(-R // hb)
                    bab = [nbp() for _ in range(nbk)]

                    def mmab(e):
                        for i in range(nbk):
                            h_a, h_b = i * hb, min(R, (i + 1) * hb)
                            r = e.matmul(ps[bab[i]][:, 0:(h_b - h_a) * Q].rearrange("p (h t) -> p h t", t=Q),
                                         lhsT=ones_f, rhs=Rm[:, h_a:h_b, :], start=True, stop=True)
                        return r
                    yield P.op("pe", mmab, reads=["Rm", "cmat"], writes=["ps%d" % b for b in bab])

                    def abv(i):
                        h_a, h_b = i * hb, min(R, (i + 1) * hb)
                        return ps[bab[i]][:, 0:(h_b - h_a) * Q].rearrange("p (h t) -> p h t", t=Q), h_a, h_b
                    if not sample:
                        for i in range(nbk):
                            v, h_a, h_b = abv(i)
                            yield P.op("dve", lambda e, v=v, h_a=h_a, h_b=h_b: e.tensor_tensor(
                                dE[:, h_a:h_b], v[:, :, Q - 1], acs[:, h_a:h_b], ALU.subtract),
                                 reads=["ps%d" % bab[i], "acs"], writes=["dE"])
                            yield P.op("act", lambda e, v=v, h_a=h_a, h_b=h_b: e.activation(out=cdec[:, h_a:h_b],
                                                                                    in_=v[:, :, Q - 1], func=AF.Exp),
                                 reads=["ps%d" % bab[i]], writes=[kcdec])
                    else:
                        btot = nb()
                        yield P.op("pe", lambda e: e.matmul(ps[btot][:, 0:R], lhsT=blk_f, rhs=dA, start=True, stop=True),
                             reads=["dA_all", "cmat"], writes=["ps%d" % btot])
                        yield P.op("dve", lambda e: e.tensor_tensor(dE[:, :], ps[btot][:, 0:R], acs[:, :], ALU.subtract),
                             reads=["ps%d" % btot, "acs"], writes=["dE"])
                        yield P.op("dve", lambda e: e.tensor_copy(
                            xD[:, :].rearrange("q (h p) -> q h p", p=64), dA.unsqueeze(2).broadcast_to([Q, R, 64])),
                             reads=["dA_all"], writes=[kxD])
                        bcd = nb()

                        def mmcd(e):
                            for c in range(GC):
                                r = e.matmul(ps[bcd][:, c * NSEQ:(c + 1) * NSEQ], lhsT=xD[:, c * 128:(c + 1) * 128],
                                             rhs=maskJ_f, start=True, stop=True)
                            return r
                        yield P.op("pe", mmcd, reads=[kxD, "cmat"], writes=["ps%d" % bcd])
                        yield P.op("act", lambda e: e.activation(
                            out=cdecF[:, :, :], in_=ps[bcd][:, 0:GC * NSEQ].rearrange("p (c j) -> p c j", j=NSEQ),
                            func=AF.Exp), reads=["ps%d" % bcd], writes=["cdecF"])
                    yield P.op("act", lambda e: e.activation(out=dE[:, :], in_=dE[:, :], func=AF.Exp), reads=["dE"],
                         writes=["dE"])
                    if full:
                        yield P.op("act", lambda e: e.activation(out=expa[:, :], in_=acs[:, :], func=AF.Exp),
                             reads=["acs"], writes=[kexpa])
                        for i in range(nbk):
                            v, h_a, h_b = abv(i)
                            yield P.op("dve", lambda e, v=v, h_a=h_a, h_b=h_b: e.tensor_tensor(
                                Rm[:, h_a:h_b, :], v, acs[:, h_a:h_b].unsqueeze(2).broadcast_to([Q, h_b - h_a, Q]),
                                ALU.subtract), reads=["ps%d" % bab[i], "acs", "Rm"], writes=["Rm"])
                        yield P.op("dve", lambda e: e.tensor_scalar(Rm[:, :, :], Rm[:, :, :], 0.0, None, ALU.min),
                             reads=["Rm"], writes=["Rm"])
                        yield P.op("act", lambda e: e.activation(out=Wm[:, :, :], in_=Rm[:, :, :], func=AF.Exp),
                             reads=["Rm"], writes=[kWm])
                        bm = nbp()
                        yield P.op("pe", lambda e: e.matmul(ps[bm][:, 0:Q], lhsT=Bg[:, cs], rhs=Cg[:, cs], start=True,
                                                     stop=True), reads=["Bg", "Cg"], writes=["ps%d" % bm])
                        yield P.op("dve", lambda e: e.tensor_tensor(MTm[:, :], ps[bm][:, 0:Q], TRI, ALU.mult),
                             reads=["ps%d" % bm, "cmat"], writes=["MTm"])
                        yield P.op("dve", lambda e: e.tensor_tensor(
                            Wm[:, :, :], Wm[:, :, :], MTm[:, :].unsqueeze(1).broadcast_to([Q, R, Q]),
                            ALU.mult), reads=["MTm", kWm], writes=[kWm])
                    bx = nbp()

                    def trx(e):
                        for c in range(GC):
                            r = e.transpose(psb[bx][:, c * 128:(c + 1) * 128], xsg[:, c, cs], ident_b)
                        return r
                    yield P.op("pe", trx, reads=[("xsg", c) for c in range(GC)] + ["cmb"], writes=["ps%d" % bx])
                    yield P.op("dve", lambda e: e.tensor_tensor(
                        xdt[:, :].rearrange("q (h p) -> q h p", p=64),
                        psb[bx][:, 0:GW].rearrange("q (h p) -> q h p", p=64),
                        dtt.unsqueeze(2).broadcast_to([Q, R, 64]), ALU.mult),
                         reads=["ps%d" % bx, "dt_all"], writes=[kxdt])
                    if full:
                        yield P.op("dve", lambda e: e.tensor_tensor(
                            xD[:, :].rearrange("q (h p) -> q h p", p=64),
                            psb[bx][:, 0:GW].rearrange("q (h p) -> q h p", p=64),
                            vecs[:, o_dsk + g * R:o_dsk + (g + 1) * R].unsqueeze(2).broadcast_to([Q, R, 64]), ALU.mult),
                             reads=["ps%d" % bx, "vecs", kxD], writes=[kxD])
                    bB = nbp()
                    yield P.op("pe", lambda e: e.transpose(psb[bB][:, 0:128], Bg[:, cs], ident_b), reads=["Bg", "cmb"],
                         writes=["ps%d" % bB])
                    yield P.op("act", lambda e: e.copy(Btok[:, :], psb[bB][:, 0:128]), reads=["ps%d" % bB],
                         writes=[kBtok])
                    yield P.op("dve", lambda e: e.tensor_tensor(
                        xw[:, :].rearrange("q (h p) -> q h p", p=64), xdt[:, :].rearrange("q (h p) -> q h p", p=64),
                        dE[:, :].unsqueeze(2).broadcast_to([Q, R, 64]), ALU.mult), reads=[kxdt, "dE"], writes=[kxw])
                    bi = None
                    if sample:
                        bi = nb()
                        reserved.add(bi)
                        yield P.op("dve", lambda e: e.memset(Cm_t[:, :], 0.0), writes=[cmk, "Btcv0s"])
                        yield P.op("dve", lambda e: e.tensor_copy(Cmdiag, Cg[:, cs].rearrange("p (j r) -> p j r", r=8)),
                             reads=["Cg"], writes=[cmk, "Btcv0s"])
                        for jq in range(NSEQ):
                            s = jq % 2
                            yield P.op("pool", lambda e, jq=jq, s=s: e.dma_start(
                                out=h0[s][:, :, :],
                                in_=ssd_state[jq, g * GW:(g + 1) * GW, :].rearrange("(c p) n -> p c n", p=128)),
                                 writes=["h0_%d" % s], dma="h0_%d" % s)
                            bh = nb()

                            def trh(e, bh=bh, s=s):
                                for c in range(GC):
                                    r = e.transpose(ps[bh][:, c * 128:(c + 1) * 128], h0[s][:, c, :], ident_f)
                                return r
                            yield P.op("pe", trh, reads=["h0_%d" % s, "cmat"], writes=["ps%d" % bh])
                            yield P.op("act", lambda e, bh=bh, s=s: e.copy(h0T[0][:, :], ps[bh][:, 0:GW]), reads=["ps%d" % bh],
                                 writes=["h0T_0"])
                            yield P.op("pe", lambda e, jq=jq, s=s: e.matmul(ps[bi][:, 0:GW], lhsT=Cm_t[:, jq * 128:(jq + 1) * 128], rhs=h0T[0][:, :],
                                                                     start=(jq == 0), stop=(jq == NSEQ - 1)),
                                 reads=[cmk, "Btcv0s", "h0T_0"], writes=["ps%d" % bi])
                            yield P.op("dve", lambda e, jq=jq, s=s: e.tensor_scalar(xwj[0][:, :], xw[:, :], maskJ_f[:, jq:jq + 1],
                                                                            None, ALU.mult),
                                 reads=[kxw, "cmat"], writes=["xwj_0"])
                            bsj = nb()

                            def mmsj(e, bsj=bsj, s=s):
                                for c in range(GC):
                                    r = e.matmul(ps[bsj][:, c * 128:(c + 1) * 128], lhsT=xwj[0][:, c * 128:(c + 1) * 128],
                                                 rhs=Btok[:, :], start=True, stop=True)
                                return r
                            yield P.op("pe", mmsj, reads=["xwj_0", kBtok], writes=["ps%d" % bsj])
                            yield P.op("dve", lambda e, jq=jq, s=s: e.tensor_tensor(
                                sost[s][:, :, :], h0[s][:, :, :],
                                cdecF[:, :, jq].unsqueeze(2).broadcast_to([128, GC, 128]), ALU.mult),
                                 reads=["h0_%d" % s, "cdecF"], writes=["sost_%d" % s])
                            yield P.op("dve", lambda e, bsj=bsj, s=s: e.tensor_tensor(
                                sost[s][:, :, :], sost[s][:, :, :],
                                ps[bsj][:, 0:GW].rearrange("p (c n) -> p c n", n=128), ALU.add),
                                 reads=["sost_%d" % s, "ps%d" % bsj], writes=["sost_%d" % s])
                            yield P.op("sp", lambda e, jq=jq, s=s: e.dma_start(
                                out=st_out_s[jq, g * GW:(g + 1) * GW, :].rearrange("(c p) n -> p c n", p=128),
                                in_=sost[s][:, :, :]), reads=["sost_%d" % s], dma="sost_%d" % s)
                    yield "SPLIT"
                    if full:
                        bz = nbt()

                        def mmz(e):
                            r = None
                            for i in range(GW // XB):
                                for k in range(KC):
                                    r = e.matmul(ps[bz][:, i * XB:(i + 1) * XB], lhsT=xn[:, k, cs], rhs=wz[i][:, k, :],
                                                 start=(k == 0), stop=(k == KC - 1))
                            return r
                        yield P.op("pe", mmz, reads=["xn"] + kz, writes=["ps%d" % bz])
                        if sample:
                            yield P.op("dve", lambda e: e.tensor_tensor(
                                y1[:, :].rearrange("q (h p) -> q h p", p=64),
                                ps[bi][:, 0:GW].rearrange("q (h p) -> q h p", p=64),
                                expa[:, :].unsqueeze(2).broadcast_to([Q, R, 64]), ALU.mult),
                                 reads=["ps%d" % bi, kexpa], writes=["y1"])
                            reserved.discard(bi)
                        by = nbt()

                        def mmy(e):
                            for hh in range(R):
                                r = e.matmul(ps[by][:, hh * 64:(hh + 1) * 64], lhsT=Wm[:, hh, :],
                                             rhs=xdt[:, hh * 64:(hh + 1) * 64], start=True, stop=True)
                            return r
                        yield P.op("pe", mmy, reads=[kWm, kxdt], writes=["ps%d" % by])
                        if not sample:
                            bi = nbt()
                            yield P.op("pe", lambda e: e.matmul(ps[bi][:, 0:GW], lhsT=Cg[:, cs], rhs=Sb[:, :], start=True,
                                                         stop=True), reads=["Cg", "Sb"], writes=["ps%d" % bi])
                            yield P.op("dve", lambda e: e.tensor_tensor(
                                y1[:, :].rearrange("q (h p) -> q h p", p=64),
                                ps[bi][:, 0:GW].rearrange("q (h p) -> q h p", p=64),
                                expa[:, :].unsqueeze(2).broadcast_to([Q, R, 64]), ALU.mult),
                                 reads=["ps%d" % bi, kexpa], writes=["y1"])
                    if not sample:
                        bs = nbt()
                        yield P.op("pe", lambda e: e.matmul(ps[bs][:, 0:GW], lhsT=Btok[:, :], rhs=xw[:, :], start=True, stop=True),
                             reads=[kBtok, kxw], writes=["ps%d" % bs])
                        yield P.op("dve", lambda e: e.tensor_tensor(
                            S[:, :].rearrange("n (h p) -> n h p", p=64), S[:, :].rearrange("n (h p) -> n h p", p=64),
                            cdec[:, :].unsqueeze(2).broadcast_to([128, R, 64]), ALU.mult), reads=["S", kcdec, "Sb"],
                             writes=["S"])
                        yield P.op("dve", lambda e: e.tensor_tensor(S[:, :], S[:, :], ps[bs][:, 0:GW], ALU.add),
                             reads=["S", "ps%d" % bs], writes=["S"])
                        yield P.op("act", lambda e: e.copy(Sb[:, :], S[:, :]), reads=["S"], writes=["Sb"])

                    if full:
                        yield P.op("dve", lambda e: e.tensor_tensor(y1[:, :], y1[:, :], ps[by][:, 0:GW], ALU.add),
                             reads=["y1", "ps%d" % by], writes=["y1"])
                        yield P.op("dve", lambda e: e.tensor_tensor(y1[:, :], y1[:, :], xD[:, :], ALU.add),
                             reads=["y1", kxD], writes=["y1"])
                        yield P.op("act", lambda e: e.activation(out=xD[:, :], in_=ps[bz][:, 0:GW], func=AF.Tanh, scale=0.5),
                             reads=["ps%d" % bz, kxD], writes=[kxD])
                        yield P.op("dve", lambda e: e.scalar_tensor_tensor(xD[:, :], xD[:, :], 1.0, ps[bz][:, 0:GW], ALU.add,
                                                                    ALU.mult),
                             reads=[kxD, "ps%d" % bz], writes=[kxD])
                        yield P.op("dve", lambda e: e.scalar_tensor_tensor(y1[:, :], y1[:, :], 0.5, xD[:, :], ALU.mult, ALU.mult),
                             reads=["y1", kxD], writes=["y1"])
                        yield P.op("act", lambda e: e.activation(out=xD[:, :], in_=y1[:, :], func=AF.Square,
                                                          accum_out=ss[:, 0:1]), reads=["y1", kxD],
                             writes=[kxD, "ss"])
                        yield P.op("dve", lambda e: e.tensor_scalar(ss[:, 1:2], ss[:, 0:1], 1.0 / GW, EPS, ALU.mult, ALU.add),
                             reads=["ss"], writes=["ss"])
                        yield P.op("pool", lambda e: e.tensor_tensor(ss[:, 1:2], ss[:, 1:2], mhalf[:, 0:1], ALU.pow),
                             reads=["ss"], writes=["ss"])
                        yield P.op("dve", lambda e: e.tensor_scalar(gyn[:, :], y1[:, :], ss[:, 1:2], None, ALU.mult),
                             reads=["y1", "ss"], writes=["gyn"])
                        bt = nbt()

                        def trg(e):
                            for c in range(GC):
                                r = e.transpose(psb[bt][:, c * 128:c * 128 + Q], gyn[:, c * 128:(c + 1) * 128], ident_b)
                            return r
                        yield P.op("pe", trg, reads=["gyn", "cmb"], writes=["ps%d" % bt])
                        o_ng = cfg.voff["ng"][0]
                        for c in range(GC):
                            yield P.op("dve" if c % 2 == 0 else "act",
                                 (lambda e, c=c: e.tensor_scalar(ygn[:, c, cs], psb[bt][:, c * 128:c * 128 + Q],
                                                                 vecs[:, o_ng + g * GC + c:o_ng + g * GC + c + 1], None,
                                                                 ALU.mult)) if c % 2 == 0 else
                                 (lambda e, c=c: e.activation(out=ygn[:, c, cs], in_=psb[bt][:, c * 128:c * 128 + Q],
                                                              func=AF.Copy,
                                                              scale=vecs[:, o_ng + g * GC + c:o_ng + g * GC + c + 1])),
                                 reads=["ps%d" % bt, "vecs"], writes=[("ygn", c)])
                def run_pipelined(gens):
                    for v in gens[0]:
                        if v == "SPLIT":
                            break
                    for q in range(len(gens)):
                        ga = gens[q]
                        gb = gens[q + 1] if q + 1 < len(gens) else None
                        a_done, b_done = False, gb is None
                        while not (a_done and b_done):
                            if not a_done:
                                try:
                                    next(ga)
                                except StopIteration:
                                    a_done = True
                            if not b_done:
                                try:
                                    if next(gb) == "SPLIT":
                                        b_done = True
                                except StopIteration:
                                    b_done = True

                def state_out(dst_ap_fn):
                    bo = nb()

                    def tro(e):
                        for c in range(GC):
                            r = e.transpose(ps[bo][:, c * 128:(c + 1) * 128], S[:, c * 128:(c + 1) * 128], ident_f)
                        return r
                    P.op("pe", tro, reads=["S", "cmat"], writes=["ps%d" % bo])
                    P.op("act", lambda e: e.copy(sost[0][:, :, :], ps[bo][:, 0:GW].rearrange("p (c n) -> p c n", n=128)),
                         reads=["ps%d" % bo], writes=["sost_0"])
                    P.op("sp", lambda e: e.dma_start(out=dst_ap_fn(), in_=sost[0][:, :, :]), reads=["sost_0"],
                         dma="sost_0")

                sidx = [0]
                for g in range(G):
                    specs = []
                    for i in range(GW // XB):
                        col = DI + g * GW + i * XB
                        for j in range(XB // 128):
                            c = i * (XB // 128) + j
                            specs.append((ssd_w_in[:, col:col + XB], XB, j, g * GC + c, ("x", c), i))
                    colB = 2 * DI + g * 128
                    specs.append((ssd_w_in[:, colB:colB + 128], 128, 0, DI // 128 + g, ("B", 0), "B"))
                    colC = 2 * DI + GN + g * 128
                    specs.append((ssd_w_in[:, colC:colC + 128], 128, 0, DI // 128 + G + g, ("C", 0), "C"))
                    loaded = {}

                    def ensure(k):
                        if k < len(specs) and specs[k][5] not in loaded:
                            loaded[specs[k][5]] = wload(specs[k][0], KC, specs[k][1])
                    LOOK = 3
                    for k in range(LOOK):
                        ensure(k)
                    for idx in range(len(specs)):
                        ensure(idx + LOOK)
                        _, _, j, cidx, (kind, c), lk = specs[idx]
                        wv_, kv_ = loaded[lk]
                        sl = sidx[0] % 2
                        conv_state_io("B", pre, pre_s, sl, 4, ssdc_state, cidx, None, None, stg_in, None, None, 0)
                        for (c0, n) in tts:
                            b = nb()
                            mm_fm(b, wv_, kv_, KC, j, xn, ["xn"], c0, n)
                            pp, sp_ = split_cols(c0, n)
                            if pp:
                                a, b2 = pp
                                P.op("act", lambda e, a=a, b2=b2, b=b, c0=c0, sl=sl: e.copy(pre[:, sl, 3 + a:3 + b2],
                                                                                   ps[b][:, a - c0:b2 - c0]),
                                     reads=["ps%d" % b], writes=["Bpre%d" % sl])
                            if sp_:
                                a, b2 = sp_
                                P.op("dve", lambda e, a=a, b2=b2, b=b, c0=c0, sl=sl: e.tensor_copy(
                                    pre_s[:, sl, :, 3:11], ps[b][:, a - c0:b2 - c0].rearrange("p (s t) -> p s t", t=8)),
                                     reads=["ps%d" % b], writes=["Bpres%d" % sl])
                        conv_state_out("B", pre, pre_s, sl, 4, cidx, ssdc_out_p, ssdc_out_s, stg_out, cost,
                                       ["Bpre%d" % sl, "Bpres%d" % sl, "Bpres%dh" % sl], 0)
                        kt = conv_block2("B", pre, pre_s, tcv, sl, 4, ["cw0", "cw1", "cw2", "cw3"], cidx,
                                         ["Bpre%d" % sl, "Bpre%dz" % sl], ["Bpres%d" % sl, "Bpres%dh" % sl])
                        if kind == "x":
                            dst, dk = xsg[:, c, :], ("xsg", c)
                        elif kind == "B":
                            dst, dk = Bg[:, :], "Bg"
                        else:
                            dst, dk = Cg[:, :], "Cg"
                        P.op("act", lambda e, dst=dst, cidx=cidx: e.activation(out=dst, in_=tcv[:, 0, :], func=AF.Silu,
                                                                             bias=V("cb", cidx)),
                             reads=kt + ["vecs"], writes=[dk])
                        sidx[0] += 1
                    wz, kz = [], []
                    for i in range(GW // XB):
                        col = g * GW + i * XB
                        w_, k_ = wload(ssd_w_in[:, col:col + XB], KC, XB)
                        wz.append(w_)
                        kz.append(k_)
                    P.op("dve", lambda e: e.memset(S[:, :], 0.0), reads=["Sb"], writes=["S"])
                    run_pipelined([chunk(g, q, False, wz, kz, pp=q % 2, nbp=nb_prep, nbt=nb_tail) for q in range(NQ)])
                    ibk, obk = "ib%d" % g, "ob%d" % g
                    P.op("sp", lambda e, g=g: e.dma_start(out=ibs[g][:, :], in_=S[:, :]), reads=["S"], writes=[ibk],
                         dma="xch")
                    P.op("pool", lambda e, g=g: e.collective_compute(
                        "AllGather", ALU.bypass, replica_groups=[[0, 1], [2, 3], [4, 5], [6, 7]],
                        ins=[ibs[g].ap().opt()], outs=[obs[g].ap().opt()]), reads=[ibk], writes=[obk],
                         dma="cc%d" % g, inc=1)
                    for _ in chunk(g, NQ, True, wz, kz, sample=True, pp=0):
                        pass
                    P.op("sp", lambda e, g=g: e.dma_start(out=S[:, :], in_=obs[g][0:128, :]), reads=[obk, "Sb"],
                         writes=["S"], dma="xch")
                    P.op("dve", lambda e: e.tensor_scalar(S[:, :], S[:, :], mko[:, 0:1], None, ALU.mult),
                         reads=["S", "mko"], writes=["S"])
                    P.op("act", lambda e: e.copy(Sb[:, :], S[:, :]), reads=["S"], writes=["Sb"])
                    run_pipelined([chunk(g, q, True, wz, kz, pp=q % 2, nbp=nb_prep, nbt=nb_tail) for q in range(NQ)])
                    state_out(lambda g=g: st_out_p[g * GW:(g + 1) * GW, :].rearrange("(c p) n -> p c n", p=128))
                    for mcp in range(D // 256):
                        wo, ko = wload(ssd_w_out[g * GW:(g + 1) * GW, mcp * 256:mcp * 256 + 256], GC, 256)
                        for j in range(2):
                            mc = mcp * 2 + j
                            for (c0, n) in tts:
                                b = nb()
                                mm_fm(b, wo, ko, GC, j, ygn, [("ygn", k) for k in range(GC)], c0, n)
                                acc_into_h(b, mc, c0, n)
                P.end_phase()

        def out_phase():
            with ExitStack() as ph:
                yst = [sb("ystg%d" % i, [128, D], F32, ph) for i in range(2)]
                for ri, (r0, n) in enumerate(rts):
                    s = ri % 2
                    for cg in range(KC // 4):
                        b = nb()

                        def tr(e, n=n, cg=cg, b=b, r0=r0):
                            for j in range(4):
                                c = cg * 4 + j
                                r = e.transpose(ps[b][0:n, j * 128:(j + 1) * 128], h[:, c, r0:r0 + n], ident_f)
                            return r
                        P.op("pe", tr, reads=["h", "cmat"], writes=["ps%d" % b])
                        if cg % 2 == 0:
                            P.op("dve", lambda e, s=s, n=n, cg=cg, b=b: e.tensor_copy(yst[s][0:n, cg * 512:(cg + 1) * 512],
                                                                                    ps[b][0:n, :]),
                                 reads=["ps%d" % b], writes=[("yst%d" % s, cg)])
                        else:
                            P.op("act", lambda e, s=s, n=n, cg=cg, b=b: e.copy(yst[s][0:n, cg * 512:(cg + 1) * 512],
                                                                             ps[b][0:n, :]),
                                 reads=["ps%d" % b], writes=[("yst%d" % s, cg)])
                    P.op("sp", lambda e, s=s, r0=r0, n=n: e.dma_start(out=y_out[r0:r0 + n, :], in_=yst[s][0:n, :]),
                         reads=[("yst%d" % s, cg) for cg in range(KC // 4)], dma="yst%d" % s)
                P.end_phase()

        phases = [lambda: rmsnorm_phase("g_mix0"), l0_mixer_phase,
                  lambda: rmsnorm_phase("g_ffn0"), lambda: ffn_phase(0),
                  lambda: rmsnorm_phase("g_ple0"), lambda: ple_phase(0),
                  lambda: rmsnorm_phase("g_mix1"), ssd_phase,
                  lambda: rmsnorm_phase("g_ffn1"), lambda: ffn_phase(1),
                  lambda: rmsnorm_phase("g_ple1"), lambda: ple_phase(1),
                  lambda: rmsnorm_phase("g_final", final=True)]
        for pi, phf in enumerate(phases):
            if stop is not None and pi >= stop:
                break
            phf()
        out_phase()
        for k, v in P.cnt.items():
            if k[0] == "dma" and P.waited["sp"].get(k, 0) < v:
                nc.sync.wait_ge(P.getsem(k), v)
        print("ops", len(P.ops), "waits", P.nwait, "sems", len(P.sems), file=sys.stderr)
    return nc


def host_inputs(cfg, inp):
    D, KC, T, TP, NSEQ = cfg.D, cfg.KC, cfg.T, cfg.TP, cfg.NSEQ
    f = np.float32

    def pm(v):
        v = np.asarray(v, f)
        return np.ascontiguousarray(v.reshape(-1, 128).T)

    def bc(v):
        v = np.asarray(v, f).reshape(1, -1)
        return np.ascontiguousarray(np.broadcast_to(v, (128, v.shape[1])))
    vec = np.zeros((128, cfg.NV), f)

    def put(nm, a):
        o, w = cfg.voff[nm]
        assert a.shape == (128, w), (nm, a.shape, w)
        vec[:, o:o + w] = a
    for l in range(2):
        put("g_mix%d" % l, pm(inp["g_mix"][l]))
        put("g_ffn%d" % l, pm(inp["g_ffn"][l]))
        put("g_ple%d" % l, pm(inp["g_ple"][l]))
    put("g_final", pm(inp["g_final"]))
    for k in range(3):
        put("scw%d" % k, pm(inp["sc_w_conv"][0, k]))
    for k in range(4):
        put("cw%d" % k, pm(inp["ssd_conv_w"][0, k]))
    put("cb", pm(inp["ssd_conv_b"][0]))
    put("ng", pm(inp["ssd_norm_g"][0]))
    put("dtb", bc(inp["ssd_dt_bias"][0]))
    put("alog", bc(inp["ssd_a_log"][0]))
    put("dsk", bc(inp["ssd_d"][0]))
    cm = np.zeros((128, 640 + NSEQ), f)
    cm[:, 0:128] = np.eye(128, dtype=f)
    cm[:, 128:256] = np.triu(np.ones((128, 128), f))
    cm[:, 256:384] = 1.0
    sid = np.arange(128) // 8
    same = (sid[:, None] == sid[None, :]).astype(f)
    cm[:, 384:512] = cm[:, 128:256] * same
    cm[:, 512:640] = same
    cm[:, 640:640 + NSEQ] = (sid[:, None] == np.arange(NSEQ)[None, :]).astype(f)
    mt = np.zeros((128, NSEQ, 128), f)
    mt[:, sid, np.arange(128)] = 1.0
    shared = dict(vecs=vec, cmat=cm,
                  sc_w_in=np.ascontiguousarray(inp["sc_w_in"][0]), sc_w_out=np.ascontiguousarray(inp["sc_w_out"][0]),
                  ssd_w_in=np.ascontiguousarray(inp["ssd_w_in"][0]), ssd_w_out=np.ascontiguousarray(inp["ssd_w_out"][0]),
                  ffn_w_gate=np.ascontiguousarray(inp["ffn_w_gate"]), ffn_w_up=np.ascontiguousarray(inp["ffn_w_up"]),
                  ffn_w_down=np.ascontiguousarray(inp["ffn_w_down"]), ple_w_proj=np.ascontiguousarray(inp["ple_w_proj"]),
                  ple_w_gate=np.ascontiguousarray(inp["ple_w_gate"]))
    maps = []
    for core in range(8):
        b, half = core // 2, core % 2
        xin = np.zeros((T, D), f)
        pin = np.zeros((2, T, cfg.PLE), f)
        t0 = half * TP
        if half == 1:
            xin[0:HALO] = inp["x_prompt"][b, t0 - HALO:t0]
            pin[:, 0:HALO] = inp["p_prompt"][:, b, t0 - HALO:t0]
        xin[HALO:HALO + TP] = inp["x_prompt"][b, t0:t0 + TP]
        pin[:, HALO:HALO + TP] = inp["p_prompt"][:, b, t0:t0 + TP]
        sl = slice(core * NSEQ, (core + 1) * NSEQ)
        xin[HALO + TP:] = inp["x_sample"][sl].reshape(-1, D)
        pin[:, HALO + TP:] = inp["p_sample"][:, sl].reshape(2, -1, cfg.PLE)
        m = dict(shared)
        m.update(xin=xin, pin=pin,
                 sc_state=np.ascontiguousarray(inp["state_sc_conv"][0, sl].reshape(NSEQ * 2, D)),
                 ssdc_state=np.ascontiguousarray(inp["state_ssd_conv"][0, sl].reshape(NSEQ * 3, cfg.CONV)),
                 ssd_state=np.ascontiguousarray(inp["state_ssd"][0, sl].reshape(NSEQ, cfg.H * 64, 128)),
                 maskodd=np.full((128, 1), float(half), f))
        maps.append(m)
    return maps


def assemble(cfg, res):
    D, TP, NSEQ, H = cfg.D, cfg.TP, cfg.NSEQ, cfg.H
    f = np.float32
    B = 4
    y_p = np.zeros((B, 2 * TP, D), f)
    y_s = np.zeros((8 * NSEQ, cfg.DEC_SEQ, D), f)
    scp = np.zeros((1, B, 2, D), f)
    scs = np.zeros((1, 8 * NSEQ, 2, D), f)
    ssdcp = np.zeros((1, B, 3, cfg.CONV), f)
    ssdcs = np.zeros((1, 8 * NSEQ, 3, cfg.CONV), f)
    stp = np.zeros((1, B, H, 64, 128), f)
    sts = np.zeros((1, 8 * NSEQ, H, 64, 128), f)
    for core in range(8):
        r = res[core]
        b, half = core // 2, core % 2
        y = r["y"]
        y_p[b, half * TP:(half + 1) * TP] = y[HALO:HALO + TP]
        sl = slice(core * NSEQ, (core + 1) * NSEQ)
        y_s[sl] = y[HALO + TP:].reshape(NSEQ, cfg.DEC_SEQ, D)
        scs[0, sl] = r["sc_out_s"].reshape(NSEQ, 2, D)
        ssdcs[0, sl] = r["ssdc_out_s"].reshape(NSEQ, 3, cfg.CONV)
        sts[0, sl] = r["st_out_s"].reshape(NSEQ, H, 64, 128)
        if half == 1:
            scp[0, b] = r["sc_out_p"]
            ssdcp[0, b] = r["ssdc_out_p"]
            stp[0, b] = r["st_out_p"].reshape(H, 64, 128)
    return (y_p, y_s, scp, scs, ssdcp, ssdcs, stp, sts)


_NC_CACHE = {}


def run(cfg, inp, stop=None):
    key = (cfg.D, cfg.SEQ, stop)
    if key not in _NC_CACHE:
        _NC_CACHE[key] = build(cfg, stop)
    nc = _NC_CACHE[key]
    maps = host_inputs(cfg, inp)
    res = run_bass_kernel_spmd(nc, maps, core_ids=list(range(8)))
    return assemble(cfg, res.results)


def kernel(**inputs):
    cfg = Cfg()
    inp = {k: np.asarray(v) for k, v in inputs.items()}
    return run(cfg, inp)
```
